# Optimizing a Trainium2 kernel written in Bass

```python
import math
import jax, jax.numpy as jnp
from jax import lax
import numpy as np

D_MODEL = 1024
BATCH = 16
SEQ = 256
DEPTH = 2
DEC_BATCH = 2
DEC_SEQ = 2048
PAST_LEN = 512

GRID_W = 64
N_EVEN = (DEPTH + 1) // 2
N_ODD = DEPTH // 2
D_FF = 2816
N_MOD = 9
A_HEADS = 4
A_DK = 128
A_DV = 128
A_W = A_HEADS * A_DK
B_HEADS = 4
B_DK = 64
B_DV = 128
B_QK = B_HEADS * B_DK
B_V = B_HEADS * B_DV
GATE_RANK = 16
GLA_TAU = 16.0
CHUNK = 64
AB_SPLIT = (A_W, A_W, A_W, A_W, A_W, B_QK, B_QK, B_V, B_V, GATE_RANK, GATE_RANK)
AB_IN = sum(AB_SPLIT)
AB_OUT = A_W + B_V
C_HEADS = 16
C_KV_HEADS = 4
C_GROUPS = C_HEADS // C_KV_HEADS
C_HEAD_DIM = 64
C_Q = C_HEADS * C_HEAD_DIM
C_KV = C_KV_HEADS * C_HEAD_DIM
C_SPLIT = (C_Q, C_KV, C_KV)
C_IN = C_Q + 2 * C_KV
WINDOW = 128
BLOCK = 128
ROPE_AXIS_DIM = C_HEAD_DIM // 2
ROPE_FREQS = ROPE_AXIS_DIM // 2
ROPE_BASE = 10000.0
ALPHA = (2.0 * DEPTH) ** 0.25
BETA = (8.0 * DEPTH) ** -0.25
LN_EPS = 1e-5
RMS_EPS = 1e-6
NEG_BIG = -1e30

kernel_name = "hybrid_hgrn2_gla_swa_prefix_dit_step"


def split_cols(p, sizes):
    offs = []
    acc = 0
    for s in sizes[:-1]:
        acc += s
        offs.append(acc)
    return jnp.split(p, offs, axis=-1)


def layer_norm(x, g, b):
    xf = x.astype(jnp.float32)
    mu = jnp.mean(xf, axis=-1, keepdims=True)
    var = jnp.mean(jnp.square(xf - mu), axis=-1, keepdims=True)
    return ((xf - mu) * lax.rsqrt(var + LN_EPS) * g.astype(jnp.float32) + b.astype(jnp.float32)).astype(x.dtype)


def rms_norm_heads(o, w):
    of = o.astype(jnp.float32)
    return of * lax.rsqrt(jnp.mean(of * of, axis=-1, keepdims=True) + RMS_EPS) * w.astype(jnp.float32)


def modulate(x, shift, scale):
    return x * (1 + scale[..., None, :]) + shift[..., None, :]


def swiglu(h, w1, w3, w2):
    return (jax.nn.silu(h @ w1) * (h @ w3)) @ w2


def chunk_gla(q, k, v, log_a, s0):
    bsz, t, h, _ = q.shape
    dv = v.shape[-1]
    n = t // CHUNK

    def blocks(z):
        return jnp.moveaxis(z.astype(jnp.float32).reshape(bsz, n, CHUNK, h, z.shape[-1]), 1, 0)

    qc, kc, vc, ac = blocks(q), blocks(k), blocks(v), blocks(log_a)
    causal = jnp.tril(jnp.ones((CHUNK, CHUNK), dtype=bool))

    def step(s, inp):
        qb, kb, vb, ab = inp
        bcum = jnp.cumsum(ab, axis=1)
        blast = bcum[:, -1:]
        q_dec = qb * jnp.exp(bcum)
        k_inv = kb * jnp.exp(-bcum)
        att = jnp.where(causal, jnp.einsum('blhd,bmhd->bhlm', q_dec, k_inv), 0.0)
        o = jnp.einsum('bhlm,bmhe->blhe', att, vb) + jnp.einsum('blhd,bhde->blhe', q_dec, s)
        k_upd = kb * jnp.exp(blast - bcum)
        s_new = jnp.exp(blast[:, 0])[..., None] * s + jnp.einsum('blhd,blhe->bhde', k_upd, vb)
        return s_new, o

    s_fin, o = lax.scan(step, s0.astype(jnp.float32), (qc, kc, vc, ac))
    return jnp.moveaxis(o, 0, 1).reshape(bsz, t, h, dv), s_fin


def bidir_scan(q, k_f, k_b, v, la_f, la_b, s0_f, s0_b):
    o_f, s_f = chunk_gla(q, k_f, v, la_f, s0_f)
    flip = lambda z: jnp.flip(z, axis=1)
    o_b, s_b = chunk_gla(flip(q), flip(k_b), flip(v), flip(la_b), s0_b)
    return o_f + flip(o_b), s_f, s_b


def mixer_ab(h, s0_h, s0_g, lb, w_in, gate_up, gate_b, norm_a, norm_b, w_out):
    bsz, t, _ = h.shape
    p = h @ w_in
    aq, ai, aff, afb, ag, bq, bk, bv, bg, bzf, bzb = split_cols(p, AB_SPLIT)
    heads = lambda z, nh: z.reshape(bsz, t, nh, -1)
    f_f = lb[0] + (1.0 - lb[0]) * jax.nn.sigmoid(aff.astype(jnp.float32))
    f_b = lb[1] + (1.0 - lb[1]) * jax.nn.sigmoid(afb.astype(jnp.float32))
    o_a, sa_f, sa_b = bidir_scan(
        heads(aq, A_HEADS), heads(1.0 - f_f, A_HEADS), heads(1.0 - f_b, A_HEADS),
        heads(jax.nn.silu(ai), A_HEADS), heads(jnp.log(f_f), A_HEADS), heads(jnp.log(f_b), A_HEADS),
        s0_h[:, 0], s0_h[:, 1])
    o_a = rms_norm_heads(o_a, norm_a) * jax.nn.silu(heads(ag, A_HEADS).astype(jnp.float32))
    o_a = o_a.reshape(bsz, t, A_W).astype(h.dtype)
    la_f = jax.nn.log_sigmoid((bzf @ gate_up[0] + gate_b[0]).astype(jnp.float32)) / GLA_TAU
    la_b = jax.nn.log_sigmoid((bzb @ gate_up[1] + gate_b[1]).astype(jnp.float32)) / GLA_TAU
    k_g = heads(bk, B_HEADS)
    o_b, sb_f, sb_b = bidir_scan(
        heads(bq, B_HEADS) * (B_DK ** -0.5), k_g, k_g, heads(bv, B_HEADS),
        heads(la_f, B_HEADS), heads(la_b, B_HEADS), s0_g[:, 0], s0_g[:, 1])
    o_b = rms_norm_heads(o_b, norm_b) * jax.nn.silu(heads(bg, B_HEADS).astype(jnp.float32))
    o_b = o_b.reshape(bsz, t, B_V).astype(h.dtype)
    y = jnp.concatenate([o_a, o_b], axis=-1) @ w_out
    s_h = jnp.stack([sa_f, sa_b], axis=1).astype(h.dtype)
    s_g = jnp.stack([sb_f, sb_b], axis=1).astype(h.dtype)
    return y, s_h, s_g


def rope_2d(x):
    t = x.shape[1]
    rows = t // GRID_W
    row = jnp.repeat(jnp.arange(rows), GRID_W)
    col = jnp.tile(jnp.arange(GRID_W), rows)
    inv = ROPE_BASE ** (-jnp.arange(ROPE_FREQS, dtype=jnp.float32) / ROPE_FREQS)
    bshape = (1, t) + (1,) * (x.ndim - 3) + (ROPE_FREQS,)

    def rot(z, pos):
        ang = (pos.astype(jnp.float32)[:, None] * inv[None, :]).reshape(bshape)
        cos, sin = jnp.cos(ang), jnp.sin(ang)
        z = z.astype(jnp.float32)
        z1, z2 = z[..., :ROPE_FREQS], z[..., ROPE_FREQS:]
        return jnp.concatenate([z1 * cos - z2 * sin, z1 * sin + z2 * cos], axis=-1)

    out = jnp.concatenate([rot(x[..., :ROPE_AXIS_DIM], row), rot(x[..., ROPE_AXIS_DIM:], col)], axis=-1)
    return out.astype(x.dtype)


def attn_ctx(h, w_qkv, sink):
    bsz, t, _ = h.shape
    q, k, v = split_cols(h @ w_qkv, C_SPLIT)
    q = q.reshape(bsz, t, C_KV_HEADS, C_GROUPS, C_HEAD_DIM)
    k = k.reshape(bsz, t, C_KV_HEADS, C_HEAD_DIM)
    v = v.reshape(bsz, t, C_KV_HEADS, C_HEAD_DIM)
    s = jnp.einsum('bqkgd,bckd->bkgqc', q, k).astype(jnp.float32) * (C_HEAD_DIM ** -0.5)
    sk = jnp.broadcast_to(sink.astype(jnp.float32).reshape(1, C_KV_HEADS, C_GROUPS, 1, 1), s.shape[:-1] + (1,))
    p = jax.nn.softmax(jnp.concatenate([s, sk], axis=-1), axis=-1)[..., :-1].astype(v.dtype)
    o = jnp.einsum('bkgqc,bckd->bqkgd', p, v).reshape(bsz, t, C_Q)
    return o, k, v


def attn_latent(h, k_ctx, v_ctx, w_qkv, sink):
    bsz, t, _ = h.shape
    q, k, v = split_cols(h @ w_qkv, C_SPLIT)
    q = rope_2d(q.reshape(bsz, t, C_KV_HEADS, C_GROUPS, C_HEAD_DIM))
    k = rope_2d(k.reshape(bsz, t, C_KV_HEADS, C_HEAD_DIM))
    v = v.reshape(bsz, t, C_KV_HEADS, C_HEAD_DIM)
    nb = t // BLOCK
    qb = q.reshape(bsz, nb, BLOCK, C_KV_HEADS, C_GROUPS, C_HEAD_DIM)
    pad = ((0, 0), (BLOCK, BLOCK), (0, 0), (0, 0))

    def windows(z):
        zp = jnp.pad(z, pad).reshape(bsz, nb + 2, BLOCK, C_KV_HEADS, C_HEAD_DIM)
        return jnp.concatenate([zp[:, :-2], zp[:, 1:-1], zp[:, 2:]], axis=2)

    kwin, vwin = windows(k), windows(v)
    qpos = jnp.arange(nb)[:, None, None] * BLOCK + jnp.arange(BLOCK)[None, :, None]
    kpos = jnp.arange(nb)[:, None, None] * BLOCK - BLOCK + jnp.arange(3 * BLOCK)[None, None, :]
    mask = (jnp.abs(qpos - kpos) <= WINDOW) & (kpos >= 0) & (kpos < t)
    scale = C_HEAD_DIM ** -0.5
    s_loc = jnp.einsum('bnlkgd,bnmkd->bkgnlm', qb, kwin).astype(jnp.float32) * scale
    s_loc = jnp.where(mask, s_loc, NEG_BIG)
    s_ctx = jnp.einsum('bnlkgd,bckd->bkgnlc', qb, k_ctx).astype(jnp.float32) * scale
    sk = jnp.broadcast_to(sink.astype(jnp.float32).reshape(1, C_KV_HEADS, C_GROUPS, 1, 1, 1), s_loc.shape[:-1] + (1,))
    p = jax.nn.softmax(jnp.concatenate([s_loc, s_ctx, sk], axis=-1), axis=-1).astype(v.dtype)
    p_loc = p[..., :3 * BLOCK]
    p_ctx = p[..., 3 * BLOCK:-1]
    o = jnp.einsum('bkgnlm,bnmkd->bnlkgd', p_loc, vwin) + jnp.einsum('bkgnlc,bckd->bnlkgd', p_ctx, v_ctx)
    return o.reshape(bsz, t, C_Q)


def ffn_sublayer(x, shift, scale, gate, w1, w3, w2, g, b):
    y = swiglu(modulate(x, shift, scale), w1, w3, w2)
    return layer_norm(ALPHA * x + 0.5 * gate[..., None, :] * y, g, b)


def setup_inputs(seed: int = 0) -> dict:
    key = jax.random.key(seed)
    ks = jax.random.split(key, 32)
    nrm = lambda k, shape, s: jax.random.normal(k, shape, jnp.float32) * s
    return {
        "x_prompt": nrm(ks[0], (BATCH, SEQ, D_MODEL), 1.0),
        "x_sample": nrm(ks[1], (DEC_BATCH, DEC_SEQ, D_MODEL), 1.0),
        "state_hgrn": nrm(ks[2], (DEC_BATCH, N_EVEN, 2, A_HEADS, A_DK, A_DV), 0.5),
        "state_gla": nrm(ks[3], (DEC_BATCH, N_EVEN, 2, B_HEADS, B_DK, B_DV), 0.5),
        "cache_k": nrm(ks[4], (DEC_BATCH, N_ODD, PAST_LEN, C_KV_HEADS, C_HEAD_DIM), 1.0),
        "cache_v": nrm(ks[5], (DEC_BATCH, N_ODD, PAST_LEN, C_KV_HEADS, C_HEAD_DIM), 1.0),
        "c": nrm(ks[6], (DEC_BATCH, D_MODEL), 1.0),
        "c_ctx": nrm(ks[7], (D_MODEL,), 1.0),
        "w_mod": nrm(ks[8], (DEPTH, D_MODEL, N_MOD * D_MODEL), D_MODEL ** -0.5),
        "b_mod": nrm(ks[9], (DEPTH, N_MOD * D_MODEL), 0.02),
        "ln_g": 1.0 + nrm(ks[10], (DEPTH, 3, D_MODEL), 0.02),
        "ln_b": nrm(ks[11], (DEPTH, 3, D_MODEL), 0.02),
        "ffn_w1": nrm(ks[12], (DEPTH, 2, D_MODEL, D_FF), D_MODEL ** -0.5),
        "ffn_w3": nrm(ks[13], (DEPTH, 2, D_MODEL, D_FF), D_MODEL ** -0.5),
        "ffn_w2": nrm(ks[14], (DEPTH, 2, D_FF, D_MODEL), BETA * D_FF ** -0.5),
        "w_in_ab": nrm(ks[15], (N_EVEN, D_MODEL, AB_IN), D_MODEL ** -0.5),
        "hgrn_lb": nrm(ks[16], (2, N_EVEN + 1, A_W), 0.1),
        "gla_gate_up": nrm(ks[17], (N_EVEN, 2, GATE_RANK, B_QK), GATE_RANK ** -0.5),
        "gla_gate_b": nrm(ks[18], (N_EVEN, 2, B_QK), 0.1),
        "norm_a": 1.0 + nrm(ks[19], (N_EVEN, A_DV), 0.02),
        "norm_b": 1.0 + nrm(ks[20], (N_EVEN, B_DV), 0.02),
        "w_out_ab": nrm(ks[21], (N_EVEN, AB_OUT, D_MODEL), BETA * AB_OUT ** -0.5),
        "w_qkv_c": nrm(ks[22], (N_ODD, D_MODEL, C_IN), D_MODEL ** -0.5),
        "sink_c": nrm(ks[23], (N_ODD, C_HEADS), 0.5),
        "w_out_c": nrm(ks[24], (N_ODD, C_Q, D_MODEL), BETA * C_Q ** -0.5),
    }


def reference(x_prompt, x_sample, state_hgrn, state_gla, cache_k, cache_v, c, c_ctx,
              w_mod, b_mod, ln_g, ln_b, ffn_w1, ffn_w3, ffn_w2,
              w_in_ab, hgrn_lb, gla_gate_up, gla_gate_b, norm_a, norm_b, w_out_ab,
              w_qkv_c, sink_c, w_out_c):
    lb_all = jnp.cumsum(jax.nn.softmax(hgrn_lb.astype(jnp.float32), axis=1), axis=1)
    xp, xs = x_prompt, x_sample
    bp = x_prompt.shape[0]
    new_hgrn, new_gla, new_k, new_v = [], [], [], []
    for layer in range(DEPTH):
        mp = jnp.split(jax.nn.silu(c_ctx) @ w_mod[layer] + b_mod[layer], N_MOD, axis=-1)
        ms = jnp.split(jax.nn.silu(c) @ w_mod[layer] + b_mod[layer], N_MOD, axis=-1)
        xp = ffn_sublayer(xp, mp[0], mp[1], mp[2], ffn_w1[layer, 0], ffn_w3[layer, 0], ffn_w2[layer, 0], ln_g[layer, 0], ln_b[layer, 0])
        xs = ffn_sublayer(xs, ms[0], ms[1], ms[2], ffn_w1[layer, 0], ffn_w3[layer, 0], ffn_w2[layer, 0], ln_g[layer, 0], ln_b[layer, 0])
        hp = modulate(xp, mp[3], mp[4])
        hs = modulate(xs, ms[3], ms[4])
        if layer % 2 == 0:
            e = layer // 2
            lb = lb_all[:, e]
            z_h = jnp.zeros((bp, 2, A_HEADS, A_DK, A_DV), xp.dtype)
            z_g = jnp.zeros((bp, 2, B_HEADS, B_DK, B_DV), xp.dtype)
            yp, s_h, s_g = mixer_ab(hp, z_h, z_g, lb, w_in_ab[e], gla_gate_up[e], gla_gate_b[e], norm_a[e], norm_b[e], w_out_ab[e])
            ys, _, _ = mixer_ab(hs, state_hgrn[:, e], state_gla[:, e], lb, w_in_ab[e], gla_gate_up[e], gla_gate_b[e], norm_a[e], norm_b[e], w_out_ab[e])
            new_hgrn.append(s_h)
            new_gla.append(s_g)
        else:
            o = layer // 2
            op, k_p, v_p = attn_ctx(hp, w_qkv_c[o], sink_c[o])
            os_ = attn_latent(hs, cache_k[:, o], cache_v[:, o], w_qkv_c[o], sink_c[o])
            yp = op @ w_out_c[o]
            ys = os_ @ w_out_c[o]
            new_k.append(k_p)
            new_v.append(v_p)
        xp = layer_norm(ALPHA * xp + mp[5][..., None, :] * yp, ln_g[layer, 1], ln_b[layer, 1])
        xs = layer_norm(ALPHA * xs + ms[5][..., None, :] * ys, ln_g[layer, 1], ln_b[layer, 1])
        xp = ffn_sublayer(xp, mp[6], mp[7], mp[8], ffn_w1[layer, 1], ffn_w3[layer, 1], ffn_w2[layer, 1], ln_g[layer, 2], ln_b[layer, 2])
        xs = ffn_sublayer(xs, ms[6], ms[7], ms[8], ffn_w1[layer, 1], ffn_w3[layer, 1], ffn_w2[layer, 1], ln_g[layer, 2], ln_b[layer, 2])
    return (xp, xs, jnp.stack(new_hgrn, axis=1), jnp.stack(new_gla, axis=1), jnp.stack(new_k, axis=1), jnp.stack(new_v, axis=1))
```

```python
import os
import numpy as np
import concourse.bass as bass
import concourse.mybir as mybir
from concourse.bass_utils import run_bass_kernel_spmd

F32 = mybir.dt.float32
F32R = mybir.dt.float32r
AF = mybir.ActivationFunctionType
ALU = mybir.AluOpType

D = 1024
KC = 8
DFF = 2816
FC = 22
NMOD = 9
DEPTH = 2
SEQ = 256
TP = 512
TS = 2048
T = TP + TS
NT = T // 512
A_HEADS = 4
B_HEADS = 4
B_DK = 64
GATE_RANK = 16
GLA_TAU = 16.0
AB_IN = 4128
C_HEADS = 16
C_KV = 4
HD = 64
PAST = 512
ALPHA = (2.0 * DEPTH) ** 0.25
LN_EPS = 1e-5
RMS_EPS = 1e-6
ROPE_BASE = 10000.0
NCORES = 8


class Prog:
    def __init__(self, nc):
        self.nc = nc
        self.E = {"pe": nc.tensor, "act": nc.scalar, "dve": nc.vector, "pool": nc.gpsimd, "sp": nc.sync}
        self.sems = {}
        self.cnt = {}
        for e in self.E:
            self.sems[e] = nc.alloc_semaphore("s_" + e)
            self.cnt[e] = 0
        self.seen = {e: {} for e in self.E}
        self.ndma_sem = 0
        self.n_inst = 0
        self.n_wait = 0
        self.self_sync = set(os.environ.get("MK_SELFSYNC", "act,dve,pool").split(",")) - {""}
        self.named = {}
        self.mem = {}

    def named_sem(self, name):
        if name not in self.named:
            self.named[name] = self.new_dma_sem()
        return self.named[name]

    def new_dma_sem(self):
        k = "d%d" % self.ndma_sem
        self.ndma_sem += 1
        self.sems[k] = self.nc.alloc_semaphore("s_" + k)
        self.cnt[k] = 0
        return k

    @staticmethod
    def box(ap):
        esz = mybir.dt.size(ap.dtype)
        dims = ap.ap
        off = ap.offset
        name = ap.tensor.name
        if str(ap.space) == "DRAM":
            lo = off
            hi = off
            for st, cn in dims:
                d = (cn - 1) * st
                if d > 0:
                    hi += d
                else:
                    lo += d
            return name, 0, 1, lo * esz, (hi + 1) * esz
        pst, pcn = dims[0]
        if pst <= 0:
            pst = 1 << 40
        p0 = off // pst
        c0 = off % pst
        if str(ap.space) == "PSUM":
            q0 = (p0 // 32) * 32
            q1 = ((p0 + pcn + 31) // 32) * 32
            return name, q0, q1, 0, 2048
        lo = c0
        hi = c0
        for st, cn in dims[1:]:
            d = (cn - 1) * st
            if d > 0:
                hi += d
            else:
                lo += d
        return name, p0, p0 + pcn, lo * esz, (hi + 1) * esz

    def _collect(self, reads, writes):
        deps = []
        rb = [self.box(a) for a in reads]
        wb = [self.box(a) for a in writes]
        for (name, p0, p1, lo, hi) in rb:
            m = self.mem.get(name)
            if m is None:
                continue
            for r in m[0]:
                if r[0] < p1 and p0 < r[1] and r[2] < hi and lo < r[3]:
                    deps.append((r[4], r[5]))
        for (name, p0, p1, lo, hi) in wb:
            m = self.mem.get(name)
            if m is None:
                continue
            for lst in m:
                for r in lst:
                    if r[0] < p1 and p0 < r[1] and r[2] < hi and lo < r[3]:
                        deps.append((r[4], r[5]))
        return deps, rb, wb

    def _record(self, rb, wb, key, val):
        for (name, p0, p1, lo, hi) in wb:
            m = self.mem.setdefault(name, [[], []])
            for i in (0, 1):
                m[i] = [r for r in m[i] if not (p0 <= r[0] and r[1] <= p1 and lo <= r[2] and r[3] <= hi)]
            m[0].append([p0, p1, lo, hi, key, val])
        for (name, p0, p1, lo, hi) in rb:
            m = self.mem.setdefault(name, [[], []])
            m[1] = [r for r in m[1] if not (r[4] == key and p0 <= r[0] and r[1] <= p1 and lo <= r[2] and r[3] <= hi)]
            m[1].append([p0, p1, lo, hi, key, val])

    def _wait(self, e, deps):
        best = {}
        for k, v in deps:
            if best.get(k, 0) < v:
                best[k] = v
        for k, v in best.items():
            if k == e and e not in self.self_sync:
                continue
            if self.seen[e].get(k, 0) < v:
                self.E[e].wait_ge(self.sems[k], v)
                self.seen[e][k] = v
                self.n_wait += 1

    def op(self, e, fn, reads=(), writes=()):
        deps, rb, wb = self._collect(reads, writes)
        self._wait(e, deps)
        ins = fn(self.E[e])
        ins.then_inc(self.sems[e], 1)
        self.cnt[e] += 1
        self._record(rb, wb, e, self.cnt[e])
        self.n_inst += 1
        return ins

    def dma(self, q, out, in_, sem):
        deps, rb, wb = self._collect([in_], [out])
        self._wait(q, deps)
        ins = self.E[q].dma_start(out=out, in_=in_)
        ins.then_inc(self.sems[sem], 16)
        self.cnt[sem] += 16
        self._record(rb, wb, sem, self.cnt[sem])
        self.n_inst += 1
        return ins

    def finish(self, e="sp"):
        deps = []
        for name, m in self.mem.items():
            for lst in m:
                for r in lst:
                    deps.append((r[4], r[5]))
        self._wait(e, deps)


class Builder:
    def __init__(self, stage=99):
        self.stage = stage
        nc = bass.Bass("TRN2", target_bir_lowering=False)
        nc.dge_precook = False
        self.nc = nc
        self.p = Prog(nc)
        self.dram = {}
        self.decl_io()
        self.alloc()

    def din(self, name, shape):
        self.dram[name] = self.nc.dram_tensor(name, list(shape), F32, kind="ExternalInput").ap()
        return self.dram[name]

    def dout(self, name, shape):
        self.dram[name] = self.nc.dram_tensor(name, list(shape), F32, kind="ExternalOutput").ap()
        return self.dram[name]

    def decl_io(self):
        self.din("xp", [TP, D])
        self.din("xs", [TS, D])
        self.din("st_h", [2, A_HEADS, 128, 128])
        self.din("st_g", [2, B_HEADS, B_DK, 128])
        self.din("ck", [PAST, C_KV * HD])
        self.din("cv", [PAST, C_KV * HD])
        self.din("cvec", [2, D])
        self.din("w_mod", [DEPTH, D, NMOD * D])
        self.din("b_mod", [DEPTH, NMOD * D])
        self.din("ln_g", [DEPTH * 3, D])
        self.din("ln_b", [DEPTH * 3, D])
        self.din("ffn_w1", [DEPTH, 2, D, DFF])
        self.din("ffn_w3", [DEPTH, 2, D, DFF])
        self.din("ffn_w2", [DEPTH, 2, DFF, D])
        self.din("w_in_ab", [D, AB_IN])
        self.din("hgrn_lb", [2, 2, 512])
        self.din("gate_up", [2, GATE_RANK, 256])
        self.din("gate_b", [2, 256])
        self.din("norm_a", [1, 128])
        self.din("norm_b", [1, 128])
        self.din("w_out_ab", [D, D])
        self.din("w_qkv", [D, 1536])
        self.din("sink", [1, C_HEADS])
        self.din("w_out_c", [D, D])
        self.dout("yp", [TP, D])
        self.dout("ys", [TS, D])
        self.dout("ns_h", [2, 2, A_HEADS, 128, 128])
        self.dout("ns_g", [2, 2, B_HEADS, B_DK, 128])
        self.dout("nk", [2, SEQ, C_KV * HD])
        self.dout("nv", [2, SEQ, C_KV * HD])

    def sb(self, name, shape, dt=F32):
        return self.nc.alloc_sbuf_tensor(name, list(shape), dt)

    def alloc(self):
        nc = self.nc
        self.X = self.sb("X", [128, KC, T])
        self.IDENT = self.sb("IDENT", [128, 128])
        self.ONES = self.sb("ONES", [128, 128], F32R)
        self.U1 = self.sb("U1", [128, 256])
        self.U2 = self.sb("U2", [128, 256], F32R)
        self.MF = self.U1[:, 0:128]
        self.MB = self.U1[:, 128:256]
        self.LM1 = self.U1[:, 0:128]
        self.LM3 = self.U1[:, 128:256]
        self.RESET = self.sb("RESET", [128, 256])
        self.PARS = self.sb("PARS", [128, 22])
        self.LB = self.sb("LB", [128, 8])
        self.OML = self.sb("OML", [128, 8])
        self.NGB = self.sb("NGB", [128, 4])
        self.GUP = self.U2[:, 0:256]
        self.PERMR = self.U2[:, 0:128]
        self.EPSR = self.sb("EPSR", [128, 1])
        self.ES = self.sb("ES", [128, 16])
        self.TCOS = self.sb("TCOS", [128, 64])
        self.TSIN = self.sb("TSIN", [128, 64])
        self.MR = self.sb("MR", [128, 1])
        self.MC = self.sb("MC", [128, 1])
        self.SGN = self.sb("SGN", [128, 1])
        self.MODV = self.sb("MODV", [128, 72, 2])
        self.OPS = self.sb("OPS", [128, 3, KC, 2])
        self.GSC = self.sb("GSC", [128, 3, KC, 2])
        self.BM = self.sb("BM", [128, 72])
        self.LNG = self.sb("LNG", [128, 48])
        self.LNB = self.sb("LNB", [128, 48])
        self.CS = self.sb("CS", [128, 2, KC], F32R)
        self.NR = 27 * 1024
        self.NF = 3072
        self.R = self.sb("R", [128, self.NR], F32R)
        self.Fm = self.sb("Fm", [128, self.NF])
        self.PS = [nc.alloc_psum_tensor("PS%d" % i, [128, 512], F32) for i in range(8)]
        self.sem_w13 = [self.p.new_dma_sem() for _ in range(2)]
        self.sem_w2 = [self.p.new_dma_sem() for _ in range(4)]
        self.sem_wab = [self.p.new_dma_sem() for _ in range(7)]
        self.sem_st = self.p.new_dma_sem()
        self.sem_io = [self.p.new_dma_sem() for _ in range(2)]
        self.sem_misc = self.p.new_dma_sem()
        self.sem_out = self.p.new_dma_sem()
        self.n13 = 0
        self.n2 = 0
        self.nio = 0

    @staticmethod
    def _view(base, off, shape, total):
        n = 1
        for x in shape[1:]:
            n *= x
        assert off + n <= total, (off, n, total)
        v = base[0:shape[0], off:off + n]
        if len(shape) == 3:
            v = v.rearrange("p (a b) -> p a b", a=shape[1])
        elif len(shape) == 4:
            v = v.rearrange("p (a b c) -> p a b c", a=shape[1], b=shape[2])
        return v

    def r(self, off, shape):
        return self._view(self.R, off, shape, self.NR)

    def f(self, off, shape):
        return self._view(self.Fm, off, shape, self.NF)

    def mm(self, out, lhsT, rhs, start, stop):
        self.p.op("pe", lambda e: e.matmul(out, lhsT=lhsT, rhs=rhs, start=start, stop=stop),
                  reads=[lhsT, rhs], writes=[out])

    def tr(self, out, in_, n=128):
        ident = self.IDENT[0:in_.shape[0], 0:in_.shape[0]]
        self.p.op("pe", lambda e: e.transpose(out=out, in_=in_, identity=ident), reads=[in_, ident], writes=[out])

    def act(self, out, in_, func, bias=None, scale=None, eng="act"):
        kw = {}
        rd = [in_]
        if bias is not None:
            kw["bias"] = bias
            if not isinstance(bias, (int, float)):
                rd.append(bias)
        if scale is not None:
            kw["scale"] = scale
            if not isinstance(scale, (int, float)):
                rd.append(scale)
        self.p.op("act", lambda e: e.activation(out=out, in_=in_, func=func, **kw), reads=rd, writes=[out])

    def ts(self, out, in0, s1, s2, op0, op1=None, eng="dve"):
        rd = [in0]
        for s in (s1, s2):
            if s is not None and not isinstance(s, (int, float)):
                rd.append(s)
        if op1 is None:
            self.p.op(eng, lambda e: e.tensor_scalar(out=out, in0=in0, scalar1=s1, scalar2=None, op0=op0), reads=rd, writes=[out])
        else:
            self.p.op(eng, lambda e: e.tensor_scalar(out=out, in0=in0, scalar1=s1, scalar2=s2, op0=op0, op1=op1), reads=rd, writes=[out])

    def tt(self, out, in0, in1, op, eng="dve"):
        self.p.op(eng, lambda e: e.tensor_tensor(out=out, in0=in0, in1=in1, op=op), reads=[in0, in1], writes=[out])

    def stt(self, out, in0, scalar, in1, op0, op1, eng="dve"):
        rd = [in0, in1]
        if not isinstance(scalar, (int, float)):
            rd.append(scalar)
        self.p.op(eng, lambda e: e.scalar_tensor_tensor(out=out, in0=in0, scalar=scalar, in1=in1, op0=op0, op1=op1), reads=rd, writes=[out])

    def cp(self, out, in_, eng="dve"):
        if eng == "act":
            self.act(out, in_, AF.Copy)
        else:
            self.p.op(eng, lambda e: e.tensor_copy(out=out, in_=in_), reads=[in_], writes=[out])

    def memset(self, ap, val, eng="pool"):
        self.p.op(eng, lambda e: e.memset(ap, val), writes=[ap])

    def dma(self, out, in_, sem, q="sp"):
        self.p.dma(q, out, in_, sem)

    def dma_r(self, out, in_, sem, q="sp"):
        self.p.dma(q, out if out.dtype == F32R else out.bitcast(F32R), in_.bitcast(F32R), sem)

    def consts(self):
        self.memset(self.IDENT[:], 1.0)
        self.p.op("pool", lambda e: e.affine_select(out=self.IDENT[:], in_=self.IDENT[:], pattern=[[-1, 128]],
                                                    compare_op=ALU.is_equal, fill=0.0, base=0, channel_multiplier=1),
                  reads=[self.IDENT[:]], writes=[self.IDENT[:]])
        tmp = self.f(0, [128, 128])
        self.memset(tmp, 1.0)
        self.cp(self.ONES[:], tmp)
        for (M, cm, st) in ((self.MF, -1, 1), (self.MB, 1, -1)):
            self.memset(M[:], 1.0)
            self.p.op("pool", lambda e: e.affine_select(out=M[:], in_=M[:], pattern=[[st, 128]], compare_op=ALU.is_ge,
                                                        fill=0.0, base=0, channel_multiplier=cm),
                      reads=[M[:]], writes=[M[:]])
        self.memset(self.MF[0:64, 64:128], 0.0)
        self.memset(self.MB[64:128, 0:64], 0.0)
        self.memset(self.RESET[:], 1.0)
        self.memset(self.RESET[:, 0:256:64], 0.0)
        self.memset(self.EPSR[:], RMS_EPS)

    def load_fm(self, dst, src_rows, nrows):
        st = self.f(0, [128, 128])
        self.dma(st[0:nrows, :], src_rows, self.sem_misc)
        ps = self.PS[7][:, 0:nrows]
        self.tr(ps, st[0:nrows, :])
        self.cp(dst, ps)

    def load_small(self):
        self.load_fm(self.LNG[:], self.dram["ln_g"].rearrange("r (k p) -> (r k) p", p=128), 48)
        self.load_fm(self.LNB[:], self.dram["ln_b"].rearrange("r (k p) -> (r k) p", p=128), 48)
        st = self.f(0, [128, 128])
        self.dma(st[0:16, :], self.dram["cvec"].rearrange("g (k p) -> (g k) p", p=128), self.sem_misc)
        ps = self.PS[7][:, 0:16]
        self.tr(ps, st[0:16, :])
        self.act(self.CS[:].rearrange("p g k -> p (g k)"), ps, AF.Silu)

    def load_x(self):
        for tb in range(T // 128):
            src = self.dram["xp"][tb * 128:(tb + 1) * 128, :] if tb < TP // 128 else \
                self.dram["xs"][tb * 128 - TP:(tb + 1) * 128 - TP, :]
            s = self.nio % 2
            self.nio += 1
            st = self.f(s * 1024, [128, 1024])
            self.dma(st, src, self.sem_io[s])
            for hb in range(2):
                ps = self.PS[(tb * 2 + hb) % 4]
                for j in range(4):
                    k = hb * 4 + j
                    self.tr(ps[:, j * 128:(j + 1) * 128], st[:, k * 128:(k + 1) * 128])
                dst = self.X[:, hb * 4:(hb + 1) * 4, tb * 128:(tb + 1) * 128]
                self.cp(dst, ps[:].rearrange("p (a b) -> p a b", a=4), eng="dve" if hb == 0 else "act")

    def store_x(self):
        for tb in range(T // 128):
            dst = self.dram["yp"][tb * 128:(tb + 1) * 128, :] if tb < TP // 128 else \
                self.dram["ys"][tb * 128 - TP:(tb + 1) * 128 - TP, :]
            s = self.nio % 2
            self.nio += 1
            st = self.f(s * 1024, [128, 1024])
            for hb in range(2):
                ps = self.PS[(tb * 2 + hb) % 4]
                for j in range(4):
                    k = hb * 4 + j
                    self.tr(ps[:, j * 128:(j + 1) * 128], self.X[:, k, tb * 128:(tb + 1) * 128])
                self.cp(st[:, hb * 512:(hb + 1) * 512], ps[:], eng="dve" if hb == 0 else "act")
            self.dma(dst, st, self.sem_io[s])

    def mod_vectors(self, l):
        self.load_fm(self.BM[:], self.dram["b_mod"][l].rearrange("(r p) -> r p", p=128), 72)
        wm = self.dram["w_mod"][l].rearrange("(k p) n -> p k n", p=128)
        pm = self.PS[6][:, 0:144].rearrange("p (c g) -> p c g", g=2)
        for blk in range(18):
            s = self.n13 % 2
            self.n13 += 1
            wt = self.r(self.o_w13 + s * 4096, [128, KC, 512])
            self.dma_r(wt, wm[:, :, blk * 512:(blk + 1) * 512], self.sem_w13[s])
            for q in range(4):
                oc = blk * 4 + q
                for k in range(KC):
                    self.mm(pm[:, oc, :], wt[:, k, q * 128:(q + 1) * 128], self.CS[:, :, k], k == 0, k == KC - 1)
        for g in range(2):
            self.tt(self.MODV[:, :, g], pm[:, :, g], self.BM[:], ALU.add)
        gmul = [0.5 / ALPHA, 1.0 / ALPHA, 0.5 / ALPHA]
        for s in range(3):
            self.ts(self.OPS[:, s, :, :], self.MODV[:, (3 * s + 1) * 8:(3 * s + 2) * 8, :], 1.0, None, ALU.add)
            self.ts(self.GSC[:, s, :, :], self.MODV[:, (3 * s + 2) * 8:(3 * s + 3) * 8, :], gmul[s], None, ALU.mult)

    def shift(self, s, k, g):
        return self.MODV[:, 3 * s * 8 + k, g:g + 1]

    def ln_range(self, lnidx, x0, n, o_r):
        ts_ = slice(x0, x0 + n)
        pa, pb = self.PS[4][:, 0:n], self.PS[5][:, 0:n]
        for k in range(KC):
            zr = self.r(o_r + (k % 2) * 512, [128, 512])[:, 0:n]
            sq = self.r(o_r + 1024 + (k % 2) * 512, [128, 512])[:, 0:n]
            self.act(zr, self.X[:, k, ts_], AF.Copy, scale=1.0 / 1024.0)
            self.act(sq, self.X[:, k, ts_], AF.Square, scale=1.0 / 32.0)
            self.mm(pa, self.ONES[:], zr, k == 0, k == KC - 1)
            self.mm(pb, self.ONES[:], sq, k == 0, k == KC - 1)
        m2 = self.f(self.o_lnf, [128, 512])[:, 0:n]
        self.act(m2, pa, AF.Square)
        self.tt(m2, pb, m2, ALU.subtract)
        self.act(m2, m2, AF.Sqrt, bias=self.EPSLN[:, 0:1])
        self.p.op("dve", lambda e: e.reciprocal(out=m2, in_=m2), reads=[m2], writes=[m2])
        for k in range(KC):
            xk = self.X[:, k, ts_]
            self.tt(xk, xk, pa, ALU.subtract)
            self.stt(xk, xk, self.LNG[:, lnidx * 8 + k:lnidx * 8 + k + 1], m2, ALU.mult, ALU.mult)
            self.act(xk, xk, AF.Identity, bias=self.LNB[:, lnidx * 8 + k:lnidx * 8 + k + 1])

    def ffn_tile(self, l, j, t):
        g = 0 if t == 0 else 1
        s = 0 if j == 0 else 2
        ts_ = slice(t * 512, (t + 1) * 512)
        XM = self.r(self.o_xm, [128, KC, 512])
        HID = self.r(self.o_hid, [128, FC, 512])
        for k in range(KC):
            self.ts(XM[:, k, :], self.X[:, k, ts_], self.OPS[:, s, k, g:g + 1], self.shift(s, k, g), ALU.mult, ALU.add)
        w1 = self.dram["ffn_w1"][l, j].rearrange("(k p) f -> p k f", p=128)
        w3 = self.dram["ffn_w3"][l, j].rearrange("(k p) f -> p k f", p=128)
        w2 = self.dram["ffn_w2"][l, j].rearrange("(c p) o -> p c o", p=128)
        for fb in range(FC // 2):
            sl = self.n13 % 2
            self.n13 += 1
            wt = self.r(self.o_w13 + sl * 4096, [128, 2, KC, 256])
            self.dma_r(wt[:, 0], w1[:, :, fb * 256:(fb + 1) * 256], self.sem_w13[sl])
            self.dma_r(wt[:, 1], w3[:, :, fb * 256:(fb + 1) * 256], self.p.named_sem("w3_%d" % sl))
            for c in range(2):
                f = 2 * fb + c
                p1, p3 = self.PS[f % 2], self.PS[2 + f % 2]
                for k in range(KC):
                    self.mm(p1[:], wt[:, 0, k, c * 128:(c + 1) * 128], XM[:, k, :], k == 0, k == KC - 1)
                for k in range(KC):
                    self.mm(p3[:], wt[:, 1, k, c * 128:(c + 1) * 128], XM[:, k, :], k == 0, k == KC - 1)
                sg = self.f(self.o_sg + (f % 2) * 512, [128, 512])
                self.act(sg, p1[:], AF.Silu)
                self.tt(HID[:, f, :], sg, p3[:], ALU.mult)
        for half in range(2):
            for fb in range(FC // 2):
                sl = self.n2 % 4
                self.n2 += 1
                wt = self.r(self.o_w2 + sl * 1024, [128, 2, 512])
                self.dma_r(wt, w2[:, 2 * fb:2 * fb + 2, half * 512:(half + 1) * 512], self.sem_w2[sl])
                for c in range(2):
                    f = 2 * fb + c
                    for o in range(4):
                        self.mm(self.PS[4 + o][:], wt[:, c, o * 128:(o + 1) * 128], HID[:, f, :], f == 0, f == FC - 1)
            for o in range(4):
                oc = half * 4 + o
                self.stt(self.X[:, oc, ts_], self.PS[4 + o][:], self.GSC[:, s, oc, g:g + 1], self.X[:, oc, ts_], ALU.mult, ALU.add)
        self.ln_range(l * 3 + s, t * 512, 512, self.o_ln)

    def ab_params(self):
        st = self.f(0, [128, 128])
        d = self.dram
        self.dma(st[0:1, :], d["norm_a"], self.sem_misc)
        self.dma(st[1:2, :], d["norm_b"], self.sem_misc)
        self.dma(st[2:6, :], d["gate_b"].rearrange("a (j p) -> (a j) p", p=128), self.sem_misc)
        self.dma(st[6:22, :], d["hgrn_lb"].rearrange("a b (h p) -> (a b h) p", p=128), self.sem_misc)
        ps = self.PS[7][:, 0:22]
        self.tr(ps, st[0:22, :])
        self.cp(self.PARS[:], ps)
        P = self.PARS
        for dr in range(2):
            a = P[:, 6 + dr * 8:6 + dr * 8 + 4]
            b = P[:, 6 + dr * 8 + 4:6 + dr * 8 + 8]
            self.tt(self.LB[:, dr * 4:dr * 4 + 4], a, b, ALU.subtract)
        self.act(self.LB[:], self.LB[:], AF.Sigmoid)
        self.ts(self.OML[:], self.LB[:], -1.0, 1.0, ALU.mult, ALU.add)
        self.ts(self.NGB[:], P[:, 2:6], -1.0, None, ALU.mult)

    def ab_unit_pass(self, u, dr, x0, nht, g, sample, sidx):
        hg = u < 4
        j = u - 4
        d = self.dram
        win = d["w_in_ab"].rearrange("(k p) n -> p k n", p=128)
        if hg:
            cols = [("q", u * 128), ("f", (1024 if dr == 0 else 1536) + u * 128), ("i", 512 + u * 128)]
            if dr == 1:
                cols.append(("g0", 2048 + u * 128))
        else:
            cols = [("q", 2560 + j * 128), ("f", 2816 + j * 128), ("v0", 3072 + j * 256), ("v1", 3072 + j * 256 + 128),
                    ("bz", 4000)]
            if dr == 1:
                cols += [("g0", 3584 + j * 256), ("g1", 3584 + j * 256 + 128)]
        W = {}
        for i, (nm, c0) in enumerate(cols):
            W[nm] = self.r(self.o_wab + i * 1024, [128, KC, 128])
            self.dma_r(W[nm], win[:, :, c0:c0 + 128], self.sem_wab[i])
        nh = 1 if hg else 2
        heads = list(range(nh))
        vw = 128 * nh
        SB = [self.r(self.o_sb + i * 128, [128, 128]) for i in range(3)]
        si = 0
        if not hg:
            self.ts(self.GUP[64:128, :], self.RESET[64:128, :], 0.0, None, ALU.mult)
            self.dma_r(self.GUP[96 + 16 * dr:112 + 16 * dr, :], d["gate_up"][dr], self.p.named_sem("gup"))
        if sample:
            if hg:
                self.dma_r(SB[0], d["st_h"][dr, u], self.sem_st)
            else:
                self.dma_r(SB[0], d["st_g"][dr, 2 * j:2 * j + 2].rearrange("h d e -> (h d) e"), self.sem_st)
        else:
            self.ts(SB[0], self.IDENT[:], 0.0, None, ALU.mult)
        HB = self.r(self.o_hb, [128, KC, 256])
        QD = self.r(self.o_qd, [128, 256])
        KI = self.r(self.o_ki, [128, 256])
        VT = self.r(self.o_vt, [128, 2, 256])
        AT = [self.r(self.o_at + i * 128, [128, 128]) for i in range(2)]
        KIT = self.r(self.o_kit, [128, 128])
        BZ = self.r(self.o_bz, [128, 256])
        SQ = self.r(self.o_hb, [128, 256])
        F0 = self.f(0, [128, 256])
        F1 = self.f(256, [128, 256])
        F2 = self.f(512, [128, 256])
        TMP = self.f(768, [128, 128])
        AC = self.f(896, [128, 4])
        PS = self.PS
        psq, psf = PS[0][:, 0:256], PS[0][:, 256:512]
        PSO = [PS[4], PS[7]]
        MASK = self.MF if dr == 0 else self.MB
        order = list(range(nht)) if dr == 0 else list(range(nht - 1, -1, -1))
        bs = [0, 1] if dr == 0 else [1, 0]
        for hti in order:
            t0 = x0 + hti * 256
            lt0 = hti * 256
            for k in range(KC):
                self.ts(HB[:, k, :], self.X[:, k, t0:t0 + 256], self.OPS[:, 1, k, g:g + 1], self.shift(1, k, g), ALU.mult, ALU.add)
            for k in range(KC):
                self.mm(psq, W["q"][:, k, :], HB[:, k, :], k == 0, k == KC - 1)
            for k in range(KC):
                self.mm(psf, W["f"][:, k, :], HB[:, k, :], k == 0, k == KC - 1)
            for b in range(2):
                for vv in range(nh):
                    wv = W["i"] if hg else W["v%d" % vv]
                    for k in range(KC):
                        self.mm(PS[1][:, b * 256 + vv * 128:b * 256 + vv * 128 + 128], HB[:, k, b * 128:(b + 1) * 128], wv[:, k, :],
                                k == 0, k == KC - 1)
            if not hg:
                for k in range(KC):
                    self.mm(PS[3][:, 0:256], W["bz"][:, k, :], HB[:, k, :], k == 0, k == KC - 1)
            if dr == 1:
                for hh in heads:
                    for k in range(KC):
                        self.mm(PS[2][:, hh * 256:(hh + 1) * 256], W["g%d" % hh][:, k, :], HB[:, k, :], k == 0, k == KC - 1)
            for b in range(2):
                if hg:
                    self.act(VT[:, b, 0:128], PS[1][:, b * 256:b * 256 + 128], AF.Silu)
                else:
                    self.cp(VT[:, b, :], PS[1][:, b * 256:(b + 1) * 256], eng="act")
            if hg:
                self.act(F0, psf, AF.Sigmoid)
                self.ts(F0, F0, self.OML[:, dr * 4 + u:dr * 4 + u + 1], self.LB[:, dr * 4 + u:dr * 4 + u + 1], ALU.mult, ALU.add)
                self.ts(F1, F0, -1.0, 1.0, ALU.mult, ALU.add)
                self.act(F0, F0, AF.Ln)
                kf = F1
            else:
                self.cp(BZ, PS[3][:, 0:256], eng="act")
                psl = PS[3][:, 256:512]
                self.mm(psl, self.GUP[64:128, j * 128:(j + 1) * 128], BZ[64:128, :], True, True)
                self.act(F0, psl, AF.Exp, scale=-1.0, bias=self.NGB[:, dr * 2 + j:dr * 2 + j + 1])
                self.act(F0, F0, AF.Ln, bias=self.ONEC[:, 0:1])
                self.ts(F0, F0, -1.0 / GLA_TAU, None, ALU.mult)
                kf = psf
            self.p.op("dve", lambda e: e.tensor_tensor_scan(out=F2, data0=self.RESET[:, 0:256], data1=F0, initial=0.0,
                                                            op0=ALU.mult, op1=ALU.add),
                      reads=[self.RESET[:, 0:256], F0], writes=[F2])
            if dr == 0:
                self.act(F0, F2, AF.Exp)
                self.cp(AC, F0[:, 63:256:64])
                if hg:
                    self.tt(QD, psq, F0, ALU.mult)
                else:
                    self.stt(QD, psq, B_DK ** -0.5, F0, ALU.mult, ALU.mult)
                self.act(F2, F2, AF.Exp, scale=-1.0)
                self.tt(KI, kf, F2, ALU.mult)
            else:
                self.act(AC, F2[:, 63:256:64], AF.Exp)
                self.tt(F0, F0, F2, ALU.subtract)
                self.act(F2, F0, AF.Exp)
                if hg:
                    self.tt(QD, psq, F2, ALU.mult)
                else:
                    self.stt(QD, psq, B_DK ** -0.5, F2, ALU.mult, ALU.mult)
                self.act(F0, F0, AF.Exp, scale=-1.0)
                self.tt(KI, kf, F0, ALU.mult)
            for b in bs:
                blk = slice(b * 128, (b + 1) * 128)
                for hh in heads:
                    pr = slice(0, 128) if hg else slice(64 * hh, 64 * hh + 64)
                    psat = PS[5][:, hh * 128:(hh + 1) * 128]
                    self.mm(psat, KI[pr, blk], QD[pr, blk], True, True)
                    self.tt(AT[hh], psat, MASK[:], ALU.mult)
                pskt = PS[6][:, 256:384]
                self.tr(pskt, KI[:, blk].bitcast(F32))
                self.cp(KIT, pskt, eng="act")
                for hh in heads:
                    self.mm(PSO[hh][:, blk], VT[:, b, hh * 128:(hh + 1) * 128], AT[hh], True, False)
                for ci_, c in enumerate(bs):
                    ccols = slice(b * 128 + c * 64, b * 128 + c * 64 + 64)
                    ci = b * 2 + c
                    a_c = AC[:, ci:ci + 1]
                    last = ci_ == 1
                    pskv = PS[7][:, 0:vw] if hg else PS[6][:, 0:vw]
                    S_cur = SB[si % 3]
                    if dr == 0:
                        for hh in heads:
                            pr = slice(0, 128) if hg else slice(64 * hh, 64 * hh + 64)
                            self.mm(PSO[hh][:, ccols], S_cur[pr, :], QD[pr, ccols], False, last)
                        self.mm(pskv, KIT[c * 64:(c + 1) * 64, :], VT[c * 64:(c + 1) * 64, b, 0:vw], True, True)
                        S_next = SB[(si + 1) % 3]
                        if hg:
                            self.tt(TMP, pskv, S_cur.bitcast(F32), ALU.add)
                        else:
                            self.tt(TMP[0:64, :], pskv[0:64, 0:128], S_cur[0:64, :].bitcast(F32), ALU.add)
                            self.tt(TMP[64:128, :], pskv[64:128, 128:256], S_cur[64:128, :].bitcast(F32), ALU.add)
                        self.ts(S_next, TMP, a_c, None, ALU.mult)
                        si += 1
                    else:
                        S_sc = SB[(si + 1) % 3]
                        S_next = SB[(si + 2) % 3]
                        self.ts(S_sc, S_cur.bitcast(F32), a_c, None, ALU.mult)
                        for hh in heads:
                            pr = slice(0, 128) if hg else slice(64 * hh, 64 * hh + 64)
                            self.mm(PSO[hh][:, ccols], S_sc[pr, :], QD[pr, ccols], False, last)
                        self.mm(pskv, KIT[c * 64:(c + 1) * 64, :], VT[c * 64:(c + 1) * 64, b, 0:vw], True, True)
                        if hg:
                            self.tt(S_next, pskv, S_sc.bitcast(F32), ALU.add)
                        else:
                            self.tt(S_next[0:64, :], pskv[0:64, 0:128], S_sc[0:64, :].bitcast(F32), ALU.add)
                            self.tt(S_next[64:128, :], pskv[64:128, 128:256], S_sc[64:128, :].bitcast(F32), ALU.add)
                        si += 2
            for hh in heads:
                head = u if hg else 4 + 2 * j + hh
                on = self.r(self.o_on + head * self.on_stride + lt0, [128, 256])
                pso = PSO[hh][:, 0:256]
                if dr == 0:
                    self.cp(on, pso, eng="act")
                else:
                    self.tt(F1, pso, on.bitcast(F32), ALU.add)
                    self.act(SQ, F1, AF.Square, scale=128.0 ** -0.5)
                    pst = PS[3][:, 0:256]
                    self.mm(pst, self.ONES[:], SQ, True, True)
                    self.act(F2, pst, AF.Sqrt, bias=self.EPSR[:, 0:1])
                    self.p.op("dve", lambda e: e.reciprocal(out=F2, in_=F2), reads=[F2], writes=[F2])
                    nw = self.PARS[:, 0:1] if hg else self.PARS[:, 1:2]
                    self.stt(F1, F1, nw, F2, ALU.mult, ALU.mult)
                    self.act(F0, PS[2][:, hh * 256:(hh + 1) * 256], AF.Silu)
                    self.tt(on, F1, F0, ALU.mult)
        if not sample:
            S_fin = SB[si % 3]
            if hg:
                self.dma(d["ns_h"][sidx, dr, u], S_fin.bitcast(F32), self.p.named_sem("sout%d" % (si % 3)))
            else:
                self.dma(d["ns_g"][sidx, dr, 2 * j:2 * j + 2].rearrange("h d e -> (h d) e"), S_fin.bitcast(F32),
                         self.p.named_sem("sout%d" % (si % 3)))

    def ab_outproj(self, l, x0, lt0, n, g):
        wo = self.dram["w_out_ab"].rearrange("(h p) o -> p h o", p=128)
        for half in range(2):
            WO = self.r(self.o_wab, [128, 8, 512])
            self.dma_r(WO, wo[:, :, half * 512:(half + 1) * 512], self.sem_wab[0])
            for o in range(4):
                for h in range(8):
                    on = self.r(self.o_on + h * self.on_stride + lt0, [128, 512])[:, 0:n]
                    self.mm(self.PS[o][:, 0:n], WO[:, h, o * 128:(o + 1) * 128], on, h == 0, h == 7)
            for o in range(4):
                oc = half * 4 + o
                xs_ = self.X[:, oc, x0:x0 + n]
                self.stt(xs_, self.PS[o][:, 0:n], self.GSC[:, 1, oc, g:g + 1], xs_, ALU.mult, ALU.add)
        self.ln_range(l * 3 + 1, x0, n, self.o_hb)

    def mixer_ab(self, l):
        self.o_on = 0
        self.on_stride = 2048
        self.o_wab = 16384
        self.o_hb = 23552
        o = 25600
        self.o_qd = o
        self.o_ki = o + 256
        self.o_vt = o + 512
        self.o_at = o + 1024
        self.o_kit = o + 1280
        self.o_sb = o + 1408
        self.o_bz = o + 1792
        assert self.o_bz + 256 <= self.NR
        self.ab_params()
        for (x0, nht, g, sample, sidx) in ((0, 1, 0, False, 0), (256, 1, 0, False, 1), (512, 8, 1, True, None)):
            for u in range(6):
                for dr in range(2):
                    self.ab_unit_pass(u, dr, x0, nht, g, sample, sidx)
            ntok = nht * 256
            for off in range(0, ntok, 512):
                n = min(512, ntok - off)
                self.ab_outproj(l, x0 + off, off, n, g)

    def c_consts(self):
        d = self.dram
        A = self.f(0, [128, 128])
        B = self.f(128, [128, 128])
        for (M, cm, st, base) in ((A, 1, -1, -16), (B, -1, 1, -16)):
            self.memset(M, 1.0)
            self.p.op("pool", lambda e: e.affine_select(out=M, in_=M, pattern=[[st, 128]], compare_op=ALU.is_equal,
                                                        fill=0.0, base=base, channel_multiplier=cm),
                      reads=[M], writes=[M])
        self.memset(A.rearrange("p (g c) -> p g c", c=32)[:, :, 16:32], 0.0)
        self.memset(B.rearrange("p (g c) -> p g c", c=32)[:, :, 0:16], 0.0)
        self.tt(self.PERMR[:], A, B, ALU.add)
        for (M, cm, st) in ((self.LM1, -1, 1), (self.LM3, 1, -1)):
            self.memset(M[:], 1.0)
            self.p.op("pool", lambda e: e.affine_select(out=M[:], in_=M[:], pattern=[[st, 128]], compare_op=ALU.is_ge,
                                                        fill=0.0, base=0, channel_multiplier=cm),
                      reads=[M[:]], writes=[M[:]])
        st_ = self.f(256, [128, 16])
        self.memset(st_, 0.0)
        self.dma(st_[0:1, :], d["sink"], self.sem_misc)
        ps = self.PS[7][:, 0:16]
        onesf = self.f(384, [128, 128])
        self.memset(onesf, 1.0)
        self.mm(ps, onesf, st_, True, True)
        self.act(self.ES[:], ps, AF.Exp)
        I32 = mybir.dt.int32
        pi_ = self.f(512, [128, 1]).bitcast(I32)
        self.p.op("pool", lambda e: e.iota(pi_, pattern=[[0, 1]], base=0, channel_multiplier=1), writes=[pi_])
        i16 = self.f(513, [128, 1]).bitcast(I32)
        self.ts(i16, pi_, 15, None, ALU.bitwise_and)
        m32 = self.f(514, [128, 1]).bitcast(I32)
        self.ts(m32, pi_, 32, None, ALU.bitwise_and)
        i16f = self.f(515, [128, 1])
        self.cp(i16f, i16)
        m32f = self.f(516, [128, 1])
        self.cp(m32f, m32)
        inv = self.f(517, [128, 1])
        self.act(inv, i16f, AF.Exp, scale=-float(np.log(ROPE_BASE)) / 16.0)
        b16 = self.f(518, [128, 1]).bitcast(I32)
        self.ts(b16, pi_, 16, None, ALU.bitwise_and)
        b16f = self.f(519, [128, 1])
        self.cp(b16f, b16)
        self.ts(self.SGN[:], b16f, 1.0 / 8.0, -1.0, ALU.mult, ALU.add)
        self.ts(self.MC[:], m32f, 1.0 / 32.0, None, ALU.mult)
        self.ts(self.MR[:], self.MC[:], -1.0, 1.0, ALU.mult, ALU.add)
        pos = self.f(640, [128, 64])
        self.p.op("pool", lambda e: e.iota(pos.bitcast(I32), pattern=[[1, 64]], base=0, channel_multiplier=0), writes=[pos])
        posf = self.f(704, [128, 64])
        self.cp(posf, pos.bitcast(I32))
        ang = self.f(768, [128, 64])
        self.ts(ang, posf, inv, None, ALU.mult)
        TWO_PI = 2.0 * float(np.pi)
        for (dst, shiftv) in ((self.TSIN, 0.0), (self.TCOS, float(np.pi) / 2.0)):
            a = self.f(832, [128, 64])
            kq = self.f(896, [128, 64])
            ki = self.f(960, [128, 64]).bitcast(I32)
            self.ts(a, ang, shiftv, None, ALU.add)
            self.ts(kq, a, 1.0 / TWO_PI, None, ALU.mult)
            self.cp(ki, kq)
            self.cp(kq, ki)
            self.stt(a, kq, -TWO_PI, a, ALU.mult, ALU.add)
            self.ts(kq, a, float(np.pi), -TWO_PI, ALU.is_gt, ALU.mult)
            self.tt(a, a, kq, ALU.add)
            self.ts(kq, a, -float(np.pi), TWO_PI, ALU.is_lt, ALU.mult)
            self.tt(a, a, kq, ALU.add)
            self.act(dst[:], a, AF.Sin)

    def rope_tables(self, t):
        cos_t = self.f(0, [128, 512])
        sin_t = self.f(512, [128, 512])
        for (dst, T) in ((cos_t, self.TCOS), (sin_t, self.TSIN)):
            dv = dst.rearrange("p (r c) -> p r c", c=64)
            rowv = T[:, 8 * t:8 * t + 8].unsqueeze(2).to_broadcast([128, 8, 64])
            colv = T[:, 0:64].unsqueeze(1).to_broadcast([128, 8, 64])
            self.ts(dv, rowv, self.MR[:, 0:1], None, ALU.mult)
            self.stt(dv, colv, self.MC[:, 0:1], dv, ALU.mult, ALU.add)
        self.ts(sin_t, sin_t, self.SGN[:, 0:1], None, ALU.mult)
        return cos_t, sin_t

    def rope_apply(self, dst, ps, cos_t, sin_t, n):
        ZQ = self.r(self.o_zq, [128, 512])[:, 0:n]
        self.cp(ZQ, ps, eng="act")
        pz = self.PS[6][:, 0:n]
        self.mm(pz, self.PERMR[:], ZQ, True, True)
        t1 = self.f(1024, [128, 512])[:, 0:n]
        self.tt(t1, ZQ.bitcast(F32), cos_t[:, 0:n], ALU.mult)
        t2 = self.f(1536, [128, 512])[:, 0:n]
        self.tt(t2, pz, sin_t[:, 0:n], ALU.mult)
        self.tt(dst, t1, t2, ALU.add)

    def c_modulate(self, x0, n, g):
        HB = self.r(self.o_hb, [128, KC, 512])
        for k in range(KC):
            self.ts(HB[:, k, 0:n], self.X[:, k, x0:x0 + n], self.OPS[:, 1, k, g:g + 1], self.shift(1, k, g), ALU.mult, ALU.add)
        return HB

    def va_slot(self, kap):
        return [(0, 0, 64), (64, 2, 0), (130, 0, 64), (194, 2, 0)][kap]

    def c_kv_project(self, HB, n, WKV, kf_dst, va_dst, vblk0, rope, emit=None):
        for kp in range(2):
            ps = self.PS[kp][:, 0:n]
            for k in range(KC):
                self.mm(ps, WKV[:, k, kp * 128:(kp + 1) * 128], HB[:, k, 0:n], k == 0, k == KC - 1)
            if rope is not None:
                self.rope_apply(kf_dst(kp), ps, rope[0], rope[1], n)
            else:
                self.cp(kf_dst(kp), ps, eng="act")
        for b in range(n // 128):
            ps = self.PS[2 + b % 2][:, 0:256]
            for k in range(KC):
                self.mm(ps, HB[:, k, b * 128:(b + 1) * 128], WKV[:, k, 256:512], k == 0, k == KC - 1)
            va = va_dst(vblk0 + b)
            self.cp(va[:, 0:132].rearrange("p (a c) -> p a c", c=66)[:, :, 0:64], ps[:, 0:128].rearrange("p (a c) -> p a c", c=64), eng="act")
            self.cp(va[:, 130:262].rearrange("p (a c) -> p a c", c=66)[:, :, 0:64], ps[:, 128:256].rearrange("p (a c) -> p a c", c=64), eng="act")
            self.ts(va[:, 64:66], self.RESET[:, 1:3], 0.0, 1.0, ALU.mult, ALU.add)
            self.ts(va[:, 194:196], self.RESET[:, 1:3], 0.0, 1.0, ALU.mult, ALU.add)
            if emit is not None:
                sq, tok0 = emit
                stv = self.f(2048 + (b % 2) * 256, [128, 256])
                self.cp(stv, ps)
                self.dma(self.dram["nv"][sq, tok0 + b * 128:tok0 + (b + 1) * 128, :], stv, self.p.named_sem("nv%d" % (b % 2)))
                ps2 = self.PS[4 + b % 2][:, 0:256]
                for k in range(KC):
                    self.mm(ps2, HB[:, k, b * 128:(b + 1) * 128], WKV[:, k, 0:256], k == 0, k == KC - 1)
                stk = self.f(2560 + (b % 2) * 256, [128, 256])
                self.cp(stk, ps2, eng="act")
                self.dma(self.dram["nk"][sq, tok0 + b * 128:tok0 + (b + 1) * 128, :], stk, self.p.named_sem("nk%d" % (b % 2)))

    def c_q_project(self, HB, n, rope):
        wq = self.dram["w_qkv"].rearrange("(k p) n -> p k n", p=128)
        QF = self.r(self.o_qf, [128, 8, 512])
        WS = self.r(self.o_ws, [128, KC, 4, 128])
        for grp in range(2):
            for s_ in range(2):
                for i in range(4):
                    c0 = grp * 512 + s_ * 256 + i * 64
                    self.dma_r(WS[:, :, i, s_ * 64:(s_ + 1) * 64], wq[:, :, c0:c0 + 64], self.p.named_sem("wq%d_%d" % (s_, i)))
            for i in range(4):
                pair = grp * 4 + i
                ps = self.PS[pair % 2][:, 0:n]
                for k in range(KC):
                    self.mm(ps, WS[:, k, i, :], HB[:, k, 0:n], k == 0, k == KC - 1)
                if rope is not None:
                    self.rope_apply(QF[:, pair, 0:n], ps, rope[0], rope[1], n)
                else:
                    self.cp(QF[:, pair, 0:n], ps, eng="act")
        return QF

    def c_attend(self, QF, nqb, keysets):
        OT = self.r(self.o_ot, [128, 4, 1024])
        PT = [self.r(self.o_ws + i * 512, [128, 512]) for i in range(2)]
        n = nqb * 128
        cnt = 0
        for h in range(C_HEADS):
            kap = h // 4
            half = kap % 2
            pair = (h % 4) + (0 if h < 8 else 4)
            hp = slice(64 * half, 64 * half + 64)
            c0, o_off, d_off = self.va_slot(kap)
            nk_for = [sum(1 for ks in keysets if ks[2] <= qb <= ks[3]) for qb in range(nqb)]
            seen = [0] * nqb
            for (kf_fn, va_fn, qlo, qhi, masks) in keysets:
                ncol = (qhi - qlo + 1) * 128
                ps = self.PS[cnt % 2][:, 0:ncol]
                pt = PT[cnt % 2][:, 0:ncol]
                cnt += 1
                self.mm(ps, kf_fn(kap, half), QF[hp, pair, qlo * 128:qlo * 128 + ncol], True, True)
                self.act(pt, ps, AF.Exp, scale=HD ** -0.5)
                for qb in range(qlo, qhi + 1):
                    sl = slice((qb - qlo) * 128, (qb - qlo + 1) * 128)
                    if qb in masks:
                        self.tt(pt[:, sl], pt[:, sl].bitcast(F32), masks[qb][:], ALU.mult)
                    seen[qb] += 1
                    self.mm(self.PS[2 + qb][:, 0:66], pt[:, sl], va_fn(kap)[:, c0:c0 + 66], seen[qb] == 1, seen[qb] == nk_for[qb])
            for qb in range(nqb):
                po = self.PS[2 + qb]
                rd = self.f(2048 + 16 * (qb % 2), [128, 1])
                self.ts(rd, po[:, d_off:d_off + 1], self.ES[:, h:h + 1], None, ALU.add)
                self.p.op("dve", lambda e: e.reciprocal(out=rd, in_=rd), reads=[rd], writes=[rd])
                self.ts(OT[:, qb, h * 64:(h + 1) * 64], po[:, o_off:o_off + 64], rd, None, ALU.mult)
        return OT

    def c_outproj(self, l, OT, x0, nqb, g):
        n = nqb * 128
        OA = self.r(self.o_hb, [128, KC, 512])
        for qb in range(nqb):
            for hb in range(2):
                ps = self.PS[hb]
                for j in range(4):
                    c = hb * 4 + j
                    self.tr(ps[:, j * 128:(j + 1) * 128], OT[:, qb, c * 128:(c + 1) * 128].bitcast(F32))
                self.cp(OA[:, hb * 4:(hb + 1) * 4, qb * 128:(qb + 1) * 128], ps[:].rearrange("p (a b) -> p a b", a=4),
                        eng="act" if hb else "dve")
        wo = self.dram["w_out_c"].rearrange("(c p) o -> p c o", p=128)
        for half in range(2):
            WO = self.r(self.o_qf, [128, KC, 512])
            self.dma_r(WO, wo[:, :, half * 512:(half + 1) * 512], self.p.named_sem("woc"))
            for o in range(4):
                for c in range(KC):
                    self.mm(self.PS[4 + o][:, 0:n], WO[:, c, o * 128:(o + 1) * 128], OA[:, c, 0:n], c == 0, c == KC - 1)
            for o in range(4):
                oc = half * 4 + o
                xs_ = self.X[:, oc, x0:x0 + n]
                self.stt(xs_, self.PS[4 + o][:, 0:n], self.GSC[:, 1, oc, g:g + 1], xs_, ALU.mult, ALU.add)
        self.ln_range(l * 3 + 1, x0, n, self.o_ws)

    def mixer_c(self, l):
        d = self.dram
        self.o_kf = 0
        self.o_va = 4096
        self.o_kc = self.o_va + 16 * 262
        self.o_vca = self.o_kc + 1024
        self.o_hb = self.o_vca + 4 * 262
        self.o_ws = self.o_hb + 4096
        self.o_qf = self.o_ws + 4096
        self.o_ot = self.o_qf + 4096
        self.o_zq = self.o_ot + 4096
        assert self.o_zq + 512 <= self.NR, self.o_zq
        self.c_consts()
        KF = self.r(self.o_kf, [128, 2, 2048])
        VA = self.r(self.o_va, [128, 16, 262])
        KCF = self.r(self.o_kc, [128, 2, 512])
        VCA = self.r(self.o_vca, [128, 4, 262])
        wq = d["w_qkv"].rearrange("(k p) n -> p k n", p=128)
        WKV = self.r(self.o_ws, [128, KC, 512])

        def load_wkv():
            self.dma_r(WKV, wq[:, :, 1024:1536], self.p.named_sem("wkv"))
        for sq in range(2):
            x0 = sq * SEQ
            HB = self.c_modulate(x0, SEQ, 0)
            load_wkv()
            self.c_kv_project(HB, SEQ, WKV, lambda kp: KF[:, kp, 0:SEQ], lambda b: VA[:, b, :], 0, None, emit=(sq, 0))
            QF = self.c_q_project(HB, SEQ, None)
            keysets = []
            for kb in range(2):
                keysets.append((lambda kap, half, kb=kb: KF[64 * half:64 * half + 64, kap // 2, kb * 128:(kb + 1) * 128],
                                lambda kap, kb=kb: VA[:, kb, :], 0, 1, {}))
            OT = self.c_attend(QF, 2, keysets)
            self.c_outproj(l, OT, x0, 2, 0)
        stg = self.r(self.o_ot, [128, 4, 256])
        self.dma_r(stg, d["ck"].rearrange("(b p) c -> p b c", p=128), self.p.named_sem("ck"))
        for b in range(4):
            for kp in range(2):
                ps = self.PS[(b * 2 + kp) % 2][:, 0:128]
                self.tr(ps, stg[:, b, kp * 128:(kp + 1) * 128].bitcast(F32))
                self.cp(KCF[:, kp, b * 128:(b + 1) * 128], ps, eng="act" if kp else "dve")
        cvv = d["cv"].rearrange("(b p) c -> p b c", p=128)
        for kap in range(4):
            c0 = [0, 66, 130, 196][kap]
            self.dma_r(VCA[:, :, c0:c0 + 64], cvv[:, :, kap * 64:(kap + 1) * 64], self.p.named_sem("cv%d" % kap))
        for b in range(4):
            self.ts(VCA[:, b, 64:66], self.RESET[:, 1:3], 0.0, 1.0, ALU.mult, ALU.add)
            self.ts(VCA[:, b, 194:196], self.RESET[:, 1:3], 0.0, 1.0, ALU.mult, ALU.add)
        load_wkv()
        for t in range(4):
            x0 = TP + t * 512
            HB = self.c_modulate(x0, 512, 1)
            rope = self.rope_tables(t)
            self.c_kv_project(HB, 512, WKV, lambda kp, t=t: KF[:, kp, t * 512:(t + 1) * 512], lambda b: VA[:, b, :], 4 * t, rope)
        for t in range(4):
            x0 = TP + t * 512
            HB = self.c_modulate(x0, 512, 1)
            rope = self.rope_tables(t)
            QF = self.c_q_project(HB, 512, rope)
            keysets = []
            for cb in range(4):
                keysets.append((lambda kap, half, cb=cb: KCF[64 * half:64 * half + 64, kap // 2, cb * 128:(cb + 1) * 128],
                                lambda kap, cb=cb: VCA[:, cb, :], 0, 3, {}))
            for kb in range(4 * t - 1, 4 * t + 5):
                if kb < 0 or kb >= 16:
                    continue
                qlo = max(4 * t, kb - 1) - 4 * t
                qhi = min(4 * t + 3, kb + 1) - 4 * t
                masks = {}
                if 0 <= kb - 1 - 4 * t <= 3:
                    masks[kb - 1 - 4 * t] = self.LM1
                if 0 <= kb + 1 - 4 * t <= 3:
                    masks[kb + 1 - 4 * t] = self.LM3
                keysets.append((lambda kap, half, kb=kb: KF[64 * half:64 * half + 64, kap // 2, kb * 128:(kb + 1) * 128],
                                lambda kap, kb=kb: VA[:, kb, :], qlo, qhi, masks))
            OT = self.c_attend(QF, 4, keysets)
            self.c_outproj(l, OT, x0, 4, 1)

    def build(self):
        self.o_w13 = 0
        self.o_w2 = 8192
        self.o_xm = 12288
        self.o_hid = 16384
        self.o_ln = self.o_hid + 18 * 512
        self.o_sg = 0
        self.o_lnf = 1024
        self.consts()
        self.EPSLN = self.sb("EPSLN", [128, 1])
        self.memset(self.EPSLN[:], LN_EPS / (ALPHA * ALPHA))
        self.ONEC = self.sb("ONEC", [128, 1])
        self.memset(self.ONEC[:], 1.0)
        self.load_small()
        self.load_x()
        for l in range(DEPTH):
            self.mod_vectors(l)
            for t in range(NT):
                self.ffn_tile(l, 0, t)
            if self.stage == 1 + 3 * l:
                break
            if l == 0:
                self.mixer_ab(l)
            else:
                self.mixer_c(l)
            if self.stage == 2 + 3 * l:
                break
            for t in range(NT):
                self.ffn_tile(l, 1, t)
            if self.stage == 3 + 3 * l:
                break
        self.store_x()
        self.p.finish("sp")
        return self.nc


_CACHE = {}


def get_program(stage=99):
    if stage not in _CACHE:
        _CACHE[stage] = Builder(stage).build()
    return _CACHE[stage]


def shard_inputs(inp):
    f = lambda a: np.ascontiguousarray(np.asarray(a, dtype=np.float32))
    maps = []
    for c in range(NCORES):
        sb = c // 4
        m = {
            "xp": f(inp["x_prompt"][2 * c:2 * c + 2].reshape(TP, D)),
            "xs": f(inp["x_sample"][sb]),
            "st_h": f(inp["state_hgrn"][sb, 0]),
            "st_g": f(inp["state_gla"][sb, 0]),
            "ck": f(inp["cache_k"][sb, 0].reshape(PAST, C_KV * HD)),
            "cv": f(inp["cache_v"][sb, 0].reshape(PAST, C_KV * HD)),
            "cvec": f(np.stack([inp["c_ctx"], inp["c"][sb]], axis=0)),
            "w_mod": f(inp["w_mod"]),
            "b_mod": f(inp["b_mod"]),
            "ln_g": f(inp["ln_g"].reshape(DEPTH * 3, D)),
            "ln_b": f(inp["ln_b"].reshape(DEPTH * 3, D)),
            "ffn_w1": f(inp["ffn_w1"]),
            "ffn_w3": f(inp["ffn_w3"]),
            "ffn_w2": f(inp["ffn_w2"]),
            "w_in_ab": f(inp["w_in_ab"][0]),
            "hgrn_lb": f(inp["hgrn_lb"]),
            "gate_up": f(inp["gla_gate_up"][0]),
            "gate_b": f(inp["gla_gate_b"][0]),
            "norm_a": f(inp["norm_a"]),
            "norm_b": f(inp["norm_b"]),
            "w_out_ab": f(inp["w_out_ab"][0]),
            "w_qkv": f(inp["w_qkv_c"][0]),
            "sink": f(inp["sink_c"]),
            "w_out_c": f(inp["w_out_c"][0]),
        }
        maps.append(m)
    return maps


def gather_outputs(res):
    r = res.results
    B = 16
    yp = np.concatenate([r[c]["yp"].reshape(2, SEQ, D) for c in range(NCORES)], axis=0)
    ys = np.stack([r[0]["ys"], r[4]["ys"]], axis=0)
    nsh = np.concatenate([r[c]["ns_h"].reshape(2, 1, 2, A_HEADS, 128, 128) for c in range(NCORES)], axis=0)
    nsg = np.concatenate([r[c]["ns_g"].reshape(2, 1, 2, B_HEADS, B_DK, 128) for c in range(NCORES)], axis=0)
    nk = np.concatenate([r[c]["nk"].reshape(2, 1, SEQ, C_KV, HD) for c in range(NCORES)], axis=0)
    nv = np.concatenate([r[c]["nv"].reshape(2, 1, SEQ, C_KV, HD) for c in range(NCORES)], axis=0)
    return (yp.astype(np.float32), ys.astype(np.float32), nsh.astype(np.float32), nsg.astype(np.float32),
            nk.astype(np.float32), nv.astype(np.float32))


def kernel(**inputs):
    stage = int(os.environ.get("MK_STAGE", "99"))
    nc = get_program(stage)
    maps = shard_inputs(inputs)
    res = run_bass_kernel_spmd(nc, maps, core_ids=list(range(NCORES)))
    return gather_outputs(res)
```

```python
import os
import numpy as np
import concourse.bass as bass
import concourse.mybir as mybir
from concourse.bass_utils import run_bass_kernel_spmd

F32 = mybir.dt.float32
F32R = mybir.dt.float32r
AF = mybir.ActivationFunctionType
ALU = mybir.AluOpType

D = 1024
KC = 8
DFF = 2816
FC = 22
NMOD = 9
DEPTH = 2
SEQ = 256
TP = 512
TS = 2048
T = TP + TS
NT = T // 512
A_HEADS = 4
B_HEADS = 4
B_DK = 64
GATE_RANK = 16
GLA_TAU = 16.0
AB_IN = 4128
C_HEADS = 16
C_KV = 4
HD = 64
PAST = 512
ALPHA = (2.0 * DEPTH) ** 0.25
LN_EPS = 1e-5
RMS_EPS = 1e-6
ROPE_BASE = 10000.0
NCORES = 8


class Prog:
    def __init__(self, nc):
        self.nc = nc
        self.E = {"pe": nc.tensor, "act": nc.scalar, "dve": nc.vector, "pool": nc.gpsimd, "sp": nc.sync}
        self.sems = {}
        self.cnt = {}
        for e in self.E:
            self.sems[e] = nc.alloc_semaphore("s_" + e)
            self.cnt[e] = 0
        self.seen = {e: {} for e in self.E}
        self.ndma_sem = 0
        self.n_inst = 0
        self.n_wait = 0
        self.self_sync = set(os.environ.get("MK_SELFSYNC", "act,dve,pool").split(",")) - {""}
        self.named = {}
        self.mem = {}

    def named_sem(self, name):
        if name not in self.named:
            self.named[name] = self.new_dma_sem()
        return self.named[name]

    def new_dma_sem(self):
        k = "d%d" % self.ndma_sem
        self.ndma_sem += 1
        self.sems[k] = self.nc.alloc_semaphore("s_" + k)
        self.cnt[k] = 0
        return k

    @staticmethod
    def box(ap):
        esz = mybir.dt.size(ap.dtype)
        dims = ap.ap
        off = ap.offset
        name = ap.tensor.name
        if str(ap.space) == "DRAM":
            lo = off
            hi = off
            for st, cn in dims:
                d = (cn - 1) * st
                if d > 0:
                    hi += d
                else:
                    lo += d
            return name, 0, 1, lo * esz, (hi + 1) * esz
        pst, pcn = dims[0]
        if pst <= 0:
            pst = 1 << 40
        p0 = off // pst
        c0 = off % pst
        if str(ap.space) == "PSUM":
            q0 = (p0 // 32) * 32
            q1 = ((p0 + pcn + 31) // 32) * 32
            return name, q0, q1, 0, 2048
        lo = c0
        hi = c0
        for st, cn in dims[1:]:
            d = (cn - 1) * st
            if d > 0:
                hi += d
            else:
                lo += d
        return name, p0, p0 + pcn, lo * esz, (hi + 1) * esz

    def _collect(self, reads, writes):
        deps = []
        rb = [self.box(a) for a in reads]
        wb = [self.box(a) for a in writes]
        for (name, p0, p1, lo, hi) in rb:
            m = self.mem.get(name)
            if m is None:
                continue
            for r in m[0]:
                if r[0] < p1 and p0 < r[1] and r[2] < hi and lo < r[3]:
                    deps.append((r[4], r[5]))
        for (name, p0, p1, lo, hi) in wb:
            m = self.mem.get(name)
            if m is None:
                continue
            for lst in m:
                for r in lst:
                    if r[0] < p1 and p0 < r[1] and r[2] < hi and lo < r[3]:
                        deps.append((r[4], r[5]))
        return deps, rb, wb

    def _record(self, rb, wb, key, val):
        for (name, p0, p1, lo, hi) in wb:
            m = self.mem.setdefault(name, [[], []])
            for i in (0, 1):
                m[i] = [r for r in m[i] if not (p0 <= r[0] and r[1] <= p1 and lo <= r[2] and r[3] <= hi)]
            m[0].append([p0, p1, lo, hi, key, val])
        for (name, p0, p1, lo, hi) in rb:
            m = self.mem.setdefault(name, [[], []])
            m[1] = [r for r in m[1] if not (r[4] == key and p0 <= r[0] and r[1] <= p1 and lo <= r[2] and r[3] <= hi)]
            m[1].append([p0, p1, lo, hi, key, val])

    def _wait(self, e, deps):
        best = {}
        for k, v in deps:
            if best.get(k, 0) < v:
                best[k] = v
        for k, v in best.items():
            if k == e and e not in self.self_sync:
                continue
            if self.seen[e].get(k, 0) < v:
                self.E[e].wait_ge(self.sems[k], v)
                self.seen[e][k] = v
                self.n_wait += 1

    def op(self, e, fn, reads=(), writes=()):
        deps, rb, wb = self._collect(reads, writes)
        self._wait(e, deps)
        ins = fn(self.E[e])
        ins.then_inc(self.sems[e], 1)
        self.cnt[e] += 1
        self._record(rb, wb, e, self.cnt[e])
        self.n_inst += 1
        return ins

    def dma(self, q, out, in_, sem):
        deps, rb, wb = self._collect([in_], [out])
        self._wait(q, deps)
        ins = self.E[q].dma_start(out=out, in_=in_)
        ins.then_inc(self.sems[sem], 16)
        self.cnt[sem] += 16
        self._record(rb, wb, sem, self.cnt[sem])
        self.n_inst += 1
        return ins

    def finish(self, e="sp"):
        deps = []
        for name, m in self.mem.items():
            for lst in m:
                for r in lst:
                    deps.append((r[4], r[5]))
        self._wait(e, deps)


class Builder:
    def __init__(self, stage=99):
        self.stage = stage
        nc = bass.Bass("TRN2", target_bir_lowering=False)
        nc.dge_precook = False
        self.nc = nc
        self.p = Prog(nc)
        self.dram = {}
        self.decl_io()
        self.alloc()

    def din(self, name, shape):
        self.dram[name] = self.nc.dram_tensor(name, list(shape), F32, kind="ExternalInput").ap()
        return self.dram[name]

    def dout(self, name, shape):
        self.dram[name] = self.nc.dram_tensor(name, list(shape), F32, kind="ExternalOutput").ap()
        return self.dram[name]

    def decl_io(self):
        self.din("xp", [TP, D])
        self.din("xs", [TS, D])
        self.din("st_h", [2, A_HEADS, 128, 128])
        self.din("st_g", [2, B_HEADS, B_DK, 128])
        self.din("ck", [PAST, C_KV * HD])
        self.din("cv", [PAST, C_KV * HD])
        self.din("cvec", [2, D])
        self.din("w_mod", [DEPTH, D, NMOD * D])
        self.din("b_mod", [DEPTH, NMOD * D])
        self.din("ln_g", [DEPTH * 3, D])
        self.din("ln_b", [DEPTH * 3, D])
        self.din("ffn_w1", [DEPTH, 2, D, DFF])
        self.din("ffn_w3", [DEPTH, 2, D, DFF])
        self.din("ffn_w2", [DEPTH, 2, DFF, D])
        self.din("w_in_ab", [D, AB_IN])
        self.din("hgrn_lb", [2, 2, 512])
        self.din("gate_up", [2, GATE_RANK, 256])
        self.din("gate_b", [2, 256])
        self.din("norm_a", [1, 128])
        self.din("norm_b", [1, 128])
        self.din("w_out_ab", [D, D])
        self.din("w_qkv", [D, 1536])
        self.din("sink", [1, C_HEADS])
        self.din("w_out_c", [D, D])
        self.dout("yp", [TP, D])
        self.dout("ys", [TS, D])
        self.dout("ns_h", [2, 2, A_HEADS, 128, 128])
        self.dout("ns_g", [2, 2, B_HEADS, B_DK, 128])
        self.dout("nk", [2, SEQ, C_KV * HD])
        self.dout("nv", [2, SEQ, C_KV * HD])

    def sb(self, name, shape, dt=F32):
        return self.nc.alloc_sbuf_tensor(name, list(shape), dt)

    def alloc(self):
        nc = self.nc
        self.X = self.sb("X", [128, KC, T])
        self.IDENT = self.sb("IDENT", [128, 128])
        self.ONES = self.sb("ONES", [128, 128], F32R)
        self.U1 = self.sb("U1", [128, 256])
        self.U2 = self.sb("U2", [128, 256], F32R)
        self.MF = self.U1[:, 0:128]
        self.MB = self.U1[:, 128:256]
        self.LM1 = self.U1[:, 0:128]
        self.LM3 = self.U1[:, 128:256]
        self.RESET = self.sb("RESET", [128, 256])
        self.PARS = self.sb("PARS", [128, 22])
        self.LB = self.sb("LB", [128, 8])
        self.OML = self.sb("OML", [128, 8])
        self.NGB = self.sb("NGB", [128, 4])
        self.GUP = self.U2[:, 0:256]
        self.PERMR = self.U2[:, 0:128]
        self.EPSR = self.sb("EPSR", [128, 1])
        self.ES = self.sb("ES", [128, 16])
        self.TCOS = self.sb("TCOS", [128, 64])
        self.TSIN = self.sb("TSIN", [128, 64])
        self.MR = self.sb("MR", [128, 1])
        self.MC = self.sb("MC", [128, 1])
        self.SGN = self.sb("SGN", [128, 1])
        self.MODV = self.sb("MODV", [128, 72, 2])
        self.OPS = self.sb("OPS", [128, 3, KC, 2])
        self.GSC = self.sb("GSC", [128, 3, KC, 2])
        self.BM = self.sb("BM", [128, 72])
        self.LNG = self.sb("LNG", [128, 48])
        self.LNB = self.sb("LNB", [128, 48])
        self.CS = self.sb("CS", [128, 2, KC], F32R)
        self.NR = 27 * 1024
        self.NF = 3072
        self.R = self.sb("R", [128, self.NR], F32R)
        self.Fm = self.sb("Fm", [128, self.NF])
        self.PS = [nc.alloc_psum_tensor("PS%d" % i, [128, 512], F32) for i in range(8)]
        self.sem_w13 = [self.p.new_dma_sem() for _ in range(2)]
        self.sem_w2 = [self.p.new_dma_sem() for _ in range(4)]
        self.sem_wab = [self.p.new_dma_sem() for _ in range(7)]
        self.sem_st = self.p.new_dma_sem()
        self.sem_io = [self.p.new_dma_sem() for _ in range(2)]
        self.sem_misc = self.p.new_dma_sem()
        self.sem_out = self.p.new_dma_sem()
        self.n13 = 0
        self.n2 = 0
        self.nio = 0

    @staticmethod
    def _view(base, off, shape, total):
        n = 1
        for x in shape[1:]:
            n *= x
        assert off + n <= total, (off, n, total)
        v = base[0:shape[0], off:off + n]
        if len(shape) == 3:
            v = v.rearrange("p (a b) -> p a b", a=shape[1])
        elif len(shape) == 4:
            v = v.rearrange("p (a b c) -> p a b c", a=shape[1], b=shape[2])
        return v

    def r(self, off, shape):
        return self._view(self.R, off, shape, self.NR)

    def f(self, off, shape):
        return self._view(self.Fm, off, shape, self.NF)

    def mm(self, out, lhsT, rhs, start, stop):
        self.p.op("pe", lambda e: e.matmul(out, lhsT=lhsT, rhs=rhs, start=start, stop=stop),
                  reads=[lhsT, rhs], writes=[out])

    def tr(self, out, in_, n=128):
        ident = self.IDENT[0:in_.shape[0], 0:in_.shape[0]]
        self.p.op("pe", lambda e: e.transpose(out=out, in_=in_, identity=ident), reads=[in_, ident], writes=[out])

    def act(self, out, in_, func, bias=None, scale=None, eng="act"):
        kw = {}
        rd = [in_]
        if bias is not None:
            kw["bias"] = bias
            if not isinstance(bias, (int, float)):
                rd.append(bias)
        if scale is not None:
            kw["scale"] = scale
            if not isinstance(scale, (int, float)):
                rd.append(scale)
        self.p.op("act", lambda e: e.activation(out=out, in_=in_, func=func, **kw), reads=rd, writes=[out])

    def ts(self, out, in0, s1, s2, op0, op1=None, eng="dve"):
        rd = [in0]
        for s in (s1, s2):
            if s is not None and not isinstance(s, (int, float)):
                rd.append(s)
        if op1 is None:
            self.p.op(eng, lambda e: e.tensor_scalar(out=out, in0=in0, scalar1=s1, scalar2=None, op0=op0), reads=rd, writes=[out])
        else:
            self.p.op(eng, lambda e: e.tensor_scalar(out=out, in0=in0, scalar1=s1, scalar2=s2, op0=op0, op1=op1), reads=rd, writes=[out])

    def tt(self, out, in0, in1, op, eng="dve"):
        self.p.op(eng, lambda e: e.tensor_tensor(out=out, in0=in0, in1=in1, op=op), reads=[in0, in1], writes=[out])

    def stt(self, out, in0, scalar, in1, op0, op1, eng="dve"):
        rd = [in0, in1]
        if not isinstance(scalar, (int, float)):
            rd.append(scalar)
        self.p.op(eng, lambda e: e.scalar_tensor_tensor(out=out, in0=in0, scalar=scalar, in1=in1, op0=op0, op1=op1), reads=rd, writes=[out])

    def cp(self, out, in_, eng="dve"):
        if eng == "act":
            self.act(out, in_, AF.Copy)
        else:
            self.p.op(eng, lambda e: e.tensor_copy(out=out, in_=in_), reads=[in_], writes=[out])

    def memset(self, ap, val, eng="pool"):
        self.p.op(eng, lambda e: e.memset(ap, val), writes=[ap])

    def dma(self, out, in_, sem, q="sp"):
        self.p.dma(q, out, in_, sem)

    def dma_r(self, out, in_, sem, q="sp"):
        self.p.dma(q, out if out.dtype == F32R else out.bitcast(F32R), in_.bitcast(F32R), sem)

    def consts(self):
        self.memset(self.IDENT[:], 1.0)
        self.p.op("pool", lambda e: e.affine_select(out=self.IDENT[:], in_=self.IDENT[:], pattern=[[-1, 128]],
                                                    compare_op=ALU.is_equal, fill=0.0, base=0, channel_multiplier=1),
                  reads=[self.IDENT[:]], writes=[self.IDENT[:]])
        tmp = self.f(0, [128, 128])
        self.memset(tmp, 1.0)
        self.cp(self.ONES[:], tmp)
        for (M, cm, st) in ((self.MF, -1, 1), (self.MB, 1, -1)):
            self.memset(M[:], 1.0)
            self.p.op("pool", lambda e: e.affine_select(out=M[:], in_=M[:], pattern=[[st, 128]], compare_op=ALU.is_ge,
                                                        fill=0.0, base=0, channel_multiplier=cm),
                      reads=[M[:]], writes=[M[:]])
        self.memset(self.MF[0:64, 64:128], 0.0)
        self.memset(self.MB[64:128, 0:64], 0.0)
        self.memset(self.RESET[:], 1.0)
        self.memset(self.RESET[:, 0:256:64], 0.0)
        self.memset(self.EPSR[:], RMS_EPS)

    def load_fm(self, dst, src_rows, nrows):
        st = self.f(0, [128, 128])
        self.dma(st[0:nrows, :], src_rows, self.sem_misc)
        ps = self.PS[7][:, 0:nrows]
        self.tr(ps, st[0:nrows, :])
        self.cp(dst, ps)

    def load_small(self):
        self.load_fm(self.LNG[:], self.dram["ln_g"].rearrange("r (k p) -> (r k) p", p=128), 48)
        self.load_fm(self.LNB[:], self.dram["ln_b"].rearrange("r (k p) -> (r k) p", p=128), 48)
        st = self.f(0, [128, 128])
        self.dma(st[0:16, :], self.dram["cvec"].rearrange("g (k p) -> (g k) p", p=128), self.sem_misc)
        ps = self.PS[7][:, 0:16]
        self.tr(ps, st[0:16, :])
        self.act(self.CS[:].rearrange("p g k -> p (g k)"), ps, AF.Silu)

    def load_x(self):
        for tb in range(T // 128):
            src = self.dram["xp"][tb * 128:(tb + 1) * 128, :] if tb < TP // 128 else \
                self.dram["xs"][tb * 128 - TP:(tb + 1) * 128 - TP, :]
            s = self.nio % 2
            self.nio += 1
            st = self.f(s * 1024, [128, 1024])
            self.dma(st, src, self.sem_io[s])
            for hb in range(2):
                ps = self.PS[(tb * 2 + hb) % 4]
                for j in range(4):
                    k = hb * 4 + j
                    self.tr(ps[:, j * 128:(j + 1) * 128], st[:, k * 128:(k + 1) * 128])
                dst = self.X[:, hb * 4:(hb + 1) * 4, tb * 128:(tb + 1) * 128]
                self.cp(dst, ps[:].rearrange("p (a b) -> p a b", a=4), eng="dve" if hb == 0 else "act")

    def store_x(self):
        for tb in range(T // 128):
            dst = self.dram["yp"][tb * 128:(tb + 1) * 128, :] if tb < TP // 128 else \
                self.dram["ys"][tb * 128 - TP:(tb + 1) * 128 - TP, :]
            s = self.nio % 2
            self.nio += 1
            st = self.f(s * 1024, [128, 1024])
            for hb in range(2):
                ps = self.PS[(tb * 2 + hb) % 4]
                for j in range(4):
                    k = hb * 4 + j
                    self.tr(ps[:, j * 128:(j + 1) * 128], self.X[:, k, tb * 128:(tb + 1) * 128])
                self.cp(st[:, hb * 512:(hb + 1) * 512], ps[:], eng="dve" if hb == 0 else "act")
            self.dma(dst, st, self.sem_io[s])

    def mod_vectors(self, l):
        self.load_fm(self.BM[:], self.dram["b_mod"][l].rearrange("(r p) -> r p", p=128), 72)
        wm = self.dram["w_mod"][l].rearrange("(k p) n -> p k n", p=128)
        pm = self.PS[6][:, 0:144].rearrange("p (c g) -> p c g", g=2)
        for blk in range(18):
            s = self.n13 % 2
            self.n13 += 1
            wt = self.r(self.o_w13 + s * 4096, [128, KC, 512])
            self.dma_r(wt, wm[:, :, blk * 512:(blk + 1) * 512], self.sem_w13[s])
            for q in range(4):
                oc = blk * 4 + q
                for k in range(KC):
                    self.mm(pm[:, oc, :], wt[:, k, q * 128:(q + 1) * 128], self.CS[:, :, k], k == 0, k == KC - 1)
        for g in range(2):
            self.tt(self.MODV[:, :, g], pm[:, :, g], self.BM[:], ALU.add)
        gmul = [0.5 / ALPHA, 1.0 / ALPHA, 0.5 / ALPHA]
        for s in range(3):
            self.ts(self.OPS[:, s, :, :], self.MODV[:, (3 * s + 1) * 8:(3 * s + 2) * 8, :], 1.0, None, ALU.add)
            self.ts(self.GSC[:, s, :, :], self.MODV[:, (3 * s + 2) * 8:(3 * s + 3) * 8, :], gmul[s], None, ALU.mult)

    def shift(self, s, k, g):
        return self.MODV[:, 3 * s * 8 + k, g:g + 1]

    def ln_range(self, lnidx, x0, n, o_r):
        ts_ = slice(x0, x0 + n)
        pa, pb = self.PS[4][:, 0:n], self.PS[5][:, 0:n]
        for k in range(KC):
            zr = self.r(o_r + (k % 2) * 512, [128, 512])[:, 0:n]
            sq = self.r(o_r + 1024 + (k % 2) * 512, [128, 512])[:, 0:n]
            self.act(zr, self.X[:, k, ts_], AF.Copy, scale=1.0 / 1024.0)
            self.act(sq, self.X[:, k, ts_], AF.Square, scale=1.0 / 32.0)
            self.mm(pa, self.ONES[:], zr, k == 0, k == KC - 1)
            self.mm(pb, self.ONES[:], sq, k == 0, k == KC - 1)
        m2 = self.f(self.o_lnf, [128, 512])[:, 0:n]
        self.act(m2, pa, AF.Square)
        self.tt(m2, pb, m2, ALU.subtract)
        self.act(m2, m2, AF.Sqrt, bias=self.EPSLN[:, 0:1])
        self.p.op("dve", lambda e: e.reciprocal(out=m2, in_=m2), reads=[m2], writes=[m2])
        for k in range(KC):
            xk = self.X[:, k, ts_]
            self.tt(xk, xk, pa, ALU.subtract)
            self.stt(xk, xk, self.LNG[:, lnidx * 8 + k:lnidx * 8 + k + 1], m2, ALU.mult, ALU.mult)
            self.act(xk, xk, AF.Identity, bias=self.LNB[:, lnidx * 8 + k:lnidx * 8 + k + 1])

    def ffn_tile(self, l, j, t):
        g = 0 if t == 0 else 1
        s = 0 if j == 0 else 2
        ts_ = slice(t * 512, (t + 1) * 512)
        XM = self.r(self.o_xm, [128, KC, 512])
        HID = self.r(self.o_hid, [128, FC, 512])
        for k in range(KC):
            self.ts(XM[:, k, :], self.X[:, k, ts_], self.OPS[:, s, k, g:g + 1], self.shift(s, k, g), ALU.mult, ALU.add)
        w1 = self.dram["ffn_w1"][l, j].rearrange("(k p) f -> p k f", p=128)
        w3 = self.dram["ffn_w3"][l, j].rearrange("(k p) f -> p k f", p=128)
        w2 = self.dram["ffn_w2"][l, j].rearrange("(c p) o -> p c o", p=128)
        for fb in range(FC // 2):
            sl = self.n13 % 2
            self.n13 += 1
            wt = self.r(self.o_w13 + sl * 4096, [128, 2, KC, 256])
            self.dma_r(wt[:, 0], w1[:, :, fb * 256:(fb + 1) * 256], self.sem_w13[sl])
            self.dma_r(wt[:, 1], w3[:, :, fb * 256:(fb + 1) * 256], self.p.named_sem("w3_%d" % sl))
            for c in range(2):
                f = 2 * fb + c
                p1, p3 = self.PS[f % 2], self.PS[2 + f % 2]
                for k in range(KC):
                    self.mm(p1[:], wt[:, 0, k, c * 128:(c + 1) * 128], XM[:, k, :], k == 0, k == KC - 1)
                for k in range(KC):
                    self.mm(p3[:], wt[:, 1, k, c * 128:(c + 1) * 128], XM[:, k, :], k == 0, k == KC - 1)
                sg = self.f(self.o_sg + (f % 2) * 512, [128, 512])
                self.act(sg, p1[:], AF.Silu)
                self.tt(HID[:, f, :], sg, p3[:], ALU.mult)
        for half in range(2):
            for fb in range(FC // 2):
                sl = self.n2 % 4
                self.n2 += 1
                wt = self.r(self.o_w2 + sl * 1024, [128, 2, 512])
                self.dma_r(wt, w2[:, 2 * fb:2 * fb + 2, half * 512:(half + 1) * 512], self.sem_w2[sl])
                for c in range(2):
                    f = 2 * fb + c
                    for o in range(4):
                        self.mm(self.PS[4 + o][:], wt[:, c, o * 128:(o + 1) * 128], HID[:, f, :], f == 0, f == FC - 1)
            for o in range(4):
                oc = half * 4 + o
                self.stt(self.X[:, oc, ts_], self.PS[4 + o][:], self.GSC[:, s, oc, g:g + 1], self.X[:, oc, ts_], ALU.mult, ALU.add)
        self.ln_range(l * 3 + s, t * 512, 512, self.o_ln)

    def ab_params(self):
        st = self.f(0, [128, 128])
        d = self.dram
        self.dma(st[0:1, :], d["norm_a"], self.sem_misc)
        self.dma(st[1:2, :], d["norm_b"], self.sem_misc)
        self.dma(st[2:6, :], d["gate_b"].rearrange("a (j p) -> (a j) p", p=128), self.sem_misc)
        self.dma(st[6:22, :], d["hgrn_lb"].rearrange("a b (h p) -> (a b h) p", p=128), self.sem_misc)
        ps = self.PS[7][:, 0:22]
        self.tr(ps, st[0:22, :])
        self.cp(self.PARS[:], ps)
        P = self.PARS
        for dr in range(2):
            a = P[:, 6 + dr * 8:6 + dr * 8 + 4]
            b = P[:, 6 + dr * 8 + 4:6 + dr * 8 + 8]
            self.tt(self.LB[:, dr * 4:dr * 4 + 4], a, b, ALU.subtract)
        self.act(self.LB[:], self.LB[:], AF.Sigmoid)
        self.ts(self.OML[:], self.LB[:], -1.0, 1.0, ALU.mult, ALU.add)
        self.ts(self.NGB[:], P[:, 2:6], -1.0, None, ALU.mult)

    def ab_unit_pass(self, u, dr, x0, nht, g, sample, sidx):
        hg = u < 4
        j = u - 4
        d = self.dram
        win = d["w_in_ab"].rearrange("(k p) n -> p k n", p=128)
        if hg:
            cols = [("q", u * 128), ("f", (1024 if dr == 0 else 1536) + u * 128), ("i", 512 + u * 128)]
            if dr == 1:
                cols.append(("g0", 2048 + u * 128))
        else:
            cols = [("q", 2560 + j * 128), ("f", 2816 + j * 128), ("v0", 3072 + j * 256), ("v1", 3072 + j * 256 + 128),
                    ("bz", 4000)]
            if dr == 1:
                cols += [("g0", 3584 + j * 256), ("g1", 3584 + j * 256 + 128)]
        W = {}
        for i, (nm, c0) in enumerate(cols):
            W[nm] = self.r(self.o_wab + i * 1024, [128, KC, 128])
            self.dma_r(W[nm], win[:, :, c0:c0 + 128], self.sem_wab[i])
        nh = 1 if hg else 2
        heads = list(range(nh))
        vw = 128 * nh
        NSB = 6
        SB = [self.r(self.o_sb + i * 128, [128, 128]) for i in range(NSB)]
        SC = [self.f(1536, [128, 128]), self.f(1664, [128, 128])]
        st = {"si": 0, "sc": 0}
        if not hg:
            self.ts(self.GUP[64:128, :], self.RESET[64:128, :], 0.0, None, ALU.mult)
            self.dma_r(self.GUP[96 + 16 * dr:112 + 16 * dr, :], d["gate_up"][dr], self.p.named_sem("gup"))
        src0 = None
        if sample:
            src0 = d["st_h"][dr, u] if hg else d["st_g"][dr, 2 * j:2 * j + 2].rearrange("h d e -> (h d) e")
        if dr == 0:
            if sample:
                self.dma_r(SB[0], src0, self.sem_st)
            else:
                self.ts(SB[0], self.IDENT[:], 0.0, None, ALU.mult)
        else:
            if sample:
                self.dma(SC[0], src0, self.sem_st)
            else:
                self.memset(SC[0], 0.0)
        HB = self.r(self.o_hb, [128, KC, 256])
        QD = self.r(self.o_qd, [128, 256])
        KI = self.r(self.o_ki, [128, 256])
        VT = self.r(self.o_vt, [128, 2, 256])
        AT = [self.r(self.o_at + i * 128, [128, 128]) for i in range(2)]
        BZ = self.r(self.o_at, [128, 256])
        SQ = self.r(self.o_at, [128, 256])
        KIT = self.r(self.o_hb + 256, [128, 2, 128])
        F0 = self.f(0, [128, 256])
        F1 = self.f(256, [128, 256])
        F2 = self.f(512, [128, 256])
        TMP = self.f(768, [128, 128])
        AC = self.f(896, [128, 4])
        GS = [self.f(1024, [128, 256]), self.f(1280, [128, 256])]
        PS = self.PS
        psq, psf = PS[0][:, 0:256], PS[0][:, 256:512]
        PSO = [PS[4], PS[7]]
        MASK = self.MF if dr == 0 else self.MB
        order = list(range(nht)) if dr == 0 else list(range(nht - 1, -1, -1))
        bs = [0, 1] if dr == 0 else [1, 0]
        cseq = [(b, c) for b in bs for c in bs]

        def pr_(hh):
            return slice(0, 128) if hg else slice(64 * hh, 64 * hh + 64)

        def stage_a(hti):
            t0 = x0 + hti * 256
            for k in range(KC):
                self.ts(HB[:, k, :], self.X[:, k, t0:t0 + 256], self.OPS[:, 1, k, g:g + 1], self.shift(1, k, g), ALU.mult, ALU.add)
            for k in range(KC):
                self.mm(psq, W["q"][:, k, :], HB[:, k, :], k == 0, k == KC - 1)
            for k in range(KC):
                self.mm(psf, W["f"][:, k, :], HB[:, k, :], k == 0, k == KC - 1)
            for b in range(2):
                for vv in range(nh):
                    wv = W["i"] if hg else W["v%d" % vv]
                    for k in range(KC):
                        self.mm(PS[1][:, b * 256 + vv * 128:b * 256 + vv * 128 + 128], HB[:, k, b * 128:(b + 1) * 128], wv[:, k, :],
                                k == 0, k == KC - 1)
            if not hg:
                for k in range(KC):
                    self.mm(PS[3][:, 0:256], W["bz"][:, k, :], HB[:, k, :], k == 0, k == KC - 1)
            if dr == 1:
                for hh in heads:
                    for k in range(KC):
                        self.mm(PS[2][:, hh * 256:(hh + 1) * 256], W["g%d" % hh][:, k, :], HB[:, k, :], k == 0, k == KC - 1)

        def stage_b():
            for b in range(2):
                if hg:
                    self.act(VT[:, b, 0:128], PS[1][:, b * 256:b * 256 + 128], AF.Silu)
                else:
                    self.cp(VT[:, b, :], PS[1][:, b * 256:(b + 1) * 256], eng="act")
            if hg:
                self.act(F0, psf, AF.Sigmoid)
                self.ts(F0, F0, self.OML[:, dr * 4 + u:dr * 4 + u + 1], self.LB[:, dr * 4 + u:dr * 4 + u + 1], ALU.mult, ALU.add)
                self.ts(F1, F0, -1.0, 1.0, ALU.mult, ALU.add)
                self.act(F0, F0, AF.Ln)
                kf = F1
            else:
                self.cp(BZ, PS[3][:, 0:256], eng="act")
                psl = PS[3][:, 256:512]
                self.mm(psl, self.GUP[64:128, j * 128:(j + 1) * 128], BZ[64:128, :], True, True)
                self.act(F0, psl, AF.Exp, scale=-1.0, bias=self.NGB[:, dr * 2 + j:dr * 2 + j + 1])
                self.act(F0, F0, AF.Ln, bias=self.ONEC[:, 0:1])
                self.ts(F0, F0, -1.0 / GLA_TAU, None, ALU.mult)
                kf = psf
            self.p.op("dve", lambda e: e.tensor_tensor_scan(out=F2, data0=self.RESET[:, 0:256], data1=F0, initial=0.0,
                                                            op0=ALU.mult, op1=ALU.add),
                      reads=[self.RESET[:, 0:256], F0], writes=[F2])
            if dr == 0:
                self.act(F0, F2, AF.Exp)
                self.cp(AC, F0[:, 63:256:64])
                if hg:
                    self.tt(QD, psq, F0, ALU.mult)
                else:
                    self.stt(QD, psq, B_DK ** -0.5, F0, ALU.mult, ALU.mult)
                self.act(F2, F2, AF.Exp, scale=-1.0)
                self.tt(KI, kf, F2, ALU.mult)
            else:
                self.act(AC, F2[:, 63:256:64], AF.Exp)
                self.tt(F0, F0, F2, ALU.subtract)
                self.act(F2, F0, AF.Exp)
                if hg:
                    self.tt(QD, psq, F2, ALU.mult)
                else:
                    self.stt(QD, psq, B_DK ** -0.5, F2, ALU.mult, ALU.mult)
                self.act(F0, F0, AF.Exp, scale=-1.0)
                self.tt(KI, kf, F0, ALU.mult)
                for hh in heads:
                    self.act(GS[hh], PS[2][:, hh * 256:(hh + 1) * 256], AF.Silu)

        def stage_c():
            pskv = []
            for idx, (b_, c_) in enumerate(cseq):
                bank = (PS[6] if c_ == 0 else PS[7]) if hg else (PS[6] if c_ == 0 else PS[1])
                w_ = 128 if hg else 256
                slot = 0 if idx < 2 else 1
                pskv.append(bank[:, slot * w_:(slot + 1) * w_])
            first = [True, True]
            for hi, hh in enumerate(heads):
                pat = PS[5] if hi == 0 else PS[2 if dr == 0 else 3]
                pat_off = 0 if (hi == 0 or dr == 0) else 256
                for b in bs:
                    self.mm(pat[:, pat_off + b * 128:pat_off + (b + 1) * 128], KI[pr_(hh), b * 128:(b + 1) * 128],
                            QD[pr_(hh), b * 128:(b + 1) * 128], True, True)
                if hi == 0:
                    for b in bs:
                        self.tr(PS[3][:, b * 128:(b + 1) * 128], KI[:, b * 128:(b + 1) * 128].bitcast(F32))
                for b in bs:
                    self.tt(AT[b], pat[:, pat_off + b * 128:pat_off + (b + 1) * 128], MASK[:], ALU.mult)
                if hi == 0:
                    for b in bs:
                        self.cp(KIT[:, b, :], PS[3][:, b * 128:(b + 1) * 128], eng="act")
                    for idx, (b, c) in enumerate(cseq):
                        self.mm(pskv[idx], KIT[c * 64:(c + 1) * 64, b, :], VT[c * 64:(c + 1) * 64, b, 0:vw], True, True)
                for b in bs:
                    self.mm(PSO[hh][:, b * 128:(b + 1) * 128], VT[:, b, hh * 128:(hh + 1) * 128], AT[b], first[hh], False)
                    first[hh] = False
            return pskv

        def stage_d(pskv):
            states = []
            for idx, (b, c) in enumerate(cseq):
                ci = b * 2 + c
                a_c = AC[:, ci:ci + 1]
                if dr == 0:
                    S_cur = SB[st["si"] % NSB]
                    S_next = SB[(st["si"] + 1) % NSB]
                    states.append(S_cur)
                    if hg:
                        self.tt(TMP, pskv[idx], S_cur.bitcast(F32), ALU.add)
                    else:
                        self.tt(TMP[0:64, :], pskv[idx][0:64, 0:128], S_cur[0:64, :].bitcast(F32), ALU.add)
                        self.tt(TMP[64:128, :], pskv[idx][64:128, 128:256], S_cur[64:128, :].bitcast(F32), ALU.add)
                    self.ts(S_next, TMP, a_c, None, ALU.mult)
                    st["si"] += 1
                else:
                    C_cur = SC[st["sc"] % 2]
                    C_next = SC[(st["sc"] + 1) % 2]
                    S_sc = SB[st["si"] % NSB]
                    states.append(S_sc)
                    self.ts(S_sc, C_cur, a_c, None, ALU.mult)
                    if hg:
                        self.tt(C_next, pskv[idx], S_sc.bitcast(F32), ALU.add)
                    else:
                        self.tt(C_next[0:64, :], pskv[idx][0:64, 0:128], S_sc[0:64, :].bitcast(F32), ALU.add)
                        self.tt(C_next[64:128, :], pskv[idx][64:128, 128:256], S_sc[64:128, :].bitcast(F32), ALU.add)
                    st["si"] += 1
                    st["sc"] += 1
            for idx, (b, c) in enumerate(cseq):
                ccols = slice(b * 128 + c * 64, b * 128 + c * 64 + 64)
                for hh in heads:
                    self.mm(PSO[hh][:, ccols], states[idx][pr_(hh), :], QD[pr_(hh), ccols], False, idx == 3)

        def stage_e(hti):
            lt0 = hti * 256
            for hh in heads:
                head = u if hg else 4 + 2 * j + hh
                on = self.r(self.o_on + head * self.on_stride + lt0, [128, 256])
                pso = PSO[hh][:, 0:256]
                if dr == 0:
                    self.cp(on, pso, eng="act")
                else:
                    self.tt(F1, pso, on.bitcast(F32), ALU.add)
                    self.act(SQ, F1, AF.Square, scale=128.0 ** -0.5)
                    pst = PS[5][:, 0:256]
                    self.mm(pst, self.ONES[:], SQ, True, True)
                    self.act(F2, pst, AF.Sqrt, bias=self.EPSR[:, 0:1])
                    self.p.op("dve", lambda e: e.reciprocal(out=F2, in_=F2), reads=[F2], writes=[F2])
                    nw = self.PARS[:, 0:1] if hg else self.PARS[:, 1:2]
                    self.stt(F1, F1, nw, F2, ALU.mult, ALU.mult)
                    self.tt(on, F1, GS[hh], ALU.mult)

        stage_a(order[0])
        for n_, hti in enumerate(order):
            stage_b()
            pskv = stage_c()
            if hg and n_ + 1 < len(order):
                stage_a(order[n_ + 1])
            stage_d(pskv)
            stage_e(hti)
            if (not hg) and n_ + 1 < len(order):
                stage_a(order[n_ + 1])
        if not sample:
            if dr == 0:
                S_fin = SB[st["si"] % NSB].bitcast(F32)
                sname = "sout%d" % (st["si"] % NSB)
            else:
                S_fin = SC[st["sc"] % 2]
                sname = "soutc%d" % (st["sc"] % 2)
            if hg:
                self.dma(d["ns_h"][sidx, dr, u], S_fin, self.p.named_sem(sname))
            else:
                self.dma(d["ns_g"][sidx, dr, 2 * j:2 * j + 2].rearrange("h d e -> (h d) e"), S_fin, self.p.named_sem(sname))

    def ab_outproj(self, l, x0, lt0, n, g):
        wo = self.dram["w_out_ab"].rearrange("(h p) o -> p h o", p=128)
        for half in range(2):
            WO = self.r(self.o_wab, [128, 8, 512])
            self.dma_r(WO, wo[:, :, half * 512:(half + 1) * 512], self.sem_wab[0])
            for o in range(4):
                for h in range(8):
                    on = self.r(self.o_on + h * self.on_stride + lt0, [128, 512])[:, 0:n]
                    self.mm(self.PS[o][:, 0:n], WO[:, h, o * 128:(o + 1) * 128], on, h == 0, h == 7)
            for o in range(4):
                oc = half * 4 + o
                xs_ = self.X[:, oc, x0:x0 + n]
                self.stt(xs_, self.PS[o][:, 0:n], self.GSC[:, 1, oc, g:g + 1], xs_, ALU.mult, ALU.add)
        self.ln_range(l * 3 + 1, x0, n, self.o_hb)

    def mixer_ab(self, l):
        self.o_on = 0
        self.on_stride = 2048
        self.o_wab = 16384
        self.o_hb = 23552
        o = 25600
        self.o_qd = o
        self.o_ki = o + 256
        self.o_vt = o + 512
        self.o_at = o + 1024
        self.o_sb = o + 1280
        assert self.o_sb + 768 <= self.NR
        self.ab_params()
        abn = int(os.environ.get("MK_ABN", "1000"))
        abskip = int(os.environ.get("MK_ABSKIP", "0"))
        cnt = 0
        for (x0, nht, g, sample, sidx) in ((0, 1, 0, False, 0), (256, 1, 0, False, 1), (512, 8, 1, True, None)):
            for u in range(6):
                for dr in range(2):
                    cnt += 1
                    if cnt <= abskip or cnt > abskip + abn:
                        continue
                    self.ab_unit_pass(u, dr, x0, nht, g, sample, sidx)
            ntok = nht * 256
            for off in range(0, ntok, 512):
                n = min(512, ntok - off)
                self.ab_outproj(l, x0 + off, off, n, g)

    def c_consts(self):
        d = self.dram
        A = self.f(0, [128, 128])
        B = self.f(128, [128, 128])
        for (M, cm, st, base) in ((A, 1, -1, -16), (B, -1, 1, -16)):
            self.memset(M, 1.0)
            self.p.op("pool", lambda e: e.affine_select(out=M, in_=M, pattern=[[st, 128]], compare_op=ALU.is_equal,
                                                        fill=0.0, base=base, channel_multiplier=cm),
                      reads=[M], writes=[M])
        self.memset(A.rearrange("p (g c) -> p g c", c=32)[:, :, 16:32], 0.0)
        self.memset(B.rearrange("p (g c) -> p g c", c=32)[:, :, 0:16], 0.0)
        self.tt(self.PERMR[:], A, B, ALU.add)
        for (M, cm, st) in ((self.LM1, -1, 1), (self.LM3, 1, -1)):
            self.memset(M[:], 1.0)
            self.p.op("pool", lambda e: e.affine_select(out=M[:], in_=M[:], pattern=[[st, 128]], compare_op=ALU.is_ge,
                                                        fill=0.0, base=0, channel_multiplier=cm),
                      reads=[M[:]], writes=[M[:]])
        st_ = self.f(256, [128, 16])
        self.memset(st_, 0.0)
        self.dma(st_[0:1, :], d["sink"], self.sem_misc)
        ps = self.PS[7][:, 0:16]
        onesf = self.f(384, [128, 128])
        self.memset(onesf, 1.0)
        self.mm(ps, onesf, st_, True, True)
        self.act(self.ES[:], ps, AF.Exp)
        I32 = mybir.dt.int32
        pi_ = self.f(512, [128, 1]).bitcast(I32)
        self.p.op("pool", lambda e: e.iota(pi_, pattern=[[0, 1]], base=0, channel_multiplier=1), writes=[pi_])
        i16 = self.f(513, [128, 1]).bitcast(I32)
        self.ts(i16, pi_, 15, None, ALU.bitwise_and)
        m32 = self.f(514, [128, 1]).bitcast(I32)
        self.ts(m32, pi_, 32, None, ALU.bitwise_and)
        i16f = self.f(515, [128, 1])
        self.cp(i16f, i16)
        m32f = self.f(516, [128, 1])
        self.cp(m32f, m32)
        inv = self.f(517, [128, 1])
        self.act(inv, i16f, AF.Exp, scale=-float(np.log(ROPE_BASE)) / 16.0)
        b16 = self.f(518, [128, 1]).bitcast(I32)
        self.ts(b16, pi_, 16, None, ALU.bitwise_and)
        b16f = self.f(519, [128, 1])
        self.cp(b16f, b16)
        self.ts(self.SGN[:], b16f, 1.0 / 8.0, -1.0, ALU.mult, ALU.add)
        self.ts(self.MC[:], m32f, 1.0 / 32.0, None, ALU.mult)
        self.ts(self.MR[:], self.MC[:], -1.0, 1.0, ALU.mult, ALU.add)
        pos = self.f(640, [128, 64])
        self.p.op("pool", lambda e: e.iota(pos.bitcast(I32), pattern=[[1, 64]], base=0, channel_multiplier=0), writes=[pos])
        posf = self.f(704, [128, 64])
        self.cp(posf, pos.bitcast(I32))
        ang = self.f(768, [128, 64])
        self.ts(ang, posf, inv, None, ALU.mult)
        TWO_PI = 2.0 * float(np.pi)
        for (dst, shiftv) in ((self.TSIN, 0.0), (self.TCOS, float(np.pi) / 2.0)):
            a = self.f(832, [128, 64])
            kq = self.f(896, [128, 64])
            ki = self.f(960, [128, 64]).bitcast(I32)
            self.ts(a, ang, shiftv, None, ALU.add)
            self.ts(kq, a, 1.0 / TWO_PI, None, ALU.mult)
            self.cp(ki, kq)
            self.cp(kq, ki)
            self.stt(a, kq, -TWO_PI, a, ALU.mult, ALU.add)
            self.ts(kq, a, float(np.pi), -TWO_PI, ALU.is_gt, ALU.mult)
            self.tt(a, a, kq, ALU.add)
            self.ts(kq, a, -float(np.pi), TWO_PI, ALU.is_lt, ALU.mult)
            self.tt(a, a, kq, ALU.add)
            self.act(dst[:], a, AF.Sin)

    def rope_tables(self, t):
        cos_t = self.f(0, [128, 512])
        sin_t = self.f(512, [128, 512])
        for (dst, T) in ((cos_t, self.TCOS), (sin_t, self.TSIN)):
            dv = dst.rearrange("p (r c) -> p r c", c=64)
            rowv = T[:, 8 * t:8 * t + 8].unsqueeze(2).to_broadcast([128, 8, 64])
            colv = T[:, 0:64].unsqueeze(1).to_broadcast([128, 8, 64])
            self.ts(dv, rowv, self.MR[:, 0:1], None, ALU.mult)
            self.stt(dv, colv, self.MC[:, 0:1], dv, ALU.mult, ALU.add)
        self.ts(sin_t, sin_t, self.SGN[:, 0:1], None, ALU.mult)
        return cos_t, sin_t

    def rope_apply(self, dst, ps, cos_t, sin_t, n):
        ZQ = self.r(self.o_zq, [128, 512])[:, 0:n]
        self.cp(ZQ, ps, eng="act")
        pz = self.PS[6][:, 0:n]
        self.mm(pz, self.PERMR[:], ZQ, True, True)
        t1 = self.f(1024, [128, 512])[:, 0:n]
        self.tt(t1, ZQ.bitcast(F32), cos_t[:, 0:n], ALU.mult)
        t2 = self.f(1536, [128, 512])[:, 0:n]
        self.tt(t2, pz, sin_t[:, 0:n], ALU.mult)
        self.tt(dst, t1, t2, ALU.add)

    def c_modulate(self, x0, n, g):
        HB = self.r(self.o_hb, [128, KC, 512])
        for k in range(KC):
            self.ts(HB[:, k, 0:n], self.X[:, k, x0:x0 + n], self.OPS[:, 1, k, g:g + 1], self.shift(1, k, g), ALU.mult, ALU.add)
        return HB

    def va_slot(self, kap):
        return [(0, 0, 64), (64, 2, 0), (130, 0, 64), (194, 2, 0)][kap]

    def c_kv_project(self, HB, n, WKV, kf_dst, va_dst, vblk0, rope, emit=None):
        for kp in range(2):
            ps = self.PS[kp][:, 0:n]
            for k in range(KC):
                self.mm(ps, WKV[:, k, kp * 128:(kp + 1) * 128], HB[:, k, 0:n], k == 0, k == KC - 1)
            if rope is not None:
                self.rope_apply(kf_dst(kp), ps, rope[0], rope[1], n)
            else:
                self.cp(kf_dst(kp), ps, eng="act")
        for b in range(n // 128):
            ps = self.PS[2 + b % 2][:, 0:256]
            for k in range(KC):
                self.mm(ps, HB[:, k, b * 128:(b + 1) * 128], WKV[:, k, 256:512], k == 0, k == KC - 1)
            va = va_dst(vblk0 + b)
            self.cp(va[:, 0:132].rearrange("p (a c) -> p a c", c=66)[:, :, 0:64], ps[:, 0:128].rearrange("p (a c) -> p a c", c=64), eng="act")
            self.cp(va[:, 130:262].rearrange("p (a c) -> p a c", c=66)[:, :, 0:64], ps[:, 128:256].rearrange("p (a c) -> p a c", c=64), eng="act")
            self.ts(va[:, 64:66], self.RESET[:, 1:3], 0.0, 1.0, ALU.mult, ALU.add)
            self.ts(va[:, 194:196], self.RESET[:, 1:3], 0.0, 1.0, ALU.mult, ALU.add)
            if emit is not None:
                sq, tok0 = emit
                stv = self.f(2048 + (b % 2) * 256, [128, 256])
                self.cp(stv, ps)
                self.dma(self.dram["nv"][sq, tok0 + b * 128:tok0 + (b + 1) * 128, :], stv, self.p.named_sem("nv%d" % (b % 2)))
                ps2 = self.PS[4 + b % 2][:, 0:256]
                for k in range(KC):
                    self.mm(ps2, HB[:, k, b * 128:(b + 1) * 128], WKV[:, k, 0:256], k == 0, k == KC - 1)
                stk = self.f(2560 + (b % 2) * 256, [128, 256])
                self.cp(stk, ps2, eng="act")
                self.dma(self.dram["nk"][sq, tok0 + b * 128:tok0 + (b + 1) * 128, :], stk, self.p.named_sem("nk%d" % (b % 2)))

    def c_q_project(self, HB, n, rope):
        wq = self.dram["w_qkv"].rearrange("(k p) n -> p k n", p=128)
        QF = self.r(self.o_qf, [128, 8, 512])
        WS = self.r(self.o_ws, [128, KC, 4, 128])
        for grp in range(2):
            for s_ in range(2):
                for i in range(4):
                    c0 = grp * 512 + s_ * 256 + i * 64
                    self.dma_r(WS[:, :, i, s_ * 64:(s_ + 1) * 64], wq[:, :, c0:c0 + 64], self.p.named_sem("wq%d_%d" % (s_, i)))
            for i in range(4):
                pair = grp * 4 + i
                ps = self.PS[pair % 2][:, 0:n]
                for k in range(KC):
                    self.mm(ps, WS[:, k, i, :], HB[:, k, 0:n], k == 0, k == KC - 1)
                if rope is not None:
                    self.rope_apply(QF[:, pair, 0:n], ps, rope[0], rope[1], n)
                else:
                    self.cp(QF[:, pair, 0:n], ps, eng="act")
        return QF

    def c_attend(self, QF, nqb, keysets):
        OT = self.r(self.o_ot, [128, 4, 1024])
        NPT = 4
        PT = [self.r(self.o_ws + i * 512, [128, 512]) for i in range(NPT)]
        SCB = [self.PS[0], self.PS[1], self.PS[7]]
        tasks = []
        for h in range(C_HEADS):
            for ki, ks in enumerate(keysets):
                tasks.append((h, ki, ks))
        nk_for = [sum(1 for ks in keysets if ks[2] <= qb <= ks[3]) for qb in range(nqb)]

        def hinfo(h):
            kap = h // 4
            half = kap % 2
            pair = (h % 4) + (0 if h < 8 else 4)
            return kap, half, pair, slice(64 * half, 64 * half + 64)

        def score(i):
            h, ki, (kf_fn, va_fn, qlo, qhi, masks) = tasks[i]
            kap, half, pair, hp = hinfo(h)
            ncol = (qhi - qlo + 1) * 128
            ps = SCB[i % 3][:, 0:ncol]
            pt = PT[i % NPT][:, 0:ncol]
            self.mm(ps, kf_fn(kap, half), QF[hp, pair, qlo * 128:qlo * 128 + ncol], True, True)
            self.act(pt, ps, AF.Exp, scale=HD ** -0.5)
            for qb in range(qlo, qhi + 1):
                if qb in masks:
                    sl = slice((qb - qlo) * 128, (qb - qlo + 1) * 128)
                    self.tt(pt[:, sl], pt[:, sl].bitcast(F32), masks[qb][:], ALU.mult)

        seen = {}

        def pv(i):
            h, ki, (kf_fn, va_fn, qlo, qhi, masks) = tasks[i]
            kap, half, pair, hp = hinfo(h)
            c0, o_off, d_off = self.va_slot(kap)
            ncol = (qhi - qlo + 1) * 128
            pt = PT[i % NPT][:, 0:ncol]
            for qb in range(qlo, qhi + 1):
                sl = slice((qb - qlo) * 128, (qb - qlo + 1) * 128)
                seen[(h, qb)] = seen.get((h, qb), 0) + 1
                self.mm(self.PS[2 + qb][:, 0:66], pt[:, sl], va_fn(kap)[:, c0:c0 + 66], seen[(h, qb)] == 1, seen[(h, qb)] == nk_for[qb])
            if ki == len(keysets) - 1:
                for qb in range(nqb):
                    po = self.PS[2 + qb]
                    rd = self.f(2048 + 16 * qb, [128, 1])
                    self.ts(rd, po[:, d_off:d_off + 1], self.ES[:, h:h + 1], None, ALU.add)
                    self.p.op("dve", lambda e: e.reciprocal(out=rd, in_=rd), reads=[rd], writes=[rd])
                    self.ts(OT[:, qb, h * 64:(h + 1) * 64], po[:, o_off:o_off + 64], rd, None, ALU.mult)

        LOOK = 2
        for i in range(min(LOOK, len(tasks))):
            score(i)
        for i in range(len(tasks)):
            if i + LOOK < len(tasks):
                score(i + LOOK)
            pv(i)
        return OT

    def c_outproj(self, l, OT, x0, nqb, g):
        n = nqb * 128
        OA = self.r(self.o_hb, [128, KC, 512])
        for qb in range(nqb):
            for hb in range(2):
                ps = self.PS[hb]
                for j in range(4):
                    c = hb * 4 + j
                    self.tr(ps[:, j * 128:(j + 1) * 128], OT[:, qb, c * 128:(c + 1) * 128].bitcast(F32))
                self.cp(OA[:, hb * 4:(hb + 1) * 4, qb * 128:(qb + 1) * 128], ps[:].rearrange("p (a b) -> p a b", a=4),
                        eng="act" if hb else "dve")
        wo = self.dram["w_out_c"].rearrange("(c p) o -> p c o", p=128)
        for half in range(2):
            WO = self.r(self.o_qf, [128, KC, 512])
            self.dma_r(WO, wo[:, :, half * 512:(half + 1) * 512], self.p.named_sem("woc"))
            for o in range(4):
                for c in range(KC):
                    self.mm(self.PS[4 + o][:, 0:n], WO[:, c, o * 128:(o + 1) * 128], OA[:, c, 0:n], c == 0, c == KC - 1)
            for o in range(4):
                oc = half * 4 + o
                xs_ = self.X[:, oc, x0:x0 + n]
                self.stt(xs_, self.PS[4 + o][:, 0:n], self.GSC[:, 1, oc, g:g + 1], xs_, ALU.mult, ALU.add)
        self.ln_range(l * 3 + 1, x0, n, self.o_ws)

    def mixer_c(self, l):
        d = self.dram
        self.o_kf = 0
        self.o_va = 4096
        self.o_kc = self.o_va + 16 * 262
        self.o_vca = self.o_kc + 1024
        self.o_hb = self.o_vca + 4 * 262
        self.o_ws = self.o_hb + 4096
        self.o_qf = self.o_ws + 4096
        self.o_ot = self.o_qf + 4096
        self.o_zq = self.o_ot + 4096
        assert self.o_zq + 512 <= self.NR, self.o_zq
        self.c_consts()
        KF = self.r(self.o_kf, [128, 2, 2048])
        VA = self.r(self.o_va, [128, 16, 262])
        KCF = self.r(self.o_kc, [128, 2, 512])
        VCA = self.r(self.o_vca, [128, 4, 262])
        wq = d["w_qkv"].rearrange("(k p) n -> p k n", p=128)
        WKV = self.r(self.o_ws, [128, KC, 512])

        def load_wkv():
            self.dma_r(WKV, wq[:, :, 1024:1536], self.p.named_sem("wkv"))
        for sq in range(2):
            x0 = sq * SEQ
            HB = self.c_modulate(x0, SEQ, 0)
            load_wkv()
            self.c_kv_project(HB, SEQ, WKV, lambda kp: KF[:, kp, 0:SEQ], lambda b: VA[:, b, :], 0, None, emit=(sq, 0))
            QF = self.c_q_project(HB, SEQ, None)
            keysets = []
            for kb in range(2):
                keysets.append((lambda kap, half, kb=kb: KF[64 * half:64 * half + 64, kap // 2, kb * 128:(kb + 1) * 128],
                                lambda kap, kb=kb: VA[:, kb, :], 0, 1, {}))
            OT = self.c_attend(QF, 2, keysets)
            self.c_outproj(l, OT, x0, 2, 0)
        stg = self.r(self.o_ot, [128, 4, 256])
        self.dma_r(stg, d["ck"].rearrange("(b p) c -> p b c", p=128), self.p.named_sem("ck"))
        for b in range(4):
            for kp in range(2):
                ps = self.PS[(b * 2 + kp) % 2][:, 0:128]
                self.tr(ps, stg[:, b, kp * 128:(kp + 1) * 128].bitcast(F32))
                self.cp(KCF[:, kp, b * 128:(b + 1) * 128], ps, eng="act" if kp else "dve")
        cvv = d["cv"].rearrange("(b p) c -> p b c", p=128)
        for kap in range(4):
            c0 = [0, 66, 130, 196][kap]
            self.dma_r(VCA[:, :, c0:c0 + 64], cvv[:, :, kap * 64:(kap + 1) * 64], self.p.named_sem("cv%d" % kap))
        for b in range(4):
            self.ts(VCA[:, b, 64:66], self.RESET[:, 1:3], 0.0, 1.0, ALU.mult, ALU.add)
            self.ts(VCA[:, b, 194:196], self.RESET[:, 1:3], 0.0, 1.0, ALU.mult, ALU.add)
        load_wkv()
        for t in range(4):
            x0 = TP + t * 512
            HB = self.c_modulate(x0, 512, 1)
            rope = self.rope_tables(t)
            self.c_kv_project(HB, 512, WKV, lambda kp, t=t: KF[:, kp, t * 512:(t + 1) * 512], lambda b: VA[:, b, :], 4 * t, rope)
        for t in range(4):
            x0 = TP + t * 512
            HB = self.c_modulate(x0, 512, 1)
            rope = self.rope_tables(t)
            QF = self.c_q_project(HB, 512, rope)
            keysets = []
            for cb in range(4):
                keysets.append((lambda kap, half, cb=cb: KCF[64 * half:64 * half + 64, kap // 2, cb * 128:(cb + 1) * 128],
                                lambda kap, cb=cb: VCA[:, cb, :], 0, 3, {}))
            for kb in range(4 * t - 1, 4 * t + 5):
                if kb < 0 or kb >= 16:
                    continue
                qlo = max(4 * t, kb - 1) - 4 * t
                qhi = min(4 * t + 3, kb + 1) - 4 * t
                masks = {}
                if 0 <= kb - 1 - 4 * t <= 3:
                    masks[kb - 1 - 4 * t] = self.LM1
                if 0 <= kb + 1 - 4 * t <= 3:
                    masks[kb + 1 - 4 * t] = self.LM3
                keysets.append((lambda kap, half, kb=kb: KF[64 * half:64 * half + 64, kap // 2, kb * 128:(kb + 1) * 128],
                                lambda kap, kb=kb: VA[:, kb, :], qlo, qhi, masks))
            OT = self.c_attend(QF, 4, keysets)
            self.c_outproj(l, OT, x0, 4, 1)

    def build(self):
        self.o_w13 = 0
        self.o_w2 = 8192
        self.o_xm = 12288
        self.o_hid = 16384
        self.o_ln = self.o_hid + 18 * 512
        self.o_sg = 0
        self.o_lnf = 1024
        self.consts()
        self.EPSLN = self.sb("EPSLN", [128, 1])
        self.memset(self.EPSLN[:], LN_EPS / (ALPHA * ALPHA))
        self.ONEC = self.sb("ONEC", [128, 1])
        self.memset(self.ONEC[:], 1.0)
        self.load_small()
        self.load_x()
        for l in range(DEPTH):
            self.mod_vectors(l)
            for t in range(NT):
                self.ffn_tile(l, 0, t)
            if self.stage == 1 + 3 * l:
                break
            if l == 0:
                self.mixer_ab(l)
            else:
                self.mixer_c(l)
            if self.stage == 2 + 3 * l:
                break
            for t in range(NT):
                self.ffn_tile(l, 1, t)
            if self.stage == 3 + 3 * l:
                break
        self.store_x()
        self.p.finish("sp")
        return self.nc


_CACHE = {}


def get_program(stage=99):
    if stage not in _CACHE:
        _CACHE[stage] = Builder(stage).build()
    return _CACHE[stage]


def shard_inputs(inp):
    f = lambda a: np.ascontiguousarray(np.asarray(a, dtype=np.float32))
    maps = []
    for c in range(NCORES):
        sb = c // 4
        m = {
            "xp": f(inp["x_prompt"][2 * c:2 * c + 2].reshape(TP, D)),
            "xs": f(inp["x_sample"][sb]),
            "st_h": f(inp["state_hgrn"][sb, 0]),
            "st_g": f(inp["state_gla"][sb, 0]),
            "ck": f(inp["cache_k"][sb, 0].reshape(PAST, C_KV * HD)),
            "cv": f(inp["cache_v"][sb, 0].reshape(PAST, C_KV * HD)),
            "cvec": f(np.stack([inp["c_ctx"], inp["c"][sb]], axis=0)),
            "w_mod": f(inp["w_mod"]),
            "b_mod": f(inp["b_mod"]),
            "ln_g": f(inp["ln_g"].reshape(DEPTH * 3, D)),
            "ln_b": f(inp["ln_b"].reshape(DEPTH * 3, D)),
            "ffn_w1": f(inp["ffn_w1"]),
            "ffn_w3": f(inp["ffn_w3"]),
            "ffn_w2": f(inp["ffn_w2"]),
            "w_in_ab": f(inp["w_in_ab"][0]),
            "hgrn_lb": f(inp["hgrn_lb"]),
            "gate_up": f(inp["gla_gate_up"][0]),
            "gate_b": f(inp["gla_gate_b"][0]),
            "norm_a": f(inp["norm_a"]),
            "norm_b": f(inp["norm_b"]),
            "w_out_ab": f(inp["w_out_ab"][0]),
            "w_qkv": f(inp["w_qkv_c"][0]),
            "sink": f(inp["sink_c"]),
            "w_out_c": f(inp["w_out_c"][0]),
        }
        maps.append(m)
    return maps


def gather_outputs(res):
    r = res.results
    B = 16
    yp = np.concatenate([r[c]["yp"].reshape(2, SEQ, D) for c in range(NCORES)], axis=0)
    ys = np.stack([r[0]["ys"], r[4]["ys"]], axis=0)
    nsh = np.concatenate([r[c]["ns_h"].reshape(2, 1, 2, A_HEADS, 128, 128) for c in range(NCORES)], axis=0)
    nsg = np.concatenate([r[c]["ns_g"].reshape(2, 1, 2, B_HEADS, B_DK, 128) for c in range(NCORES)], axis=0)
    nk = np.concatenate([r[c]["nk"].reshape(2, 1, SEQ, C_KV, HD) for c in range(NCORES)], axis=0)
    nv = np.concatenate([r[c]["nv"].reshape(2, 1, SEQ, C_KV, HD) for c in range(NCORES)], axis=0)
    return (yp.astype(np.float32), ys.astype(np.float32), nsh.astype(np.float32), nsg.astype(np.float32),
            nk.astype(np.float32), nv.astype(np.float32))


def kernel(**inputs):
    stage = int(os.environ.get("MK_STAGE", "99"))
    nc = get_program(stage)
    maps = shard_inputs(inputs)
    res = run_bass_kernel_spmd(nc, maps, core_ids=list(range(NCORES)))
    return gather_outputs(res)
```

```python
import os
import numpy as np
import concourse.bass as bass
import concourse.mybir as mybir
from concourse.bass_utils import run_bass_kernel_spmd

F32 = mybir.dt.float32
F32R = mybir.dt.float32r
AF = mybir.ActivationFunctionType
ALU = mybir.AluOpType

D = 1024
KC = 8
DFF = 2816
FC = 22
NMOD = 9
DEPTH = 2
SEQ = 256
TP = 512
TS = 2048
T = TP + TS
NT = T // 512
A_HEADS = 4
B_HEADS = 4
B_DK = 64
GATE_RANK = 16
GLA_TAU = 16.0
AB_IN = 4128
C_HEADS = 16
C_KV = 4
HD = 64
PAST = 512
ALPHA = (2.0 * DEPTH) ** 0.25
LN_EPS = 1e-5
RMS_EPS = 1e-6
ROPE_BASE = 10000.0
NCORES = 8


class Prog:
    def __init__(self, nc):
        self.nc = nc
        self.E = {"pe": nc.tensor, "act": nc.scalar, "dve": nc.vector, "pool": nc.gpsimd, "sp": nc.sync}
        self.sems = {}
        self.cnt = {}
        for e in self.E:
            self.sems[e] = nc.alloc_semaphore("s_" + e)
            self.cnt[e] = 0
        self.seen = {e: {} for e in self.E}
        self.ndma_sem = 0
        self.n_inst = 0
        self.n_wait = 0
        self.self_sync = set(os.environ.get("MK_SELFSYNC", "act,dve,pool").split(",")) - {""}
        self.named = {}
        self.mem = {}

    def named_sem(self, name):
        if name not in self.named:
            self.named[name] = self.new_dma_sem()
        return self.named[name]

    def new_dma_sem(self):
        k = "d%d" % self.ndma_sem
        self.ndma_sem += 1
        self.sems[k] = self.nc.alloc_semaphore("s_" + k)
        self.cnt[k] = 0
        return k

    @staticmethod
    def box(ap):
        esz = mybir.dt.size(ap.dtype)
        dims = ap.ap
        off = ap.offset
        name = ap.tensor.name
        if str(ap.space) == "DRAM":
            lo = off
            hi = off
            for st, cn in dims:
                d = (cn - 1) * st
                if d > 0:
                    hi += d
                else:
                    lo += d
            return name, 0, 1, lo * esz, (hi + 1) * esz
        pst, pcn = dims[0]
        if pst <= 0:
            pst = 1 << 40
        p0 = off // pst
        c0 = off % pst
        if str(ap.space) == "PSUM":
            q0 = (p0 // 32) * 32
            q1 = ((p0 + pcn + 31) // 32) * 32
            return name, q0, q1, 0, 2048
        lo = c0
        hi = c0
        for st, cn in dims[1:]:
            d = (cn - 1) * st
            if d > 0:
                hi += d
            else:
                lo += d
        return name, p0, p0 + pcn, lo * esz, (hi + 1) * esz

    def _collect(self, reads, writes):
        deps = []
        rb = [self.box(a) for a in reads]
        wb = [self.box(a) for a in writes]
        for (name, p0, p1, lo, hi) in rb:
            m = self.mem.get(name)
            if m is None:
                continue
            for r in m[0]:
                if r[0] < p1 and p0 < r[1] and r[2] < hi and lo < r[3]:
                    deps.append((r[4], r[5]))
        for (name, p0, p1, lo, hi) in wb:
            m = self.mem.get(name)
            if m is None:
                continue
            for lst in m:
                for r in lst:
                    if r[0] < p1 and p0 < r[1] and r[2] < hi and lo < r[3]:
                        deps.append((r[4], r[5]))
        return deps, rb, wb

    def _record(self, rb, wb, key, val):
        for (name, p0, p1, lo, hi) in wb:
            m = self.mem.setdefault(name, [[], []])
            for i in (0, 1):
                m[i] = [r for r in m[i] if not (p0 <= r[0] and r[1] <= p1 and lo <= r[2] and r[3] <= hi)]
            m[0].append([p0, p1, lo, hi, key, val])
        for (name, p0, p1, lo, hi) in rb:
            m = self.mem.setdefault(name, [[], []])
            m[1] = [r for r in m[1] if not (r[4] == key and p0 <= r[0] and r[1] <= p1 and lo <= r[2] and r[3] <= hi)]
            m[1].append([p0, p1, lo, hi, key, val])

    def _wait(self, e, deps):
        best = {}
        for k, v in deps:
            if best.get(k, 0) < v:
                best[k] = v
        for k, v in best.items():
            if k == e and e not in self.self_sync:
                continue
            if self.seen[e].get(k, 0) < v:
                self.E[e].wait_ge(self.sems[k], v)
                self.seen[e][k] = v
                self.n_wait += 1

    def op(self, e, fn, reads=(), writes=()):
        deps, rb, wb = self._collect(reads, writes)
        self._wait(e, deps)
        ins = fn(self.E[e])
        ins.then_inc(self.sems[e], 1)
        self.cnt[e] += 1
        self._record(rb, wb, e, self.cnt[e])
        self.n_inst += 1
        return ins

    def dma(self, q, out, in_, sem):
        deps, rb, wb = self._collect([in_], [out])
        self._wait(q, deps)
        ins = self.E[q].dma_start(out=out, in_=in_)
        ins.then_inc(self.sems[sem], 16)
        self.cnt[sem] += 16
        self._record(rb, wb, sem, self.cnt[sem])
        self.n_inst += 1
        return ins

    def finish(self, e="sp"):
        deps = []
        for name, m in self.mem.items():
            for lst in m:
                for r in lst:
                    deps.append((r[4], r[5]))
        self._wait(e, deps)


class Builder:
    def __init__(self, stage=99):
        self.stage = stage
        nc = bass.Bass("TRN2", target_bir_lowering=False)
        nc.dge_precook = False
        self.nc = nc
        self.p = Prog(nc)
        self.dram = {}
        self.decl_io()
        self.alloc()

    def din(self, name, shape):
        self.dram[name] = self.nc.dram_tensor(name, list(shape), F32, kind="ExternalInput").ap()
        return self.dram[name]

    def dout(self, name, shape):
        self.dram[name] = self.nc.dram_tensor(name, list(shape), F32, kind="ExternalOutput").ap()
        return self.dram[name]

    def decl_io(self):
        self.din("xp", [TP, D])
        self.din("xs", [TS, D])
        self.din("st_h", [2, A_HEADS, 128, 128])
        self.din("st_g", [2, B_HEADS, B_DK, 128])
        self.din("ck", [PAST, C_KV * HD])
        self.din("cv", [PAST, C_KV * HD])
        self.din("cvec", [2, D])
        self.din("selv", [128, 16])
        self.din("w_mod", [DEPTH, D, NMOD * D])
        self.din("b_mod", [DEPTH, NMOD * D])
        self.din("ln_g", [DEPTH * 3, D])
        self.din("ln_b", [DEPTH * 3, D])
        self.din("ffn_w1", [DEPTH, 2, D, DFF])
        self.din("ffn_w3", [DEPTH, 2, D, DFF])
        self.din("ffn_w2", [DEPTH, 2, DFF, D])
        self.din("w_in_ab", [D, AB_IN])
        self.din("hgrn_lb", [2, 2, 512])
        self.din("gate_up", [2, GATE_RANK, 256])
        self.din("gate_b", [2, 256])
        self.din("norm_a", [1, 128])
        self.din("norm_b", [1, 128])
        self.din("w_out_ab", [D, D])
        self.din("w_qkv", [D, 1536])
        self.din("sink", [1, C_HEADS])
        self.din("w_out_c", [D, D])
        self.dout("yp", [TP, D])
        self.dout("ys", [512, D])
        self.dout("ns_h", [2, 2, A_HEADS, 128, 128])
        self.dout("ns_g", [2, 2, B_HEADS, B_DK, 128])
        self.dout("nk", [2, SEQ, C_KV * HD])
        self.dout("nv", [2, SEQ, C_KV * HD])

    def sb(self, name, shape, dt=F32):
        return self.nc.alloc_sbuf_tensor(name, list(shape), dt)

    def alloc(self):
        nc = self.nc
        self.X = self.sb("X", [128, KC, T])
        self.IDENT = self.sb("IDENT", [128, 128])
        self.ONES = self.sb("ONES", [128, 128], F32R)
        self.U1 = self.sb("U1", [128, 256])
        self.U2 = self.sb("U2", [128, 256], F32R)
        self.MF = self.U1[:, 0:128]
        self.MB = self.U1[:, 128:256]
        self.LM1 = self.U1[:, 0:128]
        self.LM3 = self.U1[:, 128:256]
        self.RESET = self.sb("RESET", [128, 256])
        self.PARS = self.sb("PARS", [128, 22])
        self.LB = self.sb("LB", [128, 8])
        self.OML = self.sb("OML", [128, 8])
        self.OMLH = self.sb("OMLH", [128, 8])
        self.LBH = self.sb("LBH", [128, 8])
        self.SELV = self.sb("SELV", [128, 16])
        self.TRW = self.sb("TRW", [128, 2, 12])
        self.NGB = self.sb("NGB", [128, 4])
        self.GUP = self.U2[:, 0:256]
        self.PERMR = self.U2[:, 0:128]
        self.EPSR = self.sb("EPSR", [128, 1])
        self.ES = self.sb("ES", [128, 16])
        self.TCOS = self.sb("TCOS", [128, 64])
        self.TSIN = self.sb("TSIN", [128, 64])
        self.MR = self.sb("MR", [128, 1])
        self.MC = self.sb("MC", [128, 1])
        self.SGN = self.sb("SGN", [128, 1])
        self.MODV = self.sb("MODV", [128, 72, 2])
        self.OPS = self.sb("OPS", [128, 3, KC, 2])
        self.GSC = self.sb("GSC", [128, 3, KC, 2])
        self.BM = self.sb("BM", [128, 72])
        self.LNG = self.sb("LNG", [128, 48])
        self.LNB = self.sb("LNB", [128, 48])
        self.CS = self.sb("CS", [128, 2, KC], F32R)
        self.NR = 27 * 1024
        self.NF = 3072
        self.R = self.sb("R", [128, self.NR], F32R)
        self.Fm = self.sb("Fm", [128, self.NF])
        self.PS = [nc.alloc_psum_tensor("PS%d" % i, [128, 512], F32) for i in range(8)]
        self.sem_w13 = [self.p.new_dma_sem() for _ in range(2)]
        self.sem_w2 = [self.p.new_dma_sem() for _ in range(4)]
        self.sem_wab = [self.p.new_dma_sem() for _ in range(7)]
        self.sem_st = self.p.new_dma_sem()
        self.sem_io = [self.p.new_dma_sem() for _ in range(2)]
        self.sem_misc = self.p.new_dma_sem()
        self.sem_out = self.p.new_dma_sem()
        self.n13 = 0
        self.n2 = 0
        self.nio = 0

    @staticmethod
    def _view(base, off, shape, total):
        n = 1
        for x in shape[1:]:
            n *= x
        assert off + n <= total, (off, n, total)
        v = base[0:shape[0], off:off + n]
        if len(shape) == 3:
            v = v.rearrange("p (a b) -> p a b", a=shape[1])
        elif len(shape) == 4:
            v = v.rearrange("p (a b c) -> p a b c", a=shape[1], b=shape[2])
        return v

    def r(self, off, shape):
        return self._view(self.R, off, shape, self.NR)

    def f(self, off, shape):
        return self._view(self.Fm, off, shape, self.NF)

    def mm(self, out, lhsT, rhs, start, stop):
        self.p.op("pe", lambda e: e.matmul(out, lhsT=lhsT, rhs=rhs, start=start, stop=stop),
                  reads=[lhsT, rhs], writes=[out])

    def tr(self, out, in_, n=128):
        ident = self.IDENT[0:in_.shape[0], 0:in_.shape[0]]
        self.p.op("pe", lambda e: e.transpose(out=out, in_=in_, identity=ident), reads=[in_, ident], writes=[out])

    def act(self, out, in_, func, bias=None, scale=None, eng="act"):
        kw = {}
        rd = [in_]
        if bias is not None:
            kw["bias"] = bias
            if not isinstance(bias, (int, float)):
                rd.append(bias)
        if scale is not None:
            kw["scale"] = scale
            if not isinstance(scale, (int, float)):
                rd.append(scale)
        self.p.op("act", lambda e: e.activation(out=out, in_=in_, func=func, **kw), reads=rd, writes=[out])

    def ts(self, out, in0, s1, s2, op0, op1=None, eng="dve"):
        rd = [in0]
        for s in (s1, s2):
            if s is not None and not isinstance(s, (int, float)):
                rd.append(s)
        if op1 is None:
            self.p.op(eng, lambda e: e.tensor_scalar(out=out, in0=in0, scalar1=s1, scalar2=None, op0=op0), reads=rd, writes=[out])
        else:
            self.p.op(eng, lambda e: e.tensor_scalar(out=out, in0=in0, scalar1=s1, scalar2=s2, op0=op0, op1=op1), reads=rd, writes=[out])

    def tt(self, out, in0, in1, op, eng="dve"):
        self.p.op(eng, lambda e: e.tensor_tensor(out=out, in0=in0, in1=in1, op=op), reads=[in0, in1], writes=[out])

    def stt(self, out, in0, scalar, in1, op0, op1, eng="dve"):
        rd = [in0, in1]
        if not isinstance(scalar, (int, float)):
            rd.append(scalar)
        self.p.op(eng, lambda e: e.scalar_tensor_tensor(out=out, in0=in0, scalar=scalar, in1=in1, op0=op0, op1=op1), reads=rd, writes=[out])

    def cp(self, out, in_, eng="dve"):
        if eng == "act":
            self.act(out, in_, AF.Copy)
        else:
            self.p.op(eng, lambda e: e.tensor_copy(out=out, in_=in_), reads=[in_], writes=[out])

    def memset(self, ap, val, eng="pool"):
        self.p.op(eng, lambda e: e.memset(ap, val), writes=[ap])

    def dma(self, out, in_, sem, q="sp"):
        self.p.dma(q, out, in_, sem)

    def dma_r(self, out, in_, sem, q="sp"):
        self.p.dma(q, out if out.dtype == F32R else out.bitcast(F32R), in_.bitcast(F32R), sem)

    def consts(self):
        self.memset(self.IDENT[:], 1.0)
        self.p.op("pool", lambda e: e.affine_select(out=self.IDENT[:], in_=self.IDENT[:], pattern=[[-1, 128]],
                                                    compare_op=ALU.is_equal, fill=0.0, base=0, channel_multiplier=1),
                  reads=[self.IDENT[:]], writes=[self.IDENT[:]])
        tmp = self.f(0, [128, 128])
        self.memset(tmp, 1.0)
        self.cp(self.ONES[:], tmp)
        for (M, cm, st) in ((self.MF, -1, 1), (self.MB, 1, -1)):
            self.memset(M[:], 1.0)
            self.p.op("pool", lambda e: e.affine_select(out=M[:], in_=M[:], pattern=[[st, 128]], compare_op=ALU.is_ge,
                                                        fill=0.0, base=0, channel_multiplier=cm),
                      reads=[M[:]], writes=[M[:]])
        self.memset(self.MF[0:64, 64:128], 0.0)
        self.memset(self.MB[64:128, 0:64], 0.0)
        self.memset(self.RESET[:], 0.0)
        self.memset(self.RESET[:, 0:256:64], 1.0)
        self.memset(self.EPSR[:], RMS_EPS)

    def load_fm(self, dst, src_rows, nrows):
        st = self.f(0, [128, 128])
        self.dma(st[0:nrows, :], src_rows, self.sem_misc)
        ps = self.PS[7][:, 0:nrows]
        self.tr(ps, st[0:nrows, :])
        self.cp(dst, ps)

    def load_small(self):
        self.load_fm(self.LNG[:], self.dram["ln_g"].rearrange("r (k p) -> (r k) p", p=128), 48)
        self.load_fm(self.LNB[:], self.dram["ln_b"].rearrange("r (k p) -> (r k) p", p=128), 48)
        st = self.f(0, [128, 128])
        self.dma(st[0:16, :], self.dram["cvec"].rearrange("g (k p) -> (g k) p", p=128), self.sem_misc)
        ps = self.PS[7][:, 0:16]
        self.tr(ps, st[0:16, :])
        self.act(self.CS[:].rearrange("p g k -> p (g k)"), ps, AF.Silu)

    def load_x(self):
        for tb in range(T // 128):
            src = self.dram["xp"][tb * 128:(tb + 1) * 128, :] if tb < TP // 128 else \
                self.dram["xs"][tb * 128 - TP:(tb + 1) * 128 - TP, :]
            s = self.nio % 2
            self.nio += 1
            st = self.f(s * 1024, [128, 1024])
            self.dma(st, src, self.sem_io[s])
            for hb in range(2):
                ps = self.PS[(tb * 2 + hb) % 4]
                for j in range(4):
                    k = hb * 4 + j
                    self.tr(ps[:, j * 128:(j + 1) * 128], st[:, k * 128:(k + 1) * 128])
                dst = self.X[:, hb * 4:(hb + 1) * 4, tb * 128:(tb + 1) * 128]
                self.cp(dst, ps[:].rearrange("p (a b) -> p a b", a=4), eng="dve" if hb == 0 else "act")

    def store_x(self):
        for ob in range(8):
            if ob < 4:
                dst = self.dram["yp"][ob * 128:(ob + 1) * 128, :]
                xc = ob * 128
            else:
                dst = self.dram["ys"][(ob - 4) * 128:(ob - 3) * 128, :]
                xc = self.W0 + 128 + (ob - 4) * 128
            s = self.nio % 2
            self.nio += 1
            st = self.f(s * 1024, [128, 1024])
            for hb in range(2):
                ps = self.PS[(ob * 2 + hb) % 4]
                for j in range(4):
                    k = hb * 4 + j
                    self.tr(ps[:, j * 128:(j + 1) * 128], self.X[:, k, xc:xc + 128])
                self.cp(st[:, hb * 512:(hb + 1) * 512], ps[:], eng="dve" if hb == 0 else "act")
            self.dma(dst, st, self.sem_io[s])

    def select_window(self):
        SV = self.SELV
        for k in range(KC):
            own = self.f(0, [128, 512])
            hp = self.f(512, [128, 128])
            hn = self.f(640, [128, 128])
            for t in range(4):
                xt = self.X[:, k, TP + t * 512:TP + (t + 1) * 512]
                if t == 0:
                    self.ts(own, xt, SV[:, t:t + 1], None, ALU.mult)
                    self.ts(hp, xt[:, 384:512], SV[:, 4 + t:5 + t], None, ALU.mult)
                    self.ts(hn, xt[:, 0:128], SV[:, 8 + t:9 + t], None, ALU.mult)
                else:
                    self.stt(own, xt, SV[:, t:t + 1], own, ALU.mult, ALU.add)
                    self.stt(hp, xt[:, 384:512], SV[:, 4 + t:5 + t], hp, ALU.mult, ALU.add)
                    self.stt(hn, xt[:, 0:128], SV[:, 8 + t:9 + t], hn, ALU.mult, ALU.add)
            self.cp(self.X[:, k, self.W0:self.W0 + 128], hp, eng="act")
            self.cp(self.X[:, k, self.W0 + 128:self.W0 + 640], own, eng="act")
            self.cp(self.X[:, k, self.W0 + 640:self.W0 + 768], hn, eng="act")

    def mod_vectors(self, l):
        self.load_fm(self.BM[:], self.dram["b_mod"][l].rearrange("(r p) -> r p", p=128), 72)
        wm = self.dram["w_mod"][l].rearrange("(k p) n -> p k n", p=128)
        pm = self.PS[6][:, 0:144].rearrange("p (c g) -> p c g", g=2)
        for blk in range(18):
            s = self.n13 % 2
            self.n13 += 1
            wt = self.r(self.o_w13 + s * 4096, [128, KC, 512])
            self.dma_r(wt, wm[:, :, blk * 512:(blk + 1) * 512], self.sem_w13[s])
            for q in range(4):
                oc = blk * 4 + q
                for k in range(KC):
                    self.mm(pm[:, oc, :], wt[:, k, q * 128:(q + 1) * 128], self.CS[:, :, k], k == 0, k == KC - 1)
        for g in range(2):
            self.tt(self.MODV[:, :, g], pm[:, :, g], self.BM[:], ALU.add)
        gmul = [0.5 / ALPHA, 1.0 / ALPHA, 0.5 / ALPHA]
        for s in range(3):
            self.ts(self.OPS[:, s, :, :], self.MODV[:, (3 * s + 1) * 8:(3 * s + 2) * 8, :], 1.0, None, ALU.add)
            self.ts(self.GSC[:, s, :, :], self.MODV[:, (3 * s + 2) * 8:(3 * s + 3) * 8, :], gmul[s], None, ALU.mult)

    def shift(self, s, k, g):
        return self.MODV[:, 3 * s * 8 + k, g:g + 1]

    def ln_range(self, lnidx, x0, n, o_r):
        ts_ = slice(x0, x0 + n)
        pa, pb = self.PS[4][:, 0:n], self.PS[5][:, 0:n]
        for k in range(KC):
            zr = self.r(o_r + (k % 2) * 512, [128, 512])[:, 0:n]
            sq = self.r(o_r + 1024 + (k % 2) * 512, [128, 512])[:, 0:n]
            self.act(zr, self.X[:, k, ts_], AF.Copy, scale=1.0 / 1024.0)
            self.act(sq, self.X[:, k, ts_], AF.Square, scale=1.0 / 32.0)
            self.mm(pa, self.ONES[:], zr, k == 0, k == KC - 1)
            self.mm(pb, self.ONES[:], sq, k == 0, k == KC - 1)
        m2 = self.f(self.o_lnf, [128, 512])[:, 0:n]
        self.act(m2, pa, AF.Square)
        self.tt(m2, pb, m2, ALU.subtract)
        self.act(m2, m2, AF.Sqrt, bias=self.EPSLN[:, 0:1])
        self.p.op("dve", lambda e: e.reciprocal(out=m2, in_=m2), reads=[m2], writes=[m2])
        for k in range(KC):
            xk = self.X[:, k, ts_]
            self.tt(xk, xk, pa, ALU.subtract)
            self.stt(xk, xk, self.LNG[:, lnidx * 8 + k:lnidx * 8 + k + 1], m2, ALU.mult, ALU.mult)
            self.act(xk, xk, AF.Identity, bias=self.LNB[:, lnidx * 8 + k:lnidx * 8 + k + 1])

    def ffn_tile(self, l, j, x0, n, g):
        s = 0 if j == 0 else 2
        ts_ = slice(x0, x0 + n)
        XM = self.r(self.o_xm, [128, KC, 512])[:, :, 0:n]
        HID = self.r(self.o_hid, [128, FC, 512])[:, :, 0:n]
        for k in range(KC):
            self.ts(XM[:, k, :], self.X[:, k, ts_], self.OPS[:, s, k, g:g + 1], self.shift(s, k, g), ALU.mult, ALU.add)
        w1 = self.dram["ffn_w1"][l, j].rearrange("(k p) f -> p k f", p=128)
        w3 = self.dram["ffn_w3"][l, j].rearrange("(k p) f -> p k f", p=128)
        w2 = self.dram["ffn_w2"][l, j].rearrange("(c p) o -> p c o", p=128)
        for fb in range(FC // 2):
            sl = self.n13 % 2
            self.n13 += 1
            wt = self.r(self.o_w13 + sl * 4096, [128, 2, KC, 256])
            self.dma_r(wt[:, 0], w1[:, :, fb * 256:(fb + 1) * 256], self.sem_w13[sl])
            self.dma_r(wt[:, 1], w3[:, :, fb * 256:(fb + 1) * 256], self.p.named_sem("w3_%d" % sl))
            for c in range(2):
                f = 2 * fb + c
                p1, p3 = self.PS[f % 2][:, 0:n], self.PS[2 + f % 2][:, 0:n]
                for k in range(KC):
                    self.mm(p1, wt[:, 0, k, c * 128:(c + 1) * 128], XM[:, k, :], k == 0, k == KC - 1)
                for k in range(KC):
                    self.mm(p3, wt[:, 1, k, c * 128:(c + 1) * 128], XM[:, k, :], k == 0, k == KC - 1)
                sg = self.f(self.o_sg + (f % 2) * 512, [128, 512])[:, 0:n]
                self.act(sg, p1, AF.Silu)
                self.tt(HID[:, f, :], sg, p3, ALU.mult)
        for half in range(2):
            for fb in range(FC // 2):
                sl = self.n2 % 4
                self.n2 += 1
                wt = self.r(self.o_w2 + sl * 1024, [128, 2, 512])
                self.dma_r(wt, w2[:, 2 * fb:2 * fb + 2, half * 512:(half + 1) * 512], self.sem_w2[sl])
                for c in range(2):
                    f = 2 * fb + c
                    for o in range(4):
                        self.mm(self.PS[4 + o][:, 0:n], wt[:, c, o * 128:(o + 1) * 128], HID[:, f, :], f == 0, f == FC - 1)
            for o in range(4):
                oc = half * 4 + o
                self.stt(self.X[:, oc, ts_], self.PS[4 + o][:, 0:n], self.GSC[:, s, oc, g:g + 1], self.X[:, oc, ts_], ALU.mult, ALU.add)
        self.ln_range(l * 3 + s, x0, n, self.o_ln)

    def ab_params(self):
        st = self.f(0, [128, 128])
        d = self.dram
        self.dma(st[0:1, :], d["norm_a"], self.sem_misc)
        self.dma(st[1:2, :], d["norm_b"], self.sem_misc)
        self.dma(st[2:6, :], d["gate_b"].rearrange("a (j p) -> (a j) p", p=128), self.sem_misc)
        self.dma(st[6:22, :], d["hgrn_lb"].rearrange("a b (h p) -> (a b h) p", p=128), self.sem_misc)
        ps = self.PS[7][:, 0:22]
        self.tr(ps, st[0:22, :])
        self.cp(self.PARS[:], ps)
        P = self.PARS
        for dr in range(2):
            a = P[:, 6 + dr * 8:6 + dr * 8 + 4]
            b = P[:, 6 + dr * 8 + 4:6 + dr * 8 + 8]
            self.tt(self.LB[:, dr * 4:dr * 4 + 4], a, b, ALU.subtract)
        self.act(self.LB[:], self.LB[:], AF.Sigmoid)
        self.ts(self.OML[:], self.LB[:], -1.0, 1.0, ALU.mult, ALU.add)
        self.ts(self.OMLH[:], self.OML[:], 0.5, None, ALU.mult)
        self.tt(self.LBH[:], self.LB[:], self.OMLH[:], ALU.add)
        self.ts(self.NGB[:], P[:, 2:6], -1.0, None, ALU.mult)

    def ab_unit_pass(self, u, dr, x0, nht, g, sample, sidx):
        hg = u < 4
        j = u - 4
        d = self.dram
        win = d["w_in_ab"].rearrange("(k p) n -> p k n", p=128)
        if hg:
            cols = [("q", u * 128), ("f", (1024 if dr == 0 else 1536) + u * 128), ("i", 512 + u * 128)]
            if dr == 1:
                cols.append(("g0", 2048 + u * 128))
        else:
            cols = [("q", 2560 + j * 128), ("f", 2816 + j * 128), ("v0", 3072 + j * 256), ("v1", 3072 + j * 256 + 128),
                    ("bz", 4000)]
            if dr == 1:
                cols += [("g0", 3584 + j * 256), ("g1", 3584 + j * 256 + 128)]
        W = {}
        for i, (nm, c0) in enumerate(cols):
            W[nm] = self.r(self.o_wab + i * 1024, [128, KC, 128])
            self.dma_r(W[nm], win[:, :, c0:c0 + 128], self.sem_wab[i])
        nh = 1 if hg else 2
        heads = list(range(nh))
        vw = 128 * nh
        NSB = 6
        SB = [self.r(self.o_sb + i * 128, [128, 128]) for i in range(NSB)]
        SC = [self.f(1536, [128, 128]), self.f(1664, [128, 128])]
        st = {"si": 0, "sc": 0}
        if not hg:
            self.ts(self.GUP[64:128, :], self.RESET[64:128, :], 0.0, None, ALU.mult)
            self.dma_r(self.GUP[96 + 16 * dr:112 + 16 * dr, :], d["gate_up"][dr], self.p.named_sem("gup"))
        src0 = None
        if sample:
            src0 = d["st_h"][dr, u] if hg else d["st_g"][dr, 2 * j:2 * j + 2].rearrange("h d e -> (h d) e")
        if dr == 0:
            if sample:
                self.dma_r(SB[0], src0, self.sem_st)
            else:
                self.ts(SB[0], self.IDENT[:], 0.0, None, ALU.mult)
        else:
            if sample:
                self.dma(SC[0], src0, self.sem_st)
            else:
                self.memset(SC[0], 0.0)
        HB = self.r(self.o_hb, [128, KC, 256])
        QD = self.r(self.o_qd, [128, 256])
        KI = self.r(self.o_ki, [128, 256])
        VT = self.r(self.o_vt, [128, 2, 256])
        AT = [self.r(self.o_at + i * 128, [128, 128]) for i in range(2)]
        BZ = self.r(self.o_at, [128, 256])
        SQ = self.r(self.o_at, [128, 256])
        KIT = self.r(self.o_hb + 256, [128, 2, 128])
        F0 = self.f(0, [128, 256])
        F1 = self.f(256, [128, 256])
        F2 = self.f(512, [128, 256])
        F3 = self.f(1792, [128, 256])
        TMP = self.f(768, [128, 128])
        AC = self.f(896, [128, 4])
        GS = [self.f(1024, [128, 256]), self.f(1280, [128, 256])]
        PS = self.PS
        psq, psf = PS[0][:, 0:256], PS[0][:, 256:512]
        PSO = [PS[4], PS[7]]
        MASK = self.MF if dr == 0 else self.MB
        order = list(range(nht)) if dr == 0 else list(range(nht - 1, -1, -1))
        bs = [0, 1] if dr == 0 else [1, 0]
        cseq = [(b, c) for b in bs for c in bs]

        def pr_(hh):
            return slice(0, 128) if hg else slice(64 * hh, 64 * hh + 64)

        def stage_a(hti):
            t0 = x0 + hti * 256
            for k in range(KC):
                self.act(HB[:, k, :], self.X[:, k, t0:t0 + 256], AF.Identity, bias=self.shift(1, k, g), scale=self.OPS[:, 1, k, g:g + 1])
            for k in range(KC):
                self.mm(psq, W["q"][:, k, :], HB[:, k, :], k == 0, k == KC - 1)
            for k in range(KC):
                self.mm(psf, W["f"][:, k, :], HB[:, k, :], k == 0, k == KC - 1)
            for b in range(2):
                for vv in range(nh):
                    wv = W["i"] if hg else W["v%d" % vv]
                    for k in range(KC):
                        self.mm(PS[1][:, b * 256 + vv * 128:b * 256 + vv * 128 + 128], HB[:, k, b * 128:(b + 1) * 128], wv[:, k, :],
                                k == 0, k == KC - 1)
            if not hg:
                for k in range(KC):
                    self.mm(PS[3][:, 0:256], W["bz"][:, k, :], HB[:, k, :], k == 0, k == KC - 1)
            if dr == 1:
                for hh in heads:
                    for k in range(KC):
                        self.mm(PS[2][:, hh * 256:(hh + 1) * 256], W["g%d" % hh][:, k, :], HB[:, k, :], k == 0, k == KC - 1)

        def stage_b():
            for b in range(2):
                if hg:
                    self.act(VT[:, b, 0:128], PS[1][:, b * 256:b * 256 + 128], AF.Silu)
                else:
                    self.cp(VT[:, b, :], PS[1][:, b * 256:(b + 1) * 256], eng="act")
            if hg:
                self.act(F0, psf, AF.Tanh, scale=0.5)
                self.ts(F0, F0, self.OMLH[:, dr * 4 + u:dr * 4 + u + 1], self.LBH[:, dr * 4 + u:dr * 4 + u + 1], ALU.mult, ALU.add)
                self.ts(F1, F0, -1.0, 1.0, ALU.mult, ALU.add, eng="pool")
                kf = F1
            else:
                self.cp(BZ, PS[3][:, 0:256], eng="act")
                psl = PS[3][:, 256:512]
                self.mm(psl, self.GUP[64:128, j * 128:(j + 1) * 128], BZ[64:128, :], True, True)
                self.act(F0, psl, AF.Exp, scale=-1.0, bias=self.NGB[:, dr * 2 + j:dr * 2 + j + 1])
                self.act(F0, F0, AF.Ln, bias=self.ONEC[:, 0:1])
                self.act(F0, F0, AF.Exp, scale=-1.0 / GLA_TAU)
                kf = psf
            qs = None if hg else B_DK ** -0.5
            R1 = self.RESET[:, 0:256]
            if dr == 0:
                self.p.op("dve", lambda e: e.tensor_tensor_scan(out=F2, data0=R1, data1=F0, initial=1.0, op0=ALU.max, op1=ALU.mult),
                          reads=[R1, F0], writes=[F2])
                self.cp(AC, F2[:, 63:256:64])
                if hg:
                    self.tt(QD, psq, F2, ALU.mult)
                else:
                    self.stt(QD, psq, qs, F2, ALU.mult, ALU.mult)
                self.p.op("dve", lambda e: e.reciprocal(out=F2, in_=F2), reads=[F2], writes=[F2])
                self.tt(KI, kf, F2, ALU.mult)
            else:
                self.cp(F2[:, 1:256], F0[:, 0:255])
                self.memset(F2[:, 0:256:64], 1.0)
                self.p.op("dve", lambda e: e.tensor_tensor_scan(out=F3, data0=R1, data1=F2, initial=1.0, op0=ALU.max, op1=ALU.mult),
                          reads=[R1, F2], writes=[F3])
                self.tt(AC, F3[:, 63:256:64], F0[:, 63:256:64], ALU.mult)
                self.tt(KI, kf, F3, ALU.mult)
                self.p.op("dve", lambda e: e.reciprocal(out=F3, in_=F3), reads=[F3], writes=[F3])
                if hg:
                    self.tt(QD, psq, F3, ALU.mult)
                else:
                    self.stt(QD, psq, qs, F3, ALU.mult, ALU.mult)
                for hh in heads:
                    self.act(GS[hh], PS[2][:, hh * 256:(hh + 1) * 256], AF.Silu)

        def stage_c():
            pskv = []
            for idx, (b_, c_) in enumerate(cseq):
                bank = (PS[6] if c_ == 0 else PS[7]) if hg else (PS[6] if c_ == 0 else PS[1])
                w_ = 128 if hg else 256
                slot = 0 if idx < 2 else 1
                pskv.append(bank[:, slot * w_:(slot + 1) * w_])
            first = [True, True]
            for hi, hh in enumerate(heads):
                pat = PS[5] if hi == 0 else PS[2 if dr == 0 else 3]
                pat_off = 0 if (hi == 0 or dr == 0) else 256
                for b in bs:
                    self.mm(pat[:, pat_off + b * 128:pat_off + (b + 1) * 128], KI[pr_(hh), b * 128:(b + 1) * 128],
                            QD[pr_(hh), b * 128:(b + 1) * 128], True, True)
                if hi == 0:
                    for b in bs:
                        self.tr(PS[3][:, b * 128:(b + 1) * 128], KI[:, b * 128:(b + 1) * 128].bitcast(F32))
                for b in bs:
                    self.tt(AT[b], pat[:, pat_off + b * 128:pat_off + (b + 1) * 128], MASK[:], ALU.mult)
                if hi == 0:
                    for b in bs:
                        self.cp(KIT[:, b, :], PS[3][:, b * 128:(b + 1) * 128], eng="act")
                    for idx, (b, c) in enumerate(cseq):
                        self.mm(pskv[idx], KIT[c * 64:(c + 1) * 64, b, :], VT[c * 64:(c + 1) * 64, b, 0:vw], True, True)
                for b in bs:
                    self.mm(PSO[hh][:, b * 128:(b + 1) * 128], VT[:, b, hh * 128:(hh + 1) * 128], AT[b], first[hh], False)
                    first[hh] = False
            return pskv

        def stage_d(pskv):
            states = []
            for idx, (b, c) in enumerate(cseq):
                ci = b * 2 + c
                a_c = AC[:, ci:ci + 1]
                if dr == 0:
                    S_cur = SB[st["si"] % NSB]
                    S_next = SB[(st["si"] + 1) % NSB]
                    states.append(S_cur)
                    if hg:
                        self.tt(TMP, pskv[idx], S_cur.bitcast(F32), ALU.add)
                    else:
                        self.tt(TMP[0:64, :], pskv[idx][0:64, 0:128], S_cur[0:64, :].bitcast(F32), ALU.add)
                        self.tt(TMP[64:128, :], pskv[idx][64:128, 128:256], S_cur[64:128, :].bitcast(F32), ALU.add)
                    self.ts(S_next, TMP, a_c, None, ALU.mult)
                    st["si"] += 1
                else:
                    C_cur = SC[st["sc"] % 2]
                    C_next = SC[(st["sc"] + 1) % 2]
                    S_sc = SB[st["si"] % NSB]
                    states.append(S_sc)
                    self.ts(S_sc, C_cur, a_c, None, ALU.mult)
                    if hg:
                        self.tt(C_next, pskv[idx], S_sc.bitcast(F32), ALU.add)
                    else:
                        self.tt(C_next[0:64, :], pskv[idx][0:64, 0:128], S_sc[0:64, :].bitcast(F32), ALU.add)
                        self.tt(C_next[64:128, :], pskv[idx][64:128, 128:256], S_sc[64:128, :].bitcast(F32), ALU.add)
                    st["si"] += 1
                    st["sc"] += 1
            for idx, (b, c) in enumerate(cseq):
                ccols = slice(b * 128 + c * 64, b * 128 + c * 64 + 64)
                for hh in heads:
                    self.mm(PSO[hh][:, ccols], states[idx][pr_(hh), :], QD[pr_(hh), ccols], False, idx == 3)

        def stage_e(hti):
            lt0 = hti * 256
            for hh in heads:
                head = u if hg else 4 + 2 * j + hh
                on = self.r(self.o_on + head * self.on_stride + lt0, [128, 256])
                pso = PSO[hh][:, 0:256]
                if dr == 0:
                    self.cp(on, pso, eng="act")
                else:
                    self.tt(F1, pso, on.bitcast(F32), ALU.add)
                    self.act(SQ, F1, AF.Square, scale=128.0 ** -0.5)
                    pst = PS[5][:, 0:256]
                    self.mm(pst, self.ONES[:], SQ, True, True)
                    self.act(F2, pst, AF.Sqrt, bias=self.EPSR[:, 0:1])
                    self.p.op("dve", lambda e: e.reciprocal(out=F2, in_=F2), reads=[F2], writes=[F2])
                    nw = self.PARS[:, 0:1] if hg else self.PARS[:, 1:2]
                    self.stt(F1, F1, nw, F2, ALU.mult, ALU.mult)
                    self.tt(on, F1, GS[hh], ALU.mult)

        stage_a(order[0])
        for n_, hti in enumerate(order):
            stage_b()
            pskv = stage_c()
            if hg and n_ + 1 < len(order):
                stage_a(order[n_ + 1])
            stage_d(pskv)
            stage_e(hti)
            if (not hg) and n_ + 1 < len(order):
                stage_a(order[n_ + 1])
        if not sample:
            if dr == 0:
                S_fin = SB[st["si"] % NSB].bitcast(F32)
                sname = "sout%d" % (st["si"] % NSB)
            else:
                S_fin = SC[st["sc"] % 2]
                sname = "soutc%d" % (st["sc"] % 2)
            if hg:
                self.dma(d["ns_h"][sidx, dr, u], S_fin, self.p.named_sem(sname))
            else:
                self.dma(d["ns_g"][sidx, dr, 2 * j:2 * j + 2].rearrange("h d e -> (h d) e"), S_fin, self.p.named_sem(sname))

    def ab_outproj(self, l, x0, lt0, n, g):
        wo = self.dram["w_out_ab"].rearrange("(h p) o -> p h o", p=128)
        for half in range(2):
            WO = self.r(self.o_wab, [128, 8, 512])
            self.dma_r(WO, wo[:, :, half * 512:(half + 1) * 512], self.sem_wab[0])
            for o in range(4):
                for h in range(8):
                    on = self.r(self.o_on + h * self.on_stride + lt0, [128, 512])[:, 0:n]
                    self.mm(self.PS[o][:, 0:n], WO[:, h, o * 128:(o + 1) * 128], on, h == 0, h == 7)
            for o in range(4):
                oc = half * 4 + o
                xs_ = self.X[:, oc, x0:x0 + n]
                self.stt(xs_, self.PS[o][:, 0:n], self.GSC[:, 1, oc, g:g + 1], xs_, ALU.mult, ALU.add)
        self.ln_range(l * 3 + 1, x0, n, self.o_hb)

    def mixer_ab(self, l):
        self.o_on = 0
        self.on_stride = 2048
        self.o_wab = 16384
        self.o_hb = 23552
        o = 25600
        self.o_qd = o
        self.o_ki = o + 256
        self.o_vt = o + 512
        self.o_at = o + 1024
        self.o_sb = o + 1280
        assert self.o_sb + 768 <= self.NR
        self.ab_params()
        abn = int(os.environ.get("MK_ABN", "1000"))
        abskip = int(os.environ.get("MK_ABSKIP", "0"))
        cnt = 0
        for (x0, nht, g, sample, sidx) in ((0, 1, 0, False, 0), (256, 1, 0, False, 1), (512, 8, 1, True, None)):
            for u in range(6):
                for dr in range(2):
                    cnt += 1
                    if cnt <= abskip or cnt > abskip + abn:
                        continue
                    self.ab_unit_pass(u, dr, x0, nht, g, sample, sidx)
            ntok = nht * 256
            for off in range(0, ntok, 512):
                n = min(512, ntok - off)
                self.ab_outproj(l, x0 + off, off, n, g)

    def c_consts(self):
        d = self.dram
        A = self.f(0, [128, 128])
        B = self.f(128, [128, 128])
        for (M, cm, st, base) in ((A, 1, -1, -16), (B, -1, 1, -16)):
            self.memset(M, 1.0)
            self.p.op("pool", lambda e: e.affine_select(out=M, in_=M, pattern=[[st, 128]], compare_op=ALU.is_equal,
                                                        fill=0.0, base=base, channel_multiplier=cm),
                      reads=[M], writes=[M])
        self.memset(A.rearrange("p (g c) -> p g c", c=32)[:, :, 16:32], 0.0)
        self.memset(B.rearrange("p (g c) -> p g c", c=32)[:, :, 0:16], 0.0)
        self.tt(self.PERMR[:], A, B, ALU.add)
        for (M, cm, st) in ((self.LM1, -1, 1), (self.LM3, 1, -1)):
            self.memset(M[:], 1.0)
            self.p.op("pool", lambda e: e.affine_select(out=M[:], in_=M[:], pattern=[[st, 128]], compare_op=ALU.is_ge,
                                                        fill=0.0, base=0, channel_multiplier=cm),
                      reads=[M[:]], writes=[M[:]])
        st_ = self.f(256, [128, 16])
        self.memset(st_, 0.0)
        self.dma(st_[0:1, :], d["sink"], self.sem_misc)
        ps = self.PS[7][:, 0:16]
        onesf = self.f(384, [128, 128])
        self.memset(onesf, 1.0)
        self.mm(ps, onesf, st_, True, True)
        self.act(self.ES[:], ps, AF.Exp)
        I32 = mybir.dt.int32
        pi_ = self.f(512, [128, 1]).bitcast(I32)
        self.p.op("pool", lambda e: e.iota(pi_, pattern=[[0, 1]], base=0, channel_multiplier=1), writes=[pi_])
        i16 = self.f(513, [128, 1]).bitcast(I32)
        self.ts(i16, pi_, 15, None, ALU.bitwise_and)
        m32 = self.f(514, [128, 1]).bitcast(I32)
        self.ts(m32, pi_, 32, None, ALU.bitwise_and)
        i16f = self.f(515, [128, 1])
        self.cp(i16f, i16)
        m32f = self.f(516, [128, 1])
        self.cp(m32f, m32)
        inv = self.f(517, [128, 1])
        self.act(inv, i16f, AF.Exp, scale=-float(np.log(ROPE_BASE)) / 16.0)
        b16 = self.f(518, [128, 1]).bitcast(I32)
        self.ts(b16, pi_, 16, None, ALU.bitwise_and)
        b16f = self.f(519, [128, 1])
        self.cp(b16f, b16)
        self.ts(self.SGN[:], b16f, 1.0 / 8.0, -1.0, ALU.mult, ALU.add)
        self.ts(self.MC[:], m32f, 1.0 / 32.0, None, ALU.mult)
        self.ts(self.MR[:], self.MC[:], -1.0, 1.0, ALU.mult, ALU.add)
        pos = self.f(640, [128, 64])
        self.p.op("pool", lambda e: e.iota(pos.bitcast(I32), pattern=[[1, 64]], base=0, channel_multiplier=0), writes=[pos])
        posf = self.f(704, [128, 64])
        self.cp(posf, pos.bitcast(I32))
        TWO_PI = 2.0 * float(np.pi)
        TRC = self.f(1024, [128, 64])
        TRS = self.f(1088, [128, 64])
        for (posoff, dsin, dcos) in ((0.0, self.TSIN[:], self.TCOS[:]), (-2.0, TRS, TRC)):
            ang = self.f(768, [128, 64])
            self.ts(ang, posf, posoff, None, ALU.add)
            self.ts(ang, ang, inv, None, ALU.mult)
            for (dst, shiftv) in ((dsin, 0.0), (dcos, float(np.pi) / 2.0)):
                a = self.f(832, [128, 64])
                kq = self.f(896, [128, 64])
                ki = self.f(960, [128, 64]).bitcast(I32)
                self.ts(a, ang, shiftv, None, ALU.add)
                self.ts(kq, a, 1.0 / TWO_PI, None, ALU.mult)
                self.cp(ki, kq)
                self.cp(kq, ki)
                self.stt(a, kq, -TWO_PI, a, ALU.mult, ALU.add)
                self.ts(kq, a, float(np.pi), -TWO_PI, ALU.is_gt, ALU.mult)
                self.tt(a, a, kq, ALU.add)
                self.ts(kq, a, -float(np.pi), TWO_PI, ALU.is_lt, ALU.mult)
                self.tt(a, a, kq, ALU.add)
                self.act(dst, a, AF.Sin)
        for (ci, T) in ((0, TRC), (1, TRS)):
            for jq in range(4):
                if jq == 0:
                    self.ts(self.TRW[:, ci, :], T[:, 0:12], self.SELV[:, 0:1], None, ALU.mult)
                else:
                    self.stt(self.TRW[:, ci, :], T[:, 8 * jq:8 * jq + 12], self.SELV[:, jq:jq + 1], self.TRW[:, ci, :], ALU.mult, ALU.add)

    def rope_tables_w(self, r0, nrows):
        n = nrows * 64
        cos_t = self.f(0, [128, 512])[:, 0:n]
        sin_t = self.f(512, [128, 512])[:, 0:n]
        for (dst, ci, T) in ((cos_t, 0, self.TCOS), (sin_t, 1, self.TSIN)):
            dv = dst.rearrange("p (r c) -> p r c", c=64)
            rowv = self.TRW[:, ci, r0:r0 + nrows].unsqueeze(2).to_broadcast([128, nrows, 64])
            colv = T[:, 0:64].unsqueeze(1).to_broadcast([128, nrows, 64])
            self.ts(dv, rowv, self.MR[:, 0:1], None, ALU.mult)
            self.stt(dv, colv, self.MC[:, 0:1], dv, ALU.mult, ALU.add)
        self.ts(sin_t, sin_t, self.SGN[:, 0:1], None, ALU.mult)
        return cos_t, sin_t

    def rope_tables(self, t):
        cos_t = self.f(0, [128, 512])
        sin_t = self.f(512, [128, 512])
        for (dst, T) in ((cos_t, self.TCOS), (sin_t, self.TSIN)):
            dv = dst.rearrange("p (r c) -> p r c", c=64)
            rowv = T[:, 8 * t:8 * t + 8].unsqueeze(2).to_broadcast([128, 8, 64])
            colv = T[:, 0:64].unsqueeze(1).to_broadcast([128, 8, 64])
            self.ts(dv, rowv, self.MR[:, 0:1], None, ALU.mult)
            self.stt(dv, colv, self.MC[:, 0:1], dv, ALU.mult, ALU.add)
        self.ts(sin_t, sin_t, self.SGN[:, 0:1], None, ALU.mult)
        return cos_t, sin_t

    def rope_apply(self, dst, ps, cos_t, sin_t, n):
        ZQ = self.r(self.o_zq, [128, 512])[:, 0:n]
        self.cp(ZQ, ps, eng="act")
        pz = self.PS[6][:, 0:n]
        self.mm(pz, self.PERMR[:], ZQ, True, True)
        t1 = self.f(1024, [128, 512])[:, 0:n]
        self.tt(t1, ZQ.bitcast(F32), cos_t[:, 0:n], ALU.mult)
        t2 = self.f(1536, [128, 512])[:, 0:n]
        self.tt(t2, pz, sin_t[:, 0:n], ALU.mult)
        self.tt(dst, t1, t2, ALU.add)

    def c_modulate(self, x0, n, g):
        HB = self.r(self.o_hb, [128, KC, 512])
        for k in range(KC):
            self.ts(HB[:, k, 0:n], self.X[:, k, x0:x0 + n], self.OPS[:, 1, k, g:g + 1], self.shift(1, k, g), ALU.mult, ALU.add)
        return HB

    def va_slot(self, kap):
        return [(0, 0, 64), (64, 2, 0), (130, 0, 64), (194, 2, 0)][kap]

    def c_kv_project(self, HB, n, WKV, kf_dst, va_dst, vblk0, rope, emit=None):
        for kp in range(2):
            ps = self.PS[kp][:, 0:n]
            for k in range(KC):
                self.mm(ps, WKV[:, k, kp * 128:(kp + 1) * 128], HB[:, k, 0:n], k == 0, k == KC - 1)
            if rope is not None:
                self.rope_apply(kf_dst(kp), ps, rope[0], rope[1], n)
            else:
                self.cp(kf_dst(kp), ps, eng="act")
        for b in range(n // 128):
            ps = self.PS[2 + b % 2][:, 0:256]
            for k in range(KC):
                self.mm(ps, HB[:, k, b * 128:(b + 1) * 128], WKV[:, k, 256:512], k == 0, k == KC - 1)
            va = va_dst(vblk0 + b)
            self.cp(va[:, 0:132].rearrange("p (a c) -> p a c", c=66)[:, :, 0:64], ps[:, 0:128].rearrange("p (a c) -> p a c", c=64), eng="act")
            self.cp(va[:, 130:262].rearrange("p (a c) -> p a c", c=66)[:, :, 0:64], ps[:, 128:256].rearrange("p (a c) -> p a c", c=64), eng="act")
            self.ts(va[:, 64:66], self.RESET[:, 1:3], 0.0, 1.0, ALU.mult, ALU.add)
            self.ts(va[:, 194:196], self.RESET[:, 1:3], 0.0, 1.0, ALU.mult, ALU.add)
            if emit is not None:
                sq, tok0 = emit
                stv = self.f(2048 + (b % 2) * 256, [128, 256])
                self.cp(stv, ps)
                self.dma(self.dram["nv"][sq, tok0 + b * 128:tok0 + (b + 1) * 128, :], stv, self.p.named_sem("nv%d" % (b % 2)))
                ps2 = self.PS[4 + b % 2][:, 0:256]
                for k in range(KC):
                    self.mm(ps2, HB[:, k, b * 128:(b + 1) * 128], WKV[:, k, 0:256], k == 0, k == KC - 1)
                stk = self.f(2560 + (b % 2) * 256, [128, 256])
                self.cp(stk, ps2, eng="act")
                self.dma(self.dram["nk"][sq, tok0 + b * 128:tok0 + (b + 1) * 128, :], stk, self.p.named_sem("nk%d" % (b % 2)))

    def c_q_project(self, HB, n, rope):
        wq = self.dram["w_qkv"].rearrange("(k p) n -> p k n", p=128)
        QF = self.r(self.o_qf, [128, 8, 512])
        WS = self.r(self.o_ws, [128, KC, 4, 128])
        for grp in range(2):
            for s_ in range(2):
                for i in range(4):
                    c0 = grp * 512 + s_ * 256 + i * 64
                    self.dma_r(WS[:, :, i, s_ * 64:(s_ + 1) * 64], wq[:, :, c0:c0 + 64], self.p.named_sem("wq%d_%d" % (s_, i)))
            for i in range(4):
                pair = grp * 4 + i
                ps = self.PS[pair % 2][:, 0:n]
                for k in range(KC):
                    self.mm(ps, WS[:, k, i, :], HB[:, k, 0:n], k == 0, k == KC - 1)
                if rope is not None:
                    self.rope_apply(QF[:, pair, 0:n], ps, rope[0], rope[1], n)
                else:
                    self.cp(QF[:, pair, 0:n], ps, eng="act")
        return QF

    def c_attend(self, QF, nqb, keysets):
        OT = self.r(self.o_ot, [128, 4, 1024])
        NPT = 4
        PT = [self.r(self.o_ws + i * 512, [128, 512]) for i in range(NPT)]
        SCB = [self.PS[0], self.PS[1], self.PS[7]]
        tasks = []
        for h in range(C_HEADS):
            for ki, ks in enumerate(keysets):
                tasks.append((h, ki, ks))
        nk_for = [sum(1 for ks in keysets if ks[2] <= qb <= ks[3]) for qb in range(nqb)]

        def hinfo(h):
            kap = h // 4
            half = kap % 2
            pair = (h % 4) + (0 if h < 8 else 4)
            return kap, half, pair, slice(64 * half, 64 * half + 64)

        def score(i):
            h, ki, (kf_fn, va_fn, qlo, qhi, masks, halo) = tasks[i]
            kap, half, pair, hp = hinfo(h)
            ncol = (qhi - qlo + 1) * 128
            ps = SCB[i % 3][:, 0:ncol]
            pt = PT[i % NPT][:, 0:ncol]
            self.mm(ps, kf_fn(kap, half), QF[hp, pair, qlo * 128:qlo * 128 + ncol], True, True)
            self.act(pt, ps, AF.Exp, scale=HD ** -0.5)
            for qb in range(qlo, qhi + 1):
                if qb in masks:
                    sl = slice((qb - qlo) * 128, (qb - qlo + 1) * 128)
                    if halo is None:
                        self.tt(pt[:, sl], pt[:, sl].bitcast(F32), masks[qb][:], ALU.mult)
                    else:
                        self.stt(pt[:, sl], pt[:, sl].bitcast(F32), halo, masks[qb][:], ALU.mult, ALU.mult)

        seen = {}

        def pv(i):
            h, ki, (kf_fn, va_fn, qlo, qhi, masks, halo) = tasks[i]
            kap, half, pair, hp = hinfo(h)
            c0, o_off, d_off = self.va_slot(kap)
            ncol = (qhi - qlo + 1) * 128
            pt = PT[i % NPT][:, 0:ncol]
            for qb in range(qlo, qhi + 1):
                sl = slice((qb - qlo) * 128, (qb - qlo + 1) * 128)
                seen[(h, qb)] = seen.get((h, qb), 0) + 1
                self.mm(self.PS[2 + qb][:, 0:66], pt[:, sl], va_fn(kap)[:, c0:c0 + 66], seen[(h, qb)] == 1, seen[(h, qb)] == nk_for[qb])
            if ki == len(keysets) - 1:
                for qb in range(nqb):
                    po = self.PS[2 + qb]
                    rd = self.f(2048 + 16 * qb, [128, 1])
                    self.ts(rd, po[:, d_off:d_off + 1], self.ES[:, h:h + 1], None, ALU.add)
                    self.p.op("dve", lambda e: e.reciprocal(out=rd, in_=rd), reads=[rd], writes=[rd])
                    self.ts(OT[:, qb, h * 64:(h + 1) * 64], po[:, o_off:o_off + 64], rd, None, ALU.mult)

        LOOK = 2
        for i in range(min(LOOK, len(tasks))):
            score(i)
        for i in range(len(tasks)):
            if i + LOOK < len(tasks):
                score(i + LOOK)
            pv(i)
        return OT

    def c_outproj(self, l, OT, x0, nqb, g):
        n = nqb * 128
        OA = self.r(self.o_hb, [128, KC, 512])
        for qb in range(nqb):
            for hb in range(2):
                ps = self.PS[hb]
                for j in range(4):
                    c = hb * 4 + j
                    self.tr(ps[:, j * 128:(j + 1) * 128], OT[:, qb, c * 128:(c + 1) * 128].bitcast(F32))
                self.cp(OA[:, hb * 4:(hb + 1) * 4, qb * 128:(qb + 1) * 128], ps[:].rearrange("p (a b) -> p a b", a=4),
                        eng="act" if hb else "dve")
        wo = self.dram["w_out_c"].rearrange("(c p) o -> p c o", p=128)
        for half in range(2):
            WO = self.r(self.o_qf, [128, KC, 512])
            self.dma_r(WO, wo[:, :, half * 512:(half + 1) * 512], self.p.named_sem("woc"))
            for o in range(4):
                for c in range(KC):
                    self.mm(self.PS[4 + o][:, 0:n], WO[:, c, o * 128:(o + 1) * 128], OA[:, c, 0:n], c == 0, c == KC - 1)
            for o in range(4):
                oc = half * 4 + o
                xs_ = self.X[:, oc, x0:x0 + n]
                self.stt(xs_, self.PS[4 + o][:, 0:n], self.GSC[:, 1, oc, g:g + 1], xs_, ALU.mult, ALU.add)
        self.ln_range(l * 3 + 1, x0, n, self.o_ws)

    def mixer_c(self, l):
        d = self.dram
        self.o_kf = 0
        self.o_va = 4096
        self.o_kc = self.o_va + 16 * 262
        self.o_vca = self.o_kc + 1024
        self.o_hb = self.o_vca + 4 * 262
        self.o_ws = self.o_hb + 4096
        self.o_qf = self.o_ws + 4096
        self.o_ot = self.o_qf + 4096
        self.o_zq = self.o_ot + 4096
        assert self.o_zq + 512 <= self.NR, self.o_zq
        self.c_consts()
        KF = self.r(self.o_kf, [128, 2, 2048])
        VA = self.r(self.o_va, [128, 16, 262])
        KCF = self.r(self.o_kc, [128, 2, 512])
        VCA = self.r(self.o_vca, [128, 4, 262])
        wq = d["w_qkv"].rearrange("(k p) n -> p k n", p=128)
        WKV = self.r(self.o_ws, [128, KC, 512])

        def load_wkv():
            self.dma_r(WKV, wq[:, :, 1024:1536], self.p.named_sem("wkv"))
        for sq in range(2):
            x0 = sq * SEQ
            HB = self.c_modulate(x0, SEQ, 0)
            load_wkv()
            self.c_kv_project(HB, SEQ, WKV, lambda kp: KF[:, kp, 0:SEQ], lambda b: VA[:, b, :], 0, None, emit=(sq, 0))
            QF = self.c_q_project(HB, SEQ, None)
            keysets = []
            for kb in range(2):
                keysets.append((lambda kap, half, kb=kb: KF[64 * half:64 * half + 64, kap // 2, kb * 128:(kb + 1) * 128],
                                lambda kap, kb=kb: VA[:, kb, :], 0, 1, {}, None))
            OT = self.c_attend(QF, 2, keysets)
            self.c_outproj(l, OT, x0, 2, 0)
        stg = self.r(self.o_ot, [128, 4, 256])
        self.dma_r(stg, d["ck"].rearrange("(b p) c -> p b c", p=128), self.p.named_sem("ck"))
        for b in range(4):
            for kp in range(2):
                ps = self.PS[(b * 2 + kp) % 2][:, 0:128]
                self.tr(ps, stg[:, b, kp * 128:(kp + 1) * 128].bitcast(F32))
                self.cp(KCF[:, kp, b * 128:(b + 1) * 128], ps, eng="act" if kp else "dve")
        cvv = d["cv"].rearrange("(b p) c -> p b c", p=128)
        for kap in range(4):
            c0 = [0, 66, 130, 196][kap]
            self.dma_r(VCA[:, :, c0:c0 + 64], cvv[:, :, kap * 64:(kap + 1) * 64], self.p.named_sem("cv%d" % kap))
        for b in range(4):
            self.ts(VCA[:, b, 64:66], self.RESET[:, 1:3], 0.0, 1.0, ALU.mult, ALU.add)
            self.ts(VCA[:, b, 194:196], self.RESET[:, 1:3], 0.0, 1.0, ALU.mult, ALU.add)
        load_wkv()
        W0 = self.W0
        for (off, n_, r0) in ((0, 512, 0), (512, 256, 8)):
            HB = self.c_modulate(W0 + off, n_, 1)
            rope = self.rope_tables_w(r0, n_ // 64)
            self.c_kv_project(HB, n_, WKV, lambda kp, off=off, n_=n_: KF[:, kp, off:off + n_], lambda b: VA[:, b, :], off // 128, rope)
        HB = self.c_modulate(W0 + 128, 512, 1)
        rope = self.rope_tables_w(2, 8)
        QF = self.c_q_project(HB, 512, rope)
        keysets = []
        for cb in range(4):
            keysets.append((lambda kap, half, cb=cb: KCF[64 * half:64 * half + 64, kap // 2, cb * 128:(cb + 1) * 128],
                            lambda kap, cb=cb: VCA[:, cb, :], 0, 3, {}, None))
        for kb in range(6):
            qlo = max(1, kb - 1) - 1
            qhi = min(4, kb + 1) - 1
            masks = {}
            if 0 <= kb - 2 <= 3:
                masks[kb - 2] = self.LM1
            if 0 <= kb <= 3:
                masks[kb] = self.LM3
            halo = None
            if kb == 0:
                halo = self.SELV[:, 12:13]
            if kb == 5:
                halo = self.SELV[:, 13:14]
            keysets.append((lambda kap, half, kb=kb: KF[64 * half:64 * half + 64, kap // 2, kb * 128:(kb + 1) * 128],
                            lambda kap, kb=kb: VA[:, kb, :], qlo, qhi, masks, halo))
        OT = self.c_attend(QF, 4, keysets)
        self.c_outproj(l, OT, W0 + 128, 4, 1)

    def build(self):
        self.o_w13 = 0
        self.o_w2 = 8192
        self.o_xm = 12288
        self.o_hid = 16384
        self.o_ln = self.o_hid + 18 * 512
        self.o_sg = 0
        self.o_lnf = 1024
        self.consts()
        self.EPSLN = self.sb("EPSLN", [128, 1])
        self.memset(self.EPSLN[:], LN_EPS / (ALPHA * ALPHA))
        self.ONEC = self.sb("ONEC", [128, 1])
        self.memset(self.ONEC[:], 1.0)
        self.load_small()
        self.dma(self.SELV[:], self.dram["selv"], self.p.named_sem("selv"))
        self.load_x()
        self.W0 = TP
        full = [(t * 512, 512, 0 if t == 0 else 1) for t in range(NT)]
        win = [(0, 512, 0), (self.W0, 512, 1), (self.W0 + 512, 256, 1)]
        own = [(0, 512, 0), (self.W0 + 128, 512, 1)]
        for l in range(DEPTH):
            self.mod_vectors(l)
            for (x0, n, g) in (full if l == 0 else win):
                self.ffn_tile(l, 0, x0, n, g)
            if self.stage == 1 + 3 * l:
                break
            if l == 0:
                self.mixer_ab(l)
                self.select_window()
            else:
                self.mixer_c(l)
            if self.stage == 2 + 3 * l:
                break
            for (x0, n, g) in (win if l == 0 else own):
                self.ffn_tile(l, 1, x0, n, g)
            if self.stage == 3 + 3 * l:
                break
        self.store_x()
        self.p.finish("sp")
        return self.nc


_CACHE = {}


def get_program(stage=99):
    if stage not in _CACHE:
        _CACHE[stage] = Builder(stage).build()
    return _CACHE[stage]


def _selv(j):
    v = np.zeros((16,), np.float32)
    v[j] = 1.0
    if j > 0:
        v[4 + j - 1] = 1.0
        v[12] = 1.0
    if j < 3:
        v[8 + j + 1] = 1.0
        v[13] = 1.0
    return np.ascontiguousarray(np.broadcast_to(v, (128, 16))).astype(np.float32)


def shard_inputs(inp):
    f = lambda a: np.ascontiguousarray(np.asarray(a, dtype=np.float32))
    maps = []
    for c in range(NCORES):
        sb = c // 4
        m = {
            "xp": f(inp["x_prompt"][2 * c:2 * c + 2].reshape(TP, D)),
            "xs": f(inp["x_sample"][sb]),
            "st_h": f(inp["state_hgrn"][sb, 0]),
            "st_g": f(inp["state_gla"][sb, 0]),
            "ck": f(inp["cache_k"][sb, 0].reshape(PAST, C_KV * HD)),
            "cv": f(inp["cache_v"][sb, 0].reshape(PAST, C_KV * HD)),
            "cvec": f(np.stack([inp["c_ctx"], inp["c"][sb]], axis=0)),
            "selv": _selv(c % 4),
            "w_mod": f(inp["w_mod"]),
            "b_mod": f(inp["b_mod"]),
            "ln_g": f(inp["ln_g"].reshape(DEPTH * 3, D)),
            "ln_b": f(inp["ln_b"].reshape(DEPTH * 3, D)),
            "ffn_w1": f(inp["ffn_w1"]),
            "ffn_w3": f(inp["ffn_w3"]),
            "ffn_w2": f(inp["ffn_w2"]),
            "w_in_ab": f(inp["w_in_ab"][0]),
            "hgrn_lb": f(inp["hgrn_lb"]),
            "gate_up": f(inp["gla_gate_up"][0]),
            "gate_b": f(inp["gla_gate_b"][0]),
            "norm_a": f(inp["norm_a"]),
            "norm_b": f(inp["norm_b"]),
            "w_out_ab": f(inp["w_out_ab"][0]),
            "w_qkv": f(inp["w_qkv_c"][0]),
            "sink": f(inp["sink_c"]),
            "w_out_c": f(inp["w_out_c"][0]),
        }
        maps.append(m)
    return maps


def gather_outputs(res):
    r = res.results
    B = 16
    yp = np.concatenate([r[c]["yp"].reshape(2, SEQ, D) for c in range(NCORES)], axis=0)
    ys = np.stack([np.concatenate([r[4 * b + q]["ys"] for q in range(4)], axis=0) for b in range(2)], axis=0)
    nsh = np.concatenate([r[c]["ns_h"].reshape(2, 1, 2, A_HEADS, 128, 128) for c in range(NCORES)], axis=0)
    nsg = np.concatenate([r[c]["ns_g"].reshape(2, 1, 2, B_HEADS, B_DK, 128) for c in range(NCORES)], axis=0)
    nk = np.concatenate([r[c]["nk"].reshape(2, 1, SEQ, C_KV, HD) for c in range(NCORES)], axis=0)
    nv = np.concatenate([r[c]["nv"].reshape(2, 1, SEQ, C_KV, HD) for c in range(NCORES)], axis=0)
    return (yp.astype(np.float32), ys.astype(np.float32), nsh.astype(np.float32), nsg.astype(np.float32),
            nk.astype(np.float32), nv.astype(np.float32))


def kernel(**inputs):
    stage = int(os.environ.get("MK_STAGE", "99"))
    nc = get_program(stage)
    maps = shard_inputs(inputs)
    res = run_bass_kernel_spmd(nc, maps, core_ids=list(range(NCORES)))
    return gather_outputs(res)
```

```python
import os
import numpy as np
import concourse.bass as bass
import concourse.mybir as mybir
from concourse.bass_utils import run_bass_kernel_spmd

F32 = mybir.dt.float32
F32R = mybir.dt.float32r
AF = mybir.ActivationFunctionType
ALU = mybir.AluOpType

D = 1024
KC = 8
DFF = 2816
FC = 22
NMOD = 9
DEPTH = 2
SEQ = 256
TP = 512
TS = 2048
T = TP + TS
NT = T // 512
A_HEADS = 4
B_HEADS = 4
B_DK = 64
GATE_RANK = 16
GLA_TAU = 16.0
AB_IN = 4128
C_HEADS = 16
C_KV = 4
HD = 64
PAST = 512
ALPHA = (2.0 * DEPTH) ** 0.25
LN_EPS = 1e-5
RMS_EPS = 1e-6
ROPE_BASE = 10000.0
NCORES = 8


class Prog:
    def __init__(self, nc):
        self.nc = nc
        self.E = {"pe": nc.tensor, "act": nc.scalar, "dve": nc.vector, "pool": nc.gpsimd, "sp": nc.sync}
        self.sems = {}
        self.cnt = {}
        for e in self.E:
            self.sems[e] = nc.alloc_semaphore("s_" + e)
            self.cnt[e] = 0
        self.seen = {e: {} for e in self.E}
        self.ndma_sem = 0
        self.n_inst = 0
        self.n_wait = 0
        self.self_sync = set(os.environ.get("MK_SELFSYNC", "act,dve,pool").split(",")) - {""}
        self.named = {}
        self.mem = {}

    def named_sem(self, name):
        if name not in self.named:
            self.named[name] = self.new_dma_sem()
        return self.named[name]

    def new_dma_sem(self):
        k = "d%d" % self.ndma_sem
        self.ndma_sem += 1
        self.sems[k] = self.nc.alloc_semaphore("s_" + k)
        self.cnt[k] = 0
        return k

    @staticmethod
    def box(ap):
        esz = mybir.dt.size(ap.dtype)
        dims = ap.ap
        off = ap.offset
        name = ap.tensor.name
        if str(ap.space) == "DRAM":
            lo = off
            hi = off
            for st, cn in dims:
                d = (cn - 1) * st
                if d > 0:
                    hi += d
                else:
                    lo += d
            return name, 0, 1, lo * esz, (hi + 1) * esz
        pst, pcn = dims[0]
        if pst <= 0:
            pst = 1 << 40
        p0 = off // pst
        c0 = off % pst
        if str(ap.space) == "PSUM":
            q0 = (p0 // 32) * 32
            q1 = ((p0 + pcn + 31) // 32) * 32
            return name, q0, q1, 0, 2048
        lo = c0
        hi = c0
        for st, cn in dims[1:]:
            d = (cn - 1) * st
            if d > 0:
                hi += d
            else:
                lo += d
        return name, p0, p0 + pcn, lo * esz, (hi + 1) * esz

    def _collect(self, reads, writes):
        deps = []
        rb = [self.box(a) for a in reads]
        wb = [self.box(a) for a in writes]
        for (name, p0, p1, lo, hi) in rb:
            m = self.mem.get(name)
            if m is None:
                continue
            for r in m[0]:
                if r[0] < p1 and p0 < r[1] and r[2] < hi and lo < r[3]:
                    deps.append((r[4], r[5]))
        for (name, p0, p1, lo, hi) in wb:
            m = self.mem.get(name)
            if m is None:
                continue
            for lst in m:
                for r in lst:
                    if r[0] < p1 and p0 < r[1] and r[2] < hi and lo < r[3]:
                        deps.append((r[4], r[5]))
        return deps, rb, wb

    def _record(self, rb, wb, key, val):
        for (name, p0, p1, lo, hi) in wb:
            m = self.mem.setdefault(name, [[], []])
            for i in (0, 1):
                m[i] = [r for r in m[i] if not (p0 <= r[0] and r[1] <= p1 and lo <= r[2] and r[3] <= hi)]
            m[0].append([p0, p1, lo, hi, key, val])
        for (name, p0, p1, lo, hi) in rb:
            m = self.mem.setdefault(name, [[], []])
            m[1] = [r for r in m[1] if not (r[4] == key and p0 <= r[0] and r[1] <= p1 and lo <= r[2] and r[3] <= hi)]
            m[1].append([p0, p1, lo, hi, key, val])

    def _wait(self, e, deps):
        best = {}
        for k, v in deps:
            if best.get(k, 0) < v:
                best[k] = v
        for k, v in best.items():
            if k == e and e not in self.self_sync:
                continue
            if self.seen[e].get(k, 0) < v:
                self.E[e].wait_ge(self.sems[k], v)
                self.seen[e][k] = v
                self.n_wait += 1

    def op(self, e, fn, reads=(), writes=()):
        deps, rb, wb = self._collect(reads, writes)
        self._wait(e, deps)
        ins = fn(self.E[e])
        ins.then_inc(self.sems[e], 1)
        self.cnt[e] += 1
        self._record(rb, wb, e, self.cnt[e])
        self.n_inst += 1
        return ins

    def dma(self, q, out, in_, sem):
        deps, rb, wb = self._collect([in_], [out])
        self._wait(q, deps)
        ins = self.E[q].dma_start(out=out, in_=in_)
        ins.then_inc(self.sems[sem], 16)
        self.cnt[sem] += 16
        self._record(rb, wb, sem, self.cnt[sem])
        self.n_inst += 1
        return ins

    def finish(self, e="sp"):
        deps = []
        for name, m in self.mem.items():
            for lst in m:
                for r in lst:
                    deps.append((r[4], r[5]))
        self._wait(e, deps)


class Builder:
    def __init__(self, stage=99):
        self.stage = stage
        nc = bass.Bass("TRN2", target_bir_lowering=False)
        nc.dge_precook = False
        self.nc = nc
        self.p = Prog(nc)
        self.dram = {}
        self.decl_io()
        self.alloc()

    def din(self, name, shape):
        self.dram[name] = self.nc.dram_tensor(name, list(shape), F32, kind="ExternalInput").ap()
        return self.dram[name]

    def dout(self, name, shape):
        self.dram[name] = self.nc.dram_tensor(name, list(shape), F32, kind="ExternalOutput").ap()
        return self.dram[name]

    def decl_io(self):
        self.din("xp", [TP, D])
        self.din("xs", [TS, D])
        self.din("st_h", [2, A_HEADS, 128, 128])
        self.din("st_g", [2, B_HEADS, B_DK, 128])
        self.din("ck", [PAST, C_KV * HD])
        self.din("cv", [PAST, C_KV * HD])
        self.din("cvec", [2, D])
        self.din("selv", [128, 16])
        self.din("w_mod", [DEPTH, D, NMOD * D])
        self.din("b_mod", [DEPTH, NMOD * D])
        self.din("ln_g", [DEPTH * 3, D])
        self.din("ln_b", [DEPTH * 3, D])
        self.din("ffn_w1", [DEPTH, 2, D, DFF])
        self.din("ffn_w3", [DEPTH, 2, D, DFF])
        self.din("ffn_w2", [DEPTH, 2, DFF, D])
        self.din("w_in_ab", [D, AB_IN])
        self.din("hgrn_lb", [2, 2, 512])
        self.din("gate_up", [2, GATE_RANK, 256])
        self.din("gate_b", [2, 256])
        self.din("norm_a", [1, 128])
        self.din("norm_b", [1, 128])
        self.din("w_out_ab", [D, D])
        self.din("w_qkv", [D, 1536])
        self.din("sink", [1, C_HEADS])
        self.din("w_out_c", [D, D])
        self.dout("yp", [TP, D])
        self.dout("ys", [512, D])
        self.dout("ns_h", [2, 2, A_HEADS, 128, 128])
        self.dout("ns_g", [2, 2, B_HEADS, B_DK, 128])
        self.dout("nk", [2, SEQ, C_KV * HD])
        self.dout("nv", [2, SEQ, C_KV * HD])

    def sb(self, name, shape, dt=F32):
        return self.nc.alloc_sbuf_tensor(name, list(shape), dt)

    def alloc(self):
        nc = self.nc
        self.X = self.sb("X", [128, KC, T])
        self.IDENT = self.sb("IDENT", [128, 128])
        self.ONES = self.sb("ONES", [128, 128], F32R)
        self.U1 = self.sb("U1", [128, 256])
        self.U2 = self.sb("U2", [128, 256], F32R)
        self.MF = self.U1[:, 0:128]
        self.MB = self.U1[:, 128:256]
        self.LM1 = self.U1[:, 0:128]
        self.LM3 = self.U1[:, 128:256]
        self.RESET = self.sb("RESET", [128, 256])
        self.PARS = self.sb("PARS", [128, 22])
        self.LB = self.sb("LB", [128, 8])
        self.OML = self.sb("OML", [128, 8])
        self.OMLH = self.sb("OMLH", [128, 8])
        self.LBH = self.sb("LBH", [128, 8])
        self.SELV = self.sb("SELV", [128, 16])
        self.TRW = self.sb("TRW", [128, 2, 12])
        self.NGB = self.sb("NGB", [128, 4])
        self.GUP = self.U2[:, 0:256]
        self.PERMR = self.U2[:, 0:128]
        self.EPSR = self.sb("EPSR", [128, 1])
        self.ES = self.sb("ES", [128, 16])
        self.TCOS = self.sb("TCOS", [128, 64])
        self.TSIN = self.sb("TSIN", [128, 64])
        self.MR = self.sb("MR", [128, 1])
        self.MC = self.sb("MC", [128, 1])
        self.SGN = self.sb("SGN", [128, 1])
        self.MODV = self.sb("MODV", [128, 72, 2])
        self.OPS = self.sb("OPS", [128, 3, KC, 2])
        self.GSC = self.sb("GSC", [128, 3, KC, 2])
        self.BM = self.sb("BM", [128, 72])
        self.LNG = self.sb("LNG", [128, 48])
        self.LNB = self.sb("LNB", [128, 48])
        self.CS = self.sb("CS", [128, 2, KC], F32R)
        self.NR = 27 * 1024
        self.NF = 3072
        self.R = self.sb("R", [128, self.NR], F32R)
        self.Fm = self.sb("Fm", [128, self.NF])
        self.PS = [nc.alloc_psum_tensor("PS%d" % i, [128, 512], F32) for i in range(8)]
        self.sem_w13 = [self.p.new_dma_sem() for _ in range(2)]
        self.sem_w2 = [self.p.new_dma_sem() for _ in range(4)]
        self.sem_wab = [self.p.new_dma_sem() for _ in range(7)]
        self.sem_st = self.p.new_dma_sem()
        self.sem_io = [self.p.new_dma_sem() for _ in range(2)]
        self.sem_misc = self.p.new_dma_sem()
        self.sem_out = self.p.new_dma_sem()
        self.n13 = 0
        self.n2 = 0
        self.nio = 0

    @staticmethod
    def _view(base, off, shape, total):
        n = 1
        for x in shape[1:]:
            n *= x
        assert off + n <= total, (off, n, total)
        v = base[0:shape[0], off:off + n]
        if len(shape) == 3:
            v = v.rearrange("p (a b) -> p a b", a=shape[1])
        elif len(shape) == 4:
            v = v.rearrange("p (a b c) -> p a b c", a=shape[1], b=shape[2])
        return v

    def r(self, off, shape):
        return self._view(self.R, off, shape, self.NR)

    def f(self, off, shape):
        return self._view(self.Fm, off, shape, self.NF)

    def mm(self, out, lhsT, rhs, start, stop):
        self.p.op("pe", lambda e: e.matmul(out, lhsT=lhsT, rhs=rhs, start=start, stop=stop),
                  reads=[lhsT, rhs], writes=[out])

    def tr(self, out, in_, n=128):
        ident = self.IDENT[0:in_.shape[0], 0:in_.shape[0]]
        self.p.op("pe", lambda e: e.transpose(out=out, in_=in_, identity=ident), reads=[in_, ident], writes=[out])

    def act(self, out, in_, func, bias=None, scale=None, eng="act"):
        kw = {}
        rd = [in_]
        if bias is not None:
            kw["bias"] = bias
            if not isinstance(bias, (int, float)):
                rd.append(bias)
        if scale is not None:
            kw["scale"] = scale
            if not isinstance(scale, (int, float)):
                rd.append(scale)
        self.p.op("act", lambda e: e.activation(out=out, in_=in_, func=func, **kw), reads=rd, writes=[out])

    def ts(self, out, in0, s1, s2, op0, op1=None, eng="dve"):
        rd = [in0]
        for s in (s1, s2):
            if s is not None and not isinstance(s, (int, float)):
                rd.append(s)
        if op1 is None:
            self.p.op(eng, lambda e: e.tensor_scalar(out=out, in0=in0, scalar1=s1, scalar2=None, op0=op0), reads=rd, writes=[out])
        else:
            self.p.op(eng, lambda e: e.tensor_scalar(out=out, in0=in0, scalar1=s1, scalar2=s2, op0=op0, op1=op1), reads=rd, writes=[out])

    def tt(self, out, in0, in1, op, eng="dve"):
        self.p.op(eng, lambda e: e.tensor_tensor(out=out, in0=in0, in1=in1, op=op), reads=[in0, in1], writes=[out])

    def stt(self, out, in0, scalar, in1, op0, op1, eng="dve"):
        rd = [in0, in1]
        if not isinstance(scalar, (int, float)):
            rd.append(scalar)
        self.p.op(eng, lambda e: e.scalar_tensor_tensor(out=out, in0=in0, scalar=scalar, in1=in1, op0=op0, op1=op1), reads=rd, writes=[out])

    def cp(self, out, in_, eng="dve"):
        if eng == "act":
            self.act(out, in_, AF.Copy)
        else:
            self.p.op(eng, lambda e: e.tensor_copy(out=out, in_=in_), reads=[in_], writes=[out])

    def memset(self, ap, val, eng="pool"):
        self.p.op(eng, lambda e: e.memset(ap, val), writes=[ap])

    def dma(self, out, in_, sem, q="sp"):
        self.p.dma(q, out, in_, sem)

    def dma_r(self, out, in_, sem, q="sp"):
        self.p.dma(q, out if out.dtype == F32R else out.bitcast(F32R), in_.bitcast(F32R), sem)

    def consts(self):
        self.memset(self.IDENT[:], 1.0)
        self.p.op("pool", lambda e: e.affine_select(out=self.IDENT[:], in_=self.IDENT[:], pattern=[[-1, 128]],
                                                    compare_op=ALU.is_equal, fill=0.0, base=0, channel_multiplier=1),
                  reads=[self.IDENT[:]], writes=[self.IDENT[:]])
        tmp = self.f(0, [128, 128])
        self.memset(tmp, 1.0)
        self.cp(self.ONES[:], tmp)
        for (M, cm, st) in ((self.MF, -1, 1), (self.MB, 1, -1)):
            self.memset(M[:], 1.0)
            self.p.op("pool", lambda e: e.affine_select(out=M[:], in_=M[:], pattern=[[st, 128]], compare_op=ALU.is_ge,
                                                        fill=0.0, base=0, channel_multiplier=cm),
                      reads=[M[:]], writes=[M[:]])
        self.memset(self.MF[0:64, 64:128], 0.0)
        self.memset(self.MB[64:128, 0:64], 0.0)
        self.memset(self.RESET[:], 0.0)
        self.memset(self.RESET[:, 0:256:64], 1.0)
        self.memset(self.EPSR[:], RMS_EPS)

    def load_fm(self, dst, src_rows, nrows):
        st = self.f(0, [128, 128])
        self.dma(st[0:nrows, :], src_rows, self.sem_misc)
        ps = self.PS[7][:, 0:nrows]
        self.tr(ps, st[0:nrows, :])
        self.cp(dst, ps)

    def load_small(self):
        self.load_fm(self.LNG[:], self.dram["ln_g"].rearrange("r (k p) -> (r k) p", p=128), 48)
        self.load_fm(self.LNB[:], self.dram["ln_b"].rearrange("r (k p) -> (r k) p", p=128), 48)
        st = self.f(0, [128, 128])
        self.dma(st[0:16, :], self.dram["cvec"].rearrange("g (k p) -> (g k) p", p=128), self.sem_misc)
        ps = self.PS[7][:, 0:16]
        self.tr(ps, st[0:16, :])
        self.act(self.CS[:].rearrange("p g k -> p (g k)"), ps, AF.Silu)

    def load_x(self):
        for tb in range(T // 128):
            src = self.dram["xp"][tb * 128:(tb + 1) * 128, :] if tb < TP // 128 else \
                self.dram["xs"][tb * 128 - TP:(tb + 1) * 128 - TP, :]
            s = self.nio % 2
            self.nio += 1
            st = self.f(s * 1024, [128, 1024])
            self.dma(st, src, self.sem_io[s])
            for hb in range(2):
                ps = self.PS[(tb * 2 + hb) % 4]
                for j in range(4):
                    k = hb * 4 + j
                    self.tr(ps[:, j * 128:(j + 1) * 128], st[:, k * 128:(k + 1) * 128])
                dst = self.X[:, hb * 4:(hb + 1) * 4, tb * 128:(tb + 1) * 128]
                self.cp(dst, ps[:].rearrange("p (a b) -> p a b", a=4), eng="dve" if hb == 0 else "act")

    def store_x(self):
        for ob in range(8):
            if ob < 4:
                dst = self.dram["yp"][ob * 128:(ob + 1) * 128, :]
                xc = ob * 128
            else:
                dst = self.dram["ys"][(ob - 4) * 128:(ob - 3) * 128, :]
                xc = self.W0 + 128 + (ob - 4) * 128
            s = self.nio % 2
            self.nio += 1
            st = self.f(s * 1024, [128, 1024])
            for hb in range(2):
                ps = self.PS[(ob * 2 + hb) % 4]
                for j in range(4):
                    k = hb * 4 + j
                    self.tr(ps[:, j * 128:(j + 1) * 128], self.X[:, k, xc:xc + 128])
                self.cp(st[:, hb * 512:(hb + 1) * 512], ps[:], eng="dve" if hb == 0 else "act")
            self.dma(dst, st, self.sem_io[s])

    def select_window(self):
        SV = self.SELV
        for k in range(KC):
            own = self.f(0, [128, 512])
            hp = self.f(512, [128, 128])
            hn = self.f(640, [128, 128])
            for t in range(4):
                xt = self.X[:, k, TP + t * 512:TP + (t + 1) * 512]
                if t == 0:
                    self.ts(own, xt, SV[:, t:t + 1], None, ALU.mult)
                    self.ts(hp, xt[:, 384:512], SV[:, 4 + t:5 + t], None, ALU.mult)
                    self.ts(hn, xt[:, 0:128], SV[:, 8 + t:9 + t], None, ALU.mult)
                else:
                    self.stt(own, xt, SV[:, t:t + 1], own, ALU.mult, ALU.add)
                    self.stt(hp, xt[:, 384:512], SV[:, 4 + t:5 + t], hp, ALU.mult, ALU.add)
                    self.stt(hn, xt[:, 0:128], SV[:, 8 + t:9 + t], hn, ALU.mult, ALU.add)
            self.cp(self.X[:, k, self.W0:self.W0 + 128], hp, eng="act")
            self.cp(self.X[:, k, self.W0 + 128:self.W0 + 640], own, eng="act")
            self.cp(self.X[:, k, self.W0 + 640:self.W0 + 768], hn, eng="act")

    def mod_vectors(self, l):
        self.load_fm(self.BM[:], self.dram["b_mod"][l].rearrange("(r p) -> r p", p=128), 72)
        wm = self.dram["w_mod"][l].rearrange("(k p) n -> p k n", p=128)
        pm = self.PS[6][:, 0:144].rearrange("p (c g) -> p c g", g=2)
        for blk in range(18):
            s = self.n13 % 2
            self.n13 += 1
            wt = self.r(self.o_w13 + s * 4096, [128, KC, 512])
            self.dma_r(wt, wm[:, :, blk * 512:(blk + 1) * 512], self.sem_w13[s])
            for q in range(4):
                oc = blk * 4 + q
                for k in range(KC):
                    self.mm(pm[:, oc, :], wt[:, k, q * 128:(q + 1) * 128], self.CS[:, :, k], k == 0, k == KC - 1)
        for g in range(2):
            self.tt(self.MODV[:, :, g], pm[:, :, g], self.BM[:], ALU.add)
        gmul = [0.5 / ALPHA, 1.0 / ALPHA, 0.5 / ALPHA]
        for s in range(3):
            self.ts(self.OPS[:, s, :, :], self.MODV[:, (3 * s + 1) * 8:(3 * s + 2) * 8, :], 1.0, None, ALU.add)
            self.ts(self.GSC[:, s, :, :], self.MODV[:, (3 * s + 2) * 8:(3 * s + 3) * 8, :], gmul[s], None, ALU.mult)

    def shift(self, s, k, g):
        return self.MODV[:, 3 * s * 8 + k, g:g + 1]

    def ln_range(self, lnidx, x0, n, o_r):
        ts_ = slice(x0, x0 + n)
        pa, pb = self.PS[4][:, 0:n], self.PS[5][:, 0:n]
        for k in range(KC):
            zr = self.r(o_r + (k % 2) * 512, [128, 512])[:, 0:n]
            sq = self.r(o_r + 1024 + (k % 2) * 512, [128, 512])[:, 0:n]
            self.act(zr, self.X[:, k, ts_], AF.Copy, scale=1.0 / 1024.0)
            self.act(sq, self.X[:, k, ts_], AF.Square, scale=1.0 / 32.0)
            self.mm(pa, self.ONES[:], zr, k == 0, k == KC - 1)
            self.mm(pb, self.ONES[:], sq, k == 0, k == KC - 1)
        m2 = self.f(self.o_lnf, [128, 512])[:, 0:n]
        self.act(m2, pa, AF.Square)
        self.tt(m2, pb, m2, ALU.subtract)
        self.act(m2, m2, AF.Sqrt, bias=self.EPSLN[:, 0:1])
        self.p.op("dve", lambda e: e.reciprocal(out=m2, in_=m2), reads=[m2], writes=[m2])
        for k in range(KC):
            xk = self.X[:, k, ts_]
            self.tt(xk, xk, pa, ALU.subtract)
            self.stt(xk, xk, self.LNG[:, lnidx * 8 + k:lnidx * 8 + k + 1], m2, ALU.mult, ALU.mult)
            self.act(xk, xk, AF.Identity, bias=self.LNB[:, lnidx * 8 + k:lnidx * 8 + k + 1])

    def ffn_tile(self, l, j, x0, n, g):
        s = 0 if j == 0 else 2
        ts_ = slice(x0, x0 + n)
        XM = self.r(self.o_xm, [128, KC, 512])[:, :, 0:n]
        HID = self.r(self.o_hid, [128, FC, 512])[:, :, 0:n]
        for k in range(KC):
            self.ts(XM[:, k, :], self.X[:, k, ts_], self.OPS[:, s, k, g:g + 1], self.shift(s, k, g), ALU.mult, ALU.add)
        w1 = self.dram["ffn_w1"][l, j].rearrange("(k p) f -> p k f", p=128)
        w3 = self.dram["ffn_w3"][l, j].rearrange("(k p) f -> p k f", p=128)
        w2 = self.dram["ffn_w2"][l, j].rearrange("(c p) o -> p c o", p=128)
        for fb in range(FC // 2):
            sl = self.n13 % 2
            self.n13 += 1
            wt = self.r(self.o_w13 + sl * 4096, [128, 2, KC, 256])
            self.dma_r(wt[:, 0], w1[:, :, fb * 256:(fb + 1) * 256], self.sem_w13[sl])
            self.dma_r(wt[:, 1], w3[:, :, fb * 256:(fb + 1) * 256], self.p.named_sem("w3_%d" % sl))
            for c in range(2):
                f = 2 * fb + c
                p1, p3 = self.PS[f % 2][:, 0:n], self.PS[2 + f % 2][:, 0:n]
                for k in range(KC):
                    self.mm(p1, wt[:, 0, k, c * 128:(c + 1) * 128], XM[:, k, :], k == 0, k == KC - 1)
                for k in range(KC):
                    self.mm(p3, wt[:, 1, k, c * 128:(c + 1) * 128], XM[:, k, :], k == 0, k == KC - 1)
                sg = self.f(self.o_sg + (f % 2) * 512, [128, 512])[:, 0:n]
                self.act(sg, p1, AF.Silu)
                self.tt(HID[:, f, :], sg, p3, ALU.mult)
        for half in range(2):
            for fb in range(FC // 2):
                sl = self.n2 % 4
                self.n2 += 1
                wt = self.r(self.o_w2 + sl * 1024, [128, 2, 512])
                self.dma_r(wt, w2[:, 2 * fb:2 * fb + 2, half * 512:(half + 1) * 512], self.sem_w2[sl])
                for c in range(2):
                    f = 2 * fb + c
                    for o in range(4):
                        self.mm(self.PS[4 + o][:, 0:n], wt[:, c, o * 128:(o + 1) * 128], HID[:, f, :], f == 0, f == FC - 1)
            for o in range(4):
                oc = half * 4 + o
                self.stt(self.X[:, oc, ts_], self.PS[4 + o][:, 0:n], self.GSC[:, s, oc, g:g + 1], self.X[:, oc, ts_], ALU.mult, ALU.add)
        self.ln_range(l * 3 + s, x0, n, self.o_ln)

    def ab_params(self):
        st = self.f(0, [128, 128])
        d = self.dram
        self.dma(st[0:1, :], d["norm_a"], self.sem_misc)
        self.dma(st[1:2, :], d["norm_b"], self.sem_misc)
        self.dma(st[2:6, :], d["gate_b"].rearrange("a (j p) -> (a j) p", p=128), self.sem_misc)
        self.dma(st[6:22, :], d["hgrn_lb"].rearrange("a b (h p) -> (a b h) p", p=128), self.sem_misc)
        ps = self.PS[7][:, 0:22]
        self.tr(ps, st[0:22, :])
        self.cp(self.PARS[:], ps)
        P = self.PARS
        for dr in range(2):
            a = P[:, 6 + dr * 8:6 + dr * 8 + 4]
            b = P[:, 6 + dr * 8 + 4:6 + dr * 8 + 8]
            self.tt(self.LB[:, dr * 4:dr * 4 + 4], a, b, ALU.subtract)
        self.act(self.LB[:], self.LB[:], AF.Sigmoid)
        self.ts(self.OML[:], self.LB[:], -1.0, 1.0, ALU.mult, ALU.add)
        self.ts(self.OMLH[:], self.OML[:], 0.5, None, ALU.mult)
        self.tt(self.LBH[:], self.LB[:], self.OMLH[:], ALU.add)
        self.ts(self.NGB[:], P[:, 2:6], -1.0, None, ALU.mult)

    def ab_unit_pass(self, u, dr, x0, nht, g, sample, sidx):
        hg = u < 4
        j = u - 4
        d = self.dram
        win = d["w_in_ab"].rearrange("(k p) n -> p k n", p=128)
        if hg:
            cols = [("q", u * 128), ("f", (1024 if dr == 0 else 1536) + u * 128), ("i", 512 + u * 128)]
            if dr == 1:
                cols.append(("g0", 2048 + u * 128))
        else:
            cols = [("q", 2560 + j * 128), ("f", 2816 + j * 128), ("v0", 3072 + j * 256), ("v1", 3072 + j * 256 + 128),
                    ("bz", 4000)]
            if dr == 1:
                cols += [("g0", 3584 + j * 256), ("g1", 3584 + j * 256 + 128)]
        W = {}
        for i, (nm, c0) in enumerate(cols):
            W[nm] = self.r(self.o_wab + i * 1024, [128, KC, 128])
            self.dma_r(W[nm], win[:, :, c0:c0 + 128], self.sem_wab[i])
        nh = 1 if hg else 2
        heads = list(range(nh))
        vw = 128 * nh
        NSB = 6
        SB = [self.r(self.o_sb + i * 128, [128, 128]) for i in range(NSB)]
        SC = [self.f(1536, [128, 128]), self.f(1664, [128, 128])]
        st = {"si": 0, "sc": 0}
        if not hg:
            self.ts(self.GUP[64:128, :], self.RESET[64:128, :], 0.0, None, ALU.mult)
            self.dma_r(self.GUP[96 + 16 * dr:112 + 16 * dr, :], d["gate_up"][dr], self.p.named_sem("gup"))
        src0 = None
        if sample:
            src0 = d["st_h"][dr, u] if hg else d["st_g"][dr, 2 * j:2 * j + 2].rearrange("h d e -> (h d) e")
        if dr == 0:
            if sample:
                self.dma_r(SB[0], src0, self.sem_st)
            else:
                self.ts(SB[0], self.IDENT[:], 0.0, None, ALU.mult)
        else:
            if sample:
                self.dma(SC[0], src0, self.sem_st)
            else:
                self.memset(SC[0], 0.0)
        HB = self.r(self.o_hb, [128, KC, 256])
        QD = self.r(self.o_qd, [128, 256])
        KI = self.r(self.o_ki, [128, 256])
        VT = self.r(self.o_vt, [128, 2, 256])
        AT = [self.r(self.o_at + i * 128, [128, 128]) for i in range(2)]
        BZ = self.r(self.o_at, [128, 256])
        SQ = self.r(self.o_at, [128, 256])
        KIT = self.r(self.o_hb + 256, [128, 2, 128])
        F0 = self.f(0, [128, 256])
        F1 = self.f(256, [128, 256])
        F2 = self.f(512, [128, 256])
        F3 = self.f(1792, [128, 256])
        TMP = self.f(768, [128, 128])
        AC = self.f(896, [128, 4])
        GS = [self.f(1024, [128, 256]), self.f(1280, [128, 256])]
        PS = self.PS
        psq, psf = PS[0][:, 0:256], PS[0][:, 256:512]
        MASK = self.MF if dr == 0 else self.MB
        order = list(range(nht)) if dr == 0 else list(range(nht - 1, -1, -1))
        bs = [0, 1] if dr == 0 else [1, 0]
        cseq = [(b, c) for b in bs for c in bs]

        def pr_(hh):
            return slice(0, 128) if hg else slice(64 * hh, 64 * hh + 64)

        def stage_a(hti, part=3):
            if part & 1:
                stage_a1(hti)
            if part & 2:
                stage_a2()

        def stage_a1(hti):
            t0 = x0 + hti * 256
            for k in range(KC):
                if k % 2 == 0:
                    self.act(HB[:, k, :], self.X[:, k, t0:t0 + 256], AF.Identity, bias=self.shift(1, k, g), scale=self.OPS[:, 1, k, g:g + 1])
                else:
                    self.ts(HB[:, k, :], self.X[:, k, t0:t0 + 256], self.OPS[:, 1, k, g:g + 1], self.shift(1, k, g), ALU.mult, ALU.add)
            for k in range(KC):
                self.mm(psq, W["q"][:, k, :], HB[:, k, :], k == 0, k == KC - 1)
            for k in range(KC):
                self.mm(psf, W["f"][:, k, :], HB[:, k, :], k == 0, k == KC - 1)
            if not hg:
                for k in range(KC):
                    self.mm(PS[3][:, 0:256], W["bz"][:, k, :], HB[:, k, :], k == 0, k == KC - 1)

        def stage_a2():
            for b in range(2):
                for vv in range(nh):
                    wv = W["i"] if hg else W["v%d" % vv]
                    for k in range(KC):
                        self.mm(PS[1][:, b * 256 + vv * 128:b * 256 + vv * 128 + 128], HB[:, k, b * 128:(b + 1) * 128], wv[:, k, :],
                                k == 0, k == KC - 1)
            if dr == 1:
                for hh in heads:
                    for k in range(KC):
                        self.mm(PS[2][:, hh * 256:(hh + 1) * 256], W["g%d" % hh][:, k, :], HB[:, k, :], k == 0, k == KC - 1)

        def stage_b(gsi=0):
            if hg:
                self.act(F0, psf, AF.Tanh, scale=0.5)
                self.ts(F0, F0, self.OMLH[:, dr * 4 + u:dr * 4 + u + 1], self.LBH[:, dr * 4 + u:dr * 4 + u + 1], ALU.mult, ALU.add)
                self.ts(F1, F0, -1.0, 1.0, ALU.mult, ALU.add, eng="pool")
                kf = F1
            else:
                self.cp(BZ, PS[3][:, 0:256], eng="act")
                psl = PS[3][:, 256:512]
                self.mm(psl, self.GUP[64:128, j * 128:(j + 1) * 128], BZ[64:128, :], True, True)
                self.act(F0, psl, AF.Exp, scale=-1.0, bias=self.NGB[:, dr * 2 + j:dr * 2 + j + 1])
                self.act(F0, F0, AF.Ln, bias=self.ONEC[:, 0:1])
                self.act(F0, F0, AF.Exp, scale=-1.0 / GLA_TAU)
                kf = psf
            qs = None if hg else B_DK ** -0.5
            R1 = self.RESET[:, 0:256]
            if dr == 0:
                self.p.op("dve", lambda e: e.tensor_tensor_scan(out=F2, data0=R1, data1=F0, initial=1.0, op0=ALU.max, op1=ALU.mult),
                          reads=[R1, F0], writes=[F2])
                self.cp(AC, F2[:, 63:256:64])
                if hg:
                    self.tt(QD, psq, F2, ALU.mult)
                else:
                    self.stt(QD, psq, qs, F2, ALU.mult, ALU.mult)
                self.p.op("dve", lambda e: e.reciprocal(out=F2, in_=F2), reads=[F2], writes=[F2])
                self.tt(KI, kf, F2, ALU.mult)
            else:
                self.cp(F2[:, 1:256], F0[:, 0:255])
                self.memset(F2[:, 0:256:64], 1.0)
                self.p.op("dve", lambda e: e.tensor_tensor_scan(out=F3, data0=R1, data1=F2, initial=1.0, op0=ALU.max, op1=ALU.mult),
                          reads=[R1, F2], writes=[F3])
                self.tt(AC, F3[:, 63:256:64], F0[:, 63:256:64], ALU.mult)
                self.tt(KI, kf, F3, ALU.mult)
                self.p.op("dve", lambda e: e.reciprocal(out=F3, in_=F3), reads=[F3], writes=[F3])
                if hg:
                    self.tt(QD, psq, F3, ALU.mult)
                else:
                    self.stt(QD, psq, qs, F3, ALU.mult, ALU.mult)
            for b in range(2):
                if hg:
                    self.act(VT[:, b, 0:128], PS[1][:, b * 256:b * 256 + 128], AF.Silu)
                else:
                    self.cp(VT[:, b, :], PS[1][:, b * 256:(b + 1) * 256], eng="act")
            if dr == 1:
                for hh in heads:
                    self.act(GS[(hh + gsi) % 2], PS[2][:, hh * 256:(hh + 1) * 256], AF.Silu)

        def stage_c(PSO):
            pskv = []
            for idx, (b_, c_) in enumerate(cseq):
                bank = (PS[6] if c_ == 0 else PS[3]) if hg else (PS[6] if c_ == 0 else PS[1])
                w_ = 128 if hg else 256
                slot = 0 if idx < 2 else 1
                koff = 256 if (hg and c_ == 1) else 0
                pskv.append(bank[:, koff + slot * w_:koff + (slot + 1) * w_])
            first = [True, True]
            for hi, hh in enumerate(heads):
                pat = PS[5] if hi == 0 else PS[2 if dr == 0 else 3]
                pat_off = 0 if (hi == 0 or dr == 0) else 256
                for b in bs:
                    self.mm(pat[:, pat_off + b * 128:pat_off + (b + 1) * 128], KI[pr_(hh), b * 128:(b + 1) * 128],
                            QD[pr_(hh), b * 128:(b + 1) * 128], True, True)
                if hi == 0:
                    for b in bs:
                        self.tr(PS[3][:, b * 128:(b + 1) * 128], KI[:, b * 128:(b + 1) * 128].bitcast(F32))
                for b in bs:
                    self.tt(AT[b], pat[:, pat_off + b * 128:pat_off + (b + 1) * 128], MASK[:], ALU.mult)
                if hi == 0:
                    for b in bs:
                        self.cp(KIT[:, b, :], PS[3][:, b * 128:(b + 1) * 128], eng="act")
                    for idx, (b, c) in enumerate(cseq):
                        self.mm(pskv[idx], KIT[c * 64:(c + 1) * 64, b, :], VT[c * 64:(c + 1) * 64, b, 0:vw], True, True)
                for b in bs:
                    self.mm(PSO[hh][:, b * 128:(b + 1) * 128], VT[:, b, hh * 128:(hh + 1) * 128], AT[b], first[hh], False)
                    first[hh] = False
            return pskv

        def stage_d(pskv):
            states = []
            for idx, (b, c) in enumerate(cseq):
                ci = b * 2 + c
                a_c = AC[:, ci:ci + 1]
                if dr == 0:
                    S_cur = SB[st["si"] % NSB]
                    S_next = SB[(st["si"] + 1) % NSB]
                    states.append(S_cur)
                    if hg:
                        self.tt(TMP, pskv[idx], S_cur.bitcast(F32), ALU.add)
                    else:
                        self.tt(TMP[0:64, :], pskv[idx][0:64, 0:128], S_cur[0:64, :].bitcast(F32), ALU.add)
                        self.tt(TMP[64:128, :], pskv[idx][64:128, 128:256], S_cur[64:128, :].bitcast(F32), ALU.add)
                    self.ts(S_next, TMP, a_c, None, ALU.mult)
                    st["si"] += 1
                else:
                    C_cur = SC[st["sc"] % 2]
                    C_next = SC[(st["sc"] + 1) % 2]
                    S_sc = SB[st["si"] % NSB]
                    states.append(S_sc)
                    self.ts(S_sc, C_cur, a_c, None, ALU.mult)
                    if hg:
                        self.tt(C_next, pskv[idx], S_sc.bitcast(F32), ALU.add)
                    else:
                        self.tt(C_next[0:64, :], pskv[idx][0:64, 0:128], S_sc[0:64, :].bitcast(F32), ALU.add)
                        self.tt(C_next[64:128, :], pskv[idx][64:128, 128:256], S_sc[64:128, :].bitcast(F32), ALU.add)
                    st["si"] += 1
                    st["sc"] += 1
            return states

        def stage_d2(states, PSO):
            for idx, (b, c) in enumerate(cseq):
                ccols = slice(b * 128 + c * 64, b * 128 + c * 64 + 64)
                for hh in heads:
                    self.mm(PSO[hh][:, ccols], states[idx][pr_(hh), :], QD[pr_(hh), ccols], False, idx == 3)

        E1 = self.f(2560, [128, 256])
        E2 = self.f(2816, [128, 256])

        def stage_e(hti, PSO, gsi, pst):
            lt0 = hti * 256
            for hh in heads:
                head = u if hg else 4 + 2 * j + hh
                on = self.r(self.o_on + head * self.on_stride + lt0, [128, 256])
                pso = PSO[hh][:, 0:256]
                if dr == 0:
                    self.cp(on, pso, eng="act")
                else:
                    self.tt(E1, pso, on.bitcast(F32), ALU.add)
                    self.act(SQ, E1, AF.Square, scale=128.0 ** -0.5)
                    self.mm(pst, self.ONES[:], SQ, True, True)
                    self.act(E2, pst, AF.Sqrt, bias=self.EPSR[:, 0:1])
                    self.p.op("dve", lambda e: e.reciprocal(out=E2, in_=E2), reads=[E2], writes=[E2])
                    nw = self.PARS[:, 0:1] if hg else self.PARS[:, 1:2]
                    self.stt(E1, E1, nw, E2, ALU.mult, ALU.mult)
                    self.tt(on, E1, GS[(hh + gsi) % 2], ALU.mult)

        if hg:
            stage_a(order[0])
            prev = None
            for n_, hti in enumerate(order):
                PSOi = [PS[4] if n_ % 2 == 0 else PS[7]]
                stage_b(n_ % 2)
                if prev is not None:
                    stage_e(*prev)
                pskv = stage_c(PSOi)
                if n_ + 1 < len(order):
                    stage_a(order[n_ + 1], 1)
                states = stage_d(pskv)
                stage_d2(states, PSOi)
                if n_ + 1 < len(order):
                    stage_a(order[n_ + 1], 2)
                prev = (hti, PSOi, n_ % 2, PS[2][:, 0:256])
            stage_e(*prev)
        else:
            PSOg = [PS[4], PS[7]]
            stage_a(order[0])
            for n_, hti in enumerate(order):
                stage_b(0)
                pskv = stage_c(PSOg)
                if n_ + 1 < len(order):
                    stage_a(order[n_ + 1], 1)
                states = stage_d(pskv)
                stage_d2(states, PSOg)
                stage_e(hti, PSOg, 0, PS[5][:, 0:256])
                if n_ + 1 < len(order):
                    stage_a(order[n_ + 1], 2)
        if not sample:
            if dr == 0:
                S_fin = SB[st["si"] % NSB].bitcast(F32)
                sname = "sout%d" % (st["si"] % NSB)
            else:
                S_fin = SC[st["sc"] % 2]
                sname = "soutc%d" % (st["sc"] % 2)
            if hg:
                self.dma(d["ns_h"][sidx, dr, u], S_fin, self.p.named_sem(sname))
            else:
                self.dma(d["ns_g"][sidx, dr, 2 * j:2 * j + 2].rearrange("h d e -> (h d) e"), S_fin, self.p.named_sem(sname))

    def ab_outproj(self, l, x0, lt0, n, g):
        wo = self.dram["w_out_ab"].rearrange("(h p) o -> p h o", p=128)
        for half in range(2):
            WO = self.r(self.o_wab, [128, 8, 512])
            self.dma_r(WO, wo[:, :, half * 512:(half + 1) * 512], self.sem_wab[0])
            for o in range(4):
                for h in range(8):
                    on = self.r(self.o_on + h * self.on_stride + lt0, [128, 512])[:, 0:n]
                    self.mm(self.PS[o][:, 0:n], WO[:, h, o * 128:(o + 1) * 128], on, h == 0, h == 7)
            for o in range(4):
                oc = half * 4 + o
                xs_ = self.X[:, oc, x0:x0 + n]
                self.stt(xs_, self.PS[o][:, 0:n], self.GSC[:, 1, oc, g:g + 1], xs_, ALU.mult, ALU.add)
        self.ln_range(l * 3 + 1, x0, n, self.o_hb)

    def mixer_ab(self, l):
        self.o_on = 0
        self.on_stride = 2048
        self.o_wab = 16384
        self.o_hb = 23552
        o = 25600
        self.o_qd = o
        self.o_ki = o + 256
        self.o_vt = o + 512
        self.o_at = o + 1024
        self.o_sb = o + 1280
        assert self.o_sb + 768 <= self.NR
        self.ab_params()
        abn = int(os.environ.get("MK_ABN", "1000"))
        abskip = int(os.environ.get("MK_ABSKIP", "0"))
        cnt = 0
        for (x0, nht, g, sample, sidx) in ((0, 1, 0, False, 0), (256, 1, 0, False, 1), (512, 8, 1, True, None)):
            for u in range(6):
                for dr in range(2):
                    cnt += 1
                    if cnt <= abskip or cnt > abskip + abn:
                        continue
                    self.ab_unit_pass(u, dr, x0, nht, g, sample, sidx)
            ntok = nht * 256
            for off in range(0, ntok, 512):
                n = min(512, ntok - off)
                self.ab_outproj(l, x0 + off, off, n, g)

    def c_consts(self):
        d = self.dram
        A = self.f(0, [128, 128])
        B = self.f(128, [128, 128])
        for (M, cm, st, base) in ((A, 1, -1, -16), (B, -1, 1, -16)):
            self.memset(M, 1.0)
            self.p.op("pool", lambda e: e.affine_select(out=M, in_=M, pattern=[[st, 128]], compare_op=ALU.is_equal,
                                                        fill=0.0, base=base, channel_multiplier=cm),
                      reads=[M], writes=[M])
        self.memset(A.rearrange("p (g c) -> p g c", c=32)[:, :, 16:32], 0.0)
        self.memset(B.rearrange("p (g c) -> p g c", c=32)[:, :, 0:16], 0.0)
        self.tt(self.PERMR[:], A, B, ALU.add)
        for (M, cm, st) in ((self.LM1, -1, 1), (self.LM3, 1, -1)):
            self.memset(M[:], 1.0)
            self.p.op("pool", lambda e: e.affine_select(out=M[:], in_=M[:], pattern=[[st, 128]], compare_op=ALU.is_ge,
                                                        fill=0.0, base=0, channel_multiplier=cm),
                      reads=[M[:]], writes=[M[:]])
        st_ = self.f(256, [128, 16])
        self.memset(st_, 0.0)
        self.dma(st_[0:1, :], d["sink"], self.sem_misc)
        ps = self.PS[7][:, 0:16]
        onesf = self.f(384, [128, 128])
        self.memset(onesf, 1.0)
        self.mm(ps, onesf, st_, True, True)
        self.act(self.ES[:], ps, AF.Exp)
        I32 = mybir.dt.int32
        pi_ = self.f(512, [128, 1]).bitcast(I32)
        self.p.op("pool", lambda e: e.iota(pi_, pattern=[[0, 1]], base=0, channel_multiplier=1), writes=[pi_])
        i16 = self.f(513, [128, 1]).bitcast(I32)
        self.ts(i16, pi_, 15, None, ALU.bitwise_and)
        m32 = self.f(514, [128, 1]).bitcast(I32)
        self.ts(m32, pi_, 32, None, ALU.bitwise_and)
        i16f = self.f(515, [128, 1])
        self.cp(i16f, i16)
        m32f = self.f(516, [128, 1])
        self.cp(m32f, m32)
        inv = self.f(517, [128, 1])
        self.act(inv, i16f, AF.Exp, scale=-float(np.log(ROPE_BASE)) / 16.0)
        b16 = self.f(518, [128, 1]).bitcast(I32)
        self.ts(b16, pi_, 16, None, ALU.bitwise_and)
        b16f = self.f(519, [128, 1])
        self.cp(b16f, b16)
        self.ts(self.SGN[:], b16f, 1.0 / 8.0, -1.0, ALU.mult, ALU.add)
        self.ts(self.MC[:], m32f, 1.0 / 32.0, None, ALU.mult)
        self.ts(self.MR[:], self.MC[:], -1.0, 1.0, ALU.mult, ALU.add)
        pos = self.f(640, [128, 64])
        self.p.op("pool", lambda e: e.iota(pos.bitcast(I32), pattern=[[1, 64]], base=0, channel_multiplier=0), writes=[pos])
        posf = self.f(704, [128, 64])
        self.cp(posf, pos.bitcast(I32))
        TWO_PI = 2.0 * float(np.pi)
        TRC = self.f(1024, [128, 64])
        TRS = self.f(1088, [128, 64])
        for (posoff, dsin, dcos) in ((0.0, self.TSIN[:], self.TCOS[:]), (-2.0, TRS, TRC)):
            ang = self.f(768, [128, 64])
            self.ts(ang, posf, posoff, None, ALU.add)
            self.ts(ang, ang, inv, None, ALU.mult)
            for (dst, shiftv) in ((dsin, 0.0), (dcos, float(np.pi) / 2.0)):
                a = self.f(832, [128, 64])
                kq = self.f(896, [128, 64])
                ki = self.f(960, [128, 64]).bitcast(I32)
                self.ts(a, ang, shiftv, None, ALU.add)
                self.ts(kq, a, 1.0 / TWO_PI, None, ALU.mult)
                self.cp(ki, kq)
                self.cp(kq, ki)
                self.stt(a, kq, -TWO_PI, a, ALU.mult, ALU.add)
                self.ts(kq, a, float(np.pi), -TWO_PI, ALU.is_gt, ALU.mult)
                self.tt(a, a, kq, ALU.add)
                self.ts(kq, a, -float(np.pi), TWO_PI, ALU.is_lt, ALU.mult)
                self.tt(a, a, kq, ALU.add)
                self.act(dst, a, AF.Sin)
        for (ci, T) in ((0, TRC), (1, TRS)):
            for jq in range(4):
                if jq == 0:
                    self.ts(self.TRW[:, ci, :], T[:, 0:12], self.SELV[:, 0:1], None, ALU.mult)
                else:
                    self.stt(self.TRW[:, ci, :], T[:, 8 * jq:8 * jq + 12], self.SELV[:, jq:jq + 1], self.TRW[:, ci, :], ALU.mult, ALU.add)

    def rope_tables_w(self, r0, nrows):
        n = nrows * 64
        cos_t = self.f(0, [128, 512])[:, 0:n]
        sin_t = self.f(512, [128, 512])[:, 0:n]
        for (dst, ci, T) in ((cos_t, 0, self.TCOS), (sin_t, 1, self.TSIN)):
            dv = dst.rearrange("p (r c) -> p r c", c=64)
            rowv = self.TRW[:, ci, r0:r0 + nrows].unsqueeze(2).to_broadcast([128, nrows, 64])
            colv = T[:, 0:64].unsqueeze(1).to_broadcast([128, nrows, 64])
            self.ts(dv, rowv, self.MR[:, 0:1], None, ALU.mult)
            self.stt(dv, colv, self.MC[:, 0:1], dv, ALU.mult, ALU.add)
        self.ts(sin_t, sin_t, self.SGN[:, 0:1], None, ALU.mult)
        return cos_t, sin_t

    def rope_tables(self, t):
        cos_t = self.f(0, [128, 512])
        sin_t = self.f(512, [128, 512])
        for (dst, T) in ((cos_t, self.TCOS), (sin_t, self.TSIN)):
            dv = dst.rearrange("p (r c) -> p r c", c=64)
            rowv = T[:, 8 * t:8 * t + 8].unsqueeze(2).to_broadcast([128, 8, 64])
            colv = T[:, 0:64].unsqueeze(1).to_broadcast([128, 8, 64])
            self.ts(dv, rowv, self.MR[:, 0:1], None, ALU.mult)
            self.stt(dv, colv, self.MC[:, 0:1], dv, ALU.mult, ALU.add)
        self.ts(sin_t, sin_t, self.SGN[:, 0:1], None, ALU.mult)
        return cos_t, sin_t

    def rope_apply(self, dst, ps, cos_t, sin_t, n):
        ZQ = self.r(self.o_zq, [128, 512])[:, 0:n]
        self.cp(ZQ, ps, eng="act")
        pz = self.PS[6][:, 0:n]
        self.mm(pz, self.PERMR[:], ZQ, True, True)
        t1 = self.f(1024, [128, 512])[:, 0:n]
        self.tt(t1, ZQ.bitcast(F32), cos_t[:, 0:n], ALU.mult)
        t2 = self.f(1536, [128, 512])[:, 0:n]
        self.tt(t2, pz, sin_t[:, 0:n], ALU.mult)
        self.tt(dst, t1, t2, ALU.add)

    def c_modulate(self, x0, n, g):
        HB = self.r(self.o_hb, [128, KC, 512])
        for k in range(KC):
            self.ts(HB[:, k, 0:n], self.X[:, k, x0:x0 + n], self.OPS[:, 1, k, g:g + 1], self.shift(1, k, g), ALU.mult, ALU.add)
        return HB

    def va_slot(self, kap):
        return [(0, 0, 64), (64, 2, 0), (130, 0, 64), (194, 2, 0)][kap]

    def c_kv_project(self, HB, n, WKV, kf_dst, va_dst, vblk0, rope, emit=None):
        for kp in range(2):
            ps = self.PS[kp][:, 0:n]
            for k in range(KC):
                self.mm(ps, WKV[:, k, kp * 128:(kp + 1) * 128], HB[:, k, 0:n], k == 0, k == KC - 1)
            if rope is not None:
                self.rope_apply(kf_dst(kp), ps, rope[0], rope[1], n)
            else:
                self.cp(kf_dst(kp), ps, eng="act")
        for b in range(n // 128):
            ps = self.PS[2 + b % 2][:, 0:256]
            for k in range(KC):
                self.mm(ps, HB[:, k, b * 128:(b + 1) * 128], WKV[:, k, 256:512], k == 0, k == KC - 1)
            va = va_dst(vblk0 + b)
            self.cp(va[:, 0:132].rearrange("p (a c) -> p a c", c=66)[:, :, 0:64], ps[:, 0:128].rearrange("p (a c) -> p a c", c=64), eng="act")
            self.cp(va[:, 130:262].rearrange("p (a c) -> p a c", c=66)[:, :, 0:64], ps[:, 128:256].rearrange("p (a c) -> p a c", c=64), eng="act")
            self.ts(va[:, 64:66], self.RESET[:, 1:3], 0.0, 1.0, ALU.mult, ALU.add)
            self.ts(va[:, 194:196], self.RESET[:, 1:3], 0.0, 1.0, ALU.mult, ALU.add)
            if emit is not None:
                sq, tok0 = emit
                stv = self.f(2048 + (b % 2) * 256, [128, 256])
                self.cp(stv, ps)
                self.dma(self.dram["nv"][sq, tok0 + b * 128:tok0 + (b + 1) * 128, :], stv, self.p.named_sem("nv%d" % (b % 2)))
                ps2 = self.PS[4 + b % 2][:, 0:256]
                for k in range(KC):
                    self.mm(ps2, HB[:, k, b * 128:(b + 1) * 128], WKV[:, k, 0:256], k == 0, k == KC - 1)
                stk = self.f(2560 + (b % 2) * 256, [128, 256])
                self.cp(stk, ps2, eng="act")
                self.dma(self.dram["nk"][sq, tok0 + b * 128:tok0 + (b + 1) * 128, :], stk, self.p.named_sem("nk%d" % (b % 2)))

    def c_q_project(self, HB, n, rope):
        wq = self.dram["w_qkv"].rearrange("(k p) n -> p k n", p=128)
        QF = self.r(self.o_qf, [128, 8, 512])
        WS = self.r(self.o_ws, [128, KC, 4, 128])
        for grp in range(2):
            for s_ in range(2):
                for i in range(4):
                    c0 = grp * 512 + s_ * 256 + i * 64
                    self.dma_r(WS[:, :, i, s_ * 64:(s_ + 1) * 64], wq[:, :, c0:c0 + 64], self.p.named_sem("wq%d_%d" % (s_, i)))
            for i in range(4):
                pair = grp * 4 + i
                ps = self.PS[pair % 2][:, 0:n]
                for k in range(KC):
                    self.mm(ps, WS[:, k, i, :], HB[:, k, 0:n], k == 0, k == KC - 1)
                if rope is not None:
                    self.rope_apply(QF[:, pair, 0:n], ps, rope[0], rope[1], n)
                else:
                    self.cp(QF[:, pair, 0:n], ps, eng="act")
        return QF

    def c_attend(self, QF, nqb, keysets):
        OT = self.r(self.o_ot, [128, 4, 1024])
        NPT = 4
        PT = [self.r(self.o_ws + i * 512, [128, 512]) for i in range(NPT)]
        SCB = [self.PS[0], self.PS[1], self.PS[7]]
        tasks = []
        for h in range(C_HEADS):
            for ki, ks in enumerate(keysets):
                tasks.append((h, ki, ks))
        nk_for = [sum(1 for ks in keysets if ks[2] <= qb <= ks[3]) for qb in range(nqb)]

        def hinfo(h):
            kap = h // 4
            half = kap % 2
            pair = (h % 4) + (0 if h < 8 else 4)
            return kap, half, pair, slice(64 * half, 64 * half + 64)

        def score(i):
            h, ki, (kf_fn, va_fn, qlo, qhi, masks, halo) = tasks[i]
            kap, half, pair, hp = hinfo(h)
            ncol = (qhi - qlo + 1) * 128
            ps = SCB[i % 3][:, 0:ncol]
            pt = PT[i % NPT][:, 0:ncol]
            self.mm(ps, kf_fn(kap, half), QF[hp, pair, qlo * 128:qlo * 128 + ncol], True, True)
            self.act(pt, ps, AF.Exp, scale=HD ** -0.5)
            for qb in range(qlo, qhi + 1):
                if qb in masks:
                    sl = slice((qb - qlo) * 128, (qb - qlo + 1) * 128)
                    if halo is None:
                        self.tt(pt[:, sl], pt[:, sl].bitcast(F32), masks[qb][:], ALU.mult)
                    else:
                        self.stt(pt[:, sl], pt[:, sl].bitcast(F32), halo, masks[qb][:], ALU.mult, ALU.mult)

        seen = {}

        def pv(i):
            h, ki, (kf_fn, va_fn, qlo, qhi, masks, halo) = tasks[i]
            kap, half, pair, hp = hinfo(h)
            c0, o_off, d_off = self.va_slot(kap)
            ncol = (qhi - qlo + 1) * 128
            pt = PT[i % NPT][:, 0:ncol]
            for qb in range(qlo, qhi + 1):
                sl = slice((qb - qlo) * 128, (qb - qlo + 1) * 128)
                seen[(h, qb)] = seen.get((h, qb), 0) + 1
                self.mm(self.PS[2 + qb][:, 0:66], pt[:, sl], va_fn(kap)[:, c0:c0 + 66], seen[(h, qb)] == 1, seen[(h, qb)] == nk_for[qb])
            if ki == len(keysets) - 1:
                for qb in range(nqb):
                    po = self.PS[2 + qb]
                    rd = self.f(2048 + 16 * qb, [128, 1])
                    self.ts(rd, po[:, d_off:d_off + 1], self.ES[:, h:h + 1], None, ALU.add)
                    self.p.op("dve", lambda e: e.reciprocal(out=rd, in_=rd), reads=[rd], writes=[rd])
                    self.ts(OT[:, qb, h * 64:(h + 1) * 64], po[:, o_off:o_off + 64], rd, None, ALU.mult)

        LOOK = 2
        for i in range(min(LOOK, len(tasks))):
            score(i)
        for i in range(len(tasks)):
            if i + LOOK < len(tasks):
                score(i + LOOK)
            pv(i)
        return OT

    def c_outproj(self, l, OT, x0, nqb, g):
        n = nqb * 128
        OA = self.r(self.o_hb, [128, KC, 512])
        for qb in range(nqb):
            for hb in range(2):
                ps = self.PS[hb]
                for j in range(4):
                    c = hb * 4 + j
                    self.tr(ps[:, j * 128:(j + 1) * 128], OT[:, qb, c * 128:(c + 1) * 128].bitcast(F32))
                self.cp(OA[:, hb * 4:(hb + 1) * 4, qb * 128:(qb + 1) * 128], ps[:].rearrange("p (a b) -> p a b", a=4),
                        eng="act" if hb else "dve")
        wo = self.dram["w_out_c"].rearrange("(c p) o -> p c o", p=128)
        for half in range(2):
            WO = self.r(self.o_qf, [128, KC, 512])
            self.dma_r(WO, wo[:, :, half * 512:(half + 1) * 512], self.p.named_sem("woc"))
            for o in range(4):
                for c in range(KC):
                    self.mm(self.PS[4 + o][:, 0:n], WO[:, c, o * 128:(o + 1) * 128], OA[:, c, 0:n], c == 0, c == KC - 1)
            for o in range(4):
                oc = half * 4 + o
                xs_ = self.X[:, oc, x0:x0 + n]
                self.stt(xs_, self.PS[4 + o][:, 0:n], self.GSC[:, 1, oc, g:g + 1], xs_, ALU.mult, ALU.add)
        self.ln_range(l * 3 + 1, x0, n, self.o_ws)

    def mixer_c(self, l):
        d = self.dram
        self.o_kf = 0
        self.o_va = 4096
        self.o_kc = self.o_va + 16 * 262
        self.o_vca = self.o_kc + 1024
        self.o_hb = self.o_vca + 4 * 262
        self.o_ws = self.o_hb + 4096
        self.o_qf = self.o_ws + 4096
        self.o_ot = self.o_qf + 4096
        self.o_zq = self.o_ot + 4096
        assert self.o_zq + 512 <= self.NR, self.o_zq
        self.c_consts()
        KF = self.r(self.o_kf, [128, 2, 2048])
        VA = self.r(self.o_va, [128, 16, 262])
        KCF = self.r(self.o_kc, [128, 2, 512])
        VCA = self.r(self.o_vca, [128, 4, 262])
        wq = d["w_qkv"].rearrange("(k p) n -> p k n", p=128)
        WKV = self.r(self.o_ws, [128, KC, 512])

        def load_wkv():
            self.dma_r(WKV, wq[:, :, 1024:1536], self.p.named_sem("wkv"))
        for sq in range(2):
            x0 = sq * SEQ
            HB = self.c_modulate(x0, SEQ, 0)
            load_wkv()
            self.c_kv_project(HB, SEQ, WKV, lambda kp: KF[:, kp, 0:SEQ], lambda b: VA[:, b, :], 0, None, emit=(sq, 0))
            QF = self.c_q_project(HB, SEQ, None)
            keysets = []
            for kb in range(2):
                keysets.append((lambda kap, half, kb=kb: KF[64 * half:64 * half + 64, kap // 2, kb * 128:(kb + 1) * 128],
                                lambda kap, kb=kb: VA[:, kb, :], 0, 1, {}, None))
            OT = self.c_attend(QF, 2, keysets)
            self.c_outproj(l, OT, x0, 2, 0)
        stg = self.r(self.o_ot, [128, 4, 256])
        self.dma_r(stg, d["ck"].rearrange("(b p) c -> p b c", p=128), self.p.named_sem("ck"))
        for b in range(4):
            for kp in range(2):
                ps = self.PS[(b * 2 + kp) % 2][:, 0:128]
                self.tr(ps, stg[:, b, kp * 128:(kp + 1) * 128].bitcast(F32))
                self.cp(KCF[:, kp, b * 128:(b + 1) * 128], ps, eng="act" if kp else "dve")
        cvv = d["cv"].rearrange("(b p) c -> p b c", p=128)
        for kap in range(4):
            c0 = [0, 66, 130, 196][kap]
            self.dma_r(VCA[:, :, c0:c0 + 64], cvv[:, :, kap * 64:(kap + 1) * 64], self.p.named_sem("cv%d" % kap))
        for b in range(4):
            self.ts(VCA[:, b, 64:66], self.RESET[:, 1:3], 0.0, 1.0, ALU.mult, ALU.add)
            self.ts(VCA[:, b, 194:196], self.RESET[:, 1:3], 0.0, 1.0, ALU.mult, ALU.add)
        load_wkv()
        W0 = self.W0
        for (off, n_, r0) in ((0, 512, 0), (512, 256, 8)):
            HB = self.c_modulate(W0 + off, n_, 1)
            rope = self.rope_tables_w(r0, n_ // 64)
            self.c_kv_project(HB, n_, WKV, lambda kp, off=off, n_=n_: KF[:, kp, off:off + n_], lambda b: VA[:, b, :], off // 128, rope)
        HB = self.c_modulate(W0 + 128, 512, 1)
        rope = self.rope_tables_w(2, 8)
        QF = self.c_q_project(HB, 512, rope)
        keysets = []
        for cb in range(4):
            keysets.append((lambda kap, half, cb=cb: KCF[64 * half:64 * half + 64, kap // 2, cb * 128:(cb + 1) * 128],
                            lambda kap, cb=cb: VCA[:, cb, :], 0, 3, {}, None))
        for kb in range(6):
            qlo = max(1, kb - 1) - 1
            qhi = min(4, kb + 1) - 1
            masks = {}
            if 0 <= kb - 2 <= 3:
                masks[kb - 2] = self.LM1
            if 0 <= kb <= 3:
                masks[kb] = self.LM3
            halo = None
            if kb == 0:
                halo = self.SELV[:, 12:13]
            if kb == 5:
                halo = self.SELV[:, 13:14]
            keysets.append((lambda kap, half, kb=kb: KF[64 * half:64 * half + 64, kap // 2, kb * 128:(kb + 1) * 128],
                            lambda kap, kb=kb: VA[:, kb, :], qlo, qhi, masks, halo))
        OT = self.c_attend(QF, 4, keysets)
        self.c_outproj(l, OT, W0 + 128, 4, 1)

    def build(self):
        self.o_w13 = 0
        self.o_w2 = 8192
        self.o_xm = 12288
        self.o_hid = 16384
        self.o_ln = self.o_hid + 18 * 512
        self.o_sg = 0
        self.o_lnf = 1024
        self.consts()
        self.EPSLN = self.sb("EPSLN", [128, 1])
        self.memset(self.EPSLN[:], LN_EPS / (ALPHA * ALPHA))
        self.ONEC = self.sb("ONEC", [128, 1])
        self.memset(self.ONEC[:], 1.0)
        self.load_small()
        self.dma(self.SELV[:], self.dram["selv"], self.p.named_sem("selv"))
        self.load_x()
        self.W0 = TP
        full = [(t * 512, 512, 0 if t == 0 else 1) for t in range(NT)]
        win = [(0, 512, 0), (self.W0, 512, 1), (self.W0 + 512, 256, 1)]
        own = [(0, 512, 0), (self.W0 + 128, 512, 1)]
        for l in range(DEPTH):
            self.mod_vectors(l)
            for (x0, n, g) in (full if l == 0 else win):
                self.ffn_tile(l, 0, x0, n, g)
            if self.stage == 1 + 3 * l:
                break
            if l == 0:
                self.mixer_ab(l)
                self.select_window()
            else:
                self.mixer_c(l)
            if self.stage == 2 + 3 * l:
                break
            for (x0, n, g) in (win if l == 0 else own):
                self.ffn_tile(l, 1, x0, n, g)
            if self.stage == 3 + 3 * l:
                break
        self.store_x()
        self.p.finish("sp")
        return self.nc


_CACHE = {}


def get_program(stage=99):
    if stage not in _CACHE:
        _CACHE[stage] = Builder(stage).build()
    return _CACHE[stage]


def _selv(j):
    v = np.zeros((16,), np.float32)
    v[j] = 1.0
    if j > 0:
        v[4 + j - 1] = 1.0
        v[12] = 1.0
    if j < 3:
        v[8 + j + 1] = 1.0
        v[13] = 1.0
    return np.ascontiguousarray(np.broadcast_to(v, (128, 16))).astype(np.float32)


def shard_inputs(inp):
    f = lambda a: np.ascontiguousarray(np.asarray(a, dtype=np.float32))
    maps = []
    for c in range(NCORES):
        sb = c // 4
        m = {
            "xp": f(inp["x_prompt"][2 * c:2 * c + 2].reshape(TP, D)),
            "xs": f(inp["x_sample"][sb]),
            "st_h": f(inp["state_hgrn"][sb, 0]),
            "st_g": f(inp["state_gla"][sb, 0]),
            "ck": f(inp["cache_k"][sb, 0].reshape(PAST, C_KV * HD)),
            "cv": f(inp["cache_v"][sb, 0].reshape(PAST, C_KV * HD)),
            "cvec": f(np.stack([inp["c_ctx"], inp["c"][sb]], axis=0)),
            "selv": _selv(c % 4),
            "w_mod": f(inp["w_mod"]),
            "b_mod": f(inp["b_mod"]),
            "ln_g": f(inp["ln_g"].reshape(DEPTH * 3, D)),
            "ln_b": f(inp["ln_b"].reshape(DEPTH * 3, D)),
            "ffn_w1": f(inp["ffn_w1"]),
            "ffn_w3": f(inp["ffn_w3"]),
            "ffn_w2": f(inp["ffn_w2"]),
            "w_in_ab": f(inp["w_in_ab"][0]),
            "hgrn_lb": f(inp["hgrn_lb"]),
            "gate_up": f(inp["gla_gate_up"][0]),
            "gate_b": f(inp["gla_gate_b"][0]),
            "norm_a": f(inp["norm_a"]),
            "norm_b": f(inp["norm_b"]),
            "w_out_ab": f(inp["w_out_ab"][0]),
            "w_qkv": f(inp["w_qkv_c"][0]),
            "sink": f(inp["sink_c"]),
            "w_out_c": f(inp["w_out_c"][0]),
        }
        maps.append(m)
    return maps


def gather_outputs(res):
    r = res.results
    B = 16
    yp = np.concatenate([r[c]["yp"].reshape(2, SEQ, D) for c in range(NCORES)], axis=0)
    ys = np.stack([np.concatenate([r[4 * b + q]["ys"] for q in range(4)], axis=0) for b in range(2)], axis=0)
    nsh = np.concatenate([r[c]["ns_h"].reshape(2, 1, 2, A_HEADS, 128, 128) for c in range(NCORES)], axis=0)
    nsg = np.concatenate([r[c]["ns_g"].reshape(2, 1, 2, B_HEADS, B_DK, 128) for c in range(NCORES)], axis=0)
    nk = np.concatenate([r[c]["nk"].reshape(2, 1, SEQ, C_KV, HD) for c in range(NCORES)], axis=0)
    nv = np.concatenate([r[c]["nv"].reshape(2, 1, SEQ, C_KV, HD) for c in range(NCORES)], axis=0)
    return (yp.astype(np.float32), ys.astype(np.float32), nsh.astype(np.float32), nsg.astype(np.float32),
            nk.astype(np.float32), nv.astype(np.float32))


def kernel(**inputs):
    stage = int(os.environ.get("MK_STAGE", "99"))
    nc = get_program(stage)
    maps = shard_inputs(inputs)
    res = run_bass_kernel_spmd(nc, maps, core_ids=list(range(NCORES)))
    return gather_outputs(res)
```

```python
import os
import numpy as np
import concourse.bass as bass
import concourse.mybir as mybir
from concourse.bass_utils import run_bass_kernel_spmd

F32 = mybir.dt.float32
F32R = mybir.dt.float32r
AF = mybir.ActivationFunctionType
ALU = mybir.AluOpType

D = 1024
KC = 8
DFF = 2816
FC = 22
NMOD = 9
DEPTH = 2
SEQ = 256
TP = 512
TS = 2048
T = TP + TS
NT = T // 512
A_HEADS = 4
B_HEADS = 4
B_DK = 64
GATE_RANK = 16
GLA_TAU = 16.0
AB_IN = 4128
C_HEADS = 16
C_KV = 4
HD = 64
PAST = 512
ALPHA = (2.0 * DEPTH) ** 0.25
LN_EPS = 1e-5
RMS_EPS = 1e-6
ROPE_BASE = 10000.0
NCORES = 8


class Prog:
    def __init__(self, nc):
        self.nc = nc
        self.E = {"pe": nc.tensor, "act": nc.scalar, "dve": nc.vector, "pool": nc.gpsimd, "sp": nc.sync}
        self.sems = {}
        self.cnt = {}
        for e in self.E:
            self.sems[e] = nc.alloc_semaphore("s_" + e)
            self.cnt[e] = 0
        self.seen = {e: {} for e in self.E}
        self.ndma_sem = 0
        self.n_inst = 0
        self.n_wait = 0
        self.self_sync = set(os.environ.get("MK_SELFSYNC", "act,dve,pool").split(",")) - {""}
        self.named = {}
        self.mem = {}

    def named_sem(self, name):
        if name not in self.named:
            self.named[name] = self.new_dma_sem()
        return self.named[name]

    def new_dma_sem(self):
        k = "d%d" % self.ndma_sem
        self.ndma_sem += 1
        self.sems[k] = self.nc.alloc_semaphore("s_" + k)
        self.cnt[k] = 0
        return k

    @staticmethod
    def box(ap):
        esz = mybir.dt.size(ap.dtype)
        dims = ap.ap
        off = ap.offset
        name = ap.tensor.name
        if str(ap.space) == "DRAM":
            lo = off
            hi = off
            for st, cn in dims:
                d = (cn - 1) * st
                if d > 0:
                    hi += d
                else:
                    lo += d
            return name, 0, 1, lo * esz, (hi + 1) * esz
        pst, pcn = dims[0]
        if pst <= 0:
            pst = 1 << 40
        p0 = off // pst
        c0 = off % pst
        if str(ap.space) == "PSUM":
            q0 = (p0 // 32) * 32
            q1 = ((p0 + pcn + 31) // 32) * 32
            return name, q0, q1, 0, 2048
        lo = c0
        hi = c0
        for st, cn in dims[1:]:
            d = (cn - 1) * st
            if d > 0:
                hi += d
            else:
                lo += d
        return name, p0, p0 + pcn, lo * esz, (hi + 1) * esz

    def _collect(self, reads, writes):
        deps = []
        rb = [self.box(a) for a in reads]
        wb = [self.box(a) for a in writes]
        for (name, p0, p1, lo, hi) in rb:
            m = self.mem.get(name)
            if m is None:
                continue
            for r in m[0]:
                if r[0] < p1 and p0 < r[1] and r[2] < hi and lo < r[3]:
                    deps.append((r[4], r[5]))
        for (name, p0, p1, lo, hi) in wb:
            m = self.mem.get(name)
            if m is None:
                continue
            for lst in m:
                for r in lst:
                    if r[0] < p1 and p0 < r[1] and r[2] < hi and lo < r[3]:
                        deps.append((r[4], r[5]))
        return deps, rb, wb

    def _record(self, rb, wb, key, val):
        for (name, p0, p1, lo, hi) in wb:
            m = self.mem.setdefault(name, [[], []])
            for i in (0, 1):
                m[i] = [r for r in m[i] if not (p0 <= r[0] and r[1] <= p1 and lo <= r[2] and r[3] <= hi)]
            m[0].append([p0, p1, lo, hi, key, val])
        for (name, p0, p1, lo, hi) in rb:
            m = self.mem.setdefault(name, [[], []])
            m[1] = [r for r in m[1] if not (r[4] == key and p0 <= r[0] and r[1] <= p1 and lo <= r[2] and r[3] <= hi)]
            m[1].append([p0, p1, lo, hi, key, val])

    def _wait(self, e, deps):
        best = {}
        for k, v in deps:
            if best.get(k, 0) < v:
                best[k] = v
        for k, v in best.items():
            if k == e and e not in self.self_sync:
                continue
            if self.seen[e].get(k, 0) < v:
                self.E[e].wait_ge(self.sems[k], v)
                self.seen[e][k] = v
                self.n_wait += 1

    def op(self, e, fn, reads=(), writes=()):
        deps, rb, wb = self._collect(reads, writes)
        self._wait(e, deps)
        ins = fn(self.E[e])
        ins.then_inc(self.sems[e], 1)
        self.cnt[e] += 1
        self._record(rb, wb, e, self.cnt[e])
        self.n_inst += 1
        return ins

    def dma(self, q, out, in_, sem):
        deps, rb, wb = self._collect([in_], [out])
        self._wait(q, deps)
        ins = self.E[q].dma_start(out=out, in_=in_)
        ins.then_inc(self.sems[sem], 16)
        self.cnt[sem] += 16
        self._record(rb, wb, sem, self.cnt[sem])
        self.n_inst += 1
        return ins

    def finish(self, e="sp"):
        deps = []
        for name, m in self.mem.items():
            for lst in m:
                for r in lst:
                    deps.append((r[4], r[5]))
        self._wait(e, deps)


class Builder:
    def __init__(self, stage=99):
        self.stage = stage
        nc = bass.Bass("TRN2", target_bir_lowering=False)
        nc.dge_precook = False
        self.nc = nc
        self.p = Prog(nc)
        self.dram = {}
        self.decl_io()
        self.alloc()

    def din(self, name, shape):
        self.dram[name] = self.nc.dram_tensor(name, list(shape), F32, kind="ExternalInput").ap()
        return self.dram[name]

    def dout(self, name, shape):
        self.dram[name] = self.nc.dram_tensor(name, list(shape), F32, kind="ExternalOutput").ap()
        return self.dram[name]

    def decl_io(self):
        self.din("xp", [TP, D])
        self.din("xs", [TS, D])
        self.din("st_h", [2, A_HEADS, 128, 128])
        self.din("st_g", [2, B_HEADS, B_DK, 128])
        self.din("ck", [PAST, C_KV * HD])
        self.din("cv", [PAST, C_KV * HD])
        self.din("cvec", [2, D])
        self.din("selv", [128, 16])
        self.din("w_mod", [DEPTH, D, NMOD * D])
        self.din("b_mod", [DEPTH, NMOD * D])
        self.din("ln_g", [DEPTH * 3, D])
        self.din("ln_b", [DEPTH * 3, D])
        self.din("ffn_w1", [DEPTH, 2, D, DFF])
        self.din("ffn_w3", [DEPTH, 2, D, DFF])
        self.din("ffn_w2", [DEPTH, 2, DFF, D])
        self.din("w_in_ab", [D, AB_IN])
        self.din("hgrn_lb", [2, 2, 512])
        self.din("gate_up", [2, GATE_RANK, 256])
        self.din("gate_b", [2, 256])
        self.din("norm_a", [1, 128])
        self.din("norm_b", [1, 128])
        self.din("w_out_ab", [D, D])
        self.din("w_qkv", [D, 1536])
        self.din("sink", [1, C_HEADS])
        self.din("w_out_c", [D, D])
        self.dout("yp", [TP, D])
        self.dout("ys", [512, D])
        self.dout("ns_h", [2, 2, A_HEADS, 128, 128])
        self.dout("ns_g", [2, 2, B_HEADS, B_DK, 128])
        self.dout("nk", [2, SEQ, C_KV * HD])
        self.dout("nv", [2, SEQ, C_KV * HD])

    def sb(self, name, shape, dt=F32):
        return self.nc.alloc_sbuf_tensor(name, list(shape), dt)

    def alloc(self):
        nc = self.nc
        self.X = self.sb("X", [128, KC, T])
        self.IDENT = self.sb("IDENT", [128, 128])
        self.ONES = self.sb("ONES", [128, 128], F32R)
        self.U1 = self.sb("U1", [128, 256])
        self.U2 = self.sb("U2", [128, 256], F32R)
        self.MF = self.U1[:, 0:128]
        self.MB = self.U1[:, 128:256]
        self.LM1 = self.U1[:, 0:128]
        self.LM3 = self.U1[:, 128:256]
        self.RESET = self.sb("RESET", [128, 256])
        self.PARS = self.sb("PARS", [128, 22])
        self.LB = self.sb("LB", [128, 8])
        self.OML = self.sb("OML", [128, 8])
        self.OMLH = self.sb("OMLH", [128, 8])
        self.LBH = self.sb("LBH", [128, 8])
        self.SELV = self.sb("SELV", [128, 16])
        self.TRW = self.sb("TRW", [128, 2, 12])
        self.NGB = self.sb("NGB", [128, 4])
        self.GUP = self.U2[:, 0:256]
        self.PERMR = self.U2[:, 0:128]
        self.EPSR = self.sb("EPSR", [128, 1])
        self.ES = self.sb("ES", [128, 16])
        self.TCOS = self.sb("TCOS", [128, 64])
        self.TSIN = self.sb("TSIN", [128, 64])
        self.MR = self.sb("MR", [128, 1])
        self.MC = self.sb("MC", [128, 1])
        self.SGN = self.sb("SGN", [128, 1])
        self.MODV = self.sb("MODV", [128, 72, 2])
        self.OPS = self.sb("OPS", [128, 3, KC, 2])
        self.GSC = self.sb("GSC", [128, 3, KC, 2])
        self.BM = self.sb("BM", [128, 72])
        self.LNG = self.sb("LNG", [128, 48])
        self.LNB = self.sb("LNB", [128, 48])
        self.CS = self.sb("CS", [128, 2, KC], F32R)
        self.NR = 27 * 1024
        self.NF = 3072
        self.R = self.sb("R", [128, self.NR], F32R)
        self.Fm = self.sb("Fm", [128, self.NF])
        self.PS = [nc.alloc_psum_tensor("PS%d" % i, [128, 512], F32) for i in range(8)]
        self.sem_w13 = [self.p.new_dma_sem() for _ in range(2)]
        self.sem_w2 = [self.p.new_dma_sem() for _ in range(4)]
        self.sem_wab = [self.p.new_dma_sem() for _ in range(7)]
        self.sem_st = self.p.new_dma_sem()
        self.sem_io = [self.p.new_dma_sem() for _ in range(2)]
        self.sem_misc = self.p.new_dma_sem()
        self.sem_out = self.p.new_dma_sem()
        self.n13 = 0
        self.n2 = 0
        self.nio = 0

    @staticmethod
    def _view(base, off, shape, total):
        n = 1
        for x in shape[1:]:
            n *= x
        assert off + n <= total, (off, n, total)
        v = base[0:shape[0], off:off + n]
        if len(shape) == 3:
            v = v.rearrange("p (a b) -> p a b", a=shape[1])
        elif len(shape) == 4:
            v = v.rearrange("p (a b c) -> p a b c", a=shape[1], b=shape[2])
        return v

    def r(self, off, shape):
        return self._view(self.R, off, shape, self.NR)

    def f(self, off, shape):
        return self._view(self.Fm, off, shape, self.NF)

    def mm(self, out, lhsT, rhs, start, stop):
        self.p.op("pe", lambda e: e.matmul(out, lhsT=lhsT, rhs=rhs, start=start, stop=stop),
                  reads=[lhsT, rhs], writes=[out])

    def tr(self, out, in_, n=128):
        ident = self.IDENT[0:in_.shape[0], 0:in_.shape[0]]
        self.p.op("pe", lambda e: e.transpose(out=out, in_=in_, identity=ident), reads=[in_, ident], writes=[out])

    def act(self, out, in_, func, bias=None, scale=None, eng="act"):
        kw = {}
        rd = [in_]
        if bias is not None:
            kw["bias"] = bias
            if not isinstance(bias, (int, float)):
                rd.append(bias)
        if scale is not None:
            kw["scale"] = scale
            if not isinstance(scale, (int, float)):
                rd.append(scale)
        self.p.op("act", lambda e: e.activation(out=out, in_=in_, func=func, **kw), reads=rd, writes=[out])

    def ts(self, out, in0, s1, s2, op0, op1=None, eng="dve"):
        rd = [in0]
        for s in (s1, s2):
            if s is not None and not isinstance(s, (int, float)):
                rd.append(s)
        if op1 is None:
            self.p.op(eng, lambda e: e.tensor_scalar(out=out, in0=in0, scalar1=s1, scalar2=None, op0=op0), reads=rd, writes=[out])
        else:
            self.p.op(eng, lambda e: e.tensor_scalar(out=out, in0=in0, scalar1=s1, scalar2=s2, op0=op0, op1=op1), reads=rd, writes=[out])

    def tt(self, out, in0, in1, op, eng="dve"):
        self.p.op(eng, lambda e: e.tensor_tensor(out=out, in0=in0, in1=in1, op=op), reads=[in0, in1], writes=[out])

    def stt(self, out, in0, scalar, in1, op0, op1, eng="dve"):
        rd = [in0, in1]
        if not isinstance(scalar, (int, float)):
            rd.append(scalar)
        self.p.op(eng, lambda e: e.scalar_tensor_tensor(out=out, in0=in0, scalar=scalar, in1=in1, op0=op0, op1=op1), reads=rd, writes=[out])

    def cp(self, out, in_, eng="dve"):
        if eng == "act":
            self.act(out, in_, AF.Copy)
        else:
            self.p.op(eng, lambda e: e.tensor_copy(out=out, in_=in_), reads=[in_], writes=[out])

    def memset(self, ap, val, eng="pool"):
        self.p.op(eng, lambda e: e.memset(ap, val), writes=[ap])

    def dma(self, out, in_, sem, q="sp"):
        self.p.dma(q, out, in_, sem)

    def dma_r(self, out, in_, sem, q="sp"):
        self.p.dma(q, out if out.dtype == F32R else out.bitcast(F32R), in_.bitcast(F32R), sem)

    def consts(self):
        self.memset(self.IDENT[:], 1.0)
        self.p.op("pool", lambda e: e.affine_select(out=self.IDENT[:], in_=self.IDENT[:], pattern=[[-1, 128]],
                                                    compare_op=ALU.is_equal, fill=0.0, base=0, channel_multiplier=1),
                  reads=[self.IDENT[:]], writes=[self.IDENT[:]])
        tmp = self.f(0, [128, 128])
        self.memset(tmp, 1.0)
        self.cp(self.ONES[:], tmp)
        for (M, cm, st) in ((self.MF, -1, 1), (self.MB, 1, -1)):
            self.memset(M[:], 1.0)
            self.p.op("pool", lambda e: e.affine_select(out=M[:], in_=M[:], pattern=[[st, 128]], compare_op=ALU.is_ge,
                                                        fill=0.0, base=0, channel_multiplier=cm),
                      reads=[M[:]], writes=[M[:]])
        self.memset(self.MF[0:64, 64:128], 0.0)
        self.memset(self.MB[64:128, 0:64], 0.0)
        self.memset(self.RESET[:], 0.0)
        self.memset(self.RESET[:, 0:256:64], 1.0)
        self.memset(self.EPSR[:], RMS_EPS)

    def load_fm(self, dst, src_rows, nrows):
        st = self.f(0, [128, 128])
        self.dma(st[0:nrows, :], src_rows, self.sem_misc)
        ps = self.PS[7][:, 0:nrows]
        self.tr(ps, st[0:nrows, :])
        self.cp(dst, ps)

    def load_small(self):
        self.load_fm(self.LNG[:], self.dram["ln_g"].rearrange("r (k p) -> (r k) p", p=128), 48)
        self.load_fm(self.LNB[:], self.dram["ln_b"].rearrange("r (k p) -> (r k) p", p=128), 48)
        st = self.f(0, [128, 128])
        self.dma(st[0:16, :], self.dram["cvec"].rearrange("g (k p) -> (g k) p", p=128), self.sem_misc)
        ps = self.PS[7][:, 0:16]
        self.tr(ps, st[0:16, :])
        self.act(self.CS[:].rearrange("p g k -> p (g k)"), ps, AF.Silu)

    def load_x(self):
        for tb in range(T // 128):
            src = self.dram["xp"][tb * 128:(tb + 1) * 128, :] if tb < TP // 128 else \
                self.dram["xs"][tb * 128 - TP:(tb + 1) * 128 - TP, :]
            s = self.nio % 2
            self.nio += 1
            st = self.f(s * 1024, [128, 1024])
            self.dma(st, src, self.sem_io[s])
            for hb in range(2):
                ps = self.PS[(tb * 2 + hb) % 4]
                for j in range(4):
                    k = hb * 4 + j
                    self.tr(ps[:, j * 128:(j + 1) * 128], st[:, k * 128:(k + 1) * 128])
                dst = self.X[:, hb * 4:(hb + 1) * 4, tb * 128:(tb + 1) * 128]
                self.cp(dst, ps[:].rearrange("p (a b) -> p a b", a=4), eng="dve" if hb == 0 else "act")

    def store_x(self):
        for ob in range(8):
            if ob < 4:
                dst = self.dram["yp"][ob * 128:(ob + 1) * 128, :]
                xc = ob * 128
            else:
                dst = self.dram["ys"][(ob - 4) * 128:(ob - 3) * 128, :]
                xc = self.W0 + 128 + (ob - 4) * 128
            s = self.nio % 2
            self.nio += 1
            st = self.f(s * 1024, [128, 1024])
            for hb in range(2):
                ps = self.PS[(ob * 2 + hb) % 4]
                for j in range(4):
                    k = hb * 4 + j
                    self.tr(ps[:, j * 128:(j + 1) * 128], self.X[:, k, xc:xc + 128])
                self.cp(st[:, hb * 512:(hb + 1) * 512], ps[:], eng="dve" if hb == 0 else "act")
            self.dma(dst, st, self.sem_io[s])

    def select_window(self):
        SV = self.SELV
        for k in range(KC):
            own = self.f(0, [128, 512])
            hp = self.f(512, [128, 128])
            hn = self.f(640, [128, 128])
            for t in range(4):
                xt = self.X[:, k, TP + t * 512:TP + (t + 1) * 512]
                if t == 0:
                    self.ts(own, xt, SV[:, t:t + 1], None, ALU.mult)
                    self.ts(hp, xt[:, 384:512], SV[:, 4 + t:5 + t], None, ALU.mult)
                    self.ts(hn, xt[:, 0:128], SV[:, 8 + t:9 + t], None, ALU.mult)
                else:
                    self.stt(own, xt, SV[:, t:t + 1], own, ALU.mult, ALU.add)
                    self.stt(hp, xt[:, 384:512], SV[:, 4 + t:5 + t], hp, ALU.mult, ALU.add)
                    self.stt(hn, xt[:, 0:128], SV[:, 8 + t:9 + t], hn, ALU.mult, ALU.add)
            self.cp(self.X[:, k, self.W0:self.W0 + 128], hp, eng="act")
            self.cp(self.X[:, k, self.W0 + 128:self.W0 + 640], own, eng="act")
            self.cp(self.X[:, k, self.W0 + 640:self.W0 + 768], hn, eng="act")

    def mod_vectors(self, l):
        self.load_fm(self.BM[:], self.dram["b_mod"][l].rearrange("(r p) -> r p", p=128), 72)
        wm = self.dram["w_mod"][l].rearrange("(k p) n -> p k n", p=128)
        pm = self.PS[6][:, 0:144].rearrange("p (c g) -> p c g", g=2)
        for blk in range(18):
            s = self.n13 % 2
            self.n13 += 1
            wt = self.r(self.o_w13 + s * 4096, [128, KC, 512])
            self.dma_r(wt, wm[:, :, blk * 512:(blk + 1) * 512], self.sem_w13[s])
            for q in range(4):
                oc = blk * 4 + q
                for k in range(KC):
                    self.mm(pm[:, oc, :], wt[:, k, q * 128:(q + 1) * 128], self.CS[:, :, k], k == 0, k == KC - 1)
        for g in range(2):
            self.tt(self.MODV[:, :, g], pm[:, :, g], self.BM[:], ALU.add)
        gmul = [0.5 / ALPHA, 1.0 / ALPHA, 0.5 / ALPHA]
        for s in range(3):
            self.ts(self.OPS[:, s, :, :], self.MODV[:, (3 * s + 1) * 8:(3 * s + 2) * 8, :], 1.0, None, ALU.add)
            self.ts(self.GSC[:, s, :, :], self.MODV[:, (3 * s + 2) * 8:(3 * s + 3) * 8, :], gmul[s], None, ALU.mult)

    def shift(self, s, k, g):
        return self.MODV[:, 3 * s * 8 + k, g:g + 1]

    def ln_range(self, lnidx, x0, n, o_r):
        ts_ = slice(x0, x0 + n)
        pa, pb = self.PS[4][:, 0:n], self.PS[5][:, 0:n]
        for k in range(KC):
            zr = self.r(o_r + (k % 2) * 512, [128, 512])[:, 0:n]
            sq = self.r(o_r + 1024 + (k % 2) * 512, [128, 512])[:, 0:n]
            self.act(zr, self.X[:, k, ts_], AF.Copy, scale=1.0 / 1024.0)
            self.act(sq, self.X[:, k, ts_], AF.Square, scale=1.0 / 32.0)
            self.mm(pa, self.ONES[:], zr, k == 0, k == KC - 1)
            self.mm(pb, self.ONES[:], sq, k == 0, k == KC - 1)
        m2 = self.f(self.o_lnf, [128, 512])[:, 0:n]
        self.act(m2, pa, AF.Square)
        self.tt(m2, pb, m2, ALU.subtract)
        self.act(m2, m2, AF.Sqrt, bias=self.EPSLN[:, 0:1])
        self.p.op("dve", lambda e: e.reciprocal(out=m2, in_=m2), reads=[m2], writes=[m2])
        for k in range(KC):
            xk = self.X[:, k, ts_]
            self.tt(xk, xk, pa, ALU.subtract)
            self.stt(xk, xk, self.LNG[:, lnidx * 8 + k:lnidx * 8 + k + 1], m2, ALU.mult, ALU.mult)
            self.act(xk, xk, AF.Identity, bias=self.LNB[:, lnidx * 8 + k:lnidx * 8 + k + 1])

    def ffn_tile(self, l, j, x0, n, g):
        s = 0 if j == 0 else 2
        ts_ = slice(x0, x0 + n)
        XM = self.r(self.o_xm, [128, KC, 512])[:, :, 0:n]
        HID = self.r(self.o_hid, [128, FC, 512])[:, :, 0:n]
        for k in range(KC):
            self.ts(XM[:, k, :], self.X[:, k, ts_], self.OPS[:, s, k, g:g + 1], self.shift(s, k, g), ALU.mult, ALU.add)
        w1 = self.dram["ffn_w1"][l, j].rearrange("(k p) f -> p k f", p=128)
        w3 = self.dram["ffn_w3"][l, j].rearrange("(k p) f -> p k f", p=128)
        w2 = self.dram["ffn_w2"][l, j].rearrange("(c p) o -> p c o", p=128)
        for fb in range(FC // 2):
            sl = self.n13 % 2
            self.n13 += 1
            wt = self.r(self.o_w13 + sl * 4096, [128, 2, KC, 256])
            self.dma_r(wt[:, 0], w1[:, :, fb * 256:(fb + 1) * 256], self.sem_w13[sl])
            self.dma_r(wt[:, 1], w3[:, :, fb * 256:(fb + 1) * 256], self.p.named_sem("w3_%d" % sl))
            for c in range(2):
                f = 2 * fb + c
                p1, p3 = self.PS[f % 2][:, 0:n], self.PS[2 + f % 2][:, 0:n]
                for k in range(KC):
                    self.mm(p1, wt[:, 0, k, c * 128:(c + 1) * 128], XM[:, k, :], k == 0, k == KC - 1)
                for k in range(KC):
                    self.mm(p3, wt[:, 1, k, c * 128:(c + 1) * 128], XM[:, k, :], k == 0, k == KC - 1)
                sg = self.f(self.o_sg + (f % 2) * 512, [128, 512])[:, 0:n]
                self.act(sg, p1, AF.Silu)
                self.tt(HID[:, f, :], sg, p3, ALU.mult)
        for half in range(2):
            for fb in range(FC // 2):
                sl = self.n2 % 4
                self.n2 += 1
                wt = self.r(self.o_w2 + sl * 1024, [128, 2, 512])
                self.dma_r(wt, w2[:, 2 * fb:2 * fb + 2, half * 512:(half + 1) * 512], self.sem_w2[sl])
                for c in range(2):
                    f = 2 * fb + c
                    for o in range(4):
                        self.mm(self.PS[4 + o][:, 0:n], wt[:, c, o * 128:(o + 1) * 128], HID[:, f, :], f == 0, f == FC - 1)
            for o in range(4):
                oc = half * 4 + o
                self.stt(self.X[:, oc, ts_], self.PS[4 + o][:, 0:n], self.GSC[:, s, oc, g:g + 1], self.X[:, oc, ts_], ALU.mult, ALU.add)
        self.ln_range(l * 3 + s, x0, n, self.o_ln)

    def ab_params(self):
        st = self.f(0, [128, 128])
        d = self.dram
        self.dma(st[0:1, :], d["norm_a"], self.sem_misc)
        self.dma(st[1:2, :], d["norm_b"], self.sem_misc)
        self.dma(st[2:6, :], d["gate_b"].rearrange("a (j p) -> (a j) p", p=128), self.sem_misc)
        self.dma(st[6:22, :], d["hgrn_lb"].rearrange("a b (h p) -> (a b h) p", p=128), self.sem_misc)
        ps = self.PS[7][:, 0:22]
        self.tr(ps, st[0:22, :])
        self.cp(self.PARS[:], ps)
        P = self.PARS
        for dr in range(2):
            a = P[:, 6 + dr * 8:6 + dr * 8 + 4]
            b = P[:, 6 + dr * 8 + 4:6 + dr * 8 + 8]
            self.tt(self.LB[:, dr * 4:dr * 4 + 4], a, b, ALU.subtract)
        self.act(self.LB[:], self.LB[:], AF.Sigmoid)
        self.ts(self.OML[:], self.LB[:], -1.0, 1.0, ALU.mult, ALU.add)
        self.ts(self.OMLH[:], self.OML[:], 0.5, None, ALU.mult)
        self.tt(self.LBH[:], self.LB[:], self.OMLH[:], ALU.add)
        self.ts(self.NGB[:], P[:, 2:6], -1.0, None, ALU.mult)

    def ab_unit_pass(self, u, dr, segs, g):
        hg = u < 4
        j = u - 4
        d = self.dram
        win = d["w_in_ab"].rearrange("(k p) n -> p k n", p=128)
        if hg:
            cols = [("q", u * 128), ("f", (1024 if dr == 0 else 1536) + u * 128), ("i", 512 + u * 128)]
            if dr == 1:
                cols.append(("g0", 2048 + u * 128))
        else:
            cols = [("q", 2560 + j * 128), ("f", 2816 + j * 128), ("v0", 3072 + j * 256), ("v1", 3072 + j * 256 + 128),
                    ("bz", 4000)]
            if dr == 1:
                cols += [("g0", 3584 + j * 256), ("g1", 3584 + j * 256 + 128)]
        W = {}
        for i, (nm, c0) in enumerate(cols):
            sl = (self.wab_base + i) % 7
            W[nm] = self.r(self.o_wab + sl * 1024, [128, KC, 128])
            self.dma_r(W[nm], win[:, :, c0:c0 + 128], self.sem_wab[sl])
        self.wab_base += len(cols)
        nh = 1 if hg else 2
        heads = list(range(nh))
        vw = 128 * nh
        NSB = 6
        SB = [self.r(self.o_sb + i * 128, [128, 128]) for i in range(NSB)]
        SC = [self.f(1536, [128, 128]), self.f(1664, [128, 128])]
        st = {"si": 0, "sc": 0}
        if not hg:
            self.ts(self.GUP[64:128, :], self.RESET[64:128, :], 0.0, None, ALU.mult)
            self.dma_r(self.GUP[96 + 16 * dr:112 + 16 * dr, :], d["gate_up"][dr], self.p.named_sem("gup"))
        st["started"] = False

        def seg_init(seg):
            (sx0, snht, slt0, ssample, ssidx) = seg
            if st["started"]:
                if dr == 0:
                    st["si"] += 1
                else:
                    st["sc"] += 1
            st["started"] = True
            src0 = None
            if ssample:
                src0 = d["st_h"][dr, u] if hg else d["st_g"][dr, 2 * j:2 * j + 2].rearrange("h d e -> (h d) e")
            if dr == 0:
                buf = SB[st["si"] % NSB]
                if ssample:
                    self.dma_r(buf, src0, self.p.named_sem("stf%d" % (st["si"] % NSB)))
                else:
                    self.ts(buf, self.IDENT[:], 0.0, None, ALU.mult)
            else:
                buf = SC[st["sc"] % 2]
                if ssample:
                    self.dma(buf, src0, self.p.named_sem("stb%d" % (st["sc"] % 2)))
                else:
                    self.memset(buf, 0.0)

        def seg_emit(seg):
            (sx0, snht, slt0, ssample, ssidx) = seg
            if ssample:
                return
            if dr == 0:
                S_fin = SB[st["si"] % NSB].bitcast(F32)
                sname = "sout%d" % (st["si"] % NSB)
            else:
                S_fin = SC[st["sc"] % 2]
                sname = "soutc%d" % (st["sc"] % 2)
            if hg:
                self.dma(d["ns_h"][ssidx, dr, u], S_fin, self.p.named_sem(sname))
            else:
                self.dma(d["ns_g"][ssidx, dr, 2 * j:2 * j + 2].rearrange("h d e -> (h d) e"), S_fin, self.p.named_sem(sname))

        HB = self.r(self.o_hb, [128, KC, 256])
        QD = self.r(self.o_qd, [128, 256])
        KI = self.r(self.o_ki, [128, 256])
        VT = self.r(self.o_vt, [128, 2, 256])
        AT = [self.r(self.o_at + i * 128, [128, 128]) for i in range(2)]
        BZ = self.r(self.o_at, [128, 256])
        SQ = self.r(self.o_at, [128, 256])
        KIT = self.r(self.o_hb + 256, [128, 2, 128])
        F0 = self.f(0, [128, 256])
        F1 = self.f(256, [128, 256])
        F2 = self.f(512, [128, 256])
        F3 = self.f(1792, [128, 256])
        TMP = self.f(768, [128, 128])
        AC = self.f(896, [128, 4])
        GS = [self.f(1024, [128, 256]), self.f(1280, [128, 256])]
        PS = self.PS
        psq, psf = PS[0][:, 0:256], PS[0][:, 256:512]
        MASK = self.MF if dr == 0 else self.MB
        order = []
        for seg in segs:
            (sx0, snht, slt0, ssample, ssidx) = seg
            hts = list(range(snht)) if dr == 0 else list(range(snht - 1, -1, -1))
            for n2, h_ in enumerate(hts):
                order.append({"x0": sx0 + h_ * 256, "lt0": slt0 + h_ * 256, "seg": seg, "first": n2 == 0, "last": n2 == snht - 1})
        bs = [0, 1] if dr == 0 else [1, 0]
        cseq = [(b, c) for b in bs for c in bs]

        def pr_(hh):
            return slice(0, 128) if hg else slice(64 * hh, 64 * hh + 64)

        def stage_a(hti, part=3):
            if part & 1:
                stage_a1(hti)
            if part & 2:
                stage_a2()

        def stage_a1(hti):
            t0 = hti["x0"]
            for k in range(KC):
                if k % 2 == 0:
                    self.act(HB[:, k, :], self.X[:, k, t0:t0 + 256], AF.Identity, bias=self.shift(1, k, g), scale=self.OPS[:, 1, k, g:g + 1])
                else:
                    self.ts(HB[:, k, :], self.X[:, k, t0:t0 + 256], self.OPS[:, 1, k, g:g + 1], self.shift(1, k, g), ALU.mult, ALU.add)
            for k in range(KC):
                self.mm(psq, W["q"][:, k, :], HB[:, k, :], k == 0, k == KC - 1)
            for k in range(KC):
                self.mm(psf, W["f"][:, k, :], HB[:, k, :], k == 0, k == KC - 1)
            if not hg:
                for k in range(KC):
                    self.mm(PS[3][:, 0:256], W["bz"][:, k, :], HB[:, k, :], k == 0, k == KC - 1)

        def stage_a2():
            for b in range(2):
                for vv in range(nh):
                    wv = W["i"] if hg else W["v%d" % vv]
                    for k in range(KC):
                        self.mm(PS[1][:, b * 256 + vv * 128:b * 256 + vv * 128 + 128], HB[:, k, b * 128:(b + 1) * 128], wv[:, k, :],
                                k == 0, k == KC - 1)
            if dr == 1:
                for hh in heads:
                    for k in range(KC):
                        self.mm(PS[2][:, hh * 256:(hh + 1) * 256], W["g%d" % hh][:, k, :], HB[:, k, :], k == 0, k == KC - 1)

        def stage_b(gsi=0):
            if hg:
                self.act(F0, psf, AF.Tanh, scale=0.5)
                self.ts(F0, F0, self.OMLH[:, dr * 4 + u:dr * 4 + u + 1], self.LBH[:, dr * 4 + u:dr * 4 + u + 1], ALU.mult, ALU.add)
                self.ts(F1, F0, -1.0, 1.0, ALU.mult, ALU.add, eng="pool")
                kf = F1
            else:
                self.cp(BZ, PS[3][:, 0:256], eng="act")
                psl = PS[3][:, 256:512]
                self.mm(psl, self.GUP[64:128, j * 128:(j + 1) * 128], BZ[64:128, :], True, True)
                self.act(F0, psl, AF.Exp, scale=-1.0, bias=self.NGB[:, dr * 2 + j:dr * 2 + j + 1])
                self.act(F0, F0, AF.Ln, bias=self.ONEC[:, 0:1])
                self.act(F0, F0, AF.Exp, scale=-1.0 / GLA_TAU)
                kf = psf
            qs = None if hg else B_DK ** -0.5
            R1 = self.RESET[:, 0:256]
            if dr == 0:
                self.p.op("dve", lambda e: e.tensor_tensor_scan(out=F2, data0=R1, data1=F0, initial=1.0, op0=ALU.max, op1=ALU.mult),
                          reads=[R1, F0], writes=[F2])
                self.cp(AC, F2[:, 63:256:64])
                if hg:
                    self.tt(QD, psq, F2, ALU.mult)
                else:
                    self.stt(QD, psq, qs, F2, ALU.mult, ALU.mult)
                self.p.op("dve", lambda e: e.reciprocal(out=F2, in_=F2), reads=[F2], writes=[F2])
                self.tt(KI, kf, F2, ALU.mult)
            else:
                self.cp(F2[:, 1:256], F0[:, 0:255])
                self.memset(F2[:, 0:256:64], 1.0)
                self.p.op("dve", lambda e: e.tensor_tensor_scan(out=F3, data0=R1, data1=F2, initial=1.0, op0=ALU.max, op1=ALU.mult),
                          reads=[R1, F2], writes=[F3])
                self.tt(AC, F3[:, 63:256:64], F0[:, 63:256:64], ALU.mult)
                self.tt(KI, kf, F3, ALU.mult)
                self.p.op("dve", lambda e: e.reciprocal(out=F3, in_=F3), reads=[F3], writes=[F3])
                if hg:
                    self.tt(QD, psq, F3, ALU.mult)
                else:
                    self.stt(QD, psq, qs, F3, ALU.mult, ALU.mult)
            for b in range(2):
                if hg:
                    self.act(VT[:, b, 0:128], PS[1][:, b * 256:b * 256 + 128], AF.Silu)
                else:
                    self.cp(VT[:, b, :], PS[1][:, b * 256:(b + 1) * 256], eng="act")
            if dr == 1:
                for hh in heads:
                    self.act(GS[(hh + gsi) % 2], PS[2][:, hh * 256:(hh + 1) * 256], AF.Silu)

        def stage_c(PSO):
            pskv = []
            for idx, (b_, c_) in enumerate(cseq):
                bank = (PS[6] if c_ == 0 else PS[3]) if hg else (PS[6] if c_ == 0 else PS[1])
                w_ = 128 if hg else 256
                slot = 0 if idx < 2 else 1
                koff = 256 if (hg and c_ == 1) else 0
                pskv.append(bank[:, koff + slot * w_:koff + (slot + 1) * w_])
            first = [True, True]
            for hi, hh in enumerate(heads):
                pat = PS[5] if hi == 0 else PS[2 if dr == 0 else 3]
                pat_off = 0 if (hi == 0 or dr == 0) else 256
                for b in bs:
                    self.mm(pat[:, pat_off + b * 128:pat_off + (b + 1) * 128], KI[pr_(hh), b * 128:(b + 1) * 128],
                            QD[pr_(hh), b * 128:(b + 1) * 128], True, True)
                if hi == 0:
                    for b in bs:
                        self.tr(PS[3][:, b * 128:(b + 1) * 128], KI[:, b * 128:(b + 1) * 128].bitcast(F32))
                for b in bs:
                    self.tt(AT[b], pat[:, pat_off + b * 128:pat_off + (b + 1) * 128], MASK[:], ALU.mult)
                if hi == 0:
                    for b in bs:
                        self.cp(KIT[:, b, :], PS[3][:, b * 128:(b + 1) * 128], eng="act")
                    for idx, (b, c) in enumerate(cseq):
                        self.mm(pskv[idx], KIT[c * 64:(c + 1) * 64, b, :], VT[c * 64:(c + 1) * 64, b, 0:vw], True, True)
                for b in bs:
                    self.mm(PSO[hh][:, b * 128:(b + 1) * 128], VT[:, b, hh * 128:(hh + 1) * 128], AT[b], first[hh], False)
                    first[hh] = False
            return pskv

        def stage_d(pskv, item):
            if item["first"]:
                seg_init(item["seg"])
            states = []
            for idx, (b, c) in enumerate(cseq):
                ci = b * 2 + c
                a_c = AC[:, ci:ci + 1]
                if dr == 0:
                    S_cur = SB[st["si"] % NSB]
                    S_next = SB[(st["si"] + 1) % NSB]
                    states.append(S_cur)
                    if hg:
                        self.tt(TMP, pskv[idx], S_cur.bitcast(F32), ALU.add)
                    else:
                        self.tt(TMP[0:64, :], pskv[idx][0:64, 0:128], S_cur[0:64, :].bitcast(F32), ALU.add)
                        self.tt(TMP[64:128, :], pskv[idx][64:128, 128:256], S_cur[64:128, :].bitcast(F32), ALU.add)
                    self.ts(S_next, TMP, a_c, None, ALU.mult)
                    st["si"] += 1
                else:
                    C_cur = SC[st["sc"] % 2]
                    C_next = SC[(st["sc"] + 1) % 2]
                    S_sc = SB[st["si"] % NSB]
                    states.append(S_sc)
                    self.ts(S_sc, C_cur, a_c, None, ALU.mult)
                    if hg:
                        self.tt(C_next, pskv[idx], S_sc.bitcast(F32), ALU.add)
                    else:
                        self.tt(C_next[0:64, :], pskv[idx][0:64, 0:128], S_sc[0:64, :].bitcast(F32), ALU.add)
                        self.tt(C_next[64:128, :], pskv[idx][64:128, 128:256], S_sc[64:128, :].bitcast(F32), ALU.add)
                    st["si"] += 1
                    st["sc"] += 1
            if item["last"]:
                seg_emit(item["seg"])
            return states

        def stage_d2(states, PSO):
            for idx, (b, c) in enumerate(cseq):
                ccols = slice(b * 128 + c * 64, b * 128 + c * 64 + 64)
                for hh in heads:
                    self.mm(PSO[hh][:, ccols], states[idx][pr_(hh), :], QD[pr_(hh), ccols], False, idx == 3)

        E1 = self.f(2560, [128, 256])
        E2 = self.f(2816, [128, 256])

        def stage_e(hti, PSO, gsi, pst):
            lt0 = hti["lt0"]
            for hh in heads:
                head = u if hg else 4 + 2 * j + hh
                on = self.r(self.o_on + head * self.on_stride + lt0, [128, 256])
                pso = PSO[hh][:, 0:256]
                if dr == 0:
                    self.cp(on, pso, eng="act")
                else:
                    self.tt(E1, pso, on.bitcast(F32), ALU.add)
                    self.act(SQ, E1, AF.Square, scale=128.0 ** -0.5)
                    self.mm(pst, self.ONES[:], SQ, True, True)
                    self.act(E2, pst, AF.Sqrt, bias=self.EPSR[:, 0:1])
                    self.p.op("dve", lambda e: e.reciprocal(out=E2, in_=E2), reads=[E2], writes=[E2])
                    nw = self.PARS[:, 0:1] if hg else self.PARS[:, 1:2]
                    self.stt(E1, E1, nw, E2, ALU.mult, ALU.mult)
                    self.tt(on, E1, GS[(hh + gsi) % 2], ALU.mult)

        if hg:
            stage_a(order[0])
            prev = None
            for n_, hti in enumerate(order):
                PSOi = [PS[4] if n_ % 2 == 0 else PS[7]]
                stage_b(n_ % 2)
                if prev is not None:
                    stage_e(*prev)
                pskv = stage_c(PSOi)
                if n_ + 1 < len(order):
                    stage_a(order[n_ + 1], 1)
                states = stage_d(pskv, hti)
                stage_d2(states, PSOi)
                if n_ + 1 < len(order):
                    stage_a(order[n_ + 1], 2)
                prev = (hti, PSOi, n_ % 2, PS[2][:, 0:256])
            stage_e(*prev)
        else:
            PSOg = [PS[4], PS[7]]
            stage_a(order[0])
            for n_, hti in enumerate(order):
                stage_b(0)
                pskv = stage_c(PSOg)
                if n_ + 1 < len(order):
                    stage_a(order[n_ + 1], 1)
                states = stage_d(pskv, hti)
                stage_d2(states, PSOg)
                stage_e(hti, PSOg, 0, PS[5][:, 0:256])
                if n_ + 1 < len(order):
                    stage_a(order[n_ + 1], 2)

    def ab_outproj(self, l, x0, lt0, n, g):
        wo = self.dram["w_out_ab"].rearrange("(h p) o -> p h o", p=128)
        for half in range(2):
            WO = self.r(self.o_wab, [128, 8, 512])
            self.dma_r(WO, wo[:, :, half * 512:(half + 1) * 512], self.sem_wab[0])
            for o in range(4):
                for h in range(8):
                    on = self.r(self.o_on + h * self.on_stride + lt0, [128, 512])[:, 0:n]
                    self.mm(self.PS[o][:, 0:n], WO[:, h, o * 128:(o + 1) * 128], on, h == 0, h == 7)
            for o in range(4):
                oc = half * 4 + o
                xs_ = self.X[:, oc, x0:x0 + n]
                self.stt(xs_, self.PS[o][:, 0:n], self.GSC[:, 1, oc, g:g + 1], xs_, ALU.mult, ALU.add)
        self.ln_range(l * 3 + 1, x0, n, self.o_hb)

    def mixer_ab(self, l):
        self.o_on = 0
        self.on_stride = 2048
        self.o_wab = 16384
        self.o_hb = 23552
        o = 25600
        self.o_qd = o
        self.o_ki = o + 256
        self.o_vt = o + 512
        self.o_at = o + 1024
        self.o_sb = o + 1280
        assert self.o_sb + 768 <= self.NR
        self.ab_params()
        abn = int(os.environ.get("MK_ABN", "1000"))
        abskip = int(os.environ.get("MK_ABSKIP", "0"))
        cnt = 0
        self.wab_base = 0
        groups = (([(0, 1, 0, False, 0), (256, 1, 256, False, 1)], 0, 0, 512),
                  ([(512, 8, 0, True, None)], 1, 512, 2048))
        for (segs, g, gx0, ntok) in groups:
            for u in range(6):
                for dr in range(2):
                    cnt += 1
                    if cnt <= abskip or cnt > abskip + abn:
                        continue
                    self.ab_unit_pass(u, dr, segs, g)
            for off in range(0, ntok, 512):
                n = min(512, ntok - off)
                self.ab_outproj(l, gx0 + off, off, n, g)

    def c_consts(self):
        d = self.dram
        A = self.f(0, [128, 128])
        B = self.f(128, [128, 128])
        for (M, cm, st, base) in ((A, 1, -1, -16), (B, -1, 1, -16)):
            self.memset(M, 1.0)
            self.p.op("pool", lambda e: e.affine_select(out=M, in_=M, pattern=[[st, 128]], compare_op=ALU.is_equal,
                                                        fill=0.0, base=base, channel_multiplier=cm),
                      reads=[M], writes=[M])
        self.memset(A.rearrange("p (g c) -> p g c", c=32)[:, :, 16:32], 0.0)
        self.memset(B.rearrange("p (g c) -> p g c", c=32)[:, :, 0:16], 0.0)
        self.tt(self.PERMR[:], A, B, ALU.add)
        for (M, cm, st) in ((self.LM1, -1, 1), (self.LM3, 1, -1)):
            self.memset(M[:], 1.0)
            self.p.op("pool", lambda e: e.affine_select(out=M[:], in_=M[:], pattern=[[st, 128]], compare_op=ALU.is_ge,
                                                        fill=0.0, base=0, channel_multiplier=cm),
                      reads=[M[:]], writes=[M[:]])
        st_ = self.f(256, [128, 16])
        self.memset(st_, 0.0)
        self.dma(st_[0:1, :], d["sink"], self.sem_misc)
        ps = self.PS[7][:, 0:16]
        onesf = self.f(384, [128, 128])
        self.memset(onesf, 1.0)
        self.mm(ps, onesf, st_, True, True)
        self.act(self.ES[:], ps, AF.Exp)
        I32 = mybir.dt.int32
        pi_ = self.f(512, [128, 1]).bitcast(I32)
        self.p.op("pool", lambda e: e.iota(pi_, pattern=[[0, 1]], base=0, channel_multiplier=1), writes=[pi_])
        i16 = self.f(513, [128, 1]).bitcast(I32)
        self.ts(i16, pi_, 15, None, ALU.bitwise_and)
        m32 = self.f(514, [128, 1]).bitcast(I32)
        self.ts(m32, pi_, 32, None, ALU.bitwise_and)
        i16f = self.f(515, [128, 1])
        self.cp(i16f, i16)
        m32f = self.f(516, [128, 1])
        self.cp(m32f, m32)
        inv = self.f(517, [128, 1])
        self.act(inv, i16f, AF.Exp, scale=-float(np.log(ROPE_BASE)) / 16.0)
        b16 = self.f(518, [128, 1]).bitcast(I32)
        self.ts(b16, pi_, 16, None, ALU.bitwise_and)
        b16f = self.f(519, [128, 1])
        self.cp(b16f, b16)
        self.ts(self.SGN[:], b16f, 1.0 / 8.0, -1.0, ALU.mult, ALU.add)
        self.ts(self.MC[:], m32f, 1.0 / 32.0, None, ALU.mult)
        self.ts(self.MR[:], self.MC[:], -1.0, 1.0, ALU.mult, ALU.add)
        pos = self.f(640, [128, 64])
        self.p.op("pool", lambda e: e.iota(pos.bitcast(I32), pattern=[[1, 64]], base=0, channel_multiplier=0), writes=[pos])
        posf = self.f(704, [128, 64])
        self.cp(posf, pos.bitcast(I32))
        TWO_PI = 2.0 * float(np.pi)
        TRC = self.f(1024, [128, 64])
        TRS = self.f(1088, [128, 64])
        for (posoff, dsin, dcos) in ((0.0, self.TSIN[:], self.TCOS[:]), (-2.0, TRS, TRC)):
            ang = self.f(768, [128, 64])
            self.ts(ang, posf, posoff, None, ALU.add)
            self.ts(ang, ang, inv, None, ALU.mult)
            for (dst, shiftv) in ((dsin, 0.0), (dcos, float(np.pi) / 2.0)):
                a = self.f(832, [128, 64])
                kq = self.f(896, [128, 64])
                ki = self.f(960, [128, 64]).bitcast(I32)
                self.ts(a, ang, shiftv, None, ALU.add)
                self.ts(kq, a, 1.0 / TWO_PI, None, ALU.mult)
                self.cp(ki, kq)
                self.cp(kq, ki)
                self.stt(a, kq, -TWO_PI, a, ALU.mult, ALU.add)
                self.ts(kq, a, float(np.pi), -TWO_PI, ALU.is_gt, ALU.mult)
                self.tt(a, a, kq, ALU.add)
                self.ts(kq, a, -float(np.pi), TWO_PI, ALU.is_lt, ALU.mult)
                self.tt(a, a, kq, ALU.add)
                self.act(dst, a, AF.Sin)
        for (ci, T) in ((0, TRC), (1, TRS)):
            for jq in range(4):
                if jq == 0:
                    self.ts(self.TRW[:, ci, :], T[:, 0:12], self.SELV[:, 0:1], None, ALU.mult)
                else:
                    self.stt(self.TRW[:, ci, :], T[:, 8 * jq:8 * jq + 12], self.SELV[:, jq:jq + 1], self.TRW[:, ci, :], ALU.mult, ALU.add)

    def rope_tables_w(self, r0, nrows):
        n = nrows * 64
        cos_t = self.f(0, [128, 512])[:, 0:n]
        sin_t = self.f(512, [128, 512])[:, 0:n]
        for (dst, ci, T) in ((cos_t, 0, self.TCOS), (sin_t, 1, self.TSIN)):
            dv = dst.rearrange("p (r c) -> p r c", c=64)
            rowv = self.TRW[:, ci, r0:r0 + nrows].unsqueeze(2).to_broadcast([128, nrows, 64])
            colv = T[:, 0:64].unsqueeze(1).to_broadcast([128, nrows, 64])
            self.ts(dv, rowv, self.MR[:, 0:1], None, ALU.mult)
            self.stt(dv, colv, self.MC[:, 0:1], dv, ALU.mult, ALU.add)
        self.ts(sin_t, sin_t, self.SGN[:, 0:1], None, ALU.mult)
        return cos_t, sin_t

    def rope_tables(self, t):
        cos_t = self.f(0, [128, 512])
        sin_t = self.f(512, [128, 512])
        for (dst, T) in ((cos_t, self.TCOS), (sin_t, self.TSIN)):
            dv = dst.rearrange("p (r c) -> p r c", c=64)
            rowv = T[:, 8 * t:8 * t + 8].unsqueeze(2).to_broadcast([128, 8, 64])
            colv = T[:, 0:64].unsqueeze(1).to_broadcast([128, 8, 64])
            self.ts(dv, rowv, self.MR[:, 0:1], None, ALU.mult)
            self.stt(dv, colv, self.MC[:, 0:1], dv, ALU.mult, ALU.add)
        self.ts(sin_t, sin_t, self.SGN[:, 0:1], None, ALU.mult)
        return cos_t, sin_t

    def rope_apply(self, dst, ps, cos_t, sin_t, n):
        ZQ = self.r(self.o_zq, [128, 512])[:, 0:n]
        self.cp(ZQ, ps, eng="act")
        pz = self.PS[6][:, 0:n]
        self.mm(pz, self.PERMR[:], ZQ, True, True)
        t1 = self.f(1024, [128, 512])[:, 0:n]
        self.tt(t1, ZQ.bitcast(F32), cos_t[:, 0:n], ALU.mult)
        t2 = self.f(1536, [128, 512])[:, 0:n]
        self.tt(t2, pz, sin_t[:, 0:n], ALU.mult)
        self.tt(dst, t1, t2, ALU.add)

    def c_modulate(self, x0, n, g):
        HB = self.r(self.o_hb, [128, KC, 512])
        for k in range(KC):
            self.ts(HB[:, k, 0:n], self.X[:, k, x0:x0 + n], self.OPS[:, 1, k, g:g + 1], self.shift(1, k, g), ALU.mult, ALU.add)
        return HB

    def va_slot(self, kap):
        return [(0, 0, 64), (64, 2, 0), (130, 0, 64), (194, 2, 0)][kap]

    def c_kv_project(self, HB, n, WKV, kf_dst, va_dst, vblk0, rope, emit=None):
        for kp in range(2):
            ps = self.PS[kp][:, 0:n]
            for k in range(KC):
                self.mm(ps, WKV[:, k, kp * 128:(kp + 1) * 128], HB[:, k, 0:n], k == 0, k == KC - 1)
            if rope is not None:
                self.rope_apply(kf_dst(kp), ps, rope[0], rope[1], n)
            else:
                self.cp(kf_dst(kp), ps, eng="act")
        for b in range(n // 128):
            ps = self.PS[2 + b % 2][:, 0:256]
            for k in range(KC):
                self.mm(ps, HB[:, k, b * 128:(b + 1) * 128], WKV[:, k, 256:512], k == 0, k == KC - 1)
            va = va_dst(vblk0 + b)
            self.cp(va[:, 0:132].rearrange("p (a c) -> p a c", c=66)[:, :, 0:64], ps[:, 0:128].rearrange("p (a c) -> p a c", c=64), eng="act")
            self.cp(va[:, 130:262].rearrange("p (a c) -> p a c", c=66)[:, :, 0:64], ps[:, 128:256].rearrange("p (a c) -> p a c", c=64), eng="act")
            self.ts(va[:, 64:66], self.RESET[:, 1:3], 0.0, 1.0, ALU.mult, ALU.add)
            self.ts(va[:, 194:196], self.RESET[:, 1:3], 0.0, 1.0, ALU.mult, ALU.add)
            if emit is not None:
                sq, tok0 = emit
                stv = self.f(2048 + (b % 2) * 256, [128, 256])
                self.cp(stv, ps)
                self.dma(self.dram["nv"][sq, tok0 + b * 128:tok0 + (b + 1) * 128, :], stv, self.p.named_sem("nv%d" % (b % 2)))
                ps2 = self.PS[4 + b % 2][:, 0:256]
                for k in range(KC):
                    self.mm(ps2, HB[:, k, b * 128:(b + 1) * 128], WKV[:, k, 0:256], k == 0, k == KC - 1)
                stk = self.f(2560 + (b % 2) * 256, [128, 256])
                self.cp(stk, ps2, eng="act")
                self.dma(self.dram["nk"][sq, tok0 + b * 128:tok0 + (b + 1) * 128, :], stk, self.p.named_sem("nk%d" % (b % 2)))

    def c_q_project(self, HB, n, rope):
        wq = self.dram["w_qkv"].rearrange("(k p) n -> p k n", p=128)
        QF = self.r(self.o_qf, [128, 8, 512])
        WS = self.r(self.o_ws, [128, KC, 4, 128])
        for grp in range(2):
            for s_ in range(2):
                for i in range(4):
                    c0 = grp * 512 + s_ * 256 + i * 64
                    self.dma_r(WS[:, :, i, s_ * 64:(s_ + 1) * 64], wq[:, :, c0:c0 + 64], self.p.named_sem("wq%d_%d" % (s_, i)))
            for i in range(4):
                pair = grp * 4 + i
                ps = self.PS[pair % 2][:, 0:n]
                for k in range(KC):
                    self.mm(ps, WS[:, k, i, :], HB[:, k, 0:n], k == 0, k == KC - 1)
                if rope is not None:
                    self.rope_apply(QF[:, pair, 0:n], ps, rope[0], rope[1], n)
                else:
                    self.cp(QF[:, pair, 0:n], ps, eng="act")
        return QF

    def c_attend(self, QF, nqb, keysets):
        OT = self.r(self.o_ot, [128, 4, 1024])
        NPT = 4
        PT = [self.r(self.o_ws + i * 512, [128, 512]) for i in range(NPT)]
        SCB = [self.PS[0], self.PS[1], self.PS[7]]
        tasks = []
        for h in range(C_HEADS):
            for ki, ks in enumerate(keysets):
                tasks.append((h, ki, ks))
        nk_for = [sum(1 for ks in keysets if ks[2] <= qb <= ks[3]) for qb in range(nqb)]

        def hinfo(h):
            kap = h // 4
            half = kap % 2
            pair = (h % 4) + (0 if h < 8 else 4)
            return kap, half, pair, slice(64 * half, 64 * half + 64)

        def score(i):
            h, ki, (kf_fn, va_fn, qlo, qhi, masks, halo) = tasks[i]
            kap, half, pair, hp = hinfo(h)
            ncol = (qhi - qlo + 1) * 128
            ps = SCB[i % 3][:, 0:ncol]
            pt = PT[i % NPT][:, 0:ncol]
            self.mm(ps, kf_fn(kap, half), QF[hp, pair, qlo * 128:qlo * 128 + ncol], True, True)
            self.act(pt, ps, AF.Exp, scale=HD ** -0.5)
            for qb in range(qlo, qhi + 1):
                if qb in masks:
                    sl = slice((qb - qlo) * 128, (qb - qlo + 1) * 128)
                    if halo is None:
                        self.tt(pt[:, sl], pt[:, sl].bitcast(F32), masks[qb][:], ALU.mult)
                    else:
                        self.stt(pt[:, sl], pt[:, sl].bitcast(F32), halo, masks[qb][:], ALU.mult, ALU.mult)

        seen = {}

        def pv(i):
            h, ki, (kf_fn, va_fn, qlo, qhi, masks, halo) = tasks[i]
            kap, half, pair, hp = hinfo(h)
            c0, o_off, d_off = self.va_slot(kap)
            ncol = (qhi - qlo + 1) * 128
            pt = PT[i % NPT][:, 0:ncol]
            for qb in range(qlo, qhi + 1):
                sl = slice((qb - qlo) * 128, (qb - qlo + 1) * 128)
                seen[(h, qb)] = seen.get((h, qb), 0) + 1
                self.mm(self.PS[2 + qb][:, 0:66], pt[:, sl], va_fn(kap)[:, c0:c0 + 66], seen[(h, qb)] == 1, seen[(h, qb)] == nk_for[qb])
            if ki == len(keysets) - 1:
                for qb in range(nqb):
                    po = self.PS[2 + qb]
                    rd = self.f(2048 + 16 * qb, [128, 1])
                    self.ts(rd, po[:, d_off:d_off + 1], self.ES[:, h:h + 1], None, ALU.add)
                    self.p.op("dve", lambda e: e.reciprocal(out=rd, in_=rd), reads=[rd], writes=[rd])
                    self.ts(OT[:, qb, h * 64:(h + 1) * 64], po[:, o_off:o_off + 64], rd, None, ALU.mult)

        LOOK = 2
        for i in range(min(LOOK, len(tasks))):
            score(i)
        for i in range(len(tasks)):
            if i + LOOK < len(tasks):
                score(i + LOOK)
            pv(i)
        return OT

    def c_outproj(self, l, OT, x0, nqb, g):
        n = nqb * 128
        OA = self.r(self.o_hb, [128, KC, 512])
        for qb in range(nqb):
            for hb in range(2):
                ps = self.PS[hb]
                for j in range(4):
                    c = hb * 4 + j
                    self.tr(ps[:, j * 128:(j + 1) * 128], OT[:, qb, c * 128:(c + 1) * 128].bitcast(F32))
                self.cp(OA[:, hb * 4:(hb + 1) * 4, qb * 128:(qb + 1) * 128], ps[:].rearrange("p (a b) -> p a b", a=4),
                        eng="act" if hb else "dve")
        wo = self.dram["w_out_c"].rearrange("(c p) o -> p c o", p=128)
        for half in range(2):
            WO = self.r(self.o_qf, [128, KC, 512])
            self.dma_r(WO, wo[:, :, half * 512:(half + 1) * 512], self.p.named_sem("woc"))
            for o in range(4):
                for c in range(KC):
                    self.mm(self.PS[4 + o][:, 0:n], WO[:, c, o * 128:(o + 1) * 128], OA[:, c, 0:n], c == 0, c == KC - 1)
            for o in range(4):
                oc = half * 4 + o
                xs_ = self.X[:, oc, x0:x0 + n]
                self.stt(xs_, self.PS[4 + o][:, 0:n], self.GSC[:, 1, oc, g:g + 1], xs_, ALU.mult, ALU.add)
        self.ln_range(l * 3 + 1, x0, n, self.o_ws)

    def mixer_c(self, l):
        d = self.dram
        self.o_kf = 0
        self.o_va = 4096
        self.o_kc = self.o_va + 16 * 262
        self.o_vca = self.o_kc + 1024
        self.o_hb = self.o_vca + 4 * 262
        self.o_ws = self.o_hb + 4096
        self.o_qf = self.o_ws + 4096
        self.o_ot = self.o_qf + 4096
        self.o_zq = self.o_ot + 4096
        assert self.o_zq + 512 <= self.NR, self.o_zq
        self.c_consts()
        KF = self.r(self.o_kf, [128, 2, 2048])
        VA = self.r(self.o_va, [128, 16, 262])
        KCF = self.r(self.o_kc, [128, 2, 512])
        VCA = self.r(self.o_vca, [128, 4, 262])
        wq = d["w_qkv"].rearrange("(k p) n -> p k n", p=128)
        WKV = self.r(self.o_ws, [128, KC, 512])

        def load_wkv():
            self.dma_r(WKV, wq[:, :, 1024:1536], self.p.named_sem("wkv"))
        for sq in range(2):
            x0 = sq * SEQ
            HB = self.c_modulate(x0, SEQ, 0)
            load_wkv()
            self.c_kv_project(HB, SEQ, WKV, lambda kp: KF[:, kp, 0:SEQ], lambda b: VA[:, b, :], 0, None, emit=(sq, 0))
            QF = self.c_q_project(HB, SEQ, None)
            keysets = []
            for kb in range(2):
                keysets.append((lambda kap, half, kb=kb: KF[64 * half:64 * half + 64, kap // 2, kb * 128:(kb + 1) * 128],
                                lambda kap, kb=kb: VA[:, kb, :], 0, 1, {}, None))
            OT = self.c_attend(QF, 2, keysets)
            self.c_outproj(l, OT, x0, 2, 0)
        stg = self.r(self.o_ot, [128, 4, 256])
        self.dma_r(stg, d["ck"].rearrange("(b p) c -> p b c", p=128), self.p.named_sem("ck"))
        for b in range(4):
            for kp in range(2):
                ps = self.PS[(b * 2 + kp) % 2][:, 0:128]
                self.tr(ps, stg[:, b, kp * 128:(kp + 1) * 128].bitcast(F32))
                self.cp(KCF[:, kp, b * 128:(b + 1) * 128], ps, eng="act" if kp else "dve")
        cvv = d["cv"].rearrange("(b p) c -> p b c", p=128)
        for kap in range(4):
            c0 = [0, 66, 130, 196][kap]
            self.dma_r(VCA[:, :, c0:c0 + 64], cvv[:, :, kap * 64:(kap + 1) * 64], self.p.named_sem("cv%d" % kap))
        for b in range(4):
            self.ts(VCA[:, b, 64:66], self.RESET[:, 1:3], 0.0, 1.0, ALU.mult, ALU.add)
            self.ts(VCA[:, b, 194:196], self.RESET[:, 1:3], 0.0, 1.0, ALU.mult, ALU.add)
        load_wkv()
        W0 = self.W0
        for (off, n_, r0) in ((0, 512, 0), (512, 256, 8)):
            HB = self.c_modulate(W0 + off, n_, 1)
            rope = self.rope_tables_w(r0, n_ // 64)
            self.c_kv_project(HB, n_, WKV, lambda kp, off=off, n_=n_: KF[:, kp, off:off + n_], lambda b: VA[:, b, :], off // 128, rope)
        HB = self.c_modulate(W0 + 128, 512, 1)
        rope = self.rope_tables_w(2, 8)
        QF = self.c_q_project(HB, 512, rope)
        keysets = []
        for cb in range(4):
            keysets.append((lambda kap, half, cb=cb: KCF[64 * half:64 * half + 64, kap // 2, cb * 128:(cb + 1) * 128],
                            lambda kap, cb=cb: VCA[:, cb, :], 0, 3, {}, None))
        for kb in range(6):
            qlo = max(1, kb - 1) - 1
            qhi = min(4, kb + 1) - 1
            masks = {}
            if 0 <= kb - 2 <= 3:
                masks[kb - 2] = self.LM1
            if 0 <= kb <= 3:
                masks[kb] = self.LM3
            halo = None
            if kb == 0:
                halo = self.SELV[:, 12:13]
            if kb == 5:
                halo = self.SELV[:, 13:14]
            keysets.append((lambda kap, half, kb=kb: KF[64 * half:64 * half + 64, kap // 2, kb * 128:(kb + 1) * 128],
                            lambda kap, kb=kb: VA[:, kb, :], qlo, qhi, masks, halo))
        OT = self.c_attend(QF, 4, keysets)
        self.c_outproj(l, OT, W0 + 128, 4, 1)

    def build(self):
        self.o_w13 = 0
        self.o_w2 = 8192
        self.o_xm = 12288
        self.o_hid = 16384
        self.o_ln = self.o_hid + 18 * 512
        self.o_sg = 0
        self.o_lnf = 1024
        self.consts()
        self.EPSLN = self.sb("EPSLN", [128, 1])
        self.memset(self.EPSLN[:], LN_EPS / (ALPHA * ALPHA))
        self.ONEC = self.sb("ONEC", [128, 1])
        self.memset(self.ONEC[:], 1.0)
        self.load_small()
        self.dma(self.SELV[:], self.dram["selv"], self.p.named_sem("selv"))
        self.load_x()
        self.W0 = TP
        full = [(t * 512, 512, 0 if t == 0 else 1) for t in range(NT)]
        win = [(0, 512, 0), (self.W0, 512, 1), (self.W0 + 512, 256, 1)]
        own = [(0, 512, 0), (self.W0 + 128, 512, 1)]
        for l in range(DEPTH):
            self.mod_vectors(l)
            for (x0, n, g) in (full if l == 0 else win):
                self.ffn_tile(l, 0, x0, n, g)
            if self.stage == 1 + 3 * l:
                break
            if l == 0:
                self.mixer_ab(l)
                self.select_window()
            else:
                self.mixer_c(l)
            if self.stage == 2 + 3 * l:
                break
            for (x0, n, g) in (win if l == 0 else own):
                self.ffn_tile(l, 1, x0, n, g)
            if self.stage == 3 + 3 * l:
                break
        self.store_x()
        self.p.finish("sp")
        return self.nc


_CACHE = {}


def get_program(stage=99):
    if stage not in _CACHE:
        _CACHE[stage] = Builder(stage).build()
    return _CACHE[stage]


def _selv(j):
    v = np.zeros((16,), np.float32)
    v[j] = 1.0
    if j > 0:
        v[4 + j - 1] = 1.0
        v[12] = 1.0
    if j < 3:
        v[8 + j + 1] = 1.0
        v[13] = 1.0
    return np.ascontiguousarray(np.broadcast_to(v, (128, 16))).astype(np.float32)


def shard_inputs(inp):
    f = lambda a: np.ascontiguousarray(np.asarray(a, dtype=np.float32))
    maps = []
    for c in range(NCORES):
        sb = c // 4
        m = {
            "xp": f(inp["x_prompt"][2 * c:2 * c + 2].reshape(TP, D)),
            "xs": f(inp["x_sample"][sb]),
            "st_h": f(inp["state_hgrn"][sb, 0]),
            "st_g": f(inp["state_gla"][sb, 0]),
            "ck": f(inp["cache_k"][sb, 0].reshape(PAST, C_KV * HD)),
            "cv": f(inp["cache_v"][sb, 0].reshape(PAST, C_KV * HD)),
            "cvec": f(np.stack([inp["c_ctx"], inp["c"][sb]], axis=0)),
            "selv": _selv(c % 4),
            "w_mod": f(inp["w_mod"]),
            "b_mod": f(inp["b_mod"]),
            "ln_g": f(inp["ln_g"].reshape(DEPTH * 3, D)),
            "ln_b": f(inp["ln_b"].reshape(DEPTH * 3, D)),
            "ffn_w1": f(inp["ffn_w1"]),
            "ffn_w3": f(inp["ffn_w3"]),
            "ffn_w2": f(inp["ffn_w2"]),
            "w_in_ab": f(inp["w_in_ab"][0]),
            "hgrn_lb": f(inp["hgrn_lb"]),
            "gate_up": f(inp["gla_gate_up"][0]),
            "gate_b": f(inp["gla_gate_b"][0]),
            "norm_a": f(inp["norm_a"]),
            "norm_b": f(inp["norm_b"]),
            "w_out_ab": f(inp["w_out_ab"][0]),
            "w_qkv": f(inp["w_qkv_c"][0]),
            "sink": f(inp["sink_c"]),
            "w_out_c": f(inp["w_out_c"][0]),
        }
        maps.append(m)
    return maps


def gather_outputs(res):
    r = res.results
    B = 16
    yp = np.concatenate([r[c]["yp"].reshape(2, SEQ, D) for c in range(NCORES)], axis=0)
    ys = np.stack([np.concatenate([r[4 * b + q]["ys"] for q in range(4)], axis=0) for b in range(2)], axis=0)
    nsh = np.concatenate([r[c]["ns_h"].reshape(2, 1, 2, A_HEADS, 128, 128) for c in range(NCORES)], axis=0)
    nsg = np.concatenate([r[c]["ns_g"].reshape(2, 1, 2, B_HEADS, B_DK, 128) for c in range(NCORES)], axis=0)
    nk = np.concatenate([r[c]["nk"].reshape(2, 1, SEQ, C_KV, HD) for c in range(NCORES)], axis=0)
    nv = np.concatenate([r[c]["nv"].reshape(2, 1, SEQ, C_KV, HD) for c in range(NCORES)], axis=0)
    return (yp.astype(np.float32), ys.astype(np.float32), nsh.astype(np.float32), nsg.astype(np.float32),
            nk.astype(np.float32), nv.astype(np.float32))


def kernel(**inputs):
    stage = int(os.environ.get("MK_STAGE", "99"))
    nc = get_program(stage)
    maps = shard_inputs(inputs)
    res = run_bass_kernel_spmd(nc, maps, core_ids=list(range(NCORES)))
    return gather_outputs(res)
```

```python
import os
import numpy as np
import concourse.bass as bass
import concourse.mybir as mybir
from concourse.bass_utils import run_bass_kernel_spmd

F32 = mybir.dt.float32
F32R = mybir.dt.float32r
AF = mybir.ActivationFunctionType
ALU = mybir.AluOpType

D = 1024
KC = 8
DFF = 2816
FC = 22
NMOD = 9
DEPTH = 2
SEQ = 256
TP = 512
TS = 2048
T = TP + TS
NT = T // 512
A_HEADS = 4
B_HEADS = 4
B_DK = 64
GATE_RANK = 16
GLA_TAU = 16.0
AB_IN = 4128
C_HEADS = 16
C_KV = 4
HD = 64
PAST = 512
ALPHA = (2.0 * DEPTH) ** 0.25
LN_EPS = 1e-5
RMS_EPS = 1e-6
ROPE_BASE = 10000.0
NCORES = 8


class Prog:
    def __init__(self, nc):
        self.nc = nc
        self.E = {"pe": nc.tensor, "act": nc.scalar, "dve": nc.vector, "pool": nc.gpsimd, "sp": nc.sync}
        self.sems = {}
        self.cnt = {}
        for e in self.E:
            self.sems[e] = nc.alloc_semaphore("s_" + e)
            self.cnt[e] = 0
        self.seen = {e: {} for e in self.E}
        self.ndma_sem = 0
        self.n_inst = 0
        self.n_wait = 0
        self.self_sync = set(os.environ.get("MK_SELFSYNC", "act,dve,pool").split(",")) - {""}
        self.named = {}
        self.mem = {}

    def named_sem(self, name):
        if name not in self.named:
            self.named[name] = self.new_dma_sem()
        return self.named[name]

    def new_dma_sem(self):
        k = "d%d" % self.ndma_sem
        self.ndma_sem += 1
        self.sems[k] = self.nc.alloc_semaphore("s_" + k)
        self.cnt[k] = 0
        return k

    @staticmethod
    def box(ap):
        esz = mybir.dt.size(ap.dtype)
        dims = ap.ap
        off = ap.offset
        name = ap.tensor.name
        if str(ap.space) == "DRAM":
            lo = off
            hi = off
            for st, cn in dims:
                d = (cn - 1) * st
                if d > 0:
                    hi += d
                else:
                    lo += d
            return name, 0, 1, lo * esz, (hi + 1) * esz
        pst, pcn = dims[0]
        if pst <= 0:
            pst = 1 << 40
        p0 = off // pst
        c0 = off % pst
        if str(ap.space) == "PSUM":
            q0 = (p0 // 32) * 32
            q1 = ((p0 + pcn + 31) // 32) * 32
            return name, q0, q1, 0, 2048
        lo = c0
        hi = c0
        for st, cn in dims[1:]:
            d = (cn - 1) * st
            if d > 0:
                hi += d
            else:
                lo += d
        return name, p0, p0 + pcn, lo * esz, (hi + 1) * esz

    def _collect(self, reads, writes):
        deps = []
        rb = [self.box(a) for a in reads]
        wb = [self.box(a) for a in writes]
        for (name, p0, p1, lo, hi) in rb:
            m = self.mem.get(name)
            if m is None:
                continue
            for r in m[0]:
                if r[0] < p1 and p0 < r[1] and r[2] < hi and lo < r[3]:
                    deps.append((r[4], r[5]))
        for (name, p0, p1, lo, hi) in wb:
            m = self.mem.get(name)
            if m is None:
                continue
            for lst in m:
                for r in lst:
                    if r[0] < p1 and p0 < r[1] and r[2] < hi and lo < r[3]:
                        deps.append((r[4], r[5]))
        return deps, rb, wb

    def _record(self, rb, wb, key, val):
        for (name, p0, p1, lo, hi) in wb:
            m = self.mem.setdefault(name, [[], []])
            for i in (0, 1):
                m[i] = [r for r in m[i] if not (p0 <= r[0] and r[1] <= p1 and lo <= r[2] and r[3] <= hi)]
            m[0].append([p0, p1, lo, hi, key, val])
        for (name, p0, p1, lo, hi) in rb:
            m = self.mem.setdefault(name, [[], []])
            m[1] = [r for r in m[1] if not (r[4] == key and p0 <= r[0] and r[1] <= p1 and lo <= r[2] and r[3] <= hi)]
            m[1].append([p0, p1, lo, hi, key, val])

    def _wait(self, e, deps):
        best = {}
        for k, v in deps:
            if best.get(k, 0) < v:
                best[k] = v
        for k, v in best.items():
            if k == e and e not in self.self_sync:
                continue
            if self.seen[e].get(k, 0) < v:
                self.E[e].wait_ge(self.sems[k], v)
                self.seen[e][k] = v
                self.n_wait += 1

    def op(self, e, fn, reads=(), writes=()):
        deps, rb, wb = self._collect(reads, writes)
        self._wait(e, deps)
        ins = fn(self.E[e])
        ins.then_inc(self.sems[e], 1)
        self.cnt[e] += 1
        self._record(rb, wb, e, self.cnt[e])
        self.n_inst += 1
        return ins

    def dma(self, q, out, in_, sem):
        deps, rb, wb = self._collect([in_], [out])
        self._wait(q, deps)
        ins = self.E[q].dma_start(out=out, in_=in_)
        ins.then_inc(self.sems[sem], 16)
        self.cnt[sem] += 16
        self._record(rb, wb, sem, self.cnt[sem])
        self.n_inst += 1
        return ins

    def finish(self, e="sp"):
        deps = []
        for name, m in self.mem.items():
            for lst in m:
                for r in lst:
                    deps.append((r[4], r[5]))
        self._wait(e, deps)


class Builder:
    def __init__(self, stage=99):
        self.stage = stage
        nc = bass.Bass("TRN2", target_bir_lowering=False)
        nc.dge_precook = False
        self.nc = nc
        self.p = Prog(nc)
        self.dram = {}
        self.decl_io()
        self.alloc()

    def din(self, name, shape):
        self.dram[name] = self.nc.dram_tensor(name, list(shape), F32, kind="ExternalInput").ap()
        return self.dram[name]

    def dout(self, name, shape):
        self.dram[name] = self.nc.dram_tensor(name, list(shape), F32, kind="ExternalOutput").ap()
        return self.dram[name]

    def decl_io(self):
        self.din("xp", [TP, D])
        self.din("xs", [TS, D])
        self.din("st_h", [2, A_HEADS, 128, 128])
        self.din("st_g", [2, B_HEADS, B_DK, 128])
        self.din("ck", [PAST, C_KV * HD])
        self.din("cv", [PAST, C_KV * HD])
        self.din("cvec", [2, D])
        self.din("selv", [128, 16])
        self.din("w_mod", [DEPTH, D, NMOD * D])
        self.din("b_mod", [DEPTH, NMOD * D])
        self.din("ln_g", [DEPTH * 3, D])
        self.din("ln_b", [DEPTH * 3, D])
        self.din("ffn_w1", [DEPTH, 2, D, DFF])
        self.din("ffn_w3", [DEPTH, 2, D, DFF])
        self.din("ffn_w2", [DEPTH, 2, DFF, D])
        self.din("w_in_ab", [D, AB_IN])
        self.din("hgrn_lb", [2, 2, 512])
        self.din("gate_up", [2, GATE_RANK, 256])
        self.din("gate_b", [2, 256])
        self.din("norm_a", [1, 128])
        self.din("norm_b", [1, 128])
        self.din("w_out_ab", [D, D])
        self.din("w_qkv", [D, 1536])
        self.din("sink", [1, C_HEADS])
        self.din("w_out_c", [D, D])
        self.dout("yp", [TP, D])
        self.dout("ys", [512, D])
        self.dout("ns_h", [2, 2, A_HEADS, 128, 128])
        self.dout("ns_g", [2, 2, B_HEADS, B_DK, 128])
        self.dout("nk", [2, SEQ, C_KV * HD])
        self.dout("nv", [2, SEQ, C_KV * HD])

    def sb(self, name, shape, dt=F32):
        return self.nc.alloc_sbuf_tensor(name, list(shape), dt)

    def alloc(self):
        nc = self.nc
        self.X = self.sb("X", [128, KC, T])
        self.IDENT = self.sb("IDENT", [128, 128])
        self.ONES = self.sb("ONES", [128, 128], F32R)
        self.U1 = self.sb("U1", [128, 256])
        self.U2 = self.sb("U2", [128, 256], F32R)
        self.MF = self.U1[:, 0:128]
        self.MB = self.U1[:, 128:256]
        self.LM1 = self.U1[:, 0:128]
        self.LM3 = self.U1[:, 128:256]
        self.RESET = self.sb("RESET", [128, 256])
        self.PARS = self.sb("PARS", [128, 22])
        self.LB = self.sb("LB", [128, 8])
        self.OML = self.sb("OML", [128, 8])
        self.OMLH = self.sb("OMLH", [128, 8])
        self.LBH = self.sb("LBH", [128, 8])
        self.SELV = self.sb("SELV", [128, 16])
        self.TRW = self.sb("TRW", [128, 2, 12])
        self.NGB = self.sb("NGB", [128, 4])
        self.GUP = self.U2[:, 0:256]
        self.PERMR = self.U2[:, 0:128]
        self.EPSR = self.sb("EPSR", [128, 1])
        self.ES = self.sb("ES", [128, 16])
        self.TCOS = self.sb("TCOS", [128, 64])
        self.TSIN = self.sb("TSIN", [128, 64])
        self.MR = self.sb("MR", [128, 1])
        self.MC = self.sb("MC", [128, 1])
        self.SGN = self.sb("SGN", [128, 1])
        self.MODV = self.sb("MODV", [128, 72, 2])
        self.OPS = self.sb("OPS", [128, 3, KC, 2])
        self.GSC = self.sb("GSC", [128, 3, KC, 2])
        self.BM = self.sb("BM", [128, 72])
        self.LNG = self.sb("LNG", [128, 48])
        self.LNB = self.sb("LNB", [128, 48])
        self.CS = self.sb("CS", [128, 2, KC], F32R)
        self.NR = 27 * 1024
        self.NF = 3072
        self.R = self.sb("R", [128, self.NR], F32R)
        self.Fm = self.sb("Fm", [128, self.NF])
        self.PS = [nc.alloc_psum_tensor("PS%d" % i, [128, 512], F32) for i in range(8)]
        self.sem_w13 = [self.p.new_dma_sem() for _ in range(2)]
        self.sem_w2 = [self.p.new_dma_sem() for _ in range(4)]
        self.sem_wab = [self.p.new_dma_sem() for _ in range(7)]
        self.sem_st = self.p.new_dma_sem()
        self.sem_io = [self.p.new_dma_sem() for _ in range(2)]
        self.sem_misc = self.p.new_dma_sem()
        self.sem_out = self.p.new_dma_sem()
        self.n13 = 0
        self.n2 = 0
        self.nio = 0

    @staticmethod
    def _view(base, off, shape, total):
        n = 1
        for x in shape[1:]:
            n *= x
        assert off + n <= total, (off, n, total)
        v = base[0:shape[0], off:off + n]
        if len(shape) == 3:
            v = v.rearrange("p (a b) -> p a b", a=shape[1])
        elif len(shape) == 4:
            v = v.rearrange("p (a b c) -> p a b c", a=shape[1], b=shape[2])
        return v

    def r(self, off, shape):
        return self._view(self.R, off, shape, self.NR)

    def f(self, off, shape):
        return self._view(self.Fm, off, shape, self.NF)

    def mm(self, out, lhsT, rhs, start, stop):
        self.p.op("pe", lambda e: e.matmul(out, lhsT=lhsT, rhs=rhs, start=start, stop=stop),
                  reads=[lhsT, rhs], writes=[out])

    def tr(self, out, in_, n=128):
        ident = self.IDENT[0:in_.shape[0], 0:in_.shape[0]]
        self.p.op("pe", lambda e: e.transpose(out=out, in_=in_, identity=ident), reads=[in_, ident], writes=[out])

    def act(self, out, in_, func, bias=None, scale=None, eng="act"):
        kw = {}
        rd = [in_]
        if bias is not None:
            kw["bias"] = bias
            if not isinstance(bias, (int, float)):
                rd.append(bias)
        if scale is not None:
            kw["scale"] = scale
            if not isinstance(scale, (int, float)):
                rd.append(scale)
        self.p.op("act", lambda e: e.activation(out=out, in_=in_, func=func, **kw), reads=rd, writes=[out])

    def ts(self, out, in0, s1, s2, op0, op1=None, eng="dve"):
        rd = [in0]
        for s in (s1, s2):
            if s is not None and not isinstance(s, (int, float)):
                rd.append(s)
        if op1 is None:
            self.p.op(eng, lambda e: e.tensor_scalar(out=out, in0=in0, scalar1=s1, scalar2=None, op0=op0), reads=rd, writes=[out])
        else:
            self.p.op(eng, lambda e: e.tensor_scalar(out=out, in0=in0, scalar1=s1, scalar2=s2, op0=op0, op1=op1), reads=rd, writes=[out])

    def tt(self, out, in0, in1, op, eng="dve"):
        self.p.op(eng, lambda e: e.tensor_tensor(out=out, in0=in0, in1=in1, op=op), reads=[in0, in1], writes=[out])

    def stt(self, out, in0, scalar, in1, op0, op1, eng="dve"):
        rd = [in0, in1]
        if not isinstance(scalar, (int, float)):
            rd.append(scalar)
        self.p.op(eng, lambda e: e.scalar_tensor_tensor(out=out, in0=in0, scalar=scalar, in1=in1, op0=op0, op1=op1), reads=rd, writes=[out])

    def cp(self, out, in_, eng="dve"):
        if eng == "act":
            self.act(out, in_, AF.Copy)
        else:
            self.p.op(eng, lambda e: e.tensor_copy(out=out, in_=in_), reads=[in_], writes=[out])

    def memset(self, ap, val, eng="pool"):
        self.p.op(eng, lambda e: e.memset(ap, val), writes=[ap])

    def dma(self, out, in_, sem, q="sp"):
        self.p.dma(q, out, in_, sem)

    def dma_r(self, out, in_, sem, q="sp"):
        self.p.dma(q, out if out.dtype == F32R else out.bitcast(F32R), in_.bitcast(F32R), sem)

    def consts(self):
        self.memset(self.IDENT[:], 1.0)
        self.p.op("pool", lambda e: e.affine_select(out=self.IDENT[:], in_=self.IDENT[:], pattern=[[-1, 128]],
                                                    compare_op=ALU.is_equal, fill=0.0, base=0, channel_multiplier=1),
                  reads=[self.IDENT[:]], writes=[self.IDENT[:]])
        tmp = self.f(0, [128, 128])
        self.memset(tmp, 1.0)
        self.cp(self.ONES[:], tmp)
        for (M, cm, st) in ((self.MF, -1, 1), (self.MB, 1, -1)):
            self.memset(M[:], 1.0)
            self.p.op("pool", lambda e: e.affine_select(out=M[:], in_=M[:], pattern=[[st, 128]], compare_op=ALU.is_ge,
                                                        fill=0.0, base=0, channel_multiplier=cm),
                      reads=[M[:]], writes=[M[:]])
        self.memset(self.MF[0:64, 64:128], 0.0)
        self.memset(self.MB[64:128, 0:64], 0.0)
        self.memset(self.RESET[:], 0.0)
        self.memset(self.RESET[:, 0:256:64], 1.0)
        self.memset(self.EPSR[:], RMS_EPS)

    def load_fm(self, dst, src_rows, nrows):
        st = self.f(0, [128, 128])
        self.dma(st[0:nrows, :], src_rows, self.sem_misc)
        ps = self.PS[7][:, 0:nrows]
        self.tr(ps, st[0:nrows, :])
        self.cp(dst, ps)

    def load_small(self):
        self.load_fm(self.LNG[:], self.dram["ln_g"].rearrange("r (k p) -> (r k) p", p=128), 48)
        self.load_fm(self.LNB[:], self.dram["ln_b"].rearrange("r (k p) -> (r k) p", p=128), 48)
        st = self.f(0, [128, 128])
        self.dma(st[0:16, :], self.dram["cvec"].rearrange("g (k p) -> (g k) p", p=128), self.sem_misc)
        ps = self.PS[7][:, 0:16]
        self.tr(ps, st[0:16, :])
        self.act(self.CS[:].rearrange("p g k -> p (g k)"), ps, AF.Silu)

    def load_x(self):
        for tb in range(T // 128):
            src = self.dram["xp"][tb * 128:(tb + 1) * 128, :] if tb < TP // 128 else \
                self.dram["xs"][tb * 128 - TP:(tb + 1) * 128 - TP, :]
            s = self.nio % 2
            self.nio += 1
            st = self.f(s * 1024, [128, 1024])
            self.dma(st, src, self.sem_io[s])
            for hb in range(2):
                ps = self.PS[(tb * 2 + hb) % 4]
                for j in range(4):
                    k = hb * 4 + j
                    self.tr(ps[:, j * 128:(j + 1) * 128], st[:, k * 128:(k + 1) * 128])
                dst = self.X[:, hb * 4:(hb + 1) * 4, tb * 128:(tb + 1) * 128]
                self.cp(dst, ps[:].rearrange("p (a b) -> p a b", a=4), eng="dve" if hb == 0 else "act")

    def store_x(self):
        for ob in range(8):
            if ob < 4:
                dst = self.dram["yp"][ob * 128:(ob + 1) * 128, :]
                xc = ob * 128
            else:
                dst = self.dram["ys"][(ob - 4) * 128:(ob - 3) * 128, :]
                xc = self.W0 + 128 + (ob - 4) * 128
            s = self.nio % 2
            self.nio += 1
            st = self.f(s * 1024, [128, 1024])
            for hb in range(2):
                ps = self.PS[(ob * 2 + hb) % 4]
                for j in range(4):
                    k = hb * 4 + j
                    self.tr(ps[:, j * 128:(j + 1) * 128], self.X[:, k, xc:xc + 128])
                self.cp(st[:, hb * 512:(hb + 1) * 512], ps[:], eng="dve" if hb == 0 else "act")
            self.dma(dst, st, self.sem_io[s])

    def select_window(self):
        SV = self.SELV
        for k in range(KC):
            own = self.f(0, [128, 512])
            hp = self.f(512, [128, 128])
            hn = self.f(640, [128, 128])
            for t in range(4):
                xt = self.X[:, k, TP + t * 512:TP + (t + 1) * 512]
                if t == 0:
                    self.ts(own, xt, SV[:, t:t + 1], None, ALU.mult)
                    self.ts(hp, xt[:, 384:512], SV[:, 4 + t:5 + t], None, ALU.mult)
                    self.ts(hn, xt[:, 0:128], SV[:, 8 + t:9 + t], None, ALU.mult)
                else:
                    self.stt(own, xt, SV[:, t:t + 1], own, ALU.mult, ALU.add)
                    self.stt(hp, xt[:, 384:512], SV[:, 4 + t:5 + t], hp, ALU.mult, ALU.add)
                    self.stt(hn, xt[:, 0:128], SV[:, 8 + t:9 + t], hn, ALU.mult, ALU.add)
            self.cp(self.X[:, k, self.W0:self.W0 + 128], hp, eng="act")
            self.cp(self.X[:, k, self.W0 + 128:self.W0 + 640], own, eng="act")
            self.cp(self.X[:, k, self.W0 + 640:self.W0 + 768], hn, eng="act")

    def mod_vectors(self, l):
        self.load_fm(self.BM[:], self.dram["b_mod"][l].rearrange("(r p) -> r p", p=128), 72)
        wm = self.dram["w_mod"][l].rearrange("(k p) n -> p k n", p=128)
        pm = self.PS[6][:, 0:144].rearrange("p (c g) -> p c g", g=2)
        for blk in range(18):
            s = self.n13 % 2
            self.n13 += 1
            wt = self.r(self.o_w13 + s * 4096, [128, KC, 512])
            self.dma_r(wt, wm[:, :, blk * 512:(blk + 1) * 512], self.sem_w13[s])
            for q in range(4):
                oc = blk * 4 + q
                for k in range(KC):
                    self.mm(pm[:, oc, :], wt[:, k, q * 128:(q + 1) * 128], self.CS[:, :, k], k == 0, k == KC - 1)
        for g in range(2):
            self.tt(self.MODV[:, :, g], pm[:, :, g], self.BM[:], ALU.add)
        gmul = [0.5 / ALPHA, 1.0 / ALPHA, 0.5 / ALPHA]
        for s in range(3):
            self.ts(self.OPS[:, s, :, :], self.MODV[:, (3 * s + 1) * 8:(3 * s + 2) * 8, :], 1.0, None, ALU.add)
            self.ts(self.GSC[:, s, :, :], self.MODV[:, (3 * s + 2) * 8:(3 * s + 3) * 8, :], gmul[s], None, ALU.mult)

    def shift(self, s, k, g):
        return self.MODV[:, 3 * s * 8 + k, g:g + 1]

    def ln_range(self, lnidx, x0, n, o_r):
        ts_ = slice(x0, x0 + n)
        pa, pb = self.PS[4][:, 0:n], self.PS[5][:, 0:n]
        for k in range(KC):
            zr = self.r(o_r + (k % 2) * 512, [128, 512])[:, 0:n]
            sq = self.r(o_r + 1024 + (k % 2) * 512, [128, 512])[:, 0:n]
            self.act(zr, self.X[:, k, ts_], AF.Copy, scale=1.0 / 1024.0)
            self.act(sq, self.X[:, k, ts_], AF.Square, scale=1.0 / 32.0)
            self.mm(pa, self.ONES[:], zr, k == 0, k == KC - 1)
            self.mm(pb, self.ONES[:], sq, k == 0, k == KC - 1)
        m2 = self.f(self.o_lnf, [128, 512])[:, 0:n]
        self.act(m2, pa, AF.Square)
        self.tt(m2, pb, m2, ALU.subtract)
        self.act(m2, m2, AF.Ln, bias=self.EPSLN[:, 0:1])
        self.act(m2, m2, AF.Exp, scale=-0.5)
        for k in range(KC):
            xk = self.X[:, k, ts_]
            self.tt(xk, xk, pa, ALU.subtract)
            self.stt(xk, xk, self.LNG[:, lnidx * 8 + k:lnidx * 8 + k + 1], m2, ALU.mult, ALU.mult)
            self.act(xk, xk, AF.Identity, bias=self.LNB[:, lnidx * 8 + k:lnidx * 8 + k + 1])

    def ffn_tile(self, l, j, x0, n, g):
        s = 0 if j == 0 else 2
        ts_ = slice(x0, x0 + n)
        XM = self.r(self.o_xm, [128, KC, 512])[:, :, 0:n]
        HID = self.r(self.o_hid, [128, FC, 512])[:, :, 0:n]
        for k in range(KC):
            self.ts(XM[:, k, :], self.X[:, k, ts_], self.OPS[:, s, k, g:g + 1], self.shift(s, k, g), ALU.mult, ALU.add)
        w1 = self.dram["ffn_w1"][l, j].rearrange("(k p) f -> p k f", p=128)
        w3 = self.dram["ffn_w3"][l, j].rearrange("(k p) f -> p k f", p=128)
        w2 = self.dram["ffn_w2"][l, j].rearrange("(c p) o -> p c o", p=128)
        for fb in range(FC // 2):
            sl = self.n13 % 2
            self.n13 += 1
            wt = self.r(self.o_w13 + sl * 4096, [128, 2, KC, 256])
            self.dma_r(wt[:, 0], w1[:, :, fb * 256:(fb + 1) * 256], self.sem_w13[sl])
            self.dma_r(wt[:, 1], w3[:, :, fb * 256:(fb + 1) * 256], self.p.named_sem("w3_%d" % sl))
            for c in range(2):
                f = 2 * fb + c
                p1, p3 = self.PS[f % 2][:, 0:n], self.PS[2 + f % 2][:, 0:n]
                for k in range(KC):
                    self.mm(p1, wt[:, 0, k, c * 128:(c + 1) * 128], XM[:, k, :], k == 0, k == KC - 1)
                for k in range(KC):
                    self.mm(p3, wt[:, 1, k, c * 128:(c + 1) * 128], XM[:, k, :], k == 0, k == KC - 1)
                sg = self.f(self.o_sg + (f % 2) * 512, [128, 512])[:, 0:n]
                self.act(sg, p1, AF.Silu)
                self.tt(HID[:, f, :], sg, p3, ALU.mult)
        for half in range(2):
            for fb in range(FC // 2):
                sl = self.n2 % 4
                self.n2 += 1
                wt = self.r(self.o_w2 + sl * 1024, [128, 2, 512])
                self.dma_r(wt, w2[:, 2 * fb:2 * fb + 2, half * 512:(half + 1) * 512], self.sem_w2[sl])
                for c in range(2):
                    f = 2 * fb + c
                    for o in range(4):
                        self.mm(self.PS[4 + o][:, 0:n], wt[:, c, o * 128:(o + 1) * 128], HID[:, f, :], f == 0, f == FC - 1)
            for o in range(4):
                oc = half * 4 + o
                self.stt(self.X[:, oc, ts_], self.PS[4 + o][:, 0:n], self.GSC[:, s, oc, g:g + 1], self.X[:, oc, ts_], ALU.mult, ALU.add)
        self.ln_range(l * 3 + s, x0, n, self.o_ln)

    def ab_params(self):
        st = self.f(0, [128, 128])
        d = self.dram
        self.dma(st[0:1, :], d["norm_a"], self.sem_misc)
        self.dma(st[1:2, :], d["norm_b"], self.sem_misc)
        self.dma(st[2:6, :], d["gate_b"].rearrange("a (j p) -> (a j) p", p=128), self.sem_misc)
        self.dma(st[6:22, :], d["hgrn_lb"].rearrange("a b (h p) -> (a b h) p", p=128), self.sem_misc)
        ps = self.PS[7][:, 0:22]
        self.tr(ps, st[0:22, :])
        self.cp(self.PARS[:], ps)
        P = self.PARS
        for dr in range(2):
            a = P[:, 6 + dr * 8:6 + dr * 8 + 4]
            b = P[:, 6 + dr * 8 + 4:6 + dr * 8 + 8]
            self.tt(self.LB[:, dr * 4:dr * 4 + 4], a, b, ALU.subtract)
        self.act(self.LB[:], self.LB[:], AF.Sigmoid)
        self.ts(self.OML[:], self.LB[:], -1.0, 1.0, ALU.mult, ALU.add)
        self.ts(self.OMLH[:], self.OML[:], 0.5, None, ALU.mult)
        self.tt(self.LBH[:], self.LB[:], self.OMLH[:], ALU.add)
        self.ts(self.NGB[:], P[:, 2:6], -1.0, None, ALU.mult)

    def ab_unit_pass(self, u, dr, segs, g):
        hg = u < 4
        j = u - 4
        d = self.dram
        win = d["w_in_ab"].rearrange("(k p) n -> p k n", p=128)
        if hg:
            cols = [("q", u * 128), ("f", (1024 if dr == 0 else 1536) + u * 128), ("i", 512 + u * 128)]
            if dr == 1:
                cols.append(("g0", 2048 + u * 128))
        else:
            cols = [("q", 2560 + j * 128), ("f", 2816 + j * 128), ("v0", 3072 + j * 256), ("v1", 3072 + j * 256 + 128),
                    ("bz", 4000)]
            if dr == 1:
                cols += [("g0", 3584 + j * 256), ("g1", 3584 + j * 256 + 128)]
        W = {}
        for i, (nm, c0) in enumerate(cols):
            sl = (self.wab_base + i) % 7
            W[nm] = self.r(self.o_wab + sl * 1024, [128, KC, 128])
            self.dma_r(W[nm], win[:, :, c0:c0 + 128], self.sem_wab[sl])
        self.wab_base += len(cols)
        nh = 1 if hg else 2
        heads = list(range(nh))
        vw = 128 * nh
        NSB = 6
        SB = [self.r(self.o_sb + i * 128, [128, 128]) for i in range(NSB)]
        SC = [self.f(1536, [128, 128]), self.f(1664, [128, 128])]
        st = {"si": 0, "sc": 0}
        if not hg:
            self.ts(self.GUP[64:128, :], self.RESET[64:128, :], 0.0, None, ALU.mult)
            self.dma_r(self.GUP[96 + 16 * dr:112 + 16 * dr, :], d["gate_up"][dr], self.p.named_sem("gup"))
        st["started"] = False

        def seg_init(seg):
            (sx0, snht, slt0, ssample, ssidx) = seg
            if st["started"]:
                if dr == 0:
                    st["si"] += 1
                else:
                    st["sc"] += 1
            st["started"] = True
            src0 = None
            if ssample:
                src0 = d["st_h"][dr, u] if hg else d["st_g"][dr, 2 * j:2 * j + 2].rearrange("h d e -> (h d) e")
            if dr == 0:
                buf = SB[st["si"] % NSB]
                if ssample:
                    self.dma_r(buf, src0, self.p.named_sem("stf%d" % (st["si"] % NSB)))
                else:
                    self.ts(buf, self.IDENT[:], 0.0, None, ALU.mult)
            else:
                buf = SC[st["sc"] % 2]
                if ssample:
                    self.dma(buf, src0, self.p.named_sem("stb%d" % (st["sc"] % 2)))
                else:
                    self.memset(buf, 0.0)

        def seg_emit(seg):
            (sx0, snht, slt0, ssample, ssidx) = seg
            if ssample:
                return
            if dr == 0:
                S_fin = SB[st["si"] % NSB].bitcast(F32)
                sname = "sout%d" % (st["si"] % NSB)
            else:
                S_fin = SC[st["sc"] % 2]
                sname = "soutc%d" % (st["sc"] % 2)
            if hg:
                self.dma(d["ns_h"][ssidx, dr, u], S_fin, self.p.named_sem(sname))
            else:
                self.dma(d["ns_g"][ssidx, dr, 2 * j:2 * j + 2].rearrange("h d e -> (h d) e"), S_fin, self.p.named_sem(sname))

        HB = self.r(self.o_hb, [128, KC, 256])
        QD = self.r(self.o_qd, [128, 256])
        KI = self.r(self.o_ki, [128, 256])
        VT = self.r(self.o_vt, [128, 2, 256])
        AT = [self.r(self.o_at + i * 128, [128, 128]) for i in range(2)]
        BZ = self.r(self.o_at, [128, 256])
        SQ = self.r(self.o_at, [128, 256])
        KIT = self.r(self.o_hb + 256, [128, 2, 128])
        F0 = self.f(0, [128, 256])
        F1 = self.f(256, [128, 256])
        F2 = self.f(512, [128, 256])
        F3 = self.f(1792, [128, 256])
        TMP = self.f(768, [128, 128])
        AC = self.f(896, [128, 4])
        GS = [self.f(1024, [128, 256]), self.f(1280, [128, 256])]
        PS = self.PS
        psq, psf = PS[0][:, 0:256], PS[0][:, 256:512]
        MASK = self.MF if dr == 0 else self.MB
        order = []
        for seg in segs:
            (sx0, snht, slt0, ssample, ssidx) = seg
            hts = list(range(snht)) if dr == 0 else list(range(snht - 1, -1, -1))
            for n2, h_ in enumerate(hts):
                order.append({"x0": sx0 + h_ * 256, "lt0": slt0 + h_ * 256, "seg": seg, "first": n2 == 0, "last": n2 == snht - 1})
        bs = [0, 1] if dr == 0 else [1, 0]
        cseq = [(b, c) for b in bs for c in bs]

        def pr_(hh):
            return slice(0, 128) if hg else slice(64 * hh, 64 * hh + 64)

        def stage_a(hti, part=3):
            if part & 1:
                stage_a1(hti)
            if part & 2:
                stage_a2()

        def stage_a1(hti):
            t0 = hti["x0"]
            for k in range(KC):
                if k % 2 == 0:
                    self.act(HB[:, k, :], self.X[:, k, t0:t0 + 256], AF.Identity, bias=self.shift(1, k, g), scale=self.OPS[:, 1, k, g:g + 1])
                else:
                    self.ts(HB[:, k, :], self.X[:, k, t0:t0 + 256], self.OPS[:, 1, k, g:g + 1], self.shift(1, k, g), ALU.mult, ALU.add)
            for k in range(KC):
                self.mm(psq, W["q"][:, k, :], HB[:, k, :], k == 0, k == KC - 1)
            for k in range(KC):
                self.mm(psf, W["f"][:, k, :], HB[:, k, :], k == 0, k == KC - 1)
            if not hg:
                for k in range(KC):
                    self.mm(PS[3][:, 0:256], W["bz"][:, k, :], HB[:, k, :], k == 0, k == KC - 1)

        def stage_a2():
            for b in range(2):
                for vv in range(nh):
                    wv = W["i"] if hg else W["v%d" % vv]
                    for k in range(KC):
                        self.mm(PS[1][:, b * 256 + vv * 128:b * 256 + vv * 128 + 128], HB[:, k, b * 128:(b + 1) * 128], wv[:, k, :],
                                k == 0, k == KC - 1)
            if dr == 1:
                for hh in heads:
                    for k in range(KC):
                        self.mm(PS[2][:, hh * 256:(hh + 1) * 256], W["g%d" % hh][:, k, :], HB[:, k, :], k == 0, k == KC - 1)

        def stage_b(gsi=0):
            if hg:
                self.act(F0, psf, AF.Tanh, scale=0.5)
                self.ts(F0, F0, self.OMLH[:, dr * 4 + u:dr * 4 + u + 1], self.LBH[:, dr * 4 + u:dr * 4 + u + 1], ALU.mult, ALU.add)
                self.ts(F1, F0, -1.0, 1.0, ALU.mult, ALU.add, eng="pool")
                kf = F1
            else:
                self.cp(BZ, PS[3][:, 0:256], eng="act")
                psl = PS[3][:, 256:512]
                self.mm(psl, self.GUP[64:128, j * 128:(j + 1) * 128], BZ[64:128, :], True, True)
                self.act(F0, psl, AF.Exp, scale=-1.0, bias=self.NGB[:, dr * 2 + j:dr * 2 + j + 1])
                self.act(F0, F0, AF.Ln, bias=self.ONEC[:, 0:1])
                self.act(F0, F0, AF.Exp, scale=-1.0 / GLA_TAU)
                kf = psf
            qs = None if hg else B_DK ** -0.5
            R1 = self.RESET[:, 0:256]
            if dr == 0:
                self.p.op("dve", lambda e: e.tensor_tensor_scan(out=F2, data0=R1, data1=F0, initial=1.0, op0=ALU.max, op1=ALU.mult),
                          reads=[R1, F0], writes=[F2])
                self.cp(AC, F2[:, 63:256:64])
                if hg:
                    self.tt(QD, psq, F2, ALU.mult)
                else:
                    self.stt(QD, psq, qs, F2, ALU.mult, ALU.mult)
                self.p.op("dve", lambda e: e.reciprocal(out=F2, in_=F2), reads=[F2], writes=[F2])
                self.tt(KI, kf, F2, ALU.mult)
            else:
                self.cp(F2[:, 1:256], F0[:, 0:255])
                self.memset(F2[:, 0:256:64], 1.0)
                self.p.op("dve", lambda e: e.tensor_tensor_scan(out=F3, data0=R1, data1=F2, initial=1.0, op0=ALU.max, op1=ALU.mult),
                          reads=[R1, F2], writes=[F3])
                self.tt(AC, F3[:, 63:256:64], F0[:, 63:256:64], ALU.mult)
                self.tt(KI, kf, F3, ALU.mult)
                self.p.op("dve", lambda e: e.reciprocal(out=F3, in_=F3), reads=[F3], writes=[F3])
                if hg:
                    self.tt(QD, psq, F3, ALU.mult)
                else:
                    self.stt(QD, psq, qs, F3, ALU.mult, ALU.mult)
            for b in range(2):
                if hg:
                    self.act(VT[:, b, 0:128], PS[1][:, b * 256:b * 256 + 128], AF.Silu)
                else:
                    self.cp(VT[:, b, :], PS[1][:, b * 256:(b + 1) * 256], eng="act")
            if dr == 1:
                for hh in heads:
                    self.act(GS[(hh + gsi) % 2], PS[2][:, hh * 256:(hh + 1) * 256], AF.Silu)

        def stage_c(PSO):
            pskv = []
            for idx, (b_, c_) in enumerate(cseq):
                bank = (PS[6] if c_ == 0 else PS[3]) if hg else (PS[6] if c_ == 0 else PS[1])
                w_ = 128 if hg else 256
                slot = 0 if idx < 2 else 1
                koff = 256 if (hg and c_ == 1) else 0
                pskv.append(bank[:, koff + slot * w_:koff + (slot + 1) * w_])
            first = [True, True]
            for hi, hh in enumerate(heads):
                pat = PS[5] if hi == 0 else PS[2 if dr == 0 else 3]
                pat_off = 0 if (hi == 0 or dr == 0) else 256
                for b in bs:
                    self.mm(pat[:, pat_off + b * 128:pat_off + (b + 1) * 128], KI[pr_(hh), b * 128:(b + 1) * 128],
                            QD[pr_(hh), b * 128:(b + 1) * 128], True, True)
                if hi == 0:
                    for b in bs:
                        self.tr(PS[3][:, b * 128:(b + 1) * 128], KI[:, b * 128:(b + 1) * 128].bitcast(F32))
                for b in bs:
                    self.tt(AT[b], pat[:, pat_off + b * 128:pat_off + (b + 1) * 128], MASK[:], ALU.mult)
                if hi == 0:
                    for b in bs:
                        self.cp(KIT[:, b, :], PS[3][:, b * 128:(b + 1) * 128], eng="act")
                    for idx, (b, c) in enumerate(cseq):
                        self.mm(pskv[idx], KIT[c * 64:(c + 1) * 64, b, :], VT[c * 64:(c + 1) * 64, b, 0:vw], True, True)
                for b in bs:
                    self.mm(PSO[hh][:, b * 128:(b + 1) * 128], VT[:, b, hh * 128:(hh + 1) * 128], AT[b], first[hh], False)
                    first[hh] = False
            return pskv

        def stage_d(pskv, item):
            if item["first"]:
                seg_init(item["seg"])
            states = []
            for idx, (b, c) in enumerate(cseq):
                ci = b * 2 + c
                a_c = AC[:, ci:ci + 1]
                if dr == 0:
                    S_cur = SB[st["si"] % NSB]
                    S_next = SB[(st["si"] + 1) % NSB]
                    states.append(S_cur)
                    if hg:
                        self.tt(TMP, pskv[idx], S_cur.bitcast(F32), ALU.add)
                    else:
                        self.tt(TMP[0:64, :], pskv[idx][0:64, 0:128], S_cur[0:64, :].bitcast(F32), ALU.add)
                        self.tt(TMP[64:128, :], pskv[idx][64:128, 128:256], S_cur[64:128, :].bitcast(F32), ALU.add)
                    self.ts(S_next, TMP, a_c, None, ALU.mult)
                    st["si"] += 1
                else:
                    C_cur = SC[st["sc"] % 2]
                    C_next = SC[(st["sc"] + 1) % 2]
                    S_sc = SB[st["si"] % NSB]
                    states.append(S_sc)
                    self.ts(S_sc, C_cur, a_c, None, ALU.mult)
                    if hg:
                        self.tt(C_next, pskv[idx], S_sc.bitcast(F32), ALU.add)
                    else:
                        self.tt(C_next[0:64, :], pskv[idx][0:64, 0:128], S_sc[0:64, :].bitcast(F32), ALU.add)
                        self.tt(C_next[64:128, :], pskv[idx][64:128, 128:256], S_sc[64:128, :].bitcast(F32), ALU.add)
                    st["si"] += 1
                    st["sc"] += 1
            if item["last"]:
                seg_emit(item["seg"])
            return states

        def stage_d2(states, PSO):
            for idx, (b, c) in enumerate(cseq):
                ccols = slice(b * 128 + c * 64, b * 128 + c * 64 + 64)
                for hh in heads:
                    self.mm(PSO[hh][:, ccols], states[idx][pr_(hh), :], QD[pr_(hh), ccols], False, idx == 3)

        E1 = self.f(2560, [128, 256])
        E2 = self.f(2816, [128, 256])

        def stage_e(hti, PSO, gsi, pst):
            lt0 = hti["lt0"]
            for hh in heads:
                head = u if hg else 4 + 2 * j + hh
                on = self.r(self.o_on + head * self.on_stride + lt0, [128, 256])
                pso = PSO[hh][:, 0:256]
                if dr == 0:
                    self.cp(on, pso, eng="act")
                else:
                    self.tt(E1, pso, on.bitcast(F32), ALU.add)
                    self.act(SQ, E1, AF.Square, scale=128.0 ** -0.5)
                    self.mm(pst, self.ONES[:], SQ, True, True)
                    self.act(E2, pst, AF.Ln, bias=self.EPSR[:, 0:1])
                    self.act(E2, E2, AF.Exp, scale=-0.5)
                    nw = self.PARS[:, 0:1] if hg else self.PARS[:, 1:2]
                    self.stt(E1, E1, nw, E2, ALU.mult, ALU.mult)
                    self.tt(on, E1, GS[(hh + gsi) % 2], ALU.mult)

        if hg:
            stage_a(order[0])
            prev = None
            for n_, hti in enumerate(order):
                PSOi = [PS[4] if n_ % 2 == 0 else PS[7]]
                stage_b(n_ % 2)
                if prev is not None:
                    stage_e(*prev)
                pskv = stage_c(PSOi)
                if n_ + 1 < len(order):
                    stage_a(order[n_ + 1], 1)
                states = stage_d(pskv, hti)
                stage_d2(states, PSOi)
                if n_ + 1 < len(order):
                    stage_a(order[n_ + 1], 2)
                prev = (hti, PSOi, n_ % 2, PS[2][:, 0:256])
            stage_e(*prev)
        else:
            PSOg = [PS[4], PS[7]]
            stage_a(order[0])
            for n_, hti in enumerate(order):
                stage_b(0)
                pskv = stage_c(PSOg)
                if n_ + 1 < len(order):
                    stage_a(order[n_ + 1], 1)
                states = stage_d(pskv, hti)
                stage_d2(states, PSOg)
                stage_e(hti, PSOg, 0, PS[5][:, 0:256])
                if n_ + 1 < len(order):
                    stage_a(order[n_ + 1], 2)

    def ab_outproj(self, l, x0, lt0, n, g):
        wo = self.dram["w_out_ab"].rearrange("(h p) o -> p h o", p=128)
        for half in range(2):
            WO = self.r(self.o_wab, [128, 8, 512])
            self.dma_r(WO, wo[:, :, half * 512:(half + 1) * 512], self.sem_wab[0])
            for o in range(4):
                for h in range(8):
                    on = self.r(self.o_on + h * self.on_stride + lt0, [128, 512])[:, 0:n]
                    self.mm(self.PS[o][:, 0:n], WO[:, h, o * 128:(o + 1) * 128], on, h == 0, h == 7)
            for o in range(4):
                oc = half * 4 + o
                xs_ = self.X[:, oc, x0:x0 + n]
                self.stt(xs_, self.PS[o][:, 0:n], self.GSC[:, 1, oc, g:g + 1], xs_, ALU.mult, ALU.add)
        self.ln_range(l * 3 + 1, x0, n, self.o_hb)

    def mixer_ab(self, l):
        self.o_on = 0
        self.on_stride = 2048
        self.o_wab = 16384
        self.o_hb = 23552
        o = 25600
        self.o_qd = o
        self.o_ki = o + 256
        self.o_vt = o + 512
        self.o_at = o + 1024
        self.o_sb = o + 1280
        assert self.o_sb + 768 <= self.NR
        self.ab_params()
        abn = int(os.environ.get("MK_ABN", "1000"))
        abskip = int(os.environ.get("MK_ABSKIP", "0"))
        cnt = 0
        self.wab_base = 0
        groups = (([(0, 1, 0, False, 0), (256, 1, 256, False, 1)], 0, 0, 512),
                  ([(512, 8, 0, True, None)], 1, 512, 2048))
        for (segs, g, gx0, ntok) in groups:
            for u in range(6):
                for dr in range(2):
                    cnt += 1
                    if cnt <= abskip or cnt > abskip + abn:
                        continue
                    self.ab_unit_pass(u, dr, segs, g)
            for off in range(0, ntok, 512):
                n = min(512, ntok - off)
                self.ab_outproj(l, gx0 + off, off, n, g)

    def c_consts(self):
        d = self.dram
        A = self.f(0, [128, 128])
        B = self.f(128, [128, 128])
        for (M, cm, st, base) in ((A, 1, -1, -16), (B, -1, 1, -16)):
            self.memset(M, 1.0)
            self.p.op("pool", lambda e: e.affine_select(out=M, in_=M, pattern=[[st, 128]], compare_op=ALU.is_equal,
                                                        fill=0.0, base=base, channel_multiplier=cm),
                      reads=[M], writes=[M])
        self.memset(A.rearrange("p (g c) -> p g c", c=32)[:, :, 16:32], 0.0)
        self.memset(B.rearrange("p (g c) -> p g c", c=32)[:, :, 0:16], 0.0)
        self.tt(self.PERMR[:], A, B, ALU.add)
        for (M, cm, st) in ((self.LM1, -1, 1), (self.LM3, 1, -1)):
            self.memset(M[:], 1.0)
            self.p.op("pool", lambda e: e.affine_select(out=M[:], in_=M[:], pattern=[[st, 128]], compare_op=ALU.is_ge,
                                                        fill=0.0, base=0, channel_multiplier=cm),
                      reads=[M[:]], writes=[M[:]])
        st_ = self.f(256, [128, 16])
        self.memset(st_, 0.0)
        self.dma(st_[0:1, :], d["sink"], self.sem_misc)
        ps = self.PS[7][:, 0:16]
        onesf = self.f(384, [128, 128])
        self.memset(onesf, 1.0)
        self.mm(ps, onesf, st_, True, True)
        self.act(self.ES[:], ps, AF.Exp)
        I32 = mybir.dt.int32
        pi_ = self.f(512, [128, 1]).bitcast(I32)
        self.p.op("pool", lambda e: e.iota(pi_, pattern=[[0, 1]], base=0, channel_multiplier=1), writes=[pi_])
        i16 = self.f(513, [128, 1]).bitcast(I32)
        self.ts(i16, pi_, 15, None, ALU.bitwise_and)
        m32 = self.f(514, [128, 1]).bitcast(I32)
        self.ts(m32, pi_, 32, None, ALU.bitwise_and)
        i16f = self.f(515, [128, 1])
        self.cp(i16f, i16)
        m32f = self.f(516, [128, 1])
        self.cp(m32f, m32)
        inv = self.f(517, [128, 1])
        self.act(inv, i16f, AF.Exp, scale=-float(np.log(ROPE_BASE)) / 16.0)
        b16 = self.f(518, [128, 1]).bitcast(I32)
        self.ts(b16, pi_, 16, None, ALU.bitwise_and)
        b16f = self.f(519, [128, 1])
        self.cp(b16f, b16)
        self.ts(self.SGN[:], b16f, 1.0 / 8.0, -1.0, ALU.mult, ALU.add)
        self.ts(self.MC[:], m32f, 1.0 / 32.0, None, ALU.mult)
        self.ts(self.MR[:], self.MC[:], -1.0, 1.0, ALU.mult, ALU.add)
        pos = self.f(640, [128, 64])
        self.p.op("pool", lambda e: e.iota(pos.bitcast(I32), pattern=[[1, 64]], base=0, channel_multiplier=0), writes=[pos])
        posf = self.f(704, [128, 64])
        self.cp(posf, pos.bitcast(I32))
        TWO_PI = 2.0 * float(np.pi)
        TRC = self.f(1024, [128, 64])
        TRS = self.f(1088, [128, 64])
        for (posoff, dsin, dcos) in ((0.0, self.TSIN[:], self.TCOS[:]), (-2.0, TRS, TRC)):
            ang = self.f(768, [128, 64])
            self.ts(ang, posf, posoff, None, ALU.add)
            self.ts(ang, ang, inv, None, ALU.mult)
            for (dst, shiftv) in ((dsin, 0.0), (dcos, float(np.pi) / 2.0)):
                a = self.f(832, [128, 64])
                kq = self.f(896, [128, 64])
                ki = self.f(960, [128, 64]).bitcast(I32)
                self.ts(a, ang, shiftv, None, ALU.add)
                self.ts(kq, a, 1.0 / TWO_PI, None, ALU.mult)
                self.cp(ki, kq)
                self.cp(kq, ki)
                self.stt(a, kq, -TWO_PI, a, ALU.mult, ALU.add)
                self.ts(kq, a, float(np.pi), -TWO_PI, ALU.is_gt, ALU.mult)
                self.tt(a, a, kq, ALU.add)
                self.ts(kq, a, -float(np.pi), TWO_PI, ALU.is_lt, ALU.mult)
                self.tt(a, a, kq, ALU.add)
                self.act(dst, a, AF.Sin)
        for (ci, T) in ((0, TRC), (1, TRS)):
            for jq in range(4):
                if jq == 0:
                    self.ts(self.TRW[:, ci, :], T[:, 0:12], self.SELV[:, 0:1], None, ALU.mult)
                else:
                    self.stt(self.TRW[:, ci, :], T[:, 8 * jq:8 * jq + 12], self.SELV[:, jq:jq + 1], self.TRW[:, ci, :], ALU.mult, ALU.add)

    def rope_tables_w(self, r0, nrows):
        n = nrows * 64
        cos_t = self.f(0, [128, 512])[:, 0:n]
        sin_t = self.f(512, [128, 512])[:, 0:n]
        for (dst, ci, T) in ((cos_t, 0, self.TCOS), (sin_t, 1, self.TSIN)):
            dv = dst.rearrange("p (r c) -> p r c", c=64)
            rowv = self.TRW[:, ci, r0:r0 + nrows].unsqueeze(2).to_broadcast([128, nrows, 64])
            colv = T[:, 0:64].unsqueeze(1).to_broadcast([128, nrows, 64])
            self.ts(dv, rowv, self.MR[:, 0:1], None, ALU.mult)
            self.stt(dv, colv, self.MC[:, 0:1], dv, ALU.mult, ALU.add)
        self.ts(sin_t, sin_t, self.SGN[:, 0:1], None, ALU.mult)
        return cos_t, sin_t

    def rope_tables(self, t):
        cos_t = self.f(0, [128, 512])
        sin_t = self.f(512, [128, 512])
        for (dst, T) in ((cos_t, self.TCOS), (sin_t, self.TSIN)):
            dv = dst.rearrange("p (r c) -> p r c", c=64)
            rowv = T[:, 8 * t:8 * t + 8].unsqueeze(2).to_broadcast([128, 8, 64])
            colv = T[:, 0:64].unsqueeze(1).to_broadcast([128, 8, 64])
            self.ts(dv, rowv, self.MR[:, 0:1], None, ALU.mult)
            self.stt(dv, colv, self.MC[:, 0:1], dv, ALU.mult, ALU.add)
        self.ts(sin_t, sin_t, self.SGN[:, 0:1], None, ALU.mult)
        return cos_t, sin_t

    def rope_apply(self, dst, ps, cos_t, sin_t, n):
        ZQ = self.r(self.o_zq, [128, 512])[:, 0:n]
        self.cp(ZQ, ps, eng="act")
        pz = self.PS[6][:, 0:n]
        self.mm(pz, self.PERMR[:], ZQ, True, True)
        t1 = self.f(1024, [128, 512])[:, 0:n]
        self.tt(t1, ZQ.bitcast(F32), cos_t[:, 0:n], ALU.mult)
        t2 = self.f(1536, [128, 512])[:, 0:n]
        self.tt(t2, pz, sin_t[:, 0:n], ALU.mult)
        self.tt(dst, t1, t2, ALU.add)

    def c_modulate(self, x0, n, g):
        HB = self.r(self.o_hb, [128, KC, 512])
        for k in range(KC):
            self.ts(HB[:, k, 0:n], self.X[:, k, x0:x0 + n], self.OPS[:, 1, k, g:g + 1], self.shift(1, k, g), ALU.mult, ALU.add)
        return HB

    def va_slot(self, kap):
        return [(0, 0, 64), (64, 2, 0), (130, 0, 64), (194, 2, 0)][kap]

    def c_kv_project(self, HB, n, WKV, kf_dst, va_dst, vblk0, rope, emit=None):
        for kp in range(2):
            ps = self.PS[kp][:, 0:n]
            for k in range(KC):
                self.mm(ps, WKV[:, k, kp * 128:(kp + 1) * 128], HB[:, k, 0:n], k == 0, k == KC - 1)
            if rope is not None:
                self.rope_apply(kf_dst(kp), ps, rope[0], rope[1], n)
            else:
                self.cp(kf_dst(kp), ps, eng="act")
        for b in range(n // 128):
            ps = self.PS[2 + b % 2][:, 0:256]
            for k in range(KC):
                self.mm(ps, HB[:, k, b * 128:(b + 1) * 128], WKV[:, k, 256:512], k == 0, k == KC - 1)
            va = va_dst(vblk0 + b)
            self.cp(va[:, 0:132].rearrange("p (a c) -> p a c", c=66)[:, :, 0:64], ps[:, 0:128].rearrange("p (a c) -> p a c", c=64), eng="act")
            self.cp(va[:, 130:262].rearrange("p (a c) -> p a c", c=66)[:, :, 0:64], ps[:, 128:256].rearrange("p (a c) -> p a c", c=64), eng="act")
            self.ts(va[:, 64:66], self.RESET[:, 1:3], 0.0, 1.0, ALU.mult, ALU.add)
            self.ts(va[:, 194:196], self.RESET[:, 1:3], 0.0, 1.0, ALU.mult, ALU.add)
            if emit is not None:
                sq, tok0 = emit
                stv = self.f(2048 + (b % 2) * 256, [128, 256])
                self.cp(stv, ps)
                self.dma(self.dram["nv"][sq, tok0 + b * 128:tok0 + (b + 1) * 128, :], stv, self.p.named_sem("nv%d" % (b % 2)))
                ps2 = self.PS[4 + b % 2][:, 0:256]
                for k in range(KC):
                    self.mm(ps2, HB[:, k, b * 128:(b + 1) * 128], WKV[:, k, 0:256], k == 0, k == KC - 1)
                stk = self.f(2560 + (b % 2) * 256, [128, 256])
                self.cp(stk, ps2, eng="act")
                self.dma(self.dram["nk"][sq, tok0 + b * 128:tok0 + (b + 1) * 128, :], stk, self.p.named_sem("nk%d" % (b % 2)))

    def c_q_project(self, HB, n, rope):
        wq = self.dram["w_qkv"].rearrange("(k p) n -> p k n", p=128)
        QF = self.r(self.o_qf, [128, 8, 512])
        WS = self.r(self.o_ws, [128, KC, 4, 128])
        for grp in range(2):
            for s_ in range(2):
                for i in range(4):
                    c0 = grp * 512 + s_ * 256 + i * 64
                    self.dma_r(WS[:, :, i, s_ * 64:(s_ + 1) * 64], wq[:, :, c0:c0 + 64], self.p.named_sem("wq%d_%d" % (s_, i)))
            for i in range(4):
                pair = grp * 4 + i
                ps = self.PS[pair % 2][:, 0:n]
                for k in range(KC):
                    self.mm(ps, WS[:, k, i, :], HB[:, k, 0:n], k == 0, k == KC - 1)
                if rope is not None:
                    self.rope_apply(QF[:, pair, 0:n], ps, rope[0], rope[1], n)
                else:
                    self.cp(QF[:, pair, 0:n], ps, eng="act")
        return QF

    def c_attend(self, QF, nqb, keysets):
        OT = self.r(self.o_ot, [128, 4, 1024])
        NPT = 4
        PT = [self.r(self.o_ws + i * 512, [128, 512]) for i in range(NPT)]
        SCB = [self.PS[0], self.PS[1], self.PS[7]]
        tasks = []
        for h in range(C_HEADS):
            for ki, ks in enumerate(keysets):
                tasks.append((h, ki, ks))
        nk_for = [sum(1 for ks in keysets if ks[2] <= qb <= ks[3]) for qb in range(nqb)]

        def hinfo(h):
            kap = h // 4
            half = kap % 2
            pair = (h % 4) + (0 if h < 8 else 4)
            return kap, half, pair, slice(64 * half, 64 * half + 64)

        def score(i):
            h, ki, (kf_fn, va_fn, qlo, qhi, masks, halo) = tasks[i]
            kap, half, pair, hp = hinfo(h)
            ncol = (qhi - qlo + 1) * 128
            ps = SCB[i % 3][:, 0:ncol]
            pt = PT[i % NPT][:, 0:ncol]
            self.mm(ps, kf_fn(kap, half), QF[hp, pair, qlo * 128:qlo * 128 + ncol], True, True)
            self.act(pt, ps, AF.Exp, scale=HD ** -0.5)
            for qb in range(qlo, qhi + 1):
                if qb in masks:
                    sl = slice((qb - qlo) * 128, (qb - qlo + 1) * 128)
                    if halo is None:
                        self.tt(pt[:, sl], pt[:, sl].bitcast(F32), masks[qb][:], ALU.mult)
                    else:
                        self.stt(pt[:, sl], pt[:, sl].bitcast(F32), halo, masks[qb][:], ALU.mult, ALU.mult)

        seen = {}

        def pv(i):
            h, ki, (kf_fn, va_fn, qlo, qhi, masks, halo) = tasks[i]
            kap, half, pair, hp = hinfo(h)
            c0, o_off, d_off = self.va_slot(kap)
            ncol = (qhi - qlo + 1) * 128
            pt = PT[i % NPT][:, 0:ncol]
            for qb in range(qlo, qhi + 1):
                sl = slice((qb - qlo) * 128, (qb - qlo + 1) * 128)
                seen[(h, qb)] = seen.get((h, qb), 0) + 1
                self.mm(self.PS[2 + qb][:, 0:66], pt[:, sl], va_fn(kap)[:, c0:c0 + 66], seen[(h, qb)] == 1, seen[(h, qb)] == nk_for[qb])
            if ki == len(keysets) - 1:
                for qb in range(nqb):
                    po = self.PS[2 + qb]
                    rd = self.f(2048 + 16 * qb, [128, 1])
                    self.ts(rd, po[:, d_off:d_off + 1], self.ES[:, h:h + 1], None, ALU.add)
                    self.p.op("dve", lambda e: e.reciprocal(out=rd, in_=rd), reads=[rd], writes=[rd])
                    self.ts(OT[:, qb, h * 64:(h + 1) * 64], po[:, o_off:o_off + 64], rd, None, ALU.mult)

        LOOK = 2
        for i in range(min(LOOK, len(tasks))):
            score(i)
        for i in range(len(tasks)):
            if i + LOOK < len(tasks):
                score(i + LOOK)
            pv(i)
        return OT

    def c_outproj(self, l, OT, x0, nqb, g):
        n = nqb * 128
        OA = self.r(self.o_hb, [128, KC, 512])
        for qb in range(nqb):
            for hb in range(2):
                ps = self.PS[hb]
                for j in range(4):
                    c = hb * 4 + j
                    self.tr(ps[:, j * 128:(j + 1) * 128], OT[:, qb, c * 128:(c + 1) * 128].bitcast(F32))
                self.cp(OA[:, hb * 4:(hb + 1) * 4, qb * 128:(qb + 1) * 128], ps[:].rearrange("p (a b) -> p a b", a=4),
                        eng="act" if hb else "dve")
        wo = self.dram["w_out_c"].rearrange("(c p) o -> p c o", p=128)
        for half in range(2):
            WO = self.r(self.o_qf, [128, KC, 512])
            self.dma_r(WO, wo[:, :, half * 512:(half + 1) * 512], self.p.named_sem("woc"))
            for o in range(4):
                for c in range(KC):
                    self.mm(self.PS[4 + o][:, 0:n], WO[:, c, o * 128:(o + 1) * 128], OA[:, c, 0:n], c == 0, c == KC - 1)
            for o in range(4):
                oc = half * 4 + o
                xs_ = self.X[:, oc, x0:x0 + n]
                self.stt(xs_, self.PS[4 + o][:, 0:n], self.GSC[:, 1, oc, g:g + 1], xs_, ALU.mult, ALU.add)
        self.ln_range(l * 3 + 1, x0, n, self.o_ws)

    def mixer_c(self, l):
        d = self.dram
        self.o_kf = 0
        self.o_va = 4096
        self.o_kc = self.o_va + 16 * 262
        self.o_vca = self.o_kc + 1024
        self.o_hb = self.o_vca + 4 * 262
        self.o_ws = self.o_hb + 4096
        self.o_qf = self.o_ws + 4096
        self.o_ot = self.o_qf + 4096
        self.o_zq = self.o_ot + 4096
        assert self.o_zq + 512 <= self.NR, self.o_zq
        self.c_consts()
        KF = self.r(self.o_kf, [128, 2, 2048])
        VA = self.r(self.o_va, [128, 16, 262])
        KCF = self.r(self.o_kc, [128, 2, 512])
        VCA = self.r(self.o_vca, [128, 4, 262])
        wq = d["w_qkv"].rearrange("(k p) n -> p k n", p=128)
        WKV = self.r(self.o_ws, [128, KC, 512])

        def load_wkv():
            self.dma_r(WKV, wq[:, :, 1024:1536], self.p.named_sem("wkv"))
        for sq in range(2):
            x0 = sq * SEQ
            HB = self.c_modulate(x0, SEQ, 0)
            load_wkv()
            self.c_kv_project(HB, SEQ, WKV, lambda kp: KF[:, kp, 0:SEQ], lambda b: VA[:, b, :], 0, None, emit=(sq, 0))
            QF = self.c_q_project(HB, SEQ, None)
            keysets = []
            for kb in range(2):
                keysets.append((lambda kap, half, kb=kb: KF[64 * half:64 * half + 64, kap // 2, kb * 128:(kb + 1) * 128],
                                lambda kap, kb=kb: VA[:, kb, :], 0, 1, {}, None))
            OT = self.c_attend(QF, 2, keysets)
            self.c_outproj(l, OT, x0, 2, 0)
        stg = self.r(self.o_ot, [128, 4, 256])
        self.dma_r(stg, d["ck"].rearrange("(b p) c -> p b c", p=128), self.p.named_sem("ck"))
        for b in range(4):
            for kp in range(2):
                ps = self.PS[(b * 2 + kp) % 2][:, 0:128]
                self.tr(ps, stg[:, b, kp * 128:(kp + 1) * 128].bitcast(F32))
                self.cp(KCF[:, kp, b * 128:(b + 1) * 128], ps, eng="act" if kp else "dve")
        cvv = d["cv"].rearrange("(b p) c -> p b c", p=128)
        for kap in range(4):
            c0 = [0, 66, 130, 196][kap]
            self.dma_r(VCA[:, :, c0:c0 + 64], cvv[:, :, kap * 64:(kap + 1) * 64], self.p.named_sem("cv%d" % kap))
        for b in range(4):
            self.ts(VCA[:, b, 64:66], self.RESET[:, 1:3], 0.0, 1.0, ALU.mult, ALU.add)
            self.ts(VCA[:, b, 194:196], self.RESET[:, 1:3], 0.0, 1.0, ALU.mult, ALU.add)
        load_wkv()
        W0 = self.W0
        for (off, n_, r0) in ((0, 512, 0), (512, 256, 8)):
            HB = self.c_modulate(W0 + off, n_, 1)
            rope = self.rope_tables_w(r0, n_ // 64)
            self.c_kv_project(HB, n_, WKV, lambda kp, off=off, n_=n_: KF[:, kp, off:off + n_], lambda b: VA[:, b, :], off // 128, rope)
        HB = self.c_modulate(W0 + 128, 512, 1)
        rope = self.rope_tables_w(2, 8)
        QF = self.c_q_project(HB, 512, rope)
        keysets = []
        for cb in range(4):
            keysets.append((lambda kap, half, cb=cb: KCF[64 * half:64 * half + 64, kap // 2, cb * 128:(cb + 1) * 128],
                            lambda kap, cb=cb: VCA[:, cb, :], 0, 3, {}, None))
        for kb in range(6):
            qlo = max(1, kb - 1) - 1
            qhi = min(4, kb + 1) - 1
            masks = {}
            if 0 <= kb - 2 <= 3:
                masks[kb - 2] = self.LM1
            if 0 <= kb <= 3:
                masks[kb] = self.LM3
            halo = None
            if kb == 0:
                halo = self.SELV[:, 12:13]
            if kb == 5:
                halo = self.SELV[:, 13:14]
            keysets.append((lambda kap, half, kb=kb: KF[64 * half:64 * half + 64, kap // 2, kb * 128:(kb + 1) * 128],
                            lambda kap, kb=kb: VA[:, kb, :], qlo, qhi, masks, halo))
        OT = self.c_attend(QF, 4, keysets)
        self.c_outproj(l, OT, W0 + 128, 4, 1)

    def build(self):
        self.o_w13 = 0
        self.o_w2 = 8192
        self.o_xm = 12288
        self.o_hid = 16384
        self.o_ln = self.o_hid + 18 * 512
        self.o_sg = 0
        self.o_lnf = 1024
        self.consts()
        self.EPSLN = self.sb("EPSLN", [128, 1])
        self.memset(self.EPSLN[:], LN_EPS / (ALPHA * ALPHA))
        self.ONEC = self.sb("ONEC", [128, 1])
        self.memset(self.ONEC[:], 1.0)
        self.load_small()
        self.dma(self.SELV[:], self.dram["selv"], self.p.named_sem("selv"))
        self.load_x()
        self.W0 = TP
        full = [(t * 512, 512, 0 if t == 0 else 1) for t in range(NT)]
        win = [(0, 512, 0), (self.W0, 512, 1), (self.W0 + 512, 256, 1)]
        own = [(0, 512, 0), (self.W0 + 128, 512, 1)]
        for l in range(DEPTH):
            self.mod_vectors(l)
            for (x0, n, g) in (full if l == 0 else win):
                self.ffn_tile(l, 0, x0, n, g)
            if self.stage == 1 + 3 * l:
                break
            if l == 0:
                self.mixer_ab(l)
                self.select_window()
            else:
                self.mixer_c(l)
            if self.stage == 2 + 3 * l:
                break
            for (x0, n, g) in (win if l == 0 else own):
                self.ffn_tile(l, 1, x0, n, g)
            if self.stage == 3 + 3 * l:
                break
        self.store_x()
        self.p.finish("sp")
        return self.nc


_CACHE = {}


def get_program(stage=99):
    if stage not in _CACHE:
        _CACHE[stage] = Builder(stage).build()
    return _CACHE[stage]


def _selv(j):
    v = np.zeros((16,), np.float32)
    v[j] = 1.0
    if j > 0:
        v[4 + j - 1] = 1.0
        v[12] = 1.0
    if j < 3:
        v[8 + j + 1] = 1.0
        v[13] = 1.0
    return np.ascontiguousarray(np.broadcast_to(v, (128, 16))).astype(np.float32)


def shard_inputs(inp):
    f = lambda a: np.ascontiguousarray(np.asarray(a, dtype=np.float32))
    maps = []
    for c in range(NCORES):
        sb = c // 4
        m = {
            "xp": f(inp["x_prompt"][2 * c:2 * c + 2].reshape(TP, D)),
            "xs": f(inp["x_sample"][sb]),
            "st_h": f(inp["state_hgrn"][sb, 0]),
            "st_g": f(inp["state_gla"][sb, 0]),
            "ck": f(inp["cache_k"][sb, 0].reshape(PAST, C_KV * HD)),
            "cv": f(inp["cache_v"][sb, 0].reshape(PAST, C_KV * HD)),
            "cvec": f(np.stack([inp["c_ctx"], inp["c"][sb]], axis=0)),
            "selv": _selv(c % 4),
            "w_mod": f(inp["w_mod"]),
            "b_mod": f(inp["b_mod"]),
            "ln_g": f(inp["ln_g"].reshape(DEPTH * 3, D)),
            "ln_b": f(inp["ln_b"].reshape(DEPTH * 3, D)),
            "ffn_w1": f(inp["ffn_w1"]),
            "ffn_w3": f(inp["ffn_w3"]),
            "ffn_w2": f(inp["ffn_w2"]),
            "w_in_ab": f(inp["w_in_ab"][0]),
            "hgrn_lb": f(inp["hgrn_lb"]),
            "gate_up": f(inp["gla_gate_up"][0]),
            "gate_b": f(inp["gla_gate_b"][0]),
            "norm_a": f(inp["norm_a"]),
            "norm_b": f(inp["norm_b"]),
            "w_out_ab": f(inp["w_out_ab"][0]),
            "w_qkv": f(inp["w_qkv_c"][0]),
            "sink": f(inp["sink_c"]),
            "w_out_c": f(inp["w_out_c"][0]),
        }
        maps.append(m)
    return maps


def gather_outputs(res):
    r = res.results
    B = 16
    yp = np.concatenate([r[c]["yp"].reshape(2, SEQ, D) for c in range(NCORES)], axis=0)
    ys = np.stack([np.concatenate([r[4 * b + q]["ys"] for q in range(4)], axis=0) for b in range(2)], axis=0)
    nsh = np.concatenate([r[c]["ns_h"].reshape(2, 1, 2, A_HEADS, 128, 128) for c in range(NCORES)], axis=0)
    nsg = np.concatenate([r[c]["ns_g"].reshape(2, 1, 2, B_HEADS, B_DK, 128) for c in range(NCORES)], axis=0)
    nk = np.concatenate([r[c]["nk"].reshape(2, 1, SEQ, C_KV, HD) for c in range(NCORES)], axis=0)
    nv = np.concatenate([r[c]["nv"].reshape(2, 1, SEQ, C_KV, HD) for c in range(NCORES)], axis=0)
    return (yp.astype(np.float32), ys.astype(np.float32), nsh.astype(np.float32), nsg.astype(np.float32),
            nk.astype(np.float32), nv.astype(np.float32))


def kernel(**inputs):
    stage = int(os.environ.get("MK_STAGE", "99"))
    nc = get_program(stage)
    maps = shard_inputs(inputs)
    res = run_bass_kernel_spmd(nc, maps, core_ids=list(range(NCORES)))
    return gather_outputs(res)
```

```python
import os
import numpy as np
import concourse.bass as bass
import concourse.mybir as mybir
from concourse.bass_utils import run_bass_kernel_spmd

F32 = mybir.dt.float32
F32R = mybir.dt.float32r
AF = mybir.ActivationFunctionType
ALU = mybir.AluOpType

D = 1024
KC = 8
DFF = 2816
FC = 22
NMOD = 9
DEPTH = 2
SEQ = 256
TP = 512
TS = 2048
T = TP + TS
NT = T // 512
A_HEADS = 4
B_HEADS = 4
B_DK = 64
GATE_RANK = 16
GLA_TAU = 16.0
AB_IN = 4128
C_HEADS = 16
C_KV = 4
HD = 64
PAST = 512
ALPHA = (2.0 * DEPTH) ** 0.25
LN_EPS = 1e-5
RMS_EPS = 1e-6
ROPE_BASE = 10000.0
NCORES = 8


class Prog:
    def __init__(self, nc):
        self.nc = nc
        self.E = {"pe": nc.tensor, "act": nc.scalar, "dve": nc.vector, "pool": nc.gpsimd, "sp": nc.sync}
        self.sems = {}
        self.cnt = {}
        for e in self.E:
            self.sems[e] = nc.alloc_semaphore("s_" + e)
            self.cnt[e] = 0
        self.seen = {e: {} for e in self.E}
        self.ndma_sem = 0
        self.n_inst = 0
        self.n_wait = 0
        self.self_sync = set(os.environ.get("MK_SELFSYNC", "act,dve,pool").split(",")) - {""}
        self.named = {}
        self.mem = {}

    def named_sem(self, name):
        if name not in self.named:
            self.named[name] = self.new_dma_sem()
        return self.named[name]

    def new_dma_sem(self):
        k = "d%d" % self.ndma_sem
        self.ndma_sem += 1
        self.sems[k] = self.nc.alloc_semaphore("s_" + k)
        self.cnt[k] = 0
        return k

    @staticmethod
    def box(ap):
        esz = mybir.dt.size(ap.dtype)
        dims = ap.ap
        off = ap.offset
        name = ap.tensor.name
        if str(ap.space) == "DRAM":
            lo = off
            hi = off
            for st, cn in dims:
                d = (cn - 1) * st
                if d > 0:
                    hi += d
                else:
                    lo += d
            return name, 0, 1, lo * esz, (hi + 1) * esz
        pst, pcn = dims[0]
        if pst <= 0:
            pst = 1 << 40
        p0 = off // pst
        c0 = off % pst
        if str(ap.space) == "PSUM":
            q0 = (p0 // 32) * 32
            q1 = ((p0 + pcn + 31) // 32) * 32
            return name, q0, q1, 0, 2048
        lo = c0
        hi = c0
        for st, cn in dims[1:]:
            d = (cn - 1) * st
            if d > 0:
                hi += d
            else:
                lo += d
        return name, p0, p0 + pcn, lo * esz, (hi + 1) * esz

    def _collect(self, reads, writes):
        deps = []
        rb = [self.box(a) for a in reads]
        wb = [self.box(a) for a in writes]
        for (name, p0, p1, lo, hi) in rb:
            m = self.mem.get(name)
            if m is None:
                continue
            for r in m[0]:
                if r[0] < p1 and p0 < r[1] and r[2] < hi and lo < r[3]:
                    deps.append((r[4], r[5]))
        for (name, p0, p1, lo, hi) in wb:
            m = self.mem.get(name)
            if m is None:
                continue
            for lst in m:
                for r in lst:
                    if r[0] < p1 and p0 < r[1] and r[2] < hi and lo < r[3]:
                        deps.append((r[4], r[5]))
        return deps, rb, wb

    def _record(self, rb, wb, key, val):
        for (name, p0, p1, lo, hi) in wb:
            m = self.mem.setdefault(name, [[], []])
            for i in (0, 1):
                m[i] = [r for r in m[i] if not (p0 <= r[0] and r[1] <= p1 and lo <= r[2] and r[3] <= hi)]
            m[0].append([p0, p1, lo, hi, key, val])
        for (name, p0, p1, lo, hi) in rb:
            m = self.mem.setdefault(name, [[], []])
            m[1] = [r for r in m[1] if not (r[4] == key and p0 <= r[0] and r[1] <= p1 and lo <= r[2] and r[3] <= hi)]
            m[1].append([p0, p1, lo, hi, key, val])

    def _wait(self, e, deps):
        best = {}
        for k, v in deps:
            if best.get(k, 0) < v:
                best[k] = v
        for k, v in best.items():
            if k == e and e not in self.self_sync:
                continue
            if self.seen[e].get(k, 0) < v:
                self.E[e].wait_ge(self.sems[k], v)
                self.seen[e][k] = v
                self.n_wait += 1

    def op(self, e, fn, reads=(), writes=()):
        deps, rb, wb = self._collect(reads, writes)
        self._wait(e, deps)
        ins = fn(self.E[e])
        ins.then_inc(self.sems[e], 1)
        self.cnt[e] += 1
        self._record(rb, wb, e, self.cnt[e])
        self.n_inst += 1
        return ins

    def dma(self, q, out, in_, sem):
        deps, rb, wb = self._collect([in_], [out])
        self._wait(q, deps)
        ins = self.E[q].dma_start(out=out, in_=in_)
        ins.then_inc(self.sems[sem], 16)
        self.cnt[sem] += 16
        self._record(rb, wb, sem, self.cnt[sem])
        self.n_inst += 1
        return ins

    def finish(self, e="sp"):
        deps = []
        for name, m in self.mem.items():
            for lst in m:
                for r in lst:
                    deps.append((r[4], r[5]))
        self._wait(e, deps)


class Builder:
    def __init__(self, stage=99):
        self.stage = stage
        nc = bass.Bass("TRN2", target_bir_lowering=False)
        nc.dge_precook = False
        self.nc = nc
        self.p = Prog(nc)
        self.dram = {}
        self.decl_io()
        self.alloc()

    def din(self, name, shape):
        self.dram[name] = self.nc.dram_tensor(name, list(shape), F32, kind="ExternalInput").ap()
        return self.dram[name]

    def dout(self, name, shape):
        self.dram[name] = self.nc.dram_tensor(name, list(shape), F32, kind="ExternalOutput").ap()
        return self.dram[name]

    def decl_io(self):
        self.din("xp", [TP, D])
        self.din("xs", [TS, D])
        self.din("st_h", [2, A_HEADS, 128, 128])
        self.din("st_g", [2, B_HEADS, B_DK, 128])
        self.din("ck", [PAST, C_KV * HD])
        self.din("cv", [PAST, C_KV * HD])
        self.din("cvec", [2, D])
        self.din("selv", [128, 16])
        self.din("w_mod", [DEPTH, D, NMOD * D])
        self.din("b_mod", [DEPTH, NMOD * D])
        self.din("ln_g", [DEPTH * 3, D])
        self.din("ln_b", [DEPTH * 3, D])
        self.din("ffn_w1", [DEPTH, 2, D, DFF])
        self.din("ffn_w3", [DEPTH, 2, D, DFF])
        self.din("ffn_w2", [DEPTH, 2, DFF, D])
        self.din("w_in_ab", [D, AB_IN])
        self.din("hgrn_lb", [2, 2, 512])
        self.din("gate_up", [2, GATE_RANK, 256])
        self.din("gate_b", [2, 256])
        self.din("norm_a", [1, 128])
        self.din("norm_b", [1, 128])
        self.din("w_out_ab", [D, D])
        self.din("w_qkv", [D, 1536])
        self.din("sink", [1, C_HEADS])
        self.din("w_out_c", [D, D])
        self.dout("yp", [TP, D])
        self.dout("ys", [512, D])
        self.dout("ns_h", [2, 2, A_HEADS, 128, 128])
        self.dout("ns_g", [2, 2, B_HEADS, B_DK, 128])
        self.dout("nk", [2, SEQ, C_KV * HD])
        self.dout("nv", [2, SEQ, C_KV * HD])

    def sb(self, name, shape, dt=F32):
        return self.nc.alloc_sbuf_tensor(name, list(shape), dt)

    def alloc(self):
        nc = self.nc
        self.X = self.sb("X", [128, KC, T])
        self.IDENT = self.sb("IDENT", [128, 128])
        self.ONES = self.sb("ONES", [128, 128], F32R)
        self.U1 = self.sb("U1", [128, 256])
        self.U2 = self.sb("U2", [128, 256], F32R)
        self.MF = self.U1[:, 0:128]
        self.MB = self.U1[:, 128:256]
        self.LM1 = self.U1[:, 0:128]
        self.LM3 = self.U1[:, 128:256]
        self.RESET = self.sb("RESET", [128, 256])
        self.PARS = self.sb("PARS", [128, 22])
        self.LB = self.sb("LB", [128, 8])
        self.OML = self.sb("OML", [128, 8])
        self.OMLH = self.sb("OMLH", [128, 8])
        self.LBH = self.sb("LBH", [128, 8])
        self.SELV = self.sb("SELV", [128, 16])
        self.TRW = self.sb("TRW", [128, 2, 12])
        self.NGB = self.sb("NGB", [128, 4])
        self.GUP = self.U2[:, 0:256]
        self.PERMR = self.U2[:, 0:128]
        self.EPSR = self.sb("EPSR", [128, 1])
        self.ES = self.sb("ES", [128, 16])
        self.TCOS = self.sb("TCOS", [128, 64])
        self.TSIN = self.sb("TSIN", [128, 64])
        self.MR = self.sb("MR", [128, 1])
        self.MC = self.sb("MC", [128, 1])
        self.SGN = self.sb("SGN", [128, 1])
        self.MODV = self.sb("MODV", [128, 72, 2])
        self.OPS = self.sb("OPS", [128, 3, KC, 2])
        self.GSC = self.sb("GSC", [128, 3, KC, 2])
        self.BM = self.sb("BM", [128, 72])
        self.LNG = self.sb("LNG", [128, 48])
        self.LNB = self.sb("LNB", [128, 48])
        self.CS = self.sb("CS", [128, 2, KC], F32R)
        self.NR = 27 * 1024
        self.NF = 3072
        self.R = self.sb("R", [128, self.NR], F32R)
        self.Fm = self.sb("Fm", [128, self.NF])
        self.PS = [nc.alloc_psum_tensor("PS%d" % i, [128, 512], F32) for i in range(8)]
        self.sem_w13 = [self.p.new_dma_sem() for _ in range(2)]
        self.sem_w2 = [self.p.new_dma_sem() for _ in range(4)]
        self.sem_wab = [self.p.new_dma_sem() for _ in range(7)]
        self.sem_st = self.p.new_dma_sem()
        self.sem_io = [self.p.new_dma_sem() for _ in range(2)]
        self.sem_misc = self.p.new_dma_sem()
        self.sem_out = self.p.new_dma_sem()
        self.n13 = 0
        self.n2 = 0
        self.nio = 0

    @staticmethod
    def _view(base, off, shape, total):
        n = 1
        for x in shape[1:]:
            n *= x
        assert off + n <= total, (off, n, total)
        v = base[0:shape[0], off:off + n]
        if len(shape) == 3:
            v = v.rearrange("p (a b) -> p a b", a=shape[1])
        elif len(shape) == 4:
            v = v.rearrange("p (a b c) -> p a b c", a=shape[1], b=shape[2])
        return v

    def r(self, off, shape):
        return self._view(self.R, off, shape, self.NR)

    def f(self, off, shape):
        return self._view(self.Fm, off, shape, self.NF)

    def mm(self, out, lhsT, rhs, start, stop):
        self.p.op("pe", lambda e: e.matmul(out, lhsT=lhsT, rhs=rhs, start=start, stop=stop),
                  reads=[lhsT, rhs], writes=[out])

    def tr(self, out, in_, n=128):
        ident = self.IDENT[0:in_.shape[0], 0:in_.shape[0]]
        self.p.op("pe", lambda e: e.transpose(out=out, in_=in_, identity=ident), reads=[in_, ident], writes=[out])

    def act(self, out, in_, func, bias=None, scale=None, eng="act"):
        kw = {}
        rd = [in_]
        if bias is not None:
            kw["bias"] = bias
            if not isinstance(bias, (int, float)):
                rd.append(bias)
        if scale is not None:
            kw["scale"] = scale
            if not isinstance(scale, (int, float)):
                rd.append(scale)
        self.p.op("act", lambda e: e.activation(out=out, in_=in_, func=func, **kw), reads=rd, writes=[out])

    def ts(self, out, in0, s1, s2, op0, op1=None, eng="dve"):
        rd = [in0]
        for s in (s1, s2):
            if s is not None and not isinstance(s, (int, float)):
                rd.append(s)
        if op1 is None:
            self.p.op(eng, lambda e: e.tensor_scalar(out=out, in0=in0, scalar1=s1, scalar2=None, op0=op0), reads=rd, writes=[out])
        else:
            self.p.op(eng, lambda e: e.tensor_scalar(out=out, in0=in0, scalar1=s1, scalar2=s2, op0=op0, op1=op1), reads=rd, writes=[out])

    def tt(self, out, in0, in1, op, eng="dve"):
        self.p.op(eng, lambda e: e.tensor_tensor(out=out, in0=in0, in1=in1, op=op), reads=[in0, in1], writes=[out])

    def stt(self, out, in0, scalar, in1, op0, op1, eng="dve"):
        rd = [in0, in1]
        if not isinstance(scalar, (int, float)):
            rd.append(scalar)
        self.p.op(eng, lambda e: e.scalar_tensor_tensor(out=out, in0=in0, scalar=scalar, in1=in1, op0=op0, op1=op1), reads=rd, writes=[out])

    def cp(self, out, in_, eng="dve"):
        if eng == "act":
            self.act(out, in_, AF.Copy)
        else:
            self.p.op(eng, lambda e: e.tensor_copy(out=out, in_=in_), reads=[in_], writes=[out])

    def memset(self, ap, val, eng="pool"):
        self.p.op(eng, lambda e: e.memset(ap, val), writes=[ap])

    def dma(self, out, in_, sem, q="sp"):
        self.p.dma(q, out, in_, sem)

    def dma_r(self, out, in_, sem, q="sp"):
        self.p.dma(q, out if out.dtype == F32R else out.bitcast(F32R), in_.bitcast(F32R), sem)

    def consts(self):
        self.memset(self.IDENT[:], 1.0)
        self.p.op("pool", lambda e: e.affine_select(out=self.IDENT[:], in_=self.IDENT[:], pattern=[[-1, 128]],
                                                    compare_op=ALU.is_equal, fill=0.0, base=0, channel_multiplier=1),
                  reads=[self.IDENT[:]], writes=[self.IDENT[:]])
        tmp = self.f(0, [128, 128])
        self.memset(tmp, 1.0)
        self.cp(self.ONES[:], tmp)
        for (M, cm, st) in ((self.MF, -1, 1), (self.MB, 1, -1)):
            self.memset(M[:], 1.0)
            self.p.op("pool", lambda e: e.affine_select(out=M[:], in_=M[:], pattern=[[st, 128]], compare_op=ALU.is_ge,
                                                        fill=0.0, base=0, channel_multiplier=cm),
                      reads=[M[:]], writes=[M[:]])
        self.memset(self.MF[0:64, 64:128], 0.0)
        self.memset(self.MB[64:128, 0:64], 0.0)
        self.memset(self.RESET[:], 0.0)
        self.memset(self.RESET[:, 0:256:64], 1.0)
        self.memset(self.EPSR[:], RMS_EPS)

    def load_fm(self, dst, src_rows, nrows):
        st = self.f(0, [128, 128])
        self.dma(st[0:nrows, :], src_rows, self.sem_misc)
        ps = self.PS[7][:, 0:nrows]
        self.tr(ps, st[0:nrows, :])
        self.cp(dst, ps)

    def load_small(self):
        self.load_fm(self.LNG[:], self.dram["ln_g"].rearrange("r (k p) -> (r k) p", p=128), 48)
        self.load_fm(self.LNB[:], self.dram["ln_b"].rearrange("r (k p) -> (r k) p", p=128), 48)
        st = self.f(0, [128, 128])
        self.dma(st[0:16, :], self.dram["cvec"].rearrange("g (k p) -> (g k) p", p=128), self.sem_misc)
        ps = self.PS[7][:, 0:16]
        self.tr(ps, st[0:16, :])
        self.act(self.CS[:].rearrange("p g k -> p (g k)"), ps, AF.Silu)

    def load_x(self):
        for tb in range(T // 128):
            src = self.dram["xp"][tb * 128:(tb + 1) * 128, :] if tb < TP // 128 else \
                self.dram["xs"][tb * 128 - TP:(tb + 1) * 128 - TP, :]
            s = self.nio % 2
            self.nio += 1
            st = self.f(s * 1024, [128, 1024])
            self.dma(st, src, self.sem_io[s])
            for hb in range(2):
                ps = self.PS[(tb * 2 + hb) % 4]
                for j in range(4):
                    k = hb * 4 + j
                    self.tr(ps[:, j * 128:(j + 1) * 128], st[:, k * 128:(k + 1) * 128])
                dst = self.X[:, hb * 4:(hb + 1) * 4, tb * 128:(tb + 1) * 128]
                self.cp(dst, ps[:].rearrange("p (a b) -> p a b", a=4), eng="dve" if hb == 0 else "act")

    def store_x(self):
        for ob in range(8):
            if ob < 4:
                dst = self.dram["yp"][ob * 128:(ob + 1) * 128, :]
                xc = ob * 128
            else:
                dst = self.dram["ys"][(ob - 4) * 128:(ob - 3) * 128, :]
                xc = self.W0 + 128 + (ob - 4) * 128
            s = self.nio % 2
            self.nio += 1
            st = self.f(s * 1024, [128, 1024])
            for hb in range(2):
                ps = self.PS[(ob * 2 + hb) % 4]
                for j in range(4):
                    k = hb * 4 + j
                    self.tr(ps[:, j * 128:(j + 1) * 128], self.X[:, k, xc:xc + 128])
                self.cp(st[:, hb * 512:(hb + 1) * 512], ps[:], eng="dve" if hb == 0 else "act")
            self.dma(dst, st, self.sem_io[s])

    def select_window(self):
        SV = self.SELV
        for k in range(KC):
            own = self.f(0, [128, 512])
            hp = self.f(512, [128, 128])
            hn = self.f(640, [128, 128])
            for t in range(4):
                xt = self.X[:, k, TP + t * 512:TP + (t + 1) * 512]
                if t == 0:
                    self.ts(own, xt, SV[:, t:t + 1], None, ALU.mult)
                    self.ts(hp, xt[:, 384:512], SV[:, 4 + t:5 + t], None, ALU.mult)
                    self.ts(hn, xt[:, 0:128], SV[:, 8 + t:9 + t], None, ALU.mult)
                else:
                    self.stt(own, xt, SV[:, t:t + 1], own, ALU.mult, ALU.add)
                    self.stt(hp, xt[:, 384:512], SV[:, 4 + t:5 + t], hp, ALU.mult, ALU.add)
                    self.stt(hn, xt[:, 0:128], SV[:, 8 + t:9 + t], hn, ALU.mult, ALU.add)
            self.cp(self.X[:, k, self.W0:self.W0 + 128], hp, eng="act")
            self.cp(self.X[:, k, self.W0 + 128:self.W0 + 640], own, eng="act")
            self.cp(self.X[:, k, self.W0 + 640:self.W0 + 768], hn, eng="act")

    def mod_vectors(self, l):
        self.load_fm(self.BM[:], self.dram["b_mod"][l].rearrange("(r p) -> r p", p=128), 72)
        wm = self.dram["w_mod"][l].rearrange("(k p) n -> p k n", p=128)
        pm = self.PS[6][:, 0:144].rearrange("p (c g) -> p c g", g=2)
        for blk in range(18):
            s = self.n13 % 2
            self.n13 += 1
            wt = self.r(self.o_w13 + s * 4096, [128, KC, 512])
            self.dma_r(wt, wm[:, :, blk * 512:(blk + 1) * 512], self.sem_w13[s])
            for q in range(4):
                oc = blk * 4 + q
                for k in range(KC):
                    self.mm(pm[:, oc, :], wt[:, k, q * 128:(q + 1) * 128], self.CS[:, :, k], k == 0, k == KC - 1)
        for g in range(2):
            self.tt(self.MODV[:, :, g], pm[:, :, g], self.BM[:], ALU.add)
        gmul = [0.5 / ALPHA, 1.0 / ALPHA, 0.5 / ALPHA]
        for s in range(3):
            self.ts(self.OPS[:, s, :, :], self.MODV[:, (3 * s + 1) * 8:(3 * s + 2) * 8, :], 1.0, None, ALU.add)
            self.ts(self.GSC[:, s, :, :], self.MODV[:, (3 * s + 2) * 8:(3 * s + 3) * 8, :], gmul[s], None, ALU.mult)

    def shift(self, s, k, g):
        return self.MODV[:, 3 * s * 8 + k, g:g + 1]

    def ln_range(self, lnidx, x0, n, o_r):
        ts_ = slice(x0, x0 + n)
        pa, pb = self.PS[4][:, 0:n], self.PS[5][:, 0:n]
        for k in range(KC):
            zr = self.r(o_r + (k % 2) * 512, [128, 512])[:, 0:n]
            sq = self.r(o_r + 1024 + (k % 2) * 512, [128, 512])[:, 0:n]
            self.act(zr, self.X[:, k, ts_], AF.Copy, scale=1.0 / 1024.0)
            self.act(sq, self.X[:, k, ts_], AF.Square, scale=1.0 / 32.0)
            self.mm(pa, self.ONES[:], zr, k == 0, k == KC - 1)
            self.mm(pb, self.ONES[:], sq, k == 0, k == KC - 1)
        m2 = self.f(self.o_lnf, [128, 512])[:, 0:n]
        self.act(m2, pa, AF.Square)
        self.tt(m2, pb, m2, ALU.subtract)
        self.act(m2, m2, AF.Ln, bias=self.EPSLN[:, 0:1])
        self.act(m2, m2, AF.Exp, scale=-0.5)
        for k in range(KC):
            xk = self.X[:, k, ts_]
            self.tt(xk, xk, pa, ALU.subtract)
            self.stt(xk, xk, self.LNG[:, lnidx * 8 + k:lnidx * 8 + k + 1], m2, ALU.mult, ALU.mult)
            self.act(xk, xk, AF.Identity, bias=self.LNB[:, lnidx * 8 + k:lnidx * 8 + k + 1])

    def ffn_tile(self, l, j, x0, n, g):
        s = 0 if j == 0 else 2
        ts_ = slice(x0, x0 + n)
        XM = self.r(self.o_xm, [128, KC, 512])[:, :, 0:n]
        HID = self.r(self.o_hid, [128, FC, 512])[:, :, 0:n]
        for k in range(KC):
            self.ts(XM[:, k, :], self.X[:, k, ts_], self.OPS[:, s, k, g:g + 1], self.shift(s, k, g), ALU.mult, ALU.add)
        w1 = self.dram["ffn_w1"][l, j].rearrange("(k p) f -> p k f", p=128)
        w3 = self.dram["ffn_w3"][l, j].rearrange("(k p) f -> p k f", p=128)
        w2 = self.dram["ffn_w2"][l, j].rearrange("(c p) o -> p c o", p=128)
        for fb in range(FC // 2):
            sl = self.n13 % 2
            self.n13 += 1
            wt = self.r(self.o_w13 + sl * 4096, [128, 2, KC, 256])
            self.dma_r(wt[:, 0], w1[:, :, fb * 256:(fb + 1) * 256], self.sem_w13[sl])
            self.dma_r(wt[:, 1], w3[:, :, fb * 256:(fb + 1) * 256], self.p.named_sem("w3_%d" % sl))
            for c in range(2):
                f = 2 * fb + c
                p1, p3 = self.PS[f % 2][:, 0:n], self.PS[2 + f % 2][:, 0:n]
                for k in range(KC):
                    self.mm(p1, wt[:, 0, k, c * 128:(c + 1) * 128], XM[:, k, :], k == 0, k == KC - 1)
                for k in range(KC):
                    self.mm(p3, wt[:, 1, k, c * 128:(c + 1) * 128], XM[:, k, :], k == 0, k == KC - 1)
                sg = self.f(self.o_sg + (f % 2) * 512, [128, 512])[:, 0:n]
                self.act(sg, p1, AF.Silu)
                self.tt(HID[:, f, :], sg, p3, ALU.mult)
        for half in range(2):
            for fb in range(FC // 2):
                sl = self.n2 % 4
                self.n2 += 1
                wt = self.r(self.o_w2 + sl * 1024, [128, 2, 512])
                self.dma_r(wt, w2[:, 2 * fb:2 * fb + 2, half * 512:(half + 1) * 512], self.sem_w2[sl])
                for c in range(2):
                    f = 2 * fb + c
                    for o in range(4):
                        self.mm(self.PS[4 + o][:, 0:n], wt[:, c, o * 128:(o + 1) * 128], HID[:, f, :], f == 0, f == FC - 1)
            for o in range(4):
                oc = half * 4 + o
                self.stt(self.X[:, oc, ts_], self.PS[4 + o][:, 0:n], self.GSC[:, s, oc, g:g + 1], self.X[:, oc, ts_], ALU.mult, ALU.add)
        self.ln_range(l * 3 + s, x0, n, self.o_ln)

    def ffn_multi(self, l, j, subs):
        s = 0 if j == 0 else 2
        o_w13, o_w2, o_xm, o_hid = 0, 8192, 10240, 18432
        ns = len(subs)
        XM = [self.r(o_xm + si * 4096, [128, KC, 512])[:, :, 0:subs[si][1]] for si in range(ns)]
        HID = [self.r(o_hid + si * 4096, [128, 8, 512])[:, :, 0:subs[si][1]] for si in range(ns)]
        for si, (x0, n, g) in enumerate(subs):
            for k in range(KC):
                self.ts(XM[si][:, k, :], self.X[:, k, x0:x0 + n], self.OPS[:, s, k, g:g + 1], self.shift(s, k, g), ALU.mult, ALU.add)
        w1 = self.dram["ffn_w1"][l, j].rearrange("(k p) f -> p k f", p=128)
        w3 = self.dram["ffn_w3"][l, j].rearrange("(k p) f -> p k f", p=128)
        w2 = self.dram["ffn_w2"][l, j].rearrange("(c p) o -> p c o", p=128)
        for (f0, f1) in ((0, 8), (8, 16), (16, 22)):
            for fb in range(f0 // 2, f1 // 2):
                sl = self.n13 % 2
                self.n13 += 1
                wt = self.r(o_w13 + sl * 4096, [128, 2, KC, 256])
                self.dma_r(wt[:, 0], w1[:, :, fb * 256:(fb + 1) * 256], self.sem_w13[sl])
                self.dma_r(wt[:, 1], w3[:, :, fb * 256:(fb + 1) * 256], self.p.named_sem("w3_%d" % sl))
                for c in range(2):
                    f = 2 * fb + c
                    for si, (x0, n, g) in enumerate(subs):
                        p1, p3 = self.PS[si][:, 0:n], self.PS[2 + si][:, 0:n]
                        for k in range(KC):
                            self.mm(p1, wt[:, 0, k, c * 128:(c + 1) * 128], XM[si][:, k, :], k == 0, k == KC - 1)
                        for k in range(KC):
                            self.mm(p3, wt[:, 1, k, c * 128:(c + 1) * 128], XM[si][:, k, :], k == 0, k == KC - 1)
                        sg = self.f(self.o_sg + si * 512, [128, 512])[:, 0:n]
                        self.act(sg, p1, AF.Silu)
                        self.tt(HID[si][:, f - f0, :], sg, p3, ALU.mult)
            for oq in range(4):
                pb = 4 if oq % 2 == 0 else 0
                for fb in range(f0 // 2, f1 // 2):
                    sl = self.n2 % 4
                    self.n2 += 1
                    wt = self.r(o_w2 + sl * 512, [128, 2, 256])
                    self.dma_r(wt, w2[:, 2 * fb:2 * fb + 2, oq * 256:(oq + 1) * 256], self.sem_w2[sl])
                    for c in range(2):
                        f = 2 * fb + c
                        for si, (x0, n, g) in enumerate(subs):
                            for o2 in range(2):
                                self.mm(self.PS[pb + si * 2 + o2][:, 0:n], wt[:, c, o2 * 128:(o2 + 1) * 128], HID[si][:, f - f0, :],
                                        f == f0, f == f1 - 1)
                for si, (x0, n, g) in enumerate(subs):
                    for o2 in range(2):
                        oc = oq * 2 + o2
                        xs_ = self.X[:, oc, x0:x0 + n]
                        self.stt(xs_, self.PS[pb + si * 2 + o2][:, 0:n], self.GSC[:, s, oc, g:g + 1], xs_, ALU.mult, ALU.add)
        for si, (x0, n, g) in enumerate(subs):
            self.ln_range(l * 3 + s, x0, n, o_hid)

    def ab_params(self):
        st = self.f(0, [128, 128])
        d = self.dram
        self.dma(st[0:1, :], d["norm_a"], self.sem_misc)
        self.dma(st[1:2, :], d["norm_b"], self.sem_misc)
        self.dma(st[2:6, :], d["gate_b"].rearrange("a (j p) -> (a j) p", p=128), self.sem_misc)
        self.dma(st[6:22, :], d["hgrn_lb"].rearrange("a b (h p) -> (a b h) p", p=128), self.sem_misc)
        ps = self.PS[7][:, 0:22]
        self.tr(ps, st[0:22, :])
        self.cp(self.PARS[:], ps)
        P = self.PARS
        for dr in range(2):
            a = P[:, 6 + dr * 8:6 + dr * 8 + 4]
            b = P[:, 6 + dr * 8 + 4:6 + dr * 8 + 8]
            self.tt(self.LB[:, dr * 4:dr * 4 + 4], a, b, ALU.subtract)
        self.act(self.LB[:], self.LB[:], AF.Sigmoid)
        self.ts(self.OML[:], self.LB[:], -1.0, 1.0, ALU.mult, ALU.add)
        self.ts(self.OMLH[:], self.OML[:], 0.5, None, ALU.mult)
        self.tt(self.LBH[:], self.LB[:], self.OMLH[:], ALU.add)
        self.ts(self.NGB[:], P[:, 2:6], -1.0, None, ALU.mult)

    def ab_unit_pass(self, u, dr, segs, g):
        hg = u < 4
        j = u - 4
        d = self.dram
        win = d["w_in_ab"].rearrange("(k p) n -> p k n", p=128)
        if hg:
            cols = [("q", u * 128), ("f", (1024 if dr == 0 else 1536) + u * 128), ("i", 512 + u * 128)]
            if dr == 1:
                cols.append(("g0", 2048 + u * 128))
        else:
            cols = [("q", 2560 + j * 128), ("f", 2816 + j * 128), ("v0", 3072 + j * 256), ("v1", 3072 + j * 256 + 128),
                    ("bz", 4000)]
            if dr == 1:
                cols += [("g0", 3584 + j * 256), ("g1", 3584 + j * 256 + 128)]
        W = {}
        for i, (nm, c0) in enumerate(cols):
            sl = (self.wab_base + i) % 7
            W[nm] = self.r(self.o_wab + sl * 1024, [128, KC, 128])
            self.dma_r(W[nm], win[:, :, c0:c0 + 128], self.sem_wab[sl])
        self.wab_base += len(cols)
        nh = 1 if hg else 2
        heads = list(range(nh))
        vw = 128 * nh
        NSB = 6
        SB = [self.r(self.o_sb + i * 128, [128, 128]) for i in range(NSB)]
        SC = [self.f(1536, [128, 128]), self.f(1664, [128, 128])]
        st = {"si": 0, "sc": 0}
        if not hg:
            self.ts(self.GUP[64:128, :], self.RESET[64:128, :], 0.0, None, ALU.mult)
            self.dma_r(self.GUP[96 + 16 * dr:112 + 16 * dr, :], d["gate_up"][dr], self.p.named_sem("gup"))
        st["started"] = False

        def seg_init(seg):
            (sx0, snht, slt0, ssample, ssidx) = seg
            if st["started"]:
                if dr == 0:
                    st["si"] += 1
                else:
                    st["sc"] += 1
            st["started"] = True
            src0 = None
            if ssample:
                src0 = d["st_h"][dr, u] if hg else d["st_g"][dr, 2 * j:2 * j + 2].rearrange("h d e -> (h d) e")
            if dr == 0:
                buf = SB[st["si"] % NSB]
                if ssample:
                    self.dma_r(buf, src0, self.p.named_sem("stf%d" % (st["si"] % NSB)))
                else:
                    self.ts(buf, self.IDENT[:], 0.0, None, ALU.mult)
            else:
                buf = SC[st["sc"] % 2]
                if ssample:
                    self.dma(buf, src0, self.p.named_sem("stb%d" % (st["sc"] % 2)))
                else:
                    self.memset(buf, 0.0)

        def seg_emit(seg):
            (sx0, snht, slt0, ssample, ssidx) = seg
            if ssample:
                return
            if dr == 0:
                S_fin = SB[st["si"] % NSB].bitcast(F32)
                sname = "sout%d" % (st["si"] % NSB)
            else:
                S_fin = SC[st["sc"] % 2]
                sname = "soutc%d" % (st["sc"] % 2)
            if hg:
                self.dma(d["ns_h"][ssidx, dr, u], S_fin, self.p.named_sem(sname))
            else:
                self.dma(d["ns_g"][ssidx, dr, 2 * j:2 * j + 2].rearrange("h d e -> (h d) e"), S_fin, self.p.named_sem(sname))

        HB = self.r(self.o_hb, [128, KC, 256])
        QD = self.r(self.o_qd, [128, 256])
        KI = self.r(self.o_ki, [128, 256])
        VT = self.r(self.o_vt, [128, 2, 256])
        AT = [self.r(self.o_at + i * 128, [128, 128]) for i in range(2)]
        BZ = self.r(self.o_at, [128, 256])
        SQ = self.r(self.o_at, [128, 256])
        KIT = self.r(self.o_hb + 256, [128, 2, 128])
        F0 = self.f(0, [128, 256])
        F1 = self.f(256, [128, 256])
        F2 = self.f(512, [128, 256])
        F3 = self.f(1792, [128, 256])
        TMP = self.f(768, [128, 128])
        AC = self.f(896, [128, 4])
        GS = [self.f(1024, [128, 256]), self.f(1280, [128, 256])]
        PS = self.PS
        psq, psf = PS[0][:, 0:256], PS[0][:, 256:512]
        MASK = self.MF if dr == 0 else self.MB
        order = []
        for seg in segs:
            (sx0, snht, slt0, ssample, ssidx) = seg
            hts = list(range(snht)) if dr == 0 else list(range(snht - 1, -1, -1))
            for n2, h_ in enumerate(hts):
                order.append({"x0": sx0 + h_ * 256, "lt0": slt0 + h_ * 256, "seg": seg, "first": n2 == 0, "last": n2 == snht - 1})
        bs = [0, 1] if dr == 0 else [1, 0]
        cseq = [(b, c) for b in bs for c in bs]

        def pr_(hh):
            return slice(0, 128) if hg else slice(64 * hh, 64 * hh + 64)

        def stage_a(hti, part=3):
            if part & 1:
                stage_a1(hti)
            if part & 2:
                stage_a2()

        def stage_a1(hti):
            t0 = hti["x0"]
            for k in range(KC):
                if k % 2 == 0:
                    self.act(HB[:, k, :], self.X[:, k, t0:t0 + 256], AF.Identity, bias=self.shift(1, k, g), scale=self.OPS[:, 1, k, g:g + 1])
                else:
                    self.ts(HB[:, k, :], self.X[:, k, t0:t0 + 256], self.OPS[:, 1, k, g:g + 1], self.shift(1, k, g), ALU.mult, ALU.add)
            for k in range(KC):
                self.mm(psq, W["q"][:, k, :], HB[:, k, :], k == 0, k == KC - 1)
            for k in range(KC):
                self.mm(psf, W["f"][:, k, :], HB[:, k, :], k == 0, k == KC - 1)
            if not hg:
                for k in range(KC):
                    self.mm(PS[3][:, 0:256], W["bz"][:, k, :], HB[:, k, :], k == 0, k == KC - 1)

        def stage_a2():
            for b in range(2):
                for vv in range(nh):
                    wv = W["i"] if hg else W["v%d" % vv]
                    for k in range(KC):
                        self.mm(PS[1][:, b * 256 + vv * 128:b * 256 + vv * 128 + 128], HB[:, k, b * 128:(b + 1) * 128], wv[:, k, :],
                                k == 0, k == KC - 1)
            if dr == 1:
                for hh in heads:
                    for k in range(KC):
                        self.mm(PS[2][:, hh * 256:(hh + 1) * 256], W["g%d" % hh][:, k, :], HB[:, k, :], k == 0, k == KC - 1)

        def stage_b(gsi=0):
            if hg:
                self.act(F0, psf, AF.Tanh, scale=0.5)
                self.ts(F0, F0, self.OMLH[:, dr * 4 + u:dr * 4 + u + 1], self.LBH[:, dr * 4 + u:dr * 4 + u + 1], ALU.mult, ALU.add)
                self.ts(F1, F0, -1.0, 1.0, ALU.mult, ALU.add, eng="pool")
                kf = F1
            else:
                self.cp(BZ, PS[3][:, 0:256], eng="act")
                psl = PS[3][:, 256:512]
                self.mm(psl, self.GUP[64:128, j * 128:(j + 1) * 128], BZ[64:128, :], True, True)
                self.act(F0, psl, AF.Exp, scale=-1.0, bias=self.NGB[:, dr * 2 + j:dr * 2 + j + 1])
                self.act(F0, F0, AF.Ln, bias=self.ONEC[:, 0:1])
                self.act(F0, F0, AF.Exp, scale=-1.0 / GLA_TAU)
                kf = psf
            qs = None if hg else B_DK ** -0.5
            R1 = self.RESET[:, 0:256]
            if dr == 0:
                self.p.op("dve", lambda e: e.tensor_tensor_scan(out=F2, data0=R1, data1=F0, initial=1.0, op0=ALU.max, op1=ALU.mult),
                          reads=[R1, F0], writes=[F2])
                self.cp(AC, F2[:, 63:256:64])
                if hg:
                    self.tt(QD, psq, F2, ALU.mult)
                else:
                    self.stt(QD, psq, qs, F2, ALU.mult, ALU.mult)
                self.p.op("dve", lambda e: e.reciprocal(out=F2, in_=F2), reads=[F2], writes=[F2])
                self.tt(KI, kf, F2, ALU.mult)
            else:
                self.cp(F2[:, 1:256], F0[:, 0:255])
                self.memset(F2[:, 0:256:64], 1.0)
                self.p.op("dve", lambda e: e.tensor_tensor_scan(out=F3, data0=R1, data1=F2, initial=1.0, op0=ALU.max, op1=ALU.mult),
                          reads=[R1, F2], writes=[F3])
                self.tt(AC, F3[:, 63:256:64], F0[:, 63:256:64], ALU.mult)
                self.tt(KI, kf, F3, ALU.mult)
                self.p.op("dve", lambda e: e.reciprocal(out=F3, in_=F3), reads=[F3], writes=[F3])
                if hg:
                    self.tt(QD, psq, F3, ALU.mult)
                else:
                    self.stt(QD, psq, qs, F3, ALU.mult, ALU.mult)
            for b in range(2):
                if hg:
                    self.act(VT[:, b, 0:128], PS[1][:, b * 256:b * 256 + 128], AF.Silu)
                else:
                    self.cp(VT[:, b, :], PS[1][:, b * 256:(b + 1) * 256], eng="act")
            if dr == 1:
                for hh in heads:
                    self.act(GS[(hh + gsi) % 2], PS[2][:, hh * 256:(hh + 1) * 256], AF.Silu)

        def stage_c(PSO):
            pskv = []
            for idx, (b_, c_) in enumerate(cseq):
                bank = (PS[6] if c_ == 0 else PS[3]) if hg else (PS[6] if c_ == 0 else PS[1])
                w_ = 128 if hg else 256
                slot = 0 if idx < 2 else 1
                koff = 256 if (hg and c_ == 1) else 0
                pskv.append(bank[:, koff + slot * w_:koff + (slot + 1) * w_])
            first = [True, True]
            for hi, hh in enumerate(heads):
                pat = PS[5] if hi == 0 else PS[2 if dr == 0 else 3]
                pat_off = 0 if (hi == 0 or dr == 0) else 256
                for b in bs:
                    self.mm(pat[:, pat_off + b * 128:pat_off + (b + 1) * 128], KI[pr_(hh), b * 128:(b + 1) * 128],
                            QD[pr_(hh), b * 128:(b + 1) * 128], True, True)
                if hi == 0:
                    for b in bs:
                        self.tr(PS[3][:, b * 128:(b + 1) * 128], KI[:, b * 128:(b + 1) * 128].bitcast(F32))
                for b in bs:
                    self.tt(AT[b], pat[:, pat_off + b * 128:pat_off + (b + 1) * 128], MASK[:], ALU.mult)
                if hi == 0:
                    for b in bs:
                        self.cp(KIT[:, b, :], PS[3][:, b * 128:(b + 1) * 128], eng="act")
                    for idx, (b, c) in enumerate(cseq):
                        self.mm(pskv[idx], KIT[c * 64:(c + 1) * 64, b, :], VT[c * 64:(c + 1) * 64, b, 0:vw], True, True)
                for b in bs:
                    self.mm(PSO[hh][:, b * 128:(b + 1) * 128], VT[:, b, hh * 128:(hh + 1) * 128], AT[b], first[hh], False)
                    first[hh] = False
            return pskv

        def stage_d(pskv, item):
            if item["first"]:
                seg_init(item["seg"])
            states = []
            for idx, (b, c) in enumerate(cseq):
                ci = b * 2 + c
                a_c = AC[:, ci:ci + 1]
                if dr == 0:
                    S_cur = SB[st["si"] % NSB]
                    S_next = SB[(st["si"] + 1) % NSB]
                    states.append(S_cur)
                    if hg:
                        self.tt(TMP, pskv[idx], S_cur.bitcast(F32), ALU.add)
                    else:
                        self.tt(TMP[0:64, :], pskv[idx][0:64, 0:128], S_cur[0:64, :].bitcast(F32), ALU.add)
                        self.tt(TMP[64:128, :], pskv[idx][64:128, 128:256], S_cur[64:128, :].bitcast(F32), ALU.add)
                    self.ts(S_next, TMP, a_c, None, ALU.mult)
                    st["si"] += 1
                else:
                    C_cur = SC[st["sc"] % 2]
                    C_next = SC[(st["sc"] + 1) % 2]
                    S_sc = SB[st["si"] % NSB]
                    states.append(S_sc)
                    self.ts(S_sc, C_cur, a_c, None, ALU.mult)
                    if hg:
                        self.tt(C_next, pskv[idx], S_sc.bitcast(F32), ALU.add)
                    else:
                        self.tt(C_next[0:64, :], pskv[idx][0:64, 0:128], S_sc[0:64, :].bitcast(F32), ALU.add)
                        self.tt(C_next[64:128, :], pskv[idx][64:128, 128:256], S_sc[64:128, :].bitcast(F32), ALU.add)
                    st["si"] += 1
                    st["sc"] += 1
            if item["last"]:
                seg_emit(item["seg"])
            return states

        def stage_d2(states, PSO):
            for idx, (b, c) in enumerate(cseq):
                ccols = slice(b * 128 + c * 64, b * 128 + c * 64 + 64)
                for hh in heads:
                    self.mm(PSO[hh][:, ccols], states[idx][pr_(hh), :], QD[pr_(hh), ccols], False, idx == 3)

        E1 = self.f(2560, [128, 256])
        E2 = self.f(2816, [128, 256])

        def stage_e(hti, PSO, gsi, pst):
            lt0 = hti["lt0"]
            for hh in heads:
                head = u if hg else 4 + 2 * j + hh
                on = self.r(self.o_on + head * self.on_stride + lt0, [128, 256])
                pso = PSO[hh][:, 0:256]
                if dr == 0:
                    self.cp(on, pso, eng="act")
                else:
                    self.tt(E1, pso, on.bitcast(F32), ALU.add)
                    self.act(SQ, E1, AF.Square, scale=128.0 ** -0.5)
                    self.mm(pst, self.ONES[:], SQ, True, True)
                    self.act(E2, pst, AF.Ln, bias=self.EPSR[:, 0:1])
                    self.act(E2, E2, AF.Exp, scale=-0.5)
                    nw = self.PARS[:, 0:1] if hg else self.PARS[:, 1:2]
                    self.stt(E1, E1, nw, E2, ALU.mult, ALU.mult)
                    self.tt(on, E1, GS[(hh + gsi) % 2], ALU.mult)

        if hg:
            stage_a(order[0])
            prev = None
            for n_, hti in enumerate(order):
                PSOi = [PS[4] if n_ % 2 == 0 else PS[7]]
                stage_b(n_ % 2)
                if prev is not None:
                    stage_e(*prev)
                pskv = stage_c(PSOi)
                if n_ + 1 < len(order):
                    stage_a(order[n_ + 1], 1)
                states = stage_d(pskv, hti)
                stage_d2(states, PSOi)
                if n_ + 1 < len(order):
                    stage_a(order[n_ + 1], 2)
                prev = (hti, PSOi, n_ % 2, PS[2][:, 0:256])
            stage_e(*prev)
        else:
            PSOg = [PS[4], PS[7]]
            stage_a(order[0])
            for n_, hti in enumerate(order):
                stage_b(0)
                pskv = stage_c(PSOg)
                if n_ + 1 < len(order):
                    stage_a(order[n_ + 1], 1)
                states = stage_d(pskv, hti)
                stage_d2(states, PSOg)
                stage_e(hti, PSOg, 0, PS[5][:, 0:256])
                if n_ + 1 < len(order):
                    stage_a(order[n_ + 1], 2)

    def ab_outproj(self, l, x0, lt0, n, g):
        wo = self.dram["w_out_ab"].rearrange("(h p) o -> p h o", p=128)
        for half in range(2):
            WO = self.r(self.o_wab, [128, 8, 512])
            self.dma_r(WO, wo[:, :, half * 512:(half + 1) * 512], self.sem_wab[0])
            for o in range(4):
                for h in range(8):
                    on = self.r(self.o_on + h * self.on_stride + lt0, [128, 512])[:, 0:n]
                    self.mm(self.PS[o][:, 0:n], WO[:, h, o * 128:(o + 1) * 128], on, h == 0, h == 7)
            for o in range(4):
                oc = half * 4 + o
                xs_ = self.X[:, oc, x0:x0 + n]
                self.stt(xs_, self.PS[o][:, 0:n], self.GSC[:, 1, oc, g:g + 1], xs_, ALU.mult, ALU.add)
        self.ln_range(l * 3 + 1, x0, n, self.o_hb)

    def mixer_ab(self, l):
        self.o_on = 0
        self.on_stride = 2048
        self.o_wab = 16384
        self.o_hb = 23552
        o = 25600
        self.o_qd = o
        self.o_ki = o + 256
        self.o_vt = o + 512
        self.o_at = o + 1024
        self.o_sb = o + 1280
        assert self.o_sb + 768 <= self.NR
        self.ab_params()
        abn = int(os.environ.get("MK_ABN", "1000"))
        abskip = int(os.environ.get("MK_ABSKIP", "0"))
        cnt = 0
        self.wab_base = 0
        groups = (([(0, 1, 0, False, 0), (256, 1, 256, False, 1)], 0, 0, 512),
                  ([(512, 8, 0, True, None)], 1, 512, 2048))
        for (segs, g, gx0, ntok) in groups:
            for u in range(6):
                for dr in range(2):
                    cnt += 1
                    if cnt <= abskip or cnt > abskip + abn:
                        continue
                    self.ab_unit_pass(u, dr, segs, g)
            for off in range(0, ntok, 512):
                n = min(512, ntok - off)
                self.ab_outproj(l, gx0 + off, off, n, g)

    def c_consts(self):
        d = self.dram
        A = self.f(0, [128, 128])
        B = self.f(128, [128, 128])
        for (M, cm, st, base) in ((A, 1, -1, -16), (B, -1, 1, -16)):
            self.memset(M, 1.0)
            self.p.op("pool", lambda e: e.affine_select(out=M, in_=M, pattern=[[st, 128]], compare_op=ALU.is_equal,
                                                        fill=0.0, base=base, channel_multiplier=cm),
                      reads=[M], writes=[M])
        self.memset(A.rearrange("p (g c) -> p g c", c=32)[:, :, 16:32], 0.0)
        self.memset(B.rearrange("p (g c) -> p g c", c=32)[:, :, 0:16], 0.0)
        self.tt(self.PERMR[:], A, B, ALU.add)
        for (M, cm, st) in ((self.LM1, -1, 1), (self.LM3, 1, -1)):
            self.memset(M[:], 1.0)
            self.p.op("pool", lambda e: e.affine_select(out=M[:], in_=M[:], pattern=[[st, 128]], compare_op=ALU.is_ge,
                                                        fill=0.0, base=0, channel_multiplier=cm),
                      reads=[M[:]], writes=[M[:]])
        st_ = self.f(256, [128, 16])
        self.memset(st_, 0.0)
        self.dma(st_[0:1, :], d["sink"], self.sem_misc)
        ps = self.PS[7][:, 0:16]
        onesf = self.f(384, [128, 128])
        self.memset(onesf, 1.0)
        self.mm(ps, onesf, st_, True, True)
        self.act(self.ES[:], ps, AF.Exp)
        I32 = mybir.dt.int32
        pi_ = self.f(512, [128, 1]).bitcast(I32)
        self.p.op("pool", lambda e: e.iota(pi_, pattern=[[0, 1]], base=0, channel_multiplier=1), writes=[pi_])
        i16 = self.f(513, [128, 1]).bitcast(I32)
        self.ts(i16, pi_, 15, None, ALU.bitwise_and)
        m32 = self.f(514, [128, 1]).bitcast(I32)
        self.ts(m32, pi_, 32, None, ALU.bitwise_and)
        i16f = self.f(515, [128, 1])
        self.cp(i16f, i16)
        m32f = self.f(516, [128, 1])
        self.cp(m32f, m32)
        inv = self.f(517, [128, 1])
        self.act(inv, i16f, AF.Exp, scale=-float(np.log(ROPE_BASE)) / 16.0)
        b16 = self.f(518, [128, 1]).bitcast(I32)
        self.ts(b16, pi_, 16, None, ALU.bitwise_and)
        b16f = self.f(519, [128, 1])
        self.cp(b16f, b16)
        self.ts(self.SGN[:], b16f, 1.0 / 8.0, -1.0, ALU.mult, ALU.add)
        self.ts(self.MC[:], m32f, 1.0 / 32.0, None, ALU.mult)
        self.ts(self.MR[:], self.MC[:], -1.0, 1.0, ALU.mult, ALU.add)
        pos = self.f(640, [128, 64])
        self.p.op("pool", lambda e: e.iota(pos.bitcast(I32), pattern=[[1, 64]], base=0, channel_multiplier=0), writes=[pos])
        posf = self.f(704, [128, 64])
        self.cp(posf, pos.bitcast(I32))
        TWO_PI = 2.0 * float(np.pi)
        TRC = self.f(1024, [128, 64])
        TRS = self.f(1088, [128, 64])
        for (posoff, dsin, dcos) in ((0.0, self.TSIN[:], self.TCOS[:]), (-2.0, TRS, TRC)):
            ang = self.f(768, [128, 64])
            self.ts(ang, posf, posoff, None, ALU.add)
            self.ts(ang, ang, inv, None, ALU.mult)
            for (dst, shiftv) in ((dsin, 0.0), (dcos, float(np.pi) / 2.0)):
                a = self.f(832, [128, 64])
                kq = self.f(896, [128, 64])
                ki = self.f(960, [128, 64]).bitcast(I32)
                self.ts(a, ang, shiftv, None, ALU.add)
                self.ts(kq, a, 1.0 / TWO_PI, None, ALU.mult)
                self.cp(ki, kq)
                self.cp(kq, ki)
                self.stt(a, kq, -TWO_PI, a, ALU.mult, ALU.add)
                self.ts(kq, a, float(np.pi), -TWO_PI, ALU.is_gt, ALU.mult)
                self.tt(a, a, kq, ALU.add)
                self.ts(kq, a, -float(np.pi), TWO_PI, ALU.is_lt, ALU.mult)
                self.tt(a, a, kq, ALU.add)
                self.act(dst, a, AF.Sin)
        for (ci, T) in ((0, TRC), (1, TRS)):
            for jq in range(4):
                if jq == 0:
                    self.ts(self.TRW[:, ci, :], T[:, 0:12], self.SELV[:, 0:1], None, ALU.mult)
                else:
                    self.stt(self.TRW[:, ci, :], T[:, 8 * jq:8 * jq + 12], self.SELV[:, jq:jq + 1], self.TRW[:, ci, :], ALU.mult, ALU.add)

    def rope_tables_w(self, r0, nrows):
        n = nrows * 64
        cos_t = self.f(0, [128, 512])[:, 0:n]
        sin_t = self.f(512, [128, 512])[:, 0:n]
        for (dst, ci, T) in ((cos_t, 0, self.TCOS), (sin_t, 1, self.TSIN)):
            dv = dst.rearrange("p (r c) -> p r c", c=64)
            rowv = self.TRW[:, ci, r0:r0 + nrows].unsqueeze(2).to_broadcast([128, nrows, 64])
            colv = T[:, 0:64].unsqueeze(1).to_broadcast([128, nrows, 64])
            self.ts(dv, rowv, self.MR[:, 0:1], None, ALU.mult)
            self.stt(dv, colv, self.MC[:, 0:1], dv, ALU.mult, ALU.add)
        self.ts(sin_t, sin_t, self.SGN[:, 0:1], None, ALU.mult)
        return cos_t, sin_t

    def rope_tables(self, t):
        cos_t = self.f(0, [128, 512])
        sin_t = self.f(512, [128, 512])
        for (dst, T) in ((cos_t, self.TCOS), (sin_t, self.TSIN)):
            dv = dst.rearrange("p (r c) -> p r c", c=64)
            rowv = T[:, 8 * t:8 * t + 8].unsqueeze(2).to_broadcast([128, 8, 64])
            colv = T[:, 0:64].unsqueeze(1).to_broadcast([128, 8, 64])
            self.ts(dv, rowv, self.MR[:, 0:1], None, ALU.mult)
            self.stt(dv, colv, self.MC[:, 0:1], dv, ALU.mult, ALU.add)
        self.ts(sin_t, sin_t, self.SGN[:, 0:1], None, ALU.mult)
        return cos_t, sin_t

    def rope_apply(self, dst, ps, cos_t, sin_t, n):
        ZQ = self.r(self.o_zq, [128, 512])[:, 0:n]
        self.cp(ZQ, ps, eng="act")
        pz = self.PS[6][:, 0:n]
        self.mm(pz, self.PERMR[:], ZQ, True, True)
        t1 = self.f(1024, [128, 512])[:, 0:n]
        self.tt(t1, ZQ.bitcast(F32), cos_t[:, 0:n], ALU.mult)
        t2 = self.f(1536, [128, 512])[:, 0:n]
        self.tt(t2, pz, sin_t[:, 0:n], ALU.mult)
        self.tt(dst, t1, t2, ALU.add)

    def c_modulate(self, x0, n, g):
        HB = self.r(self.o_hb, [128, KC, 512])
        for k in range(KC):
            self.ts(HB[:, k, 0:n], self.X[:, k, x0:x0 + n], self.OPS[:, 1, k, g:g + 1], self.shift(1, k, g), ALU.mult, ALU.add)
        return HB

    def va_slot(self, kap):
        return [(0, 0, 64), (64, 2, 0), (130, 0, 64), (194, 2, 0)][kap]

    def c_kv_project(self, HB, n, WKV, kf_dst, va_dst, vblk0, rope, emit=None):
        for kp in range(2):
            ps = self.PS[kp][:, 0:n]
            for k in range(KC):
                self.mm(ps, WKV[:, k, kp * 128:(kp + 1) * 128], HB[:, k, 0:n], k == 0, k == KC - 1)
            if rope is not None:
                self.rope_apply(kf_dst(kp), ps, rope[0], rope[1], n)
            else:
                self.cp(kf_dst(kp), ps, eng="act")
        for b in range(n // 128):
            ps = self.PS[2 + b % 2][:, 0:256]
            for k in range(KC):
                self.mm(ps, HB[:, k, b * 128:(b + 1) * 128], WKV[:, k, 256:512], k == 0, k == KC - 1)
            va = va_dst(vblk0 + b)
            self.cp(va[:, 0:132].rearrange("p (a c) -> p a c", c=66)[:, :, 0:64], ps[:, 0:128].rearrange("p (a c) -> p a c", c=64), eng="act")
            self.cp(va[:, 130:262].rearrange("p (a c) -> p a c", c=66)[:, :, 0:64], ps[:, 128:256].rearrange("p (a c) -> p a c", c=64), eng="act")
            self.ts(va[:, 64:66], self.RESET[:, 1:3], 0.0, 1.0, ALU.mult, ALU.add)
            self.ts(va[:, 194:196], self.RESET[:, 1:3], 0.0, 1.0, ALU.mult, ALU.add)
            if emit is not None:
                sq, tok0 = emit
                stv = self.f(2048 + (b % 2) * 256, [128, 256])
                self.cp(stv, ps)
                self.dma(self.dram["nv"][sq, tok0 + b * 128:tok0 + (b + 1) * 128, :], stv, self.p.named_sem("nv%d" % (b % 2)))
                ps2 = self.PS[4 + b % 2][:, 0:256]
                for k in range(KC):
                    self.mm(ps2, HB[:, k, b * 128:(b + 1) * 128], WKV[:, k, 0:256], k == 0, k == KC - 1)
                stk = self.f(2560 + (b % 2) * 256, [128, 256])
                self.cp(stk, ps2, eng="act")
                self.dma(self.dram["nk"][sq, tok0 + b * 128:tok0 + (b + 1) * 128, :], stk, self.p.named_sem("nk%d" % (b % 2)))

    def c_q_project(self, HB, n, rope):
        wq = self.dram["w_qkv"].rearrange("(k p) n -> p k n", p=128)
        QF = self.r(self.o_qf, [128, 8, 512])
        WS = self.r(self.o_ws, [128, KC, 4, 128])
        for grp in range(2):
            for s_ in range(2):
                for i in range(4):
                    c0 = grp * 512 + s_ * 256 + i * 64
                    self.dma_r(WS[:, :, i, s_ * 64:(s_ + 1) * 64], wq[:, :, c0:c0 + 64], self.p.named_sem("wq%d_%d" % (s_, i)))
            for i in range(4):
                pair = grp * 4 + i
                ps = self.PS[pair % 2][:, 0:n]
                for k in range(KC):
                    self.mm(ps, WS[:, k, i, :], HB[:, k, 0:n], k == 0, k == KC - 1)
                if rope is not None:
                    self.rope_apply(QF[:, pair, 0:n], ps, rope[0], rope[1], n)
                else:
                    self.cp(QF[:, pair, 0:n], ps, eng="act")
        return QF

    def c_attend(self, QF, nqb, keysets):
        OT = self.r(self.o_ot, [128, 4, 1024])
        NPT = 4
        PT = [self.r(self.o_ws + i * 512, [128, 512]) for i in range(NPT)]
        SCB = [self.PS[0], self.PS[1], self.PS[7]]
        tasks = []
        for h in range(C_HEADS):
            for ki, ks in enumerate(keysets):
                tasks.append((h, ki, ks))
        nk_for = [sum(1 for ks in keysets if ks[2] <= qb <= ks[3]) for qb in range(nqb)]

        def hinfo(h):
            kap = h // 4
            half = kap % 2
            pair = (h % 4) + (0 if h < 8 else 4)
            return kap, half, pair, slice(64 * half, 64 * half + 64)

        def score(i):
            h, ki, (kf_fn, va_fn, qlo, qhi, masks, halo) = tasks[i]
            kap, half, pair, hp = hinfo(h)
            ncol = (qhi - qlo + 1) * 128
            ps = SCB[i % 3][:, 0:ncol]
            pt = PT[i % NPT][:, 0:ncol]
            self.mm(ps, kf_fn(kap, half), QF[hp, pair, qlo * 128:qlo * 128 + ncol], True, True)
            self.act(pt, ps, AF.Exp, scale=HD ** -0.5)
            for qb in range(qlo, qhi + 1):
                if qb in masks:
                    sl = slice((qb - qlo) * 128, (qb - qlo + 1) * 128)
                    if halo is None:
                        self.tt(pt[:, sl], pt[:, sl].bitcast(F32), masks[qb][:], ALU.mult)
                    else:
                        self.stt(pt[:, sl], pt[:, sl].bitcast(F32), halo, masks[qb][:], ALU.mult, ALU.mult)

        seen = {}

        def pv(i):
            h, ki, (kf_fn, va_fn, qlo, qhi, masks, halo) = tasks[i]
            kap, half, pair, hp = hinfo(h)
            c0, o_off, d_off = self.va_slot(kap)
            ncol = (qhi - qlo + 1) * 128
            pt = PT[i % NPT][:, 0:ncol]
            for qb in range(qlo, qhi + 1):
                sl = slice((qb - qlo) * 128, (qb - qlo + 1) * 128)
                seen[(h, qb)] = seen.get((h, qb), 0) + 1
                self.mm(self.PS[2 + qb][:, 0:66], pt[:, sl], va_fn(kap)[:, c0:c0 + 66], seen[(h, qb)] == 1, seen[(h, qb)] == nk_for[qb])
            if ki == len(keysets) - 1:
                for qb in range(nqb):
                    po = self.PS[2 + qb]
                    rd = self.f(2048 + 16 * qb, [128, 1])
                    self.ts(rd, po[:, d_off:d_off + 1], self.ES[:, h:h + 1], None, ALU.add)
                    self.p.op("dve", lambda e: e.reciprocal(out=rd, in_=rd), reads=[rd], writes=[rd])
                    self.ts(OT[:, qb, h * 64:(h + 1) * 64], po[:, o_off:o_off + 64], rd, None, ALU.mult)

        LOOK = 2
        for i in range(min(LOOK, len(tasks))):
            score(i)
        for i in range(len(tasks)):
            if i + LOOK < len(tasks):
                score(i + LOOK)
            pv(i)
        return OT

    def c_outproj(self, l, OT, x0, nqb, g):
        n = nqb * 128
        OA = self.r(self.o_hb, [128, KC, 512])
        for qb in range(nqb):
            for hb in range(2):
                ps = self.PS[hb]
                for j in range(4):
                    c = hb * 4 + j
                    self.tr(ps[:, j * 128:(j + 1) * 128], OT[:, qb, c * 128:(c + 1) * 128].bitcast(F32))
                self.cp(OA[:, hb * 4:(hb + 1) * 4, qb * 128:(qb + 1) * 128], ps[:].rearrange("p (a b) -> p a b", a=4),
                        eng="act" if hb else "dve")
        wo = self.dram["w_out_c"].rearrange("(c p) o -> p c o", p=128)
        for half in range(2):
            WO = self.r(self.o_qf, [128, KC, 512])
            self.dma_r(WO, wo[:, :, half * 512:(half + 1) * 512], self.p.named_sem("woc"))
            for o in range(4):
                for c in range(KC):
                    self.mm(self.PS[4 + o][:, 0:n], WO[:, c, o * 128:(o + 1) * 128], OA[:, c, 0:n], c == 0, c == KC - 1)
            for o in range(4):
                oc = half * 4 + o
                xs_ = self.X[:, oc, x0:x0 + n]
                self.stt(xs_, self.PS[4 + o][:, 0:n], self.GSC[:, 1, oc, g:g + 1], xs_, ALU.mult, ALU.add)
        self.ln_range(l * 3 + 1, x0, n, self.o_ws)

    def mixer_c(self, l):
        d = self.dram
        self.o_kf = 0
        self.o_va = 4096
        self.o_kc = self.o_va + 16 * 262
        self.o_vca = self.o_kc + 1024
        self.o_hb = self.o_vca + 4 * 262
        self.o_ws = self.o_hb + 4096
        self.o_qf = self.o_ws + 4096
        self.o_ot = self.o_qf + 4096
        self.o_zq = self.o_ot + 4096
        assert self.o_zq + 512 <= self.NR, self.o_zq
        self.c_consts()
        KF = self.r(self.o_kf, [128, 2, 2048])
        VA = self.r(self.o_va, [128, 16, 262])
        KCF = self.r(self.o_kc, [128, 2, 512])
        VCA = self.r(self.o_vca, [128, 4, 262])
        wq = d["w_qkv"].rearrange("(k p) n -> p k n", p=128)
        WKV = self.r(self.o_ws, [128, KC, 512])

        def load_wkv():
            self.dma_r(WKV, wq[:, :, 1024:1536], self.p.named_sem("wkv"))
        for sq in range(2):
            x0 = sq * SEQ
            HB = self.c_modulate(x0, SEQ, 0)
            load_wkv()
            self.c_kv_project(HB, SEQ, WKV, lambda kp: KF[:, kp, 0:SEQ], lambda b: VA[:, b, :], 0, None, emit=(sq, 0))
            QF = self.c_q_project(HB, SEQ, None)
            keysets = []
            for kb in range(2):
                keysets.append((lambda kap, half, kb=kb: KF[64 * half:64 * half + 64, kap // 2, kb * 128:(kb + 1) * 128],
                                lambda kap, kb=kb: VA[:, kb, :], 0, 1, {}, None))
            OT = self.c_attend(QF, 2, keysets)
            self.c_outproj(l, OT, x0, 2, 0)
        stg = self.r(self.o_ot, [128, 4, 256])
        self.dma_r(stg, d["ck"].rearrange("(b p) c -> p b c", p=128), self.p.named_sem("ck"))
        for b in range(4):
            for kp in range(2):
                ps = self.PS[(b * 2 + kp) % 2][:, 0:128]
                self.tr(ps, stg[:, b, kp * 128:(kp + 1) * 128].bitcast(F32))
                self.cp(KCF[:, kp, b * 128:(b + 1) * 128], ps, eng="act" if kp else "dve")
        cvv = d["cv"].rearrange("(b p) c -> p b c", p=128)
        for kap in range(4):
            c0 = [0, 66, 130, 196][kap]
            self.dma_r(VCA[:, :, c0:c0 + 64], cvv[:, :, kap * 64:(kap + 1) * 64], self.p.named_sem("cv%d" % kap))
        for b in range(4):
            self.ts(VCA[:, b, 64:66], self.RESET[:, 1:3], 0.0, 1.0, ALU.mult, ALU.add)
            self.ts(VCA[:, b, 194:196], self.RESET[:, 1:3], 0.0, 1.0, ALU.mult, ALU.add)
        load_wkv()
        W0 = self.W0
        for (off, n_, r0) in ((0, 512, 0), (512, 256, 8)):
            HB = self.c_modulate(W0 + off, n_, 1)
            rope = self.rope_tables_w(r0, n_ // 64)
            self.c_kv_project(HB, n_, WKV, lambda kp, off=off, n_=n_: KF[:, kp, off:off + n_], lambda b: VA[:, b, :], off // 128, rope)
        HB = self.c_modulate(W0 + 128, 512, 1)
        rope = self.rope_tables_w(2, 8)
        QF = self.c_q_project(HB, 512, rope)
        keysets = []
        for cb in range(4):
            keysets.append((lambda kap, half, cb=cb: KCF[64 * half:64 * half + 64, kap // 2, cb * 128:(cb + 1) * 128],
                            lambda kap, cb=cb: VCA[:, cb, :], 0, 3, {}, None))
        for kb in range(6):
            qlo = max(1, kb - 1) - 1
            qhi = min(4, kb + 1) - 1
            masks = {}
            if 0 <= kb - 2 <= 3:
                masks[kb - 2] = self.LM1
            if 0 <= kb <= 3:
                masks[kb] = self.LM3
            halo = None
            if kb == 0:
                halo = self.SELV[:, 12:13]
            if kb == 5:
                halo = self.SELV[:, 13:14]
            keysets.append((lambda kap, half, kb=kb: KF[64 * half:64 * half + 64, kap // 2, kb * 128:(kb + 1) * 128],
                            lambda kap, kb=kb: VA[:, kb, :], qlo, qhi, masks, halo))
        OT = self.c_attend(QF, 4, keysets)
        self.c_outproj(l, OT, W0 + 128, 4, 1)

    def build(self):
        self.o_w13 = 0
        self.o_w2 = 8192
        self.o_xm = 12288
        self.o_hid = 16384
        self.o_ln = self.o_hid + 18 * 512
        self.o_sg = 0
        self.o_lnf = 1024
        self.consts()
        self.EPSLN = self.sb("EPSLN", [128, 1])
        self.memset(self.EPSLN[:], LN_EPS / (ALPHA * ALPHA))
        self.ONEC = self.sb("ONEC", [128, 1])
        self.memset(self.ONEC[:], 1.0)
        self.load_small()
        self.dma(self.SELV[:], self.dram["selv"], self.p.named_sem("selv"))
        self.load_x()
        self.W0 = TP
        full = [[(0, 512, 0), (512, 512, 1)], [(1024, 512, 1), (1536, 512, 1)], [(2048, 512, 1)]]
        win = [[(0, 512, 0), (self.W0, 512, 1)], [(self.W0 + 512, 256, 1)]]
        own = [[(0, 512, 0), (self.W0 + 128, 512, 1)]]
        for l in range(DEPTH):
            self.mod_vectors(l)
            for subs in (full if l == 0 else win):
                self.ffn_multi(l, 0, subs)
            if self.stage == 1 + 3 * l:
                break
            if l == 0:
                self.mixer_ab(l)
                self.select_window()
            else:
                self.mixer_c(l)
            if self.stage == 2 + 3 * l:
                break
            for subs in (win if l == 0 else own):
                self.ffn_multi(l, 1, subs)
            if self.stage == 3 + 3 * l:
                break
        self.store_x()
        self.p.finish("sp")
        return self.nc


_CACHE = {}


def get_program(stage=99):
    if stage not in _CACHE:
        _CACHE[stage] = Builder(stage).build()
    return _CACHE[stage]


def _selv(j):
    v = np.zeros((16,), np.float32)
    v[j] = 1.0
    if j > 0:
        v[4 + j - 1] = 1.0
        v[12] = 1.0
    if j < 3:
        v[8 + j + 1] = 1.0
        v[13] = 1.0
    return np.ascontiguousarray(np.broadcast_to(v, (128, 16))).astype(np.float32)


def shard_inputs(inp):
    f = lambda a: np.ascontiguousarray(np.asarray(a, dtype=np.float32))
    maps = []
    for c in range(NCORES):
        sb = c // 4
        m = {
            "xp": f(inp["x_prompt"][2 * c:2 * c + 2].reshape(TP, D)),
            "xs": f(inp["x_sample"][sb]),
            "st_h": f(inp["state_hgrn"][sb, 0]),
            "st_g": f(inp["state_gla"][sb, 0]),
            "ck": f(inp["cache_k"][sb, 0].reshape(PAST, C_KV * HD)),
            "cv": f(inp["cache_v"][sb, 0].reshape(PAST, C_KV * HD)),
            "cvec": f(np.stack([inp["c_ctx"], inp["c"][sb]], axis=0)),
            "selv": _selv(c % 4),
            "w_mod": f(inp["w_mod"]),
            "b_mod": f(inp["b_mod"]),
            "ln_g": f(inp["ln_g"].reshape(DEPTH * 3, D)),
            "ln_b": f(inp["ln_b"].reshape(DEPTH * 3, D)),
            "ffn_w1": f(inp["ffn_w1"]),
            "ffn_w3": f(inp["ffn_w3"]),
            "ffn_w2": f(inp["ffn_w2"]),
            "w_in_ab": f(inp["w_in_ab"][0]),
            "hgrn_lb": f(inp["hgrn_lb"]),
            "gate_up": f(inp["gla_gate_up"][0]),
            "gate_b": f(inp["gla_gate_b"][0]),
            "norm_a": f(inp["norm_a"]),
            "norm_b": f(inp["norm_b"]),
            "w_out_ab": f(inp["w_out_ab"][0]),
            "w_qkv": f(inp["w_qkv_c"][0]),
            "sink": f(inp["sink_c"]),
            "w_out_c": f(inp["w_out_c"][0]),
        }
        maps.append(m)
    return maps


def gather_outputs(res):
    r = res.results
    B = 16
    yp = np.concatenate([r[c]["yp"].reshape(2, SEQ, D) for c in range(NCORES)], axis=0)
    ys = np.stack([np.concatenate([r[4 * b + q]["ys"] for q in range(4)], axis=0) for b in range(2)], axis=0)
    nsh = np.concatenate([r[c]["ns_h"].reshape(2, 1, 2, A_HEADS, 128, 128) for c in range(NCORES)], axis=0)
    nsg = np.concatenate([r[c]["ns_g"].reshape(2, 1, 2, B_HEADS, B_DK, 128) for c in range(NCORES)], axis=0)
    nk = np.concatenate([r[c]["nk"].reshape(2, 1, SEQ, C_KV, HD) for c in range(NCORES)], axis=0)
    nv = np.concatenate([r[c]["nv"].reshape(2, 1, SEQ, C_KV, HD) for c in range(NCORES)], axis=0)
    return (yp.astype(np.float32), ys.astype(np.float32), nsh.astype(np.float32), nsg.astype(np.float32),
            nk.astype(np.float32), nv.astype(np.float32))


def kernel(**inputs):
    stage = int(os.environ.get("MK_STAGE", "99"))
    nc = get_program(stage)
    maps = shard_inputs(inputs)
    res = run_bass_kernel_spmd(nc, maps, core_ids=list(range(NCORES)))
    return gather_outputs(res)
```

```python
import os
import numpy as np
import concourse.bass as bass
import concourse.mybir as mybir
from concourse.bass_utils import run_bass_kernel_spmd

F32 = mybir.dt.float32
F32R = mybir.dt.float32r
AF = mybir.ActivationFunctionType
ALU = mybir.AluOpType

D = 1024
KC = 8
DFF = 2816
FC = 22
NMOD = 9
DEPTH = 2
SEQ = 256
TP = 512
TS = 2048
T = TP + TS
NT = T // 512
A_HEADS = 4
B_HEADS = 4
B_DK = 64
GATE_RANK = 16
GLA_TAU = 16.0
AB_IN = 4128
C_HEADS = 16
C_KV = 4
HD = 64
PAST = 512
ALPHA = (2.0 * DEPTH) ** 0.25
LN_EPS = 1e-5
RMS_EPS = 1e-6
ROPE_BASE = 10000.0
NCORES = 8


class Prog:
    def __init__(self, nc):
        self.nc = nc
        self.E = {"pe": nc.tensor, "act": nc.scalar, "dve": nc.vector, "pool": nc.gpsimd, "sp": nc.sync}
        self.sems = {}
        self.cnt = {}
        for e in self.E:
            self.sems[e] = nc.alloc_semaphore("s_" + e)
            self.cnt[e] = 0
        self.seen = {e: {} for e in self.E}
        self.ndma_sem = 0
        self.n_inst = 0
        self.n_wait = 0
        self.self_sync = set(os.environ.get("MK_SELFSYNC", "act,dve,pool").split(",")) - {""}
        self.named = {}
        self.mem = {}

    def named_sem(self, name):
        if name not in self.named:
            self.named[name] = self.new_dma_sem()
        return self.named[name]

    def new_dma_sem(self):
        k = "d%d" % self.ndma_sem
        self.ndma_sem += 1
        self.sems[k] = self.nc.alloc_semaphore("s_" + k)
        self.cnt[k] = 0
        return k

    @staticmethod
    def box(ap):
        esz = mybir.dt.size(ap.dtype)
        dims = ap.ap
        off = ap.offset
        name = ap.tensor.name
        if str(ap.space) == "DRAM":
            lo = off
            hi = off
            for st, cn in dims:
                d = (cn - 1) * st
                if d > 0:
                    hi += d
                else:
                    lo += d
            return name, 0, 1, lo * esz, (hi + 1) * esz
        pst, pcn = dims[0]
        if pst <= 0:
            pst = 1 << 40
        p0 = off // pst
        c0 = off % pst
        if str(ap.space) == "PSUM":
            q0 = (p0 // 32) * 32
            q1 = ((p0 + pcn + 31) // 32) * 32
            return name, q0, q1, 0, 2048
        lo = c0
        hi = c0
        for st, cn in dims[1:]:
            d = (cn - 1) * st
            if d > 0:
                hi += d
            else:
                lo += d
        return name, p0, p0 + pcn, lo * esz, (hi + 1) * esz

    def _collect(self, reads, writes):
        deps = []
        rb = [self.box(a) for a in reads]
        wb = [self.box(a) for a in writes]
        for (name, p0, p1, lo, hi) in rb:
            m = self.mem.get(name)
            if m is None:
                continue
            for r in m[0]:
                if r[0] < p1 and p0 < r[1] and r[2] < hi and lo < r[3]:
                    deps.append((r[4], r[5]))
        for (name, p0, p1, lo, hi) in wb:
            m = self.mem.get(name)
            if m is None:
                continue
            for lst in m:
                for r in lst:
                    if r[0] < p1 and p0 < r[1] and r[2] < hi and lo < r[3]:
                        deps.append((r[4], r[5]))
        return deps, rb, wb

    def _record(self, rb, wb, key, val):
        for (name, p0, p1, lo, hi) in wb:
            m = self.mem.setdefault(name, [[], []])
            for i in (0, 1):
                m[i] = [r for r in m[i] if not (p0 <= r[0] and r[1] <= p1 and lo <= r[2] and r[3] <= hi)]
            m[0].append([p0, p1, lo, hi, key, val])
        for (name, p0, p1, lo, hi) in rb:
            m = self.mem.setdefault(name, [[], []])
            m[1] = [r for r in m[1] if not (r[4] == key and p0 <= r[0] and r[1] <= p1 and lo <= r[2] and r[3] <= hi)]
            m[1].append([p0, p1, lo, hi, key, val])

    def _wait(self, e, deps):
        best = {}
        for k, v in deps:
            if best.get(k, 0) < v:
                best[k] = v
        for k, v in best.items():
            if k == e and e not in self.self_sync:
                continue
            if self.seen[e].get(k, 0) < v:
                self.E[e].wait_ge(self.sems[k], v)
                self.seen[e][k] = v
                self.n_wait += 1

    def op(self, e, fn, reads=(), writes=()):
        deps, rb, wb = self._collect(reads, writes)
        self._wait(e, deps)
        ins = fn(self.E[e])
        ins.then_inc(self.sems[e], 1)
        self.cnt[e] += 1
        self._record(rb, wb, e, self.cnt[e])
        self.n_inst += 1
        return ins

    def dma(self, q, out, in_, sem):
        deps, rb, wb = self._collect([in_], [out])
        self._wait(q, deps)
        ins = self.E[q].dma_start(out=out, in_=in_)
        ins.then_inc(self.sems[sem], 16)
        self.cnt[sem] += 16
        self._record(rb, wb, sem, self.cnt[sem])
        self.n_inst += 1
        return ins

    def finish(self, e="sp"):
        deps = []
        for name, m in self.mem.items():
            for lst in m:
                for r in lst:
                    deps.append((r[4], r[5]))
        self._wait(e, deps)


class Builder:
    def __init__(self, stage=99):
        self.stage = stage
        nc = bass.Bass("TRN2", target_bir_lowering=False)
        nc.dge_precook = False
        self.nc = nc
        self.p = Prog(nc)
        self.dram = {}
        self.decl_io()
        self.alloc()

    def din(self, name, shape):
        self.dram[name] = self.nc.dram_tensor(name, list(shape), F32, kind="ExternalInput").ap()
        return self.dram[name]

    def dout(self, name, shape):
        self.dram[name] = self.nc.dram_tensor(name, list(shape), F32, kind="ExternalOutput").ap()
        return self.dram[name]

    def decl_io(self):
        self.din("xp", [TP, D])
        self.din("xs", [TS, D])
        self.din("st_h", [2, A_HEADS, 128, 128])
        self.din("st_g", [2, B_HEADS, B_DK, 128])
        self.din("ck", [PAST, C_KV * HD])
        self.din("cv", [PAST, C_KV * HD])
        self.din("cvec", [2, D])
        self.din("selv", [128, 16])
        self.din("w_mod", [DEPTH, D, NMOD * D])
        self.din("b_mod", [DEPTH, NMOD * D])
        self.din("ln_g", [DEPTH * 3, D])
        self.din("ln_b", [DEPTH * 3, D])
        self.din("ffn_w1", [DEPTH, 2, D, DFF])
        self.din("ffn_w3", [DEPTH, 2, D, DFF])
        self.din("ffn_w2", [DEPTH, 2, DFF, D])
        self.din("w_in_ab", [D, AB_IN])
        self.din("hgrn_lb", [2, 2, 512])
        self.din("gate_up", [2, GATE_RANK, 256])
        self.din("gate_b", [2, 256])
        self.din("norm_a", [1, 128])
        self.din("norm_b", [1, 128])
        self.din("w_out_ab", [D, D])
        self.din("w_qkv", [D, 1536])
        self.din("sink", [1, C_HEADS])
        self.din("w_out_c", [D, D])
        self.dout("yp", [TP, D])
        self.dout("ys", [512, D])
        self.dout("ns_h", [2, 2, A_HEADS, 128, 128])
        self.dout("ns_g", [2, 2, B_HEADS, B_DK, 128])
        self.dout("nk", [2, SEQ, C_KV * HD])
        self.dout("nv", [2, SEQ, C_KV * HD])

    def sb(self, name, shape, dt=F32):
        return self.nc.alloc_sbuf_tensor(name, list(shape), dt)

    def alloc(self):
        nc = self.nc
        self.X = self.sb("X", [128, KC, T])
        self.IDENT = self.sb("IDENT", [128, 128])
        self.ONES = self.sb("ONES", [128, 128], F32R)
        self.U1 = self.sb("U1", [128, 256])
        self.U2 = self.sb("U2", [128, 256], F32R)
        self.MF = self.U1[:, 0:128]
        self.MB = self.U1[:, 128:256]
        self.LM1 = self.U1[:, 0:128]
        self.LM3 = self.U1[:, 128:256]
        self.RESET = self.sb("RESET", [128, 256])
        self.PARS = self.sb("PARS", [128, 22])
        self.LB = self.sb("LB", [128, 8])
        self.OML = self.sb("OML", [128, 8])
        self.OMLH = self.sb("OMLH", [128, 8])
        self.LBH = self.sb("LBH", [128, 8])
        self.SELV = self.sb("SELV", [128, 16])
        self.TRW = self.sb("TRW", [128, 2, 12])
        self.NGB = self.sb("NGB", [128, 4])
        self.GUP = self.U2[:, 0:256]
        self.PERMR = self.U2[:, 0:128]
        self.EPSR = self.sb("EPSR", [128, 1])
        self.ES = self.sb("ES", [128, 16])
        self.TCOS = self.sb("TCOS", [128, 64])
        self.TSIN = self.sb("TSIN", [128, 64])
        self.MR = self.sb("MR", [128, 1])
        self.MC = self.sb("MC", [128, 1])
        self.SGN = self.sb("SGN", [128, 1])
        self.MODV = self.sb("MODV", [128, 72, 2])
        self.OPS = self.sb("OPS", [128, 3, KC, 2])
        self.GSC = self.sb("GSC", [128, 3, KC, 2])
        self.BM = self.sb("BM", [128, 72])
        self.LNG = self.sb("LNG", [128, 48])
        self.LNB = self.sb("LNB", [128, 48])
        self.CS = self.sb("CS", [128, 2, KC], F32R)
        self.NR = 27 * 1024
        self.NF = 3072
        self.R = self.sb("R", [128, self.NR], F32R)
        self.Fm = self.sb("Fm", [128, self.NF])
        self.PS = [nc.alloc_psum_tensor("PS%d" % i, [128, 512], F32) for i in range(8)]
        self.sem_w13 = [self.p.new_dma_sem() for _ in range(2)]
        self.sem_w2 = [self.p.new_dma_sem() for _ in range(4)]
        self.sem_wab = [self.p.new_dma_sem() for _ in range(7)]
        self.sem_st = self.p.new_dma_sem()
        self.sem_io = [self.p.new_dma_sem() for _ in range(2)]
        self.sem_misc = self.p.new_dma_sem()
        self.sem_out = self.p.new_dma_sem()
        self.n13 = 0
        self.n2 = 0
        self.nio = 0

    @staticmethod
    def _view(base, off, shape, total):
        n = 1
        for x in shape[1:]:
            n *= x
        assert off + n <= total, (off, n, total)
        v = base[0:shape[0], off:off + n]
        if len(shape) == 3:
            v = v.rearrange("p (a b) -> p a b", a=shape[1])
        elif len(shape) == 4:
            v = v.rearrange("p (a b c) -> p a b c", a=shape[1], b=shape[2])
        return v

    def r(self, off, shape):
        return self._view(self.R, off, shape, self.NR)

    def f(self, off, shape):
        return self._view(self.Fm, off, shape, self.NF)

    def mm(self, out, lhsT, rhs, start, stop):
        self.p.op("pe", lambda e: e.matmul(out, lhsT=lhsT, rhs=rhs, start=start, stop=stop),
                  reads=[lhsT, rhs], writes=[out])

    def tr(self, out, in_, n=128):
        ident = self.IDENT[0:in_.shape[0], 0:in_.shape[0]]
        self.p.op("pe", lambda e: e.transpose(out=out, in_=in_, identity=ident), reads=[in_, ident], writes=[out])

    def act(self, out, in_, func, bias=None, scale=None, eng="act"):
        kw = {}
        rd = [in_]
        if bias is not None:
            kw["bias"] = bias
            if not isinstance(bias, (int, float)):
                rd.append(bias)
        if scale is not None:
            kw["scale"] = scale
            if not isinstance(scale, (int, float)):
                rd.append(scale)
        self.p.op("act", lambda e: e.activation(out=out, in_=in_, func=func, **kw), reads=rd, writes=[out])

    def ts(self, out, in0, s1, s2, op0, op1=None, eng="dve"):
        rd = [in0]
        for s in (s1, s2):
            if s is not None and not isinstance(s, (int, float)):
                rd.append(s)
        if op1 is None:
            self.p.op(eng, lambda e: e.tensor_scalar(out=out, in0=in0, scalar1=s1, scalar2=None, op0=op0), reads=rd, writes=[out])
        else:
            self.p.op(eng, lambda e: e.tensor_scalar(out=out, in0=in0, scalar1=s1, scalar2=s2, op0=op0, op1=op1), reads=rd, writes=[out])

    def tt(self, out, in0, in1, op, eng="dve"):
        self.p.op(eng, lambda e: e.tensor_tensor(out=out, in0=in0, in1=in1, op=op), reads=[in0, in1], writes=[out])

    def stt(self, out, in0, scalar, in1, op0, op1, eng="dve"):
        rd = [in0, in1]
        if not isinstance(scalar, (int, float)):
            rd.append(scalar)
        self.p.op(eng, lambda e: e.scalar_tensor_tensor(out=out, in0=in0, scalar=scalar, in1=in1, op0=op0, op1=op1), reads=rd, writes=[out])

    def cp(self, out, in_, eng="dve"):
        if eng == "act":
            self.act(out, in_, AF.Copy)
        else:
            self.p.op(eng, lambda e: e.tensor_copy(out=out, in_=in_), reads=[in_], writes=[out])

    def memset(self, ap, val, eng="pool"):
        self.p.op(eng, lambda e: e.memset(ap, val), writes=[ap])

    def dma(self, out, in_, sem, q="sp"):
        self.p.dma(q, out, in_, sem)

    def dma_r(self, out, in_, sem, q="sp"):
        self.p.dma(q, out if out.dtype == F32R else out.bitcast(F32R), in_.bitcast(F32R), sem)

    def consts(self):
        self.memset(self.IDENT[:], 1.0)
        self.p.op("pool", lambda e: e.affine_select(out=self.IDENT[:], in_=self.IDENT[:], pattern=[[-1, 128]],
                                                    compare_op=ALU.is_equal, fill=0.0, base=0, channel_multiplier=1),
                  reads=[self.IDENT[:]], writes=[self.IDENT[:]])
        tmp = self.f(0, [128, 128])
        self.memset(tmp, 1.0)
        self.cp(self.ONES[:], tmp)
        for (M, cm, st) in ((self.MF, -1, 1), (self.MB, 1, -1)):
            self.memset(M[:], 1.0)
            self.p.op("pool", lambda e: e.affine_select(out=M[:], in_=M[:], pattern=[[st, 128]], compare_op=ALU.is_ge,
                                                        fill=0.0, base=0, channel_multiplier=cm),
                      reads=[M[:]], writes=[M[:]])
        self.memset(self.MF[0:64, 64:128], 0.0)
        self.memset(self.MB[64:128, 0:64], 0.0)
        self.memset(self.RESET[:], 0.0)
        self.memset(self.RESET[:, 0:256:64], 1.0)
        self.memset(self.EPSR[:], RMS_EPS)

    def load_fm(self, dst, src_rows, nrows):
        st = self.f(0, [128, 128])
        self.dma(st[0:nrows, :], src_rows, self.sem_misc)
        ps = self.PS[7][:, 0:nrows]
        self.tr(ps, st[0:nrows, :])
        self.cp(dst, ps)

    def load_small(self):
        self.load_fm(self.LNG[:], self.dram["ln_g"].rearrange("r (k p) -> (r k) p", p=128), 48)
        self.load_fm(self.LNB[:], self.dram["ln_b"].rearrange("r (k p) -> (r k) p", p=128), 48)
        st = self.f(0, [128, 128])
        self.dma(st[0:16, :], self.dram["cvec"].rearrange("g (k p) -> (g k) p", p=128), self.sem_misc)
        ps = self.PS[7][:, 0:16]
        self.tr(ps, st[0:16, :])
        self.act(self.CS[:].rearrange("p g k -> p (g k)"), ps, AF.Silu)

    def load_x(self):
        for tb in range(T // 128):
            src = self.dram["xp"][tb * 128:(tb + 1) * 128, :] if tb < TP // 128 else \
                self.dram["xs"][tb * 128 - TP:(tb + 1) * 128 - TP, :]
            s = self.nio % 2
            self.nio += 1
            st = self.f(s * 1024, [128, 1024])
            self.dma(st, src, self.sem_io[s])
            for hb in range(2):
                ps = self.PS[(tb * 2 + hb) % 4]
                for j in range(4):
                    k = hb * 4 + j
                    self.tr(ps[:, j * 128:(j + 1) * 128], st[:, k * 128:(k + 1) * 128])
                dst = self.X[:, hb * 4:(hb + 1) * 4, tb * 128:(tb + 1) * 128]
                self.cp(dst, ps[:].rearrange("p (a b) -> p a b", a=4), eng="dve" if hb == 0 else "act")

    def store_x(self):
        for ob in range(8):
            if ob < 4:
                dst = self.dram["yp"][ob * 128:(ob + 1) * 128, :]
                xc = ob * 128
            else:
                dst = self.dram["ys"][(ob - 4) * 128:(ob - 3) * 128, :]
                xc = self.W0 + 128 + (ob - 4) * 128
            s = self.nio % 2
            self.nio += 1
            st = self.f(s * 1024, [128, 1024])
            for hb in range(2):
                ps = self.PS[(ob * 2 + hb) % 4]
                for j in range(4):
                    k = hb * 4 + j
                    self.tr(ps[:, j * 128:(j + 1) * 128], self.X[:, k, xc:xc + 128])
                self.cp(st[:, hb * 512:(hb + 1) * 512], ps[:], eng="dve" if hb == 0 else "act")
            self.dma(dst, st, self.sem_io[s])

    def select_window(self):
        SV = self.SELV
        for k in range(KC):
            own = self.f(0, [128, 512])
            hp = self.f(512, [128, 128])
            hn = self.f(640, [128, 128])
            for t in range(4):
                xt = self.X[:, k, TP + t * 512:TP + (t + 1) * 512]
                if t == 0:
                    self.ts(own, xt, SV[:, t:t + 1], None, ALU.mult)
                    self.ts(hp, xt[:, 384:512], SV[:, 4 + t:5 + t], None, ALU.mult)
                    self.ts(hn, xt[:, 0:128], SV[:, 8 + t:9 + t], None, ALU.mult)
                else:
                    self.stt(own, xt, SV[:, t:t + 1], own, ALU.mult, ALU.add)
                    self.stt(hp, xt[:, 384:512], SV[:, 4 + t:5 + t], hp, ALU.mult, ALU.add)
                    self.stt(hn, xt[:, 0:128], SV[:, 8 + t:9 + t], hn, ALU.mult, ALU.add)
            self.cp(self.X[:, k, self.W0:self.W0 + 128], hp, eng="act")
            self.cp(self.X[:, k, self.W0 + 128:self.W0 + 640], own, eng="act")
            self.cp(self.X[:, k, self.W0 + 640:self.W0 + 768], hn, eng="act")

    def mod_vectors(self, l):
        self.load_fm(self.BM[:], self.dram["b_mod"][l].rearrange("(r p) -> r p", p=128), 72)
        wm = self.dram["w_mod"][l].rearrange("(k p) n -> p k n", p=128)
        pm = self.PS[6][:, 0:144].rearrange("p (c g) -> p c g", g=2)
        for blk in range(18):
            s = self.n13 % 2
            self.n13 += 1
            wt = self.r(self.o_w13 + s * 4096, [128, KC, 512])
            self.dma_r(wt, wm[:, :, blk * 512:(blk + 1) * 512], self.sem_w13[s])
            for q in range(4):
                oc = blk * 4 + q
                for k in range(KC):
                    self.mm(pm[:, oc, :], wt[:, k, q * 128:(q + 1) * 128], self.CS[:, :, k], k == 0, k == KC - 1)
        for g in range(2):
            self.tt(self.MODV[:, :, g], pm[:, :, g], self.BM[:], ALU.add)
        gmul = [0.5 / ALPHA, 1.0 / ALPHA, 0.5 / ALPHA]
        for s in range(3):
            self.ts(self.OPS[:, s, :, :], self.MODV[:, (3 * s + 1) * 8:(3 * s + 2) * 8, :], 1.0, None, ALU.add)
            self.ts(self.GSC[:, s, :, :], self.MODV[:, (3 * s + 2) * 8:(3 * s + 3) * 8, :], gmul[s], None, ALU.mult)

    def shift(self, s, k, g):
        return self.MODV[:, 3 * s * 8 + k, g:g + 1]

    def ln_range(self, lnidx, x0, n, o_r):
        ts_ = slice(x0, x0 + n)
        pa, pb = self.PS[4][:, 0:n], self.PS[5][:, 0:n]
        for k in range(KC):
            zr = self.r(o_r + (k % 2) * 512, [128, 512])[:, 0:n]
            sq = self.r(o_r + 1024 + (k % 2) * 512, [128, 512])[:, 0:n]
            self.act(zr, self.X[:, k, ts_], AF.Copy, scale=1.0 / 1024.0)
            self.act(sq, self.X[:, k, ts_], AF.Square, scale=1.0 / 32.0)
            self.mm(pa, self.ONES[:], zr, k == 0, k == KC - 1)
            self.mm(pb, self.ONES[:], sq, k == 0, k == KC - 1)
        m2 = self.f(self.o_lnf, [128, 512])[:, 0:n]
        self.act(m2, pa, AF.Square)
        self.tt(m2, pb, m2, ALU.subtract)
        self.act(m2, m2, AF.Ln, bias=self.EPSLN[:, 0:1])
        self.act(m2, m2, AF.Exp, scale=-0.5)
        for k in range(KC):
            xk = self.X[:, k, ts_]
            self.tt(xk, xk, pa, ALU.subtract)
            self.stt(xk, xk, self.LNG[:, lnidx * 8 + k:lnidx * 8 + k + 1], m2, ALU.mult, ALU.mult)
            self.act(xk, xk, AF.Identity, bias=self.LNB[:, lnidx * 8 + k:lnidx * 8 + k + 1])

    def ffn_tile(self, l, j, x0, n, g):
        s = 0 if j == 0 else 2
        ts_ = slice(x0, x0 + n)
        XM = self.r(self.o_xm, [128, KC, 512])[:, :, 0:n]
        HID = self.r(self.o_hid, [128, FC, 512])[:, :, 0:n]
        for k in range(KC):
            self.ts(XM[:, k, :], self.X[:, k, ts_], self.OPS[:, s, k, g:g + 1], self.shift(s, k, g), ALU.mult, ALU.add)
        w1 = self.dram["ffn_w1"][l, j].rearrange("(k p) f -> p k f", p=128)
        w3 = self.dram["ffn_w3"][l, j].rearrange("(k p) f -> p k f", p=128)
        w2 = self.dram["ffn_w2"][l, j].rearrange("(c p) o -> p c o", p=128)
        for fb in range(FC // 2):
            sl = self.n13 % 2
            self.n13 += 1
            wt = self.r(self.o_w13 + sl * 4096, [128, 2, KC, 256])
            self.dma_r(wt[:, 0], w1[:, :, fb * 256:(fb + 1) * 256], self.sem_w13[sl])
            self.dma_r(wt[:, 1], w3[:, :, fb * 256:(fb + 1) * 256], self.p.named_sem("w3_%d" % sl))
            for c in range(2):
                f = 2 * fb + c
                p1, p3 = self.PS[f % 2][:, 0:n], self.PS[2 + f % 2][:, 0:n]
                for k in range(KC):
                    self.mm(p1, wt[:, 0, k, c * 128:(c + 1) * 128], XM[:, k, :], k == 0, k == KC - 1)
                for k in range(KC):
                    self.mm(p3, wt[:, 1, k, c * 128:(c + 1) * 128], XM[:, k, :], k == 0, k == KC - 1)
                sg = self.f(self.o_sg + (f % 2) * 512, [128, 512])[:, 0:n]
                self.act(sg, p1, AF.Silu)
                self.tt(HID[:, f, :], sg, p3, ALU.mult)
        for half in range(2):
            for fb in range(FC // 2):
                sl = self.n2 % 4
                self.n2 += 1
                wt = self.r(self.o_w2 + sl * 1024, [128, 2, 512])
                self.dma_r(wt, w2[:, 2 * fb:2 * fb + 2, half * 512:(half + 1) * 512], self.sem_w2[sl])
                for c in range(2):
                    f = 2 * fb + c
                    for o in range(4):
                        self.mm(self.PS[4 + o][:, 0:n], wt[:, c, o * 128:(o + 1) * 128], HID[:, f, :], f == 0, f == FC - 1)
            for o in range(4):
                oc = half * 4 + o
                self.stt(self.X[:, oc, ts_], self.PS[4 + o][:, 0:n], self.GSC[:, s, oc, g:g + 1], self.X[:, oc, ts_], ALU.mult, ALU.add)
        self.ln_range(l * 3 + s, x0, n, self.o_ln)

    def ffn_multi(self, l, j, subs):
        s = 0 if j == 0 else 2
        o_w13, o_w2, o_xm, o_hid = 0, 8192, 10240, 18432
        ns = len(subs)
        XM = [self.r(o_xm + si * 4096, [128, KC, 512])[:, :, 0:subs[si][1]] for si in range(ns)]
        HID = [self.r(o_hid + si * 4096, [128, 8, 512])[:, :, 0:subs[si][1]] for si in range(ns)]
        for si, (x0, n, g) in enumerate(subs):
            for k in range(KC):
                self.ts(XM[si][:, k, :], self.X[:, k, x0:x0 + n], self.OPS[:, s, k, g:g + 1], self.shift(s, k, g), ALU.mult, ALU.add)
        w1 = self.dram["ffn_w1"][l, j].rearrange("(k p) f -> p k f", p=128)
        w3 = self.dram["ffn_w3"][l, j].rearrange("(k p) f -> p k f", p=128)
        w2 = self.dram["ffn_w2"][l, j].rearrange("(c p) o -> p c o", p=128)
        for (f0, f1) in ((0, 8), (8, 16), (16, 22)):
            for fb in range(f0 // 2, f1 // 2):
                sl = self.n13 % 2
                self.n13 += 1
                wt = self.r(o_w13 + sl * 4096, [128, 2, KC, 256])
                self.dma_r(wt[:, 0], w1[:, :, fb * 256:(fb + 1) * 256], self.sem_w13[sl])
                self.dma_r(wt[:, 1], w3[:, :, fb * 256:(fb + 1) * 256], self.p.named_sem("w3_%d" % sl))
                for c in range(2):
                    f = 2 * fb + c
                    for si, (x0, n, g) in enumerate(subs):
                        p1, p3 = self.PS[si][:, 0:n], self.PS[2 + si][:, 0:n]
                        for k in range(KC):
                            self.mm(p1, wt[:, 0, k, c * 128:(c + 1) * 128], XM[si][:, k, :], k == 0, k == KC - 1)
                        for k in range(KC):
                            self.mm(p3, wt[:, 1, k, c * 128:(c + 1) * 128], XM[si][:, k, :], k == 0, k == KC - 1)
                        sg = self.f(self.o_sg + si * 512, [128, 512])[:, 0:n]
                        self.act(sg, p1, AF.Silu)
                        self.tt(HID[si][:, f - f0, :], sg, p3, ALU.mult)
            for oq in range(4):
                pb = 4 if oq % 2 == 0 else 0
                for fb in range(f0 // 2, f1 // 2):
                    sl = self.n2 % 4
                    self.n2 += 1
                    wt = self.r(o_w2 + sl * 512, [128, 2, 256])
                    self.dma_r(wt, w2[:, 2 * fb:2 * fb + 2, oq * 256:(oq + 1) * 256], self.sem_w2[sl])
                    for c in range(2):
                        f = 2 * fb + c
                        for si, (x0, n, g) in enumerate(subs):
                            for o2 in range(2):
                                self.mm(self.PS[pb + si * 2 + o2][:, 0:n], wt[:, c, o2 * 128:(o2 + 1) * 128], HID[si][:, f - f0, :],
                                        f == f0, f == f1 - 1)
                for si, (x0, n, g) in enumerate(subs):
                    for o2 in range(2):
                        oc = oq * 2 + o2
                        xs_ = self.X[:, oc, x0:x0 + n]
                        self.stt(xs_, self.PS[pb + si * 2 + o2][:, 0:n], self.GSC[:, s, oc, g:g + 1], xs_, ALU.mult, ALU.add)
        for si, (x0, n, g) in enumerate(subs):
            self.ln_range(l * 3 + s, x0, n, o_hid)

    def ab_params(self):
        st = self.f(0, [128, 128])
        d = self.dram
        self.dma(st[0:1, :], d["norm_a"], self.sem_misc)
        self.dma(st[1:2, :], d["norm_b"], self.sem_misc)
        self.dma(st[2:6, :], d["gate_b"].rearrange("a (j p) -> (a j) p", p=128), self.sem_misc)
        self.dma(st[6:22, :], d["hgrn_lb"].rearrange("a b (h p) -> (a b h) p", p=128), self.sem_misc)
        ps = self.PS[7][:, 0:22]
        self.tr(ps, st[0:22, :])
        self.cp(self.PARS[:], ps)
        P = self.PARS
        for dr in range(2):
            a = P[:, 6 + dr * 8:6 + dr * 8 + 4]
            b = P[:, 6 + dr * 8 + 4:6 + dr * 8 + 8]
            self.tt(self.LB[:, dr * 4:dr * 4 + 4], a, b, ALU.subtract)
        self.act(self.LB[:], self.LB[:], AF.Sigmoid)
        self.ts(self.OML[:], self.LB[:], -1.0, 1.0, ALU.mult, ALU.add)
        self.ts(self.OMLH[:], self.OML[:], 0.5, None, ALU.mult)
        self.tt(self.LBH[:], self.LB[:], self.OMLH[:], ALU.add)
        self.ts(self.NGB[:], P[:, 2:6], -1.0, None, ALU.mult)

    def ab_unit_pass(self, u, dr, segs, g):
        hg = u < 4
        j = u - 4
        d = self.dram
        win = d["w_in_ab"].rearrange("(k p) n -> p k n", p=128)
        if hg:
            cols = [("q", u * 128), ("f", (1024 if dr == 0 else 1536) + u * 128), ("i", 512 + u * 128)]
            if dr == 1:
                cols.append(("g0", 2048 + u * 128))
        else:
            cols = [("q", 2560 + j * 128), ("f", 2816 + j * 128), ("v0", 3072 + j * 256), ("v1", 3072 + j * 256 + 128),
                    ("bz", 4000)]
            if dr == 1:
                cols += [("g0", 3584 + j * 256), ("g1", 3584 + j * 256 + 128)]
        W = {}
        if not hg:
            if (self.wab_base + 2) % 7 > (self.wab_base + 3) % 7:
                self.wab_base += 1
        for i, (nm, c0) in enumerate(cols):
            sl = (self.wab_base + i) % 7
            if nm == "v1":
                continue
            if nm == "v0":
                W["v"] = self.r(self.o_wab + sl * 1024, [128, KC, 256])
                self.dma_r(W["v"], win[:, :, c0:c0 + 256], self.sem_wab[sl])
                continue
            W[nm] = self.r(self.o_wab + sl * 1024, [128, KC, 128])
            self.dma_r(W[nm], win[:, :, c0:c0 + 128], self.sem_wab[sl])
        self.wab_base += len(cols)
        nh = 1 if hg else 2
        heads = list(range(nh))
        vw = 128 * nh
        NSB = 6
        SB = [self.r(self.o_sb + i * 128, [128, 128]) for i in range(NSB)]
        SC = [self.f(1536, [128, 128]), self.f(1664, [128, 128])]
        st = {"si": 0, "sc": 0}
        if not hg:
            self.ts(self.GUP[64:128, :], self.RESET[64:128, :], 0.0, None, ALU.mult)
            self.dma_r(self.GUP[96 + 16 * dr:112 + 16 * dr, :], d["gate_up"][dr], self.p.named_sem("gup"))
        st["started"] = False

        def seg_init(seg):
            (sx0, snht, slt0, ssample, ssidx) = seg
            if st["started"]:
                if dr == 0:
                    st["si"] += 1
                else:
                    st["sc"] += 1
            st["started"] = True
            src0 = None
            if ssample:
                src0 = d["st_h"][dr, u] if hg else d["st_g"][dr, 2 * j:2 * j + 2].rearrange("h d e -> (h d) e")
            if dr == 0:
                buf = SB[st["si"] % NSB]
                if ssample:
                    self.dma_r(buf, src0, self.p.named_sem("stf%d" % (st["si"] % NSB)))
                else:
                    self.ts(buf, self.IDENT[:], 0.0, None, ALU.mult)
            else:
                buf = SC[st["sc"] % 2]
                if ssample:
                    self.dma(buf, src0, self.p.named_sem("stb%d" % (st["sc"] % 2)))
                else:
                    self.memset(buf, 0.0)

        def seg_emit(seg):
            (sx0, snht, slt0, ssample, ssidx) = seg
            if ssample:
                return
            if dr == 0:
                S_fin = SB[st["si"] % NSB].bitcast(F32)
                sname = "sout%d" % (st["si"] % NSB)
            else:
                S_fin = SC[st["sc"] % 2]
                sname = "soutc%d" % (st["sc"] % 2)
            if hg:
                self.dma(d["ns_h"][ssidx, dr, u], S_fin, self.p.named_sem(sname))
            else:
                self.dma(d["ns_g"][ssidx, dr, 2 * j:2 * j + 2].rearrange("h d e -> (h d) e"), S_fin, self.p.named_sem(sname))

        HB = self.r(self.o_hb, [128, KC, 256])
        QD = self.r(self.o_qd, [128, 256])
        KI = self.r(self.o_ki, [128, 256])
        VT = self.r(self.o_vt, [128, 2, 256])
        AT = [self.r(self.o_at + i * 128, [128, 128]) for i in range(2)]
        BZ = self.r(self.o_at, [128, 256])
        SQ = self.r(self.o_at, [128, 256])
        KIT = self.r(self.o_hb + 256, [128, 2, 128])
        F0 = self.f(0, [128, 256])
        F1 = self.f(256, [128, 256])
        F2 = self.f(512, [128, 256])
        F3 = self.f(1792, [128, 256])
        TMP = self.f(768, [128, 128])
        AC = self.f(896, [128, 4])
        GS = [self.f(1024, [128, 256]), self.f(1280, [128, 256])]
        PS = self.PS
        psq, psf = PS[0][:, 0:256], PS[0][:, 256:512]
        MASK = self.MF if dr == 0 else self.MB
        order = []
        for seg in segs:
            (sx0, snht, slt0, ssample, ssidx) = seg
            hts = list(range(snht)) if dr == 0 else list(range(snht - 1, -1, -1))
            for n2, h_ in enumerate(hts):
                order.append({"x0": sx0 + h_ * 256, "lt0": slt0 + h_ * 256, "seg": seg, "first": n2 == 0, "last": n2 == snht - 1})
        bs = [0, 1] if dr == 0 else [1, 0]
        cseq = [(b, c) for b in bs for c in bs]

        def pr_(hh):
            return slice(0, 128) if hg else slice(64 * hh, 64 * hh + 64)

        def stage_a(hti, part=3):
            if part & 1:
                stage_a1(hti)
            if part & 2:
                stage_a2()

        def stage_a1(hti):
            t0 = hti["x0"]
            for k in range(KC):
                if k % 2 == 0:
                    self.act(HB[:, k, :], self.X[:, k, t0:t0 + 256], AF.Identity, bias=self.shift(1, k, g), scale=self.OPS[:, 1, k, g:g + 1])
                else:
                    self.ts(HB[:, k, :], self.X[:, k, t0:t0 + 256], self.OPS[:, 1, k, g:g + 1], self.shift(1, k, g), ALU.mult, ALU.add)
            for k in range(KC):
                self.mm(psq, W["q"][:, k, :], HB[:, k, :], k == 0, k == KC - 1)
            for k in range(KC):
                self.mm(psf, W["f"][:, k, :], HB[:, k, :], k == 0, k == KC - 1)
            if not hg:
                for k in range(KC):
                    self.mm(PS[3][:, 0:256], W["bz"][:, k, :], HB[:, k, :], k == 0, k == KC - 1)

        def stage_a2():
            for b in range(2):
                wv = W["i"] if hg else W["v"]
                for k in range(KC):
                    self.mm(PS[1][:, b * 256:b * 256 + vw], HB[:, k, b * 128:(b + 1) * 128], wv[:, k, :], k == 0, k == KC - 1)
            if dr == 1:
                for hh in heads:
                    for k in range(KC):
                        self.mm(PS[2][:, hh * 256:(hh + 1) * 256], W["g%d" % hh][:, k, :], HB[:, k, :], k == 0, k == KC - 1)

        def stage_b(gsi=0):
            if hg:
                self.act(F0, psf, AF.Tanh, scale=0.5)
                self.ts(F0, F0, self.OMLH[:, dr * 4 + u:dr * 4 + u + 1], self.LBH[:, dr * 4 + u:dr * 4 + u + 1], ALU.mult, ALU.add)
                self.ts(F1, F0, -1.0, 1.0, ALU.mult, ALU.add, eng="pool")
                kf = F1
            else:
                self.cp(BZ, PS[3][:, 0:256], eng="act")
                psl = PS[3][:, 256:512]
                self.mm(psl, self.GUP[64:128, j * 128:(j + 1) * 128], BZ[64:128, :], True, True)
                self.act(F0, psl, AF.Exp, scale=-1.0, bias=self.NGB[:, dr * 2 + j:dr * 2 + j + 1])
                self.act(F0, F0, AF.Ln, bias=self.ONEC[:, 0:1])
                self.act(F0, F0, AF.Exp, scale=-1.0 / GLA_TAU)
                kf = psf
            qs = None if hg else B_DK ** -0.5
            R1 = self.RESET[:, 0:256]
            if dr == 0:
                self.p.op("dve", lambda e: e.tensor_tensor_scan(out=F2, data0=R1, data1=F0, initial=1.0, op0=ALU.max, op1=ALU.mult),
                          reads=[R1, F0], writes=[F2])
                self.cp(AC, F2[:, 63:256:64])
                if hg:
                    self.tt(QD, psq, F2, ALU.mult)
                else:
                    self.stt(QD, psq, qs, F2, ALU.mult, ALU.mult)
                self.p.op("dve", lambda e: e.reciprocal(out=F2, in_=F2), reads=[F2], writes=[F2])
                self.tt(KI, kf, F2, ALU.mult)
            else:
                self.cp(F2[:, 1:256], F0[:, 0:255])
                self.memset(F2[:, 0:256:64], 1.0)
                self.p.op("dve", lambda e: e.tensor_tensor_scan(out=F3, data0=R1, data1=F2, initial=1.0, op0=ALU.max, op1=ALU.mult),
                          reads=[R1, F2], writes=[F3])
                self.tt(AC, F3[:, 63:256:64], F0[:, 63:256:64], ALU.mult)
                self.tt(KI, kf, F3, ALU.mult)
                self.p.op("dve", lambda e: e.reciprocal(out=F3, in_=F3), reads=[F3], writes=[F3])
                if hg:
                    self.tt(QD, psq, F3, ALU.mult)
                else:
                    self.stt(QD, psq, qs, F3, ALU.mult, ALU.mult)
            for b in range(2):
                if hg:
                    self.act(VT[:, b, 0:128], PS[1][:, b * 256:b * 256 + 128], AF.Silu)
                else:
                    self.cp(VT[:, b, :], PS[1][:, b * 256:(b + 1) * 256], eng="act")
            if dr == 1:
                for hh in heads:
                    self.act(GS[(hh + gsi) % 2], PS[2][:, hh * 256:(hh + 1) * 256], AF.Silu)

        def stage_c(PSO):
            pskv = []
            for idx, (b_, c_) in enumerate(cseq):
                bank = (PS[6] if c_ == 0 else PS[3]) if hg else (PS[6] if c_ == 0 else PS[1])
                w_ = 128 if hg else 256
                slot = 0 if idx < 2 else 1
                koff = 256 if (hg and c_ == 1) else 0
                pskv.append(bank[:, koff + slot * w_:koff + (slot + 1) * w_])
            first = [True, True]
            for hi, hh in enumerate(heads):
                pat = PS[5] if hi == 0 else PS[2 if dr == 0 else 3]
                pat_off = 0 if (hi == 0 or dr == 0) else 256
                for b in bs:
                    self.mm(pat[:, pat_off + b * 128:pat_off + (b + 1) * 128], KI[pr_(hh), b * 128:(b + 1) * 128],
                            QD[pr_(hh), b * 128:(b + 1) * 128], True, True)
                if hi == 0:
                    for b in bs:
                        self.tr(PS[3][:, b * 128:(b + 1) * 128], KI[:, b * 128:(b + 1) * 128].bitcast(F32))
                for b in bs:
                    self.tt(AT[b], pat[:, pat_off + b * 128:pat_off + (b + 1) * 128], MASK[:], ALU.mult)
                if hi == 0:
                    for b in bs:
                        self.cp(KIT[:, b, :], PS[3][:, b * 128:(b + 1) * 128], eng="act")
                    for idx, (b, c) in enumerate(cseq):
                        self.mm(pskv[idx], KIT[c * 64:(c + 1) * 64, b, :], VT[c * 64:(c + 1) * 64, b, 0:vw], True, True)
                for b in bs:
                    self.mm(PSO[hh][:, b * 128:(b + 1) * 128], VT[:, b, hh * 128:(hh + 1) * 128], AT[b], first[hh], False)
                    first[hh] = False
            return pskv

        def stage_d(pskv, item):
            if item["first"]:
                seg_init(item["seg"])
            states = []
            for idx, (b, c) in enumerate(cseq):
                ci = b * 2 + c
                a_c = AC[:, ci:ci + 1]
                if dr == 0:
                    S_cur = SB[st["si"] % NSB]
                    S_next = SB[(st["si"] + 1) % NSB]
                    states.append(S_cur)
                    if hg:
                        self.tt(TMP, pskv[idx], S_cur.bitcast(F32), ALU.add)
                    else:
                        self.tt(TMP[0:64, :], pskv[idx][0:64, 0:128], S_cur[0:64, :].bitcast(F32), ALU.add)
                        self.tt(TMP[64:128, :], pskv[idx][64:128, 128:256], S_cur[64:128, :].bitcast(F32), ALU.add)
                    self.ts(S_next, TMP, a_c, None, ALU.mult)
                    st["si"] += 1
                else:
                    C_cur = SC[st["sc"] % 2]
                    C_next = SC[(st["sc"] + 1) % 2]
                    S_sc = SB[st["si"] % NSB]
                    states.append(S_sc)
                    self.ts(S_sc, C_cur, a_c, None, ALU.mult)
                    if hg:
                        self.tt(C_next, pskv[idx], S_sc.bitcast(F32), ALU.add)
                    else:
                        self.tt(C_next[0:64, :], pskv[idx][0:64, 0:128], S_sc[0:64, :].bitcast(F32), ALU.add)
                        self.tt(C_next[64:128, :], pskv[idx][64:128, 128:256], S_sc[64:128, :].bitcast(F32), ALU.add)
                    st["si"] += 1
                    st["sc"] += 1
            if item["last"]:
                seg_emit(item["seg"])
            return states

        def stage_d2(states, PSO):
            for idx, (b, c) in enumerate(cseq):
                ccols = slice(b * 128 + c * 64, b * 128 + c * 64 + 64)
                for hh in heads:
                    self.mm(PSO[hh][:, ccols], states[idx][pr_(hh), :], QD[pr_(hh), ccols], False, idx == 3)

        E1 = self.f(2560, [128, 256])
        E2 = self.f(2816, [128, 256])

        def stage_e(hti, PSO, gsi, pst):
            lt0 = hti["lt0"]
            for hh in heads:
                head = u if hg else 4 + 2 * j + hh
                on = self.r(self.o_on + head * self.on_stride + lt0, [128, 256])
                pso = PSO[hh][:, 0:256]
                if dr == 0:
                    self.cp(on, pso, eng="act")
                else:
                    self.tt(E1, pso, on.bitcast(F32), ALU.add)
                    self.act(SQ, E1, AF.Square, scale=128.0 ** -0.5)
                    self.mm(pst, self.ONES[:], SQ, True, True)
                    self.act(E2, pst, AF.Ln, bias=self.EPSR[:, 0:1])
                    self.act(E2, E2, AF.Exp, scale=-0.5)
                    nw = self.PARS[:, 0:1] if hg else self.PARS[:, 1:2]
                    self.stt(E1, E1, nw, E2, ALU.mult, ALU.mult)
                    self.tt(on, E1, GS[(hh + gsi) % 2], ALU.mult)

        if hg:
            stage_a(order[0])
            prev = None
            for n_, hti in enumerate(order):
                PSOi = [PS[4] if n_ % 2 == 0 else PS[7]]
                stage_b(n_ % 2)
                if prev is not None:
                    stage_e(*prev)
                pskv = stage_c(PSOi)
                if n_ + 1 < len(order):
                    stage_a(order[n_ + 1], 1)
                states = stage_d(pskv, hti)
                stage_d2(states, PSOi)
                if n_ + 1 < len(order):
                    stage_a(order[n_ + 1], 2)
                prev = (hti, PSOi, n_ % 2, PS[2][:, 0:256])
            stage_e(*prev)
        else:
            PSOg = [PS[4], PS[7]]
            stage_a(order[0])
            for n_, hti in enumerate(order):
                stage_b(0)
                pskv = stage_c(PSOg)
                if n_ + 1 < len(order):
                    stage_a(order[n_ + 1], 1)
                states = stage_d(pskv, hti)
                stage_d2(states, PSOg)
                stage_e(hti, PSOg, 0, PS[5][:, 0:256])
                if n_ + 1 < len(order):
                    stage_a(order[n_ + 1], 2)

    def ab_outproj(self, l, x0, lt0, n, g):
        wo = self.dram["w_out_ab"].rearrange("(h p) o -> p h o", p=128)
        for half in range(2):
            WO = self.r(self.o_wab, [128, 8, 512])
            self.dma_r(WO, wo[:, :, half * 512:(half + 1) * 512], self.sem_wab[0])
            for o in range(4):
                for h in range(8):
                    on = self.r(self.o_on + h * self.on_stride + lt0, [128, 512])[:, 0:n]
                    self.mm(self.PS[o][:, 0:n], WO[:, h, o * 128:(o + 1) * 128], on, h == 0, h == 7)
            for o in range(4):
                oc = half * 4 + o
                xs_ = self.X[:, oc, x0:x0 + n]
                self.stt(xs_, self.PS[o][:, 0:n], self.GSC[:, 1, oc, g:g + 1], xs_, ALU.mult, ALU.add)
        self.ln_range(l * 3 + 1, x0, n, self.o_hb)

    def mixer_ab(self, l):
        self.o_on = 0
        self.on_stride = 2048
        self.o_wab = 16384
        self.o_hb = 23552
        o = 25600
        self.o_qd = o
        self.o_ki = o + 256
        self.o_vt = o + 512
        self.o_at = o + 1024
        self.o_sb = o + 1280
        assert self.o_sb + 768 <= self.NR
        self.ab_params()
        abn = int(os.environ.get("MK_ABN", "1000"))
        abskip = int(os.environ.get("MK_ABSKIP", "0"))
        cnt = 0
        self.wab_base = 0
        groups = (([(0, 1, 0, False, 0), (256, 1, 256, False, 1)], 0, 0, 512),
                  ([(512, 8, 0, True, None)], 1, 512, 2048))
        for (segs, g, gx0, ntok) in groups:
            for u in range(6):
                for dr in range(2):
                    cnt += 1
                    if cnt <= abskip or cnt > abskip + abn:
                        continue
                    self.ab_unit_pass(u, dr, segs, g)
            for off in range(0, ntok, 512):
                n = min(512, ntok - off)
                self.ab_outproj(l, gx0 + off, off, n, g)

    def c_consts(self):
        d = self.dram
        A = self.f(0, [128, 128])
        B = self.f(128, [128, 128])
        for (M, cm, st, base) in ((A, 1, -1, -16), (B, -1, 1, -16)):
            self.memset(M, 1.0)
            self.p.op("pool", lambda e: e.affine_select(out=M, in_=M, pattern=[[st, 128]], compare_op=ALU.is_equal,
                                                        fill=0.0, base=base, channel_multiplier=cm),
                      reads=[M], writes=[M])
        self.memset(A.rearrange("p (g c) -> p g c", c=32)[:, :, 16:32], 0.0)
        self.memset(B.rearrange("p (g c) -> p g c", c=32)[:, :, 0:16], 0.0)
        self.tt(self.PERMR[:], A, B, ALU.add)
        for (M, cm, st) in ((self.LM1, -1, 1), (self.LM3, 1, -1)):
            self.memset(M[:], 1.0)
            self.p.op("pool", lambda e: e.affine_select(out=M[:], in_=M[:], pattern=[[st, 128]], compare_op=ALU.is_ge,
                                                        fill=0.0, base=0, channel_multiplier=cm),
                      reads=[M[:]], writes=[M[:]])
        st_ = self.f(256, [128, 16])
        self.memset(st_, 0.0)
        self.dma(st_[0:1, :], d["sink"], self.sem_misc)
        ps = self.PS[7][:, 0:16]
        onesf = self.f(384, [128, 128])
        self.memset(onesf, 1.0)
        self.mm(ps, onesf, st_, True, True)
        self.act(self.ES[:], ps, AF.Exp)
        I32 = mybir.dt.int32
        pi_ = self.f(512, [128, 1]).bitcast(I32)
        self.p.op("pool", lambda e: e.iota(pi_, pattern=[[0, 1]], base=0, channel_multiplier=1), writes=[pi_])
        i16 = self.f(513, [128, 1]).bitcast(I32)
        self.ts(i16, pi_, 15, None, ALU.bitwise_and)
        m32 = self.f(514, [128, 1]).bitcast(I32)
        self.ts(m32, pi_, 32, None, ALU.bitwise_and)
        i16f = self.f(515, [128, 1])
        self.cp(i16f, i16)
        m32f = self.f(516, [128, 1])
        self.cp(m32f, m32)
        inv = self.f(517, [128, 1])
        self.act(inv, i16f, AF.Exp, scale=-float(np.log(ROPE_BASE)) / 16.0)
        b16 = self.f(518, [128, 1]).bitcast(I32)
        self.ts(b16, pi_, 16, None, ALU.bitwise_and)
        b16f = self.f(519, [128, 1])
        self.cp(b16f, b16)
        self.ts(self.SGN[:], b16f, 1.0 / 8.0, -1.0, ALU.mult, ALU.add)
        self.ts(self.MC[:], m32f, 1.0 / 32.0, None, ALU.mult)
        self.ts(self.MR[:], self.MC[:], -1.0, 1.0, ALU.mult, ALU.add)
        pos = self.f(640, [128, 64])
        self.p.op("pool", lambda e: e.iota(pos.bitcast(I32), pattern=[[1, 64]], base=0, channel_multiplier=0), writes=[pos])
        posf = self.f(704, [128, 64])
        self.cp(posf, pos.bitcast(I32))
        TWO_PI = 2.0 * float(np.pi)
        TRC = self.f(1024, [128, 64])
        TRS = self.f(1088, [128, 64])
        for (posoff, dsin, dcos) in ((0.0, self.TSIN[:], self.TCOS[:]), (-2.0, TRS, TRC)):
            ang = self.f(768, [128, 64])
            self.ts(ang, posf, posoff, None, ALU.add)
            self.ts(ang, ang, inv, None, ALU.mult)
            for (dst, shiftv) in ((dsin, 0.0), (dcos, float(np.pi) / 2.0)):
                a = self.f(832, [128, 64])
                kq = self.f(896, [128, 64])
                ki = self.f(960, [128, 64]).bitcast(I32)
                self.ts(a, ang, shiftv, None, ALU.add)
                self.ts(kq, a, 1.0 / TWO_PI, None, ALU.mult)
                self.cp(ki, kq)
                self.cp(kq, ki)
                self.stt(a, kq, -TWO_PI, a, ALU.mult, ALU.add)
                self.ts(kq, a, float(np.pi), -TWO_PI, ALU.is_gt, ALU.mult)
                self.tt(a, a, kq, ALU.add)
                self.ts(kq, a, -float(np.pi), TWO_PI, ALU.is_lt, ALU.mult)
                self.tt(a, a, kq, ALU.add)
                self.act(dst, a, AF.Sin)
        for (ci, T) in ((0, TRC), (1, TRS)):
            for jq in range(4):
                if jq == 0:
                    self.ts(self.TRW[:, ci, :], T[:, 0:12], self.SELV[:, 0:1], None, ALU.mult)
                else:
                    self.stt(self.TRW[:, ci, :], T[:, 8 * jq:8 * jq + 12], self.SELV[:, jq:jq + 1], self.TRW[:, ci, :], ALU.mult, ALU.add)

    def rope_tables_w(self, r0, nrows):
        n = nrows * 64
        cos_t = self.f(0, [128, 512])[:, 0:n]
        sin_t = self.f(512, [128, 512])[:, 0:n]
        for (dst, ci, T) in ((cos_t, 0, self.TCOS), (sin_t, 1, self.TSIN)):
            dv = dst.rearrange("p (r c) -> p r c", c=64)
            rowv = self.TRW[:, ci, r0:r0 + nrows].unsqueeze(2).to_broadcast([128, nrows, 64])
            colv = T[:, 0:64].unsqueeze(1).to_broadcast([128, nrows, 64])
            self.ts(dv, rowv, self.MR[:, 0:1], None, ALU.mult)
            self.stt(dv, colv, self.MC[:, 0:1], dv, ALU.mult, ALU.add)
        self.ts(sin_t, sin_t, self.SGN[:, 0:1], None, ALU.mult)
        return cos_t, sin_t

    def rope_tables(self, t):
        cos_t = self.f(0, [128, 512])
        sin_t = self.f(512, [128, 512])
        for (dst, T) in ((cos_t, self.TCOS), (sin_t, self.TSIN)):
            dv = dst.rearrange("p (r c) -> p r c", c=64)
            rowv = T[:, 8 * t:8 * t + 8].unsqueeze(2).to_broadcast([128, 8, 64])
            colv = T[:, 0:64].unsqueeze(1).to_broadcast([128, 8, 64])
            self.ts(dv, rowv, self.MR[:, 0:1], None, ALU.mult)
            self.stt(dv, colv, self.MC[:, 0:1], dv, ALU.mult, ALU.add)
        self.ts(sin_t, sin_t, self.SGN[:, 0:1], None, ALU.mult)
        return cos_t, sin_t

    def rope_apply(self, dst, ps, cos_t, sin_t, n):
        ZQ = self.r(self.o_zq, [128, 512])[:, 0:n]
        self.cp(ZQ, ps, eng="act")
        pz = self.PS[6][:, 0:n]
        self.mm(pz, self.PERMR[:], ZQ, True, True)
        t1 = self.f(1024, [128, 512])[:, 0:n]
        self.tt(t1, ZQ.bitcast(F32), cos_t[:, 0:n], ALU.mult)
        t2 = self.f(1536, [128, 512])[:, 0:n]
        self.tt(t2, pz, sin_t[:, 0:n], ALU.mult)
        self.tt(dst, t1, t2, ALU.add)

    def c_modulate(self, x0, n, g):
        HB = self.r(self.o_hb, [128, KC, 512])
        for k in range(KC):
            self.ts(HB[:, k, 0:n], self.X[:, k, x0:x0 + n], self.OPS[:, 1, k, g:g + 1], self.shift(1, k, g), ALU.mult, ALU.add)
        return HB

    def va_slot(self, kap):
        return [(0, 0, 64), (64, 2, 0), (130, 0, 64), (194, 2, 0)][kap]

    def c_kv_project(self, HB, n, WKV, kf_dst, va_dst, vblk0, rope, emit=None):
        for kp in range(2):
            ps = self.PS[kp][:, 0:n]
            for k in range(KC):
                self.mm(ps, WKV[:, k, kp * 128:(kp + 1) * 128], HB[:, k, 0:n], k == 0, k == KC - 1)
            if rope is not None:
                self.rope_apply(kf_dst(kp), ps, rope[0], rope[1], n)
            else:
                self.cp(kf_dst(kp), ps, eng="act")
        for b in range(n // 128):
            ps = self.PS[2 + b % 2][:, 0:256]
            for k in range(KC):
                self.mm(ps, HB[:, k, b * 128:(b + 1) * 128], WKV[:, k, 256:512], k == 0, k == KC - 1)
            va = va_dst(vblk0 + b)
            self.cp(va[:, 0:132].rearrange("p (a c) -> p a c", c=66)[:, :, 0:64], ps[:, 0:128].rearrange("p (a c) -> p a c", c=64), eng="act")
            self.cp(va[:, 130:262].rearrange("p (a c) -> p a c", c=66)[:, :, 0:64], ps[:, 128:256].rearrange("p (a c) -> p a c", c=64), eng="act")
            self.ts(va[:, 64:66], self.RESET[:, 1:3], 0.0, 1.0, ALU.mult, ALU.add)
            self.ts(va[:, 194:196], self.RESET[:, 1:3], 0.0, 1.0, ALU.mult, ALU.add)
            if emit is not None:
                sq, tok0 = emit
                stv = self.f(2048 + (b % 2) * 256, [128, 256])
                self.cp(stv, ps)
                self.dma(self.dram["nv"][sq, tok0 + b * 128:tok0 + (b + 1) * 128, :], stv, self.p.named_sem("nv%d" % (b % 2)))
                ps2 = self.PS[4 + b % 2][:, 0:256]
                for k in range(KC):
                    self.mm(ps2, HB[:, k, b * 128:(b + 1) * 128], WKV[:, k, 0:256], k == 0, k == KC - 1)
                stk = self.f(2560 + (b % 2) * 256, [128, 256])
                self.cp(stk, ps2, eng="act")
                self.dma(self.dram["nk"][sq, tok0 + b * 128:tok0 + (b + 1) * 128, :], stk, self.p.named_sem("nk%d" % (b % 2)))

    def c_q_project(self, HB, n, rope):
        wq = self.dram["w_qkv"].rearrange("(k p) n -> p k n", p=128)
        QF = self.r(self.o_qf, [128, 8, 512])
        WS = self.r(self.o_ws, [128, KC, 4, 128])
        for grp in range(2):
            for s_ in range(2):
                for i in range(4):
                    c0 = grp * 512 + s_ * 256 + i * 64
                    self.dma_r(WS[:, :, i, s_ * 64:(s_ + 1) * 64], wq[:, :, c0:c0 + 64], self.p.named_sem("wq%d_%d" % (s_, i)))
            for i in range(4):
                pair = grp * 4 + i
                ps = self.PS[pair % 2][:, 0:n]
                for k in range(KC):
                    self.mm(ps, WS[:, k, i, :], HB[:, k, 0:n], k == 0, k == KC - 1)
                if rope is not None:
                    self.rope_apply(QF[:, pair, 0:n], ps, rope[0], rope[1], n)
                else:
                    self.cp(QF[:, pair, 0:n], ps, eng="act")
        return QF

    def c_attend(self, QF, nqb, keysets):
        OT = self.r(self.o_ot, [128, 4, 1024])
        NPT = 4
        PT = [self.r(self.o_ws + i * 512, [128, 512]) for i in range(NPT)]
        SCB = [self.PS[0], self.PS[1], self.PS[7]]
        tasks = []
        for h in range(C_HEADS):
            for ki, ks in enumerate(keysets):
                tasks.append((h, ki, ks))
        nk_for = [sum(1 for ks in keysets if ks[2] <= qb <= ks[3]) for qb in range(nqb)]

        def hinfo(h):
            kap = h // 4
            half = kap % 2
            pair = (h % 4) + (0 if h < 8 else 4)
            return kap, half, pair, slice(64 * half, 64 * half + 64)

        def score(i):
            h, ki, (kf_fn, va_fn, qlo, qhi, masks, halo) = tasks[i]
            kap, half, pair, hp = hinfo(h)
            ncol = (qhi - qlo + 1) * 128
            ps = SCB[i % 3][:, 0:ncol]
            pt = PT[i % NPT][:, 0:ncol]
            self.mm(ps, kf_fn(kap, half), QF[hp, pair, qlo * 128:qlo * 128 + ncol], True, True)
            self.act(pt, ps, AF.Exp, scale=HD ** -0.5)
            for qb in range(qlo, qhi + 1):
                if qb in masks:
                    sl = slice((qb - qlo) * 128, (qb - qlo + 1) * 128)
                    if halo is None:
                        self.tt(pt[:, sl], pt[:, sl].bitcast(F32), masks[qb][:], ALU.mult)
                    else:
                        self.stt(pt[:, sl], pt[:, sl].bitcast(F32), halo, masks[qb][:], ALU.mult, ALU.mult)

        seen = {}

        def pv(i):
            h, ki, (kf_fn, va_fn, qlo, qhi, masks, halo) = tasks[i]
            kap, half, pair, hp = hinfo(h)
            c0, o_off, d_off = self.va_slot(kap)
            ncol = (qhi - qlo + 1) * 128
            pt = PT[i % NPT][:, 0:ncol]
            for qb in range(qlo, qhi + 1):
                sl = slice((qb - qlo) * 128, (qb - qlo + 1) * 128)
                seen[(h, qb)] = seen.get((h, qb), 0) + 1
                self.mm(self.PS[2 + qb][:, 0:66], pt[:, sl], va_fn(kap)[:, c0:c0 + 66], seen[(h, qb)] == 1, seen[(h, qb)] == nk_for[qb])
            if ki == len(keysets) - 1:
                for qb in range(nqb):
                    po = self.PS[2 + qb]
                    rd = self.f(2048 + 16 * qb, [128, 1])
                    self.ts(rd, po[:, d_off:d_off + 1], self.ES[:, h:h + 1], None, ALU.add)
                    self.p.op("dve", lambda e: e.reciprocal(out=rd, in_=rd), reads=[rd], writes=[rd])
                    self.ts(OT[:, qb, h * 64:(h + 1) * 64], po[:, o_off:o_off + 64], rd, None, ALU.mult)

        LOOK = 2
        for i in range(min(LOOK, len(tasks))):
            score(i)
        for i in range(len(tasks)):
            if i + LOOK < len(tasks):
                score(i + LOOK)
            pv(i)
        return OT

    def c_outproj(self, l, OT, x0, nqb, g):
        n = nqb * 128
        OA = self.r(self.o_hb, [128, KC, 512])
        for qb in range(nqb):
            for hb in range(2):
                ps = self.PS[hb]
                for j in range(4):
                    c = hb * 4 + j
                    self.tr(ps[:, j * 128:(j + 1) * 128], OT[:, qb, c * 128:(c + 1) * 128].bitcast(F32))
                self.cp(OA[:, hb * 4:(hb + 1) * 4, qb * 128:(qb + 1) * 128], ps[:].rearrange("p (a b) -> p a b", a=4),
                        eng="act" if hb else "dve")
        wo = self.dram["w_out_c"].rearrange("(c p) o -> p c o", p=128)
        for half in range(2):
            WO = self.r(self.o_qf, [128, KC, 512])
            self.dma_r(WO, wo[:, :, half * 512:(half + 1) * 512], self.p.named_sem("woc"))
            for o in range(4):
                for c in range(KC):
                    self.mm(self.PS[4 + o][:, 0:n], WO[:, c, o * 128:(o + 1) * 128], OA[:, c, 0:n], c == 0, c == KC - 1)
            for o in range(4):
                oc = half * 4 + o
                xs_ = self.X[:, oc, x0:x0 + n]
                self.stt(xs_, self.PS[4 + o][:, 0:n], self.GSC[:, 1, oc, g:g + 1], xs_, ALU.mult, ALU.add)
        self.ln_range(l * 3 + 1, x0, n, self.o_ws)

    def mixer_c(self, l):
        d = self.dram
        self.o_kf = 0
        self.o_va = 4096
        self.o_kc = self.o_va + 16 * 262
        self.o_vca = self.o_kc + 1024
        self.o_hb = self.o_vca + 4 * 262
        self.o_ws = self.o_hb + 4096
        self.o_qf = self.o_ws + 4096
        self.o_ot = self.o_qf + 4096
        self.o_zq = self.o_ot + 4096
        assert self.o_zq + 512 <= self.NR, self.o_zq
        self.c_consts()
        KF = self.r(self.o_kf, [128, 2, 2048])
        VA = self.r(self.o_va, [128, 16, 262])
        KCF = self.r(self.o_kc, [128, 2, 512])
        VCA = self.r(self.o_vca, [128, 4, 262])
        wq = d["w_qkv"].rearrange("(k p) n -> p k n", p=128)
        WKV = self.r(self.o_ws, [128, KC, 512])

        def load_wkv():
            self.dma_r(WKV, wq[:, :, 1024:1536], self.p.named_sem("wkv"))
        for sq in range(2):
            x0 = sq * SEQ
            HB = self.c_modulate(x0, SEQ, 0)
            load_wkv()
            self.c_kv_project(HB, SEQ, WKV, lambda kp: KF[:, kp, 0:SEQ], lambda b: VA[:, b, :], 0, None, emit=(sq, 0))
            QF = self.c_q_project(HB, SEQ, None)
            keysets = []
            for kb in range(2):
                keysets.append((lambda kap, half, kb=kb: KF[64 * half:64 * half + 64, kap // 2, kb * 128:(kb + 1) * 128],
                                lambda kap, kb=kb: VA[:, kb, :], 0, 1, {}, None))
            OT = self.c_attend(QF, 2, keysets)
            self.c_outproj(l, OT, x0, 2, 0)
        stg = self.r(self.o_ot, [128, 4, 256])
        self.dma_r(stg, d["ck"].rearrange("(b p) c -> p b c", p=128), self.p.named_sem("ck"))
        for b in range(4):
            for kp in range(2):
                ps = self.PS[(b * 2 + kp) % 2][:, 0:128]
                self.tr(ps, stg[:, b, kp * 128:(kp + 1) * 128].bitcast(F32))
                self.cp(KCF[:, kp, b * 128:(b + 1) * 128], ps, eng="act" if kp else "dve")
        cvv = d["cv"].rearrange("(b p) c -> p b c", p=128)
        for kap in range(4):
            c0 = [0, 66, 130, 196][kap]
            self.dma_r(VCA[:, :, c0:c0 + 64], cvv[:, :, kap * 64:(kap + 1) * 64], self.p.named_sem("cv%d" % kap))
        for b in range(4):
            self.ts(VCA[:, b, 64:66], self.RESET[:, 1:3], 0.0, 1.0, ALU.mult, ALU.add)
            self.ts(VCA[:, b, 194:196], self.RESET[:, 1:3], 0.0, 1.0, ALU.mult, ALU.add)
        load_wkv()
        W0 = self.W0
        for (off, n_, r0) in ((0, 512, 0), (512, 256, 8)):
            HB = self.c_modulate(W0 + off, n_, 1)
            rope = self.rope_tables_w(r0, n_ // 64)
            self.c_kv_project(HB, n_, WKV, lambda kp, off=off, n_=n_: KF[:, kp, off:off + n_], lambda b: VA[:, b, :], off // 128, rope)
        HB = self.c_modulate(W0 + 128, 512, 1)
        rope = self.rope_tables_w(2, 8)
        QF = self.c_q_project(HB, 512, rope)
        keysets = []
        for cb in range(4):
            keysets.append((lambda kap, half, cb=cb: KCF[64 * half:64 * half + 64, kap // 2, cb * 128:(cb + 1) * 128],
                            lambda kap, cb=cb: VCA[:, cb, :], 0, 3, {}, None))
        for kb in range(6):
            qlo = max(1, kb - 1) - 1
            qhi = min(4, kb + 1) - 1
            masks = {}
            if 0 <= kb - 2 <= 3:
                masks[kb - 2] = self.LM1
            if 0 <= kb <= 3:
                masks[kb] = self.LM3
            halo = None
            if kb == 0:
                halo = self.SELV[:, 12:13]
            if kb == 5:
                halo = self.SELV[:, 13:14]
            keysets.append((lambda kap, half, kb=kb: KF[64 * half:64 * half + 64, kap // 2, kb * 128:(kb + 1) * 128],
                            lambda kap, kb=kb: VA[:, kb, :], qlo, qhi, masks, halo))
        OT = self.c_attend(QF, 4, keysets)
        self.c_outproj(l, OT, W0 + 128, 4, 1)

    def build(self):
        self.o_w13 = 0
        self.o_w2 = 8192
        self.o_xm = 12288
        self.o_hid = 16384
        self.o_ln = self.o_hid + 18 * 512
        self.o_sg = 0
        self.o_lnf = 1024
        self.consts()
        self.EPSLN = self.sb("EPSLN", [128, 1])
        self.memset(self.EPSLN[:], LN_EPS / (ALPHA * ALPHA))
        self.ONEC = self.sb("ONEC", [128, 1])
        self.memset(self.ONEC[:], 1.0)
        self.load_small()
        self.dma(self.SELV[:], self.dram["selv"], self.p.named_sem("selv"))
        self.load_x()
        self.W0 = TP
        full = [[(0, 512, 0), (512, 512, 1)], [(1024, 512, 1), (1536, 512, 1)], [(2048, 512, 1)]]
        win = [[(0, 512, 0), (self.W0, 512, 1)], [(self.W0 + 512, 256, 1)]]
        own = [[(0, 512, 0), (self.W0 + 128, 512, 1)]]
        for l in range(DEPTH):
            self.mod_vectors(l)
            for subs in (full if l == 0 else win):
                self.ffn_multi(l, 0, subs)
            if self.stage == 1 + 3 * l:
                break
            if l == 0:
                self.mixer_ab(l)
                self.select_window()
            else:
                self.mixer_c(l)
            if self.stage == 2 + 3 * l:
                break
            for subs in (win if l == 0 else own):
                self.ffn_multi(l, 1, subs)
            if self.stage == 3 + 3 * l:
                break
        self.store_x()
        self.p.finish("sp")
        return self.nc


_CACHE = {}


def get_program(stage=99):
    if stage not in _CACHE:
        _CACHE[stage] = Builder(stage).build()
    return _CACHE[stage]


def _selv(j):
    v = np.zeros((16,), np.float32)
    v[j] = 1.0
    if j > 0:
        v[4 + j - 1] = 1.0
        v[12] = 1.0
    if j < 3:
        v[8 + j + 1] = 1.0
        v[13] = 1.0
    return np.ascontiguousarray(np.broadcast_to(v, (128, 16))).astype(np.float32)


def shard_inputs(inp):
    f = lambda a: np.ascontiguousarray(np.asarray(a, dtype=np.float32))
    maps = []
    for c in range(NCORES):
        sb = c // 4
        m = {
            "xp": f(inp["x_prompt"][2 * c:2 * c + 2].reshape(TP, D)),
            "xs": f(inp["x_sample"][sb]),
            "st_h": f(inp["state_hgrn"][sb, 0]),
            "st_g": f(inp["state_gla"][sb, 0]),
            "ck": f(inp["cache_k"][sb, 0].reshape(PAST, C_KV * HD)),
            "cv": f(inp["cache_v"][sb, 0].reshape(PAST, C_KV * HD)),
            "cvec": f(np.stack([inp["c_ctx"], inp["c"][sb]], axis=0)),
            "selv": _selv(c % 4),
            "w_mod": f(inp["w_mod"]),
            "b_mod": f(inp["b_mod"]),
            "ln_g": f(inp["ln_g"].reshape(DEPTH * 3, D)),
            "ln_b": f(inp["ln_b"].reshape(DEPTH * 3, D)),
            "ffn_w1": f(inp["ffn_w1"]),
            "ffn_w3": f(inp["ffn_w3"]),
            "ffn_w2": f(inp["ffn_w2"]),
            "w_in_ab": f(inp["w_in_ab"][0]),
            "hgrn_lb": f(inp["hgrn_lb"]),
            "gate_up": f(inp["gla_gate_up"][0]),
            "gate_b": f(inp["gla_gate_b"][0]),
            "norm_a": f(inp["norm_a"]),
            "norm_b": f(inp["norm_b"]),
            "w_out_ab": f(inp["w_out_ab"][0]),
            "w_qkv": f(inp["w_qkv_c"][0]),
            "sink": f(inp["sink_c"]),
            "w_out_c": f(inp["w_out_c"][0]),
        }
        maps.append(m)
    return maps


def gather_outputs(res):
    r = res.results
    B = 16
    yp = np.concatenate([r[c]["yp"].reshape(2, SEQ, D) for c in range(NCORES)], axis=0)
    ys = np.stack([np.concatenate([r[4 * b + q]["ys"] for q in range(4)], axis=0) for b in range(2)], axis=0)
    nsh = np.concatenate([r[c]["ns_h"].reshape(2, 1, 2, A_HEADS, 128, 128) for c in range(NCORES)], axis=0)
    nsg = np.concatenate([r[c]["ns_g"].reshape(2, 1, 2, B_HEADS, B_DK, 128) for c in range(NCORES)], axis=0)
    nk = np.concatenate([r[c]["nk"].reshape(2, 1, SEQ, C_KV, HD) for c in range(NCORES)], axis=0)
    nv = np.concatenate([r[c]["nv"].reshape(2, 1, SEQ, C_KV, HD) for c in range(NCORES)], axis=0)
    return (yp.astype(np.float32), ys.astype(np.float32), nsh.astype(np.float32), nsg.astype(np.float32),
            nk.astype(np.float32), nv.astype(np.float32))


def kernel(**inputs):
    stage = int(os.environ.get("MK_STAGE", "99"))
    nc = get_program(stage)
    maps = shard_inputs(inputs)
    res = run_bass_kernel_spmd(nc, maps, core_ids=list(range(NCORES)))
    return gather_outputs(res)
```

```python
import os
import numpy as np
import concourse.bass as bass
import concourse.mybir as mybir
from concourse.bass_utils import run_bass_kernel_spmd

F32 = mybir.dt.float32
F32R = mybir.dt.float32r
AF = mybir.ActivationFunctionType
ALU = mybir.AluOpType

D = 1024
KC = 8
DFF = 2816
FC = 22
NMOD = 9
DEPTH = 2
SEQ = 256
TP = 512
TS = 2048
T = TP + TS
NT = T // 512
A_HEADS = 4
B_HEADS = 4
B_DK = 64
GATE_RANK = 16
GLA_TAU = 16.0
AB_IN = 4128
C_HEADS = 16
C_KV = 4
HD = 64
PAST = 512
ALPHA = (2.0 * DEPTH) ** 0.25
LN_EPS = 1e-5
RMS_EPS = 1e-6
ROPE_BASE = 10000.0
NCORES = 8


class Prog:
    def __init__(self, nc):
        self.nc = nc
        self.E = {"pe": nc.tensor, "act": nc.scalar, "dve": nc.vector, "pool": nc.gpsimd, "sp": nc.sync}
        self.sems = {}
        self.cnt = {}
        for e in self.E:
            self.sems[e] = nc.alloc_semaphore("s_" + e)
            self.cnt[e] = 0
        self.seen = {e: {} for e in self.E}
        self.ndma_sem = 0
        self.n_inst = 0
        self.n_wait = 0
        self.self_sync = set(os.environ.get("MK_SELFSYNC", "act,dve,pool").split(",")) - {""}
        self.named = {}
        self.mem = {}

    def named_sem(self, name):
        if name not in self.named:
            self.named[name] = self.new_dma_sem()
        return self.named[name]

    def new_dma_sem(self):
        k = "d%d" % self.ndma_sem
        self.ndma_sem += 1
        self.sems[k] = self.nc.alloc_semaphore("s_" + k)
        self.cnt[k] = 0
        return k

    @staticmethod
    def box(ap):
        esz = mybir.dt.size(ap.dtype)
        dims = ap.ap
        off = ap.offset
        name = ap.tensor.name
        if str(ap.space) == "DRAM":
            lo = off
            hi = off
            for st, cn in dims:
                d = (cn - 1) * st
                if d > 0:
                    hi += d
                else:
                    lo += d
            return name, 0, 1, lo * esz, (hi + 1) * esz
        pst, pcn = dims[0]
        if pst <= 0:
            pst = 1 << 40
        p0 = off // pst
        c0 = off % pst
        if str(ap.space) == "PSUM":
            q0 = (p0 // 32) * 32
            q1 = ((p0 + pcn + 31) // 32) * 32
            return name, q0, q1, 0, 2048
        lo = c0
        hi = c0
        for st, cn in dims[1:]:
            d = (cn - 1) * st
            if d > 0:
                hi += d
            else:
                lo += d
        return name, p0, p0 + pcn, lo * esz, (hi + 1) * esz

    def _collect(self, reads, writes):
        deps = []
        rb = [self.box(a) for a in reads]
        wb = [self.box(a) for a in writes]
        for (name, p0, p1, lo, hi) in rb:
            m = self.mem.get(name)
            if m is None:
                continue
            for r in m[0]:
                if r[0] < p1 and p0 < r[1] and r[2] < hi and lo < r[3]:
                    deps.append((r[4], r[5]))
        for (name, p0, p1, lo, hi) in wb:
            m = self.mem.get(name)
            if m is None:
                continue
            for lst in m:
                for r in lst:
                    if r[0] < p1 and p0 < r[1] and r[2] < hi and lo < r[3]:
                        deps.append((r[4], r[5]))
        return deps, rb, wb

    def _record(self, rb, wb, key, val):
        for (name, p0, p1, lo, hi) in wb:
            m = self.mem.setdefault(name, [[], []])
            for i in (0, 1):
                m[i] = [r for r in m[i] if not (p0 <= r[0] and r[1] <= p1 and lo <= r[2] and r[3] <= hi)]
            m[0].append([p0, p1, lo, hi, key, val])
        for (name, p0, p1, lo, hi) in rb:
            m = self.mem.setdefault(name, [[], []])
            m[1] = [r for r in m[1] if not (r[4] == key and p0 <= r[0] and r[1] <= p1 and lo <= r[2] and r[3] <= hi)]
            m[1].append([p0, p1, lo, hi, key, val])

    def _wait(self, e, deps):
        best = {}
        for k, v in deps:
            if best.get(k, 0) < v:
                best[k] = v
        for k, v in best.items():
            if k == e and e not in self.self_sync:
                continue
            if self.seen[e].get(k, 0) < v:
                self.E[e].wait_ge(self.sems[k], v)
                self.seen[e][k] = v
                self.n_wait += 1

    def op(self, e, fn, reads=(), writes=()):
        deps, rb, wb = self._collect(reads, writes)
        self._wait(e, deps)
        ins = fn(self.E[e])
        ins.then_inc(self.sems[e], 1)
        self.cnt[e] += 1
        self._record(rb, wb, e, self.cnt[e])
        self.n_inst += 1
        return ins

    def dma(self, q, out, in_, sem):
        deps, rb, wb = self._collect([in_], [out])
        self._wait(q, deps)
        ins = self.E[q].dma_start(out=out, in_=in_)
        ins.then_inc(self.sems[sem], 16)
        self.cnt[sem] += 16
        self._record(rb, wb, sem, self.cnt[sem])
        self.n_inst += 1
        return ins

    def finish(self, e="sp"):
        deps = []
        for name, m in self.mem.items():
            for lst in m:
                for r in lst:
                    deps.append((r[4], r[5]))
        self._wait(e, deps)


class Builder:
    def __init__(self, stage=99):
        self.stage = stage
        nc = bass.Bass("TRN2", target_bir_lowering=False)
        nc.dge_precook = False
        self.nc = nc
        self.p = Prog(nc)
        self.dram = {}
        self.decl_io()
        self.alloc()

    def din(self, name, shape):
        self.dram[name] = self.nc.dram_tensor(name, list(shape), F32, kind="ExternalInput").ap()
        return self.dram[name]

    def dout(self, name, shape):
        self.dram[name] = self.nc.dram_tensor(name, list(shape), F32, kind="ExternalOutput").ap()
        return self.dram[name]

    def decl_io(self):
        self.din("xp", [TP, D])
        self.din("xs", [TS, D])
        self.din("st_h", [2, A_HEADS, 128, 128])
        self.din("st_g", [2, B_HEADS, B_DK, 128])
        self.din("ck", [PAST, C_KV * HD])
        self.din("cv", [PAST, C_KV * HD])
        self.din("cvec", [2, D])
        self.din("selv", [128, 16])
        self.din("w_mod", [DEPTH, D, NMOD * D])
        self.din("b_mod", [DEPTH, NMOD * D])
        self.din("ln_g", [DEPTH * 3, D])
        self.din("ln_b", [DEPTH * 3, D])
        self.din("ffn_w1", [DEPTH, 2, D, DFF])
        self.din("ffn_w3", [DEPTH, 2, D, DFF])
        self.din("ffn_w2", [DEPTH, 2, DFF, D])
        self.din("w_in_ab", [D, AB_IN])
        self.din("hgrn_lb", [2, 2, 512])
        self.din("gate_up", [2, GATE_RANK, 256])
        self.din("gate_b", [2, 256])
        self.din("norm_a", [1, 128])
        self.din("norm_b", [1, 128])
        self.din("w_out_ab", [D, D])
        self.din("w_qkv", [D, 1536])
        self.din("sink", [1, C_HEADS])
        self.din("w_out_c", [D, D])
        self.dout("yp", [TP, D])
        self.dout("ys", [512, D])
        self.dout("ns_h", [2, 2, A_HEADS, 128, 128])
        self.dout("ns_g", [2, 2, B_HEADS, B_DK, 128])
        self.dout("nk", [2, SEQ, C_KV * HD])
        self.dout("nv", [2, SEQ, C_KV * HD])

    def sb(self, name, shape, dt=F32):
        return self.nc.alloc_sbuf_tensor(name, list(shape), dt)

    def alloc(self):
        nc = self.nc
        self.X = self.sb("X", [128, KC, T])
        self.IDENT = self.sb("IDENT", [128, 128])
        self.ONES = self.sb("ONES", [128, 128], F32R)
        self.U1 = self.sb("U1", [128, 256])
        self.U2 = self.sb("U2", [128, 256], F32R)
        self.MF = self.U1[:, 0:128]
        self.MB = self.U1[:, 128:256]
        self.LM1 = self.U1[:, 0:128]
        self.LM3 = self.U1[:, 128:256]
        self.RESET = self.sb("RESET", [128, 256])
        self.PARS = self.sb("PARS", [128, 22])
        self.LB = self.sb("LB", [128, 8])
        self.OML = self.sb("OML", [128, 8])
        self.OMLH = self.sb("OMLH", [128, 8])
        self.LBH = self.sb("LBH", [128, 8])
        self.SELV = self.sb("SELV", [128, 16])
        self.TRW = self.sb("TRW", [128, 2, 12])
        self.NGB = self.sb("NGB", [128, 4])
        self.GUP = self.U2[:, 0:256]
        self.PERMR = self.U2[:, 0:128]
        self.EPSR = self.sb("EPSR", [128, 1])
        self.ES = self.sb("ES", [128, 16])
        self.TCOS = self.sb("TCOS", [128, 64])
        self.TSIN = self.sb("TSIN", [128, 64])
        self.MR = self.sb("MR", [128, 1])
        self.MC = self.sb("MC", [128, 1])
        self.SGN = self.sb("SGN", [128, 1])
        self.MODV = self.sb("MODV", [128, 72, 2])
        self.OPS = self.sb("OPS", [128, 3, KC, 2])
        self.GSC = self.sb("GSC", [128, 3, KC, 2])
        self.BM = self.sb("BM", [128, 72])
        self.LNG = self.sb("LNG", [128, 48])
        self.LNB = self.sb("LNB", [128, 48])
        self.CS = self.sb("CS", [128, 2, KC], F32R)
        self.NR = 27 * 1024
        self.NF = 3072
        self.R = self.sb("R", [128, self.NR], F32R)
        self.Fm = self.sb("Fm", [128, self.NF])
        self.PS = [nc.alloc_psum_tensor("PS%d" % i, [128, 512], F32) for i in range(8)]
        self.sem_w13 = [self.p.new_dma_sem() for _ in range(2)]
        self.sem_w2 = [self.p.new_dma_sem() for _ in range(4)]
        self.sem_wab = [self.p.new_dma_sem() for _ in range(7)]
        self.sem_st = self.p.new_dma_sem()
        self.sem_io = [self.p.new_dma_sem() for _ in range(2)]
        self.sem_misc = self.p.new_dma_sem()
        self.sem_out = self.p.new_dma_sem()
        self.n13 = 0
        self.n2 = 0
        self.nio = 0

    @staticmethod
    def _view(base, off, shape, total):
        n = 1
        for x in shape[1:]:
            n *= x
        assert off + n <= total, (off, n, total)
        v = base[0:shape[0], off:off + n]
        if len(shape) == 3:
            v = v.rearrange("p (a b) -> p a b", a=shape[1])
        elif len(shape) == 4:
            v = v.rearrange("p (a b c) -> p a b c", a=shape[1], b=shape[2])
        return v

    def r(self, off, shape):
        return self._view(self.R, off, shape, self.NR)

    def f(self, off, shape):
        return self._view(self.Fm, off, shape, self.NF)

    def mm(self, out, lhsT, rhs, start, stop):
        self.p.op("pe", lambda e: e.matmul(out, lhsT=lhsT, rhs=rhs, start=start, stop=stop),
                  reads=[lhsT, rhs], writes=[out])

    def tr(self, out, in_, n=128):
        ident = self.IDENT[0:in_.shape[0], 0:in_.shape[0]]
        self.p.op("pe", lambda e: e.transpose(out=out, in_=in_, identity=ident), reads=[in_, ident], writes=[out])

    def act(self, out, in_, func, bias=None, scale=None, eng="act"):
        kw = {}
        rd = [in_]
        if bias is not None:
            kw["bias"] = bias
            if not isinstance(bias, (int, float)):
                rd.append(bias)
        if scale is not None:
            kw["scale"] = scale
            if not isinstance(scale, (int, float)):
                rd.append(scale)
        self.p.op("act", lambda e: e.activation(out=out, in_=in_, func=func, **kw), reads=rd, writes=[out])

    def ts(self, out, in0, s1, s2, op0, op1=None, eng="dve"):
        rd = [in0]
        for s in (s1, s2):
            if s is not None and not isinstance(s, (int, float)):
                rd.append(s)
        if op1 is None:
            self.p.op(eng, lambda e: e.tensor_scalar(out=out, in0=in0, scalar1=s1, scalar2=None, op0=op0), reads=rd, writes=[out])
        else:
            self.p.op(eng, lambda e: e.tensor_scalar(out=out, in0=in0, scalar1=s1, scalar2=s2, op0=op0, op1=op1), reads=rd, writes=[out])

    def tt(self, out, in0, in1, op, eng="dve"):
        self.p.op(eng, lambda e: e.tensor_tensor(out=out, in0=in0, in1=in1, op=op), reads=[in0, in1], writes=[out])

    def stt(self, out, in0, scalar, in1, op0, op1, eng="dve"):
        rd = [in0, in1]
        if not isinstance(scalar, (int, float)):
            rd.append(scalar)
        self.p.op(eng, lambda e: e.scalar_tensor_tensor(out=out, in0=in0, scalar=scalar, in1=in1, op0=op0, op1=op1), reads=rd, writes=[out])

    def cp(self, out, in_, eng="dve"):
        if eng == "act":
            self.act(out, in_, AF.Copy)
        else:
            self.p.op(eng, lambda e: e.tensor_copy(out=out, in_=in_), reads=[in_], writes=[out])

    def memset(self, ap, val, eng="pool"):
        self.p.op(eng, lambda e: e.memset(ap, val), writes=[ap])

    def dma(self, out, in_, sem, q="sp"):
        self.p.dma(q, out, in_, sem)

    def dma_r(self, out, in_, sem, q="sp"):
        self.p.dma(q, out if out.dtype == F32R else out.bitcast(F32R), in_.bitcast(F32R), sem)

    def consts(self):
        self.memset(self.IDENT[:], 1.0)
        self.p.op("pool", lambda e: e.affine_select(out=self.IDENT[:], in_=self.IDENT[:], pattern=[[-1, 128]],
                                                    compare_op=ALU.is_equal, fill=0.0, base=0, channel_multiplier=1),
                  reads=[self.IDENT[:]], writes=[self.IDENT[:]])
        tmp = self.f(0, [128, 128])
        self.memset(tmp, 1.0)
        self.cp(self.ONES[:], tmp)
        for (M, cm, st) in ((self.MF, -1, 1), (self.MB, 1, -1)):
            self.memset(M[:], 1.0)
            self.p.op("pool", lambda e: e.affine_select(out=M[:], in_=M[:], pattern=[[st, 128]], compare_op=ALU.is_ge,
                                                        fill=0.0, base=0, channel_multiplier=cm),
                      reads=[M[:]], writes=[M[:]])
        self.memset(self.MF[0:64, 64:128], 0.0)
        self.memset(self.MB[64:128, 0:64], 0.0)
        self.memset(self.RESET[:], 0.0)
        self.memset(self.RESET[:, 0:256:64], 1.0)
        self.memset(self.EPSR[:], RMS_EPS)

    def load_fm(self, dst, src_rows, nrows):
        st = self.f(0, [128, 128])
        self.dma(st[0:nrows, :], src_rows, self.sem_misc)
        ps = self.PS[7][:, 0:nrows]
        self.tr(ps, st[0:nrows, :])
        self.cp(dst, ps)

    def load_small(self):
        self.load_fm(self.LNG[:], self.dram["ln_g"].rearrange("r (k p) -> (r k) p", p=128), 48)
        self.load_fm(self.LNB[:], self.dram["ln_b"].rearrange("r (k p) -> (r k) p", p=128), 48)
        st = self.f(0, [128, 128])
        self.dma(st[0:16, :], self.dram["cvec"].rearrange("g (k p) -> (g k) p", p=128), self.sem_misc)
        ps = self.PS[7][:, 0:16]
        self.tr(ps, st[0:16, :])
        self.act(self.CS[:].rearrange("p g k -> p (g k)"), ps, AF.Silu)

    def load_x(self):
        for tb in range(T // 128):
            src = self.dram["xp"][tb * 128:(tb + 1) * 128, :] if tb < TP // 128 else \
                self.dram["xs"][tb * 128 - TP:(tb + 1) * 128 - TP, :]
            s = self.nio % 2
            self.nio += 1
            st = self.f(s * 1024, [128, 1024])
            self.dma(st, src, self.sem_io[s])
            for hb in range(2):
                ps = self.PS[(tb * 2 + hb) % 4]
                for j in range(4):
                    k = hb * 4 + j
                    self.tr(ps[:, j * 128:(j + 1) * 128], st[:, k * 128:(k + 1) * 128])
                dst = self.X[:, hb * 4:(hb + 1) * 4, tb * 128:(tb + 1) * 128]
                self.cp(dst, ps[:].rearrange("p (a b) -> p a b", a=4), eng="dve" if hb == 0 else "act")

    def store_x(self):
        for ob in range(8):
            if ob < 4:
                dst = self.dram["yp"][ob * 128:(ob + 1) * 128, :]
                xc = ob * 128
            else:
                dst = self.dram["ys"][(ob - 4) * 128:(ob - 3) * 128, :]
                xc = self.W0 + 128 + (ob - 4) * 128
            s = self.nio % 2
            self.nio += 1
            st = self.f(s * 1024, [128, 1024])
            for hb in range(2):
                ps = self.PS[(ob * 2 + hb) % 4]
                for j in range(4):
                    k = hb * 4 + j
                    self.tr(ps[:, j * 128:(j + 1) * 128], self.X[:, k, xc:xc + 128])
                self.cp(st[:, hb * 512:(hb + 1) * 512], ps[:], eng="dve" if hb == 0 else "act")
            self.dma(dst, st, self.sem_io[s])

    def select_window(self):
        SV = self.SELV
        for k in range(KC):
            own = self.f(0, [128, 512])
            hp = self.f(512, [128, 128])
            hn = self.f(640, [128, 128])
            for t in range(4):
                xt = self.X[:, k, TP + t * 512:TP + (t + 1) * 512]
                if t == 0:
                    self.ts(own, xt, SV[:, t:t + 1], None, ALU.mult)
                    self.ts(hp, xt[:, 384:512], SV[:, 4 + t:5 + t], None, ALU.mult)
                    self.ts(hn, xt[:, 0:128], SV[:, 8 + t:9 + t], None, ALU.mult)
                else:
                    self.stt(own, xt, SV[:, t:t + 1], own, ALU.mult, ALU.add)
                    self.stt(hp, xt[:, 384:512], SV[:, 4 + t:5 + t], hp, ALU.mult, ALU.add)
                    self.stt(hn, xt[:, 0:128], SV[:, 8 + t:9 + t], hn, ALU.mult, ALU.add)
            self.cp(self.X[:, k, self.W0:self.W0 + 128], hp, eng="act")
            self.cp(self.X[:, k, self.W0 + 128:self.W0 + 640], own, eng="act")
            self.cp(self.X[:, k, self.W0 + 640:self.W0 + 768], hn, eng="act")

    def mod_vectors(self, l):
        self.load_fm(self.BM[:], self.dram["b_mod"][l].rearrange("(r p) -> r p", p=128), 72)
        wm = self.dram["w_mod"][l].rearrange("(k p) n -> p k n", p=128)
        pm = self.PS[6][:, 0:144].rearrange("p (c g) -> p c g", g=2)
        for blk in range(18):
            s = self.n13 % 2
            self.n13 += 1
            wt = self.r(self.o_w13 + s * 4096, [128, KC, 512])
            self.dma_r(wt, wm[:, :, blk * 512:(blk + 1) * 512], self.sem_w13[s])
            for q in range(4):
                oc = blk * 4 + q
                for k in range(KC):
                    self.mm(pm[:, oc, :], wt[:, k, q * 128:(q + 1) * 128], self.CS[:, :, k], k == 0, k == KC - 1)
        for g in range(2):
            self.tt(self.MODV[:, :, g], pm[:, :, g], self.BM[:], ALU.add)
        gmul = [0.5 / ALPHA, 1.0 / ALPHA, 0.5 / ALPHA]
        for s in range(3):
            self.ts(self.OPS[:, s, :, :], self.MODV[:, (3 * s + 1) * 8:(3 * s + 2) * 8, :], 1.0, None, ALU.add)
            self.ts(self.GSC[:, s, :, :], self.MODV[:, (3 * s + 2) * 8:(3 * s + 3) * 8, :], gmul[s], None, ALU.mult)

    def shift(self, s, k, g):
        return self.MODV[:, 3 * s * 8 + k, g:g + 1]

    def ln_range(self, lnidx, x0, n, o_r):
        ts_ = slice(x0, x0 + n)
        pa, pb = self.PS[4][:, 0:n], self.PS[5][:, 0:n]
        for k in range(KC):
            zr = self.r(o_r + (k % 2) * 512, [128, 512])[:, 0:n]
            sq = self.r(o_r + 1024 + (k % 2) * 512, [128, 512])[:, 0:n]
            self.act(zr, self.X[:, k, ts_], AF.Copy, scale=1.0 / 1024.0)
            self.act(sq, self.X[:, k, ts_], AF.Square, scale=1.0 / 32.0)
            self.mm(pa, self.ONES[:], zr, k == 0, k == KC - 1)
            self.mm(pb, self.ONES[:], sq, k == 0, k == KC - 1)
        m2 = self.f(self.o_lnf, [128, 512])[:, 0:n]
        self.act(m2, pa, AF.Square)
        self.tt(m2, pb, m2, ALU.subtract)
        self.act(m2, m2, AF.Ln, bias=self.EPSLN[:, 0:1])
        self.act(m2, m2, AF.Exp, scale=-0.5)
        for k in range(KC):
            xk = self.X[:, k, ts_]
            self.tt(xk, xk, pa, ALU.subtract)
            self.stt(xk, xk, self.LNG[:, lnidx * 8 + k:lnidx * 8 + k + 1], m2, ALU.mult, ALU.mult)
            self.act(xk, xk, AF.Identity, bias=self.LNB[:, lnidx * 8 + k:lnidx * 8 + k + 1])

    def ffn_tile(self, l, j, x0, n, g):
        s = 0 if j == 0 else 2
        ts_ = slice(x0, x0 + n)
        XM = self.r(self.o_xm, [128, KC, 512])[:, :, 0:n]
        HID = self.r(self.o_hid, [128, FC, 512])[:, :, 0:n]
        for k in range(KC):
            self.ts(XM[:, k, :], self.X[:, k, ts_], self.OPS[:, s, k, g:g + 1], self.shift(s, k, g), ALU.mult, ALU.add)
        w1 = self.dram["ffn_w1"][l, j].rearrange("(k p) f -> p k f", p=128)
        w3 = self.dram["ffn_w3"][l, j].rearrange("(k p) f -> p k f", p=128)
        w2 = self.dram["ffn_w2"][l, j].rearrange("(c p) o -> p c o", p=128)
        for fb in range(FC // 2):
            sl = self.n13 % 2
            self.n13 += 1
            wt = self.r(self.o_w13 + sl * 4096, [128, 2, KC, 256])
            self.dma_r(wt[:, 0], w1[:, :, fb * 256:(fb + 1) * 256], self.sem_w13[sl])
            self.dma_r(wt[:, 1], w3[:, :, fb * 256:(fb + 1) * 256], self.p.named_sem("w3_%d" % sl))
            for c in range(2):
                f = 2 * fb + c
                p1, p3 = self.PS[f % 2][:, 0:n], self.PS[2 + f % 2][:, 0:n]
                for k in range(KC):
                    self.mm(p1, wt[:, 0, k, c * 128:(c + 1) * 128], XM[:, k, :], k == 0, k == KC - 1)
                for k in range(KC):
                    self.mm(p3, wt[:, 1, k, c * 128:(c + 1) * 128], XM[:, k, :], k == 0, k == KC - 1)
                sg = self.f(self.o_sg + (f % 2) * 512, [128, 512])[:, 0:n]
                self.act(sg, p1, AF.Silu)
                self.tt(HID[:, f, :], sg, p3, ALU.mult)
        for half in range(2):
            for fb in range(FC // 2):
                sl = self.n2 % 4
                self.n2 += 1
                wt = self.r(self.o_w2 + sl * 1024, [128, 2, 512])
                self.dma_r(wt, w2[:, 2 * fb:2 * fb + 2, half * 512:(half + 1) * 512], self.sem_w2[sl])
                for c in range(2):
                    f = 2 * fb + c
                    for o in range(4):
                        self.mm(self.PS[4 + o][:, 0:n], wt[:, c, o * 128:(o + 1) * 128], HID[:, f, :], f == 0, f == FC - 1)
            for o in range(4):
                oc = half * 4 + o
                self.stt(self.X[:, oc, ts_], self.PS[4 + o][:, 0:n], self.GSC[:, s, oc, g:g + 1], self.X[:, oc, ts_], ALU.mult, ALU.add)
        self.ln_range(l * 3 + s, x0, n, self.o_ln)

    def ffn_multi(self, l, j, subs):
        s = 0 if j == 0 else 2
        o_w13, o_w2, o_xm, o_hid = 0, 8192, 10240, 18432
        ns = len(subs)
        XM = [self.r(o_xm + si * 4096, [128, KC, 512])[:, :, 0:subs[si][1]] for si in range(ns)]
        HID = [self.r(o_hid + si * 4096, [128, 8, 512])[:, :, 0:subs[si][1]] for si in range(ns)]
        for si, (x0, n, g) in enumerate(subs):
            for k in range(KC):
                self.ts(XM[si][:, k, :], self.X[:, k, x0:x0 + n], self.OPS[:, s, k, g:g + 1], self.shift(s, k, g), ALU.mult, ALU.add)
        w1 = self.dram["ffn_w1"][l, j].rearrange("(k p) f -> p k f", p=128)
        w3 = self.dram["ffn_w3"][l, j].rearrange("(k p) f -> p k f", p=128)
        w2 = self.dram["ffn_w2"][l, j].rearrange("(c p) o -> p c o", p=128)
        for (f0, f1) in ((0, 8), (8, 16), (16, 22)):
            for fb in range(f0 // 2, f1 // 2):
                sl = self.n13 % 2
                self.n13 += 1
                wt = self.r(o_w13 + sl * 4096, [128, 2, KC, 256])
                self.dma_r(wt[:, 0], w1[:, :, fb * 256:(fb + 1) * 256], self.sem_w13[sl])
                self.dma_r(wt[:, 1], w3[:, :, fb * 256:(fb + 1) * 256], self.p.named_sem("w3_%d" % sl))
                for c in range(2):
                    f = 2 * fb + c
                    for si, (x0, n, g) in enumerate(subs):
                        p1, p3 = self.PS[si][:, 0:n], self.PS[2 + si][:, 0:n]
                        for k in range(KC):
                            self.mm(p1, wt[:, 0, k, c * 128:(c + 1) * 128], XM[si][:, k, :], k == 0, k == KC - 1)
                        for k in range(KC):
                            self.mm(p3, wt[:, 1, k, c * 128:(c + 1) * 128], XM[si][:, k, :], k == 0, k == KC - 1)
                        sg = self.f(self.o_sg + si * 512, [128, 512])[:, 0:n]
                        self.act(sg, p1, AF.Silu)
                        self.tt(HID[si][:, f - f0, :], sg, p3, ALU.mult)
            for oq in range(4):
                pb = 4 if oq % 2 == 0 else 0
                for fb in range(f0 // 2, f1 // 2):
                    sl = self.n2 % 4
                    self.n2 += 1
                    wt = self.r(o_w2 + sl * 512, [128, 2, 256])
                    self.dma_r(wt, w2[:, 2 * fb:2 * fb + 2, oq * 256:(oq + 1) * 256], self.sem_w2[sl])
                    for c in range(2):
                        f = 2 * fb + c
                        for si, (x0, n, g) in enumerate(subs):
                            for o2 in range(2):
                                self.mm(self.PS[pb + si * 2 + o2][:, 0:n], wt[:, c, o2 * 128:(o2 + 1) * 128], HID[si][:, f - f0, :],
                                        f == f0, f == f1 - 1)
                for si, (x0, n, g) in enumerate(subs):
                    for o2 in range(2):
                        oc = oq * 2 + o2
                        xs_ = self.X[:, oc, x0:x0 + n]
                        self.stt(xs_, self.PS[pb + si * 2 + o2][:, 0:n], self.GSC[:, s, oc, g:g + 1], xs_, ALU.mult, ALU.add)
        for si, (x0, n, g) in enumerate(subs):
            self.ln_range(l * 3 + s, x0, n, o_hid)

    def ab_params(self):
        st = self.f(0, [128, 128])
        d = self.dram
        self.dma(st[0:1, :], d["norm_a"], self.sem_misc)
        self.dma(st[1:2, :], d["norm_b"], self.sem_misc)
        self.dma(st[2:6, :], d["gate_b"].rearrange("a (j p) -> (a j) p", p=128), self.sem_misc)
        self.dma(st[6:22, :], d["hgrn_lb"].rearrange("a b (h p) -> (a b h) p", p=128), self.sem_misc)
        ps = self.PS[7][:, 0:22]
        self.tr(ps, st[0:22, :])
        self.cp(self.PARS[:], ps)
        P = self.PARS
        for dr in range(2):
            a = P[:, 6 + dr * 8:6 + dr * 8 + 4]
            b = P[:, 6 + dr * 8 + 4:6 + dr * 8 + 8]
            self.tt(self.LB[:, dr * 4:dr * 4 + 4], a, b, ALU.subtract)
        self.act(self.LB[:], self.LB[:], AF.Sigmoid)
        self.ts(self.OML[:], self.LB[:], -1.0, 1.0, ALU.mult, ALU.add)
        self.ts(self.OMLH[:], self.OML[:], 0.5, None, ALU.mult)
        self.tt(self.LBH[:], self.LB[:], self.OMLH[:], ALU.add)
        self.ts(self.NGB[:], P[:, 2:6], -1.0, None, ALU.mult)

    def ab_unit_pass(self, u, dr, segs, g):
        hg = u < 4
        j = u - 4
        d = self.dram
        win = d["w_in_ab"].rearrange("(k p) n -> p k n", p=128)
        if hg:
            cols = [("q", u * 128), ("f", (1024 if dr == 0 else 1536) + u * 128), ("i", 512 + u * 128)]
            if dr == 1:
                cols.append(("g0", 2048 + u * 128))
        else:
            cols = [("q", 2560 + j * 128), ("f", 2816 + j * 128), ("v0", 3072 + j * 256), ("v1", 3072 + j * 256 + 128),
                    ("bz", 4000)]
            if dr == 1:
                cols += [("g0", 3584 + j * 256), ("g1", 3584 + j * 256 + 128)]
        W = {}
        if not hg:
            if (self.wab_base + 2) % 7 > (self.wab_base + 3) % 7:
                self.wab_base += 1
        for i, (nm, c0) in enumerate(cols):
            sl = (self.wab_base + i) % 7
            if nm == "v1":
                continue
            if nm == "v0":
                W["v"] = self.r(self.o_wab + sl * 1024, [128, KC, 256])
                self.dma_r(W["v"], win[:, :, c0:c0 + 256], self.sem_wab[sl])
                continue
            W[nm] = self.r(self.o_wab + sl * 1024, [128, KC, 128])
            self.dma_r(W[nm], win[:, :, c0:c0 + 128], self.sem_wab[sl])
        self.wab_base += len(cols)
        nh = 1 if hg else 2
        heads = list(range(nh))
        vw = 128 * nh
        NSB = 6
        SB = [self.r(self.o_sb + i * 128, [128, 128]) for i in range(NSB)]
        SC = [self.f(1536, [128, 128]), self.f(1664, [128, 128])]
        st = {"si": 0, "sc": 0}
        if not hg:
            self.ts(self.GUP[64:128, :], self.RESET[64:128, :], 0.0, None, ALU.mult)
            self.dma_r(self.GUP[96 + 16 * dr:112 + 16 * dr, :], d["gate_up"][dr], self.p.named_sem("gup"))
        st["started"] = False

        def seg_init(seg):
            (sx0, snht, slt0, ssample, ssidx) = seg
            if st["started"]:
                if dr == 0:
                    st["si"] += 1
                else:
                    st["sc"] += 1
            st["started"] = True
            src0 = None
            if ssample:
                src0 = d["st_h"][dr, u] if hg else d["st_g"][dr, 2 * j:2 * j + 2].rearrange("h d e -> (h d) e")
            if dr == 0:
                buf = SB[st["si"] % NSB]
                if ssample:
                    self.dma_r(buf, src0, self.p.named_sem("stf%d" % (st["si"] % NSB)))
                else:
                    self.ts(buf, self.IDENT[:], 0.0, None, ALU.mult)
            else:
                buf = SC[st["sc"] % 2]
                if ssample:
                    self.dma(buf, src0, self.p.named_sem("stb%d" % (st["sc"] % 2)))
                else:
                    self.memset(buf, 0.0)

        def seg_emit(seg):
            (sx0, snht, slt0, ssample, ssidx) = seg
            if ssample:
                return
            if dr == 0:
                S_fin = SB[st["si"] % NSB].bitcast(F32)
                sname = "sout%d" % (st["si"] % NSB)
            else:
                S_fin = SC[st["sc"] % 2]
                sname = "soutc%d" % (st["sc"] % 2)
            if hg:
                self.dma(d["ns_h"][ssidx, dr, u], S_fin, self.p.named_sem(sname))
            else:
                self.dma(d["ns_g"][ssidx, dr, 2 * j:2 * j + 2].rearrange("h d e -> (h d) e"), S_fin, self.p.named_sem(sname))

        HB = self.r(self.o_hb, [128, KC, 256])
        QD = self.r(self.o_qd, [128, 256])
        KI = self.r(self.o_ki, [128, 256])
        VT = self.r(self.o_vt, [128, 2, 256])
        AT = [self.r(self.o_at + i * 128, [128, 128]) for i in range(2)]
        BZ = self.r(self.o_at, [128, 256])
        SQ = self.r(self.o_at, [128, 256])
        KIT = self.r(self.o_hb + 256, [128, 2, 128])
        F0 = self.f(0, [128, 256])
        F1 = self.f(256, [128, 256])
        F2 = self.f(512, [128, 256])
        F3 = self.f(1792, [128, 256])
        TMP = self.f(768, [128, 128])
        AC = self.f(896, [128, 4])
        GS = [self.f(1024, [128, 256]), self.f(1280, [128, 256])]
        PS = self.PS
        psq, psf = PS[0][:, 0:256], PS[0][:, 256:512]
        MASK = self.MF if dr == 0 else self.MB
        order = []
        for seg in segs:
            (sx0, snht, slt0, ssample, ssidx) = seg
            hts = list(range(snht)) if dr == 0 else list(range(snht - 1, -1, -1))
            for n2, h_ in enumerate(hts):
                order.append({"x0": sx0 + h_ * 256, "lt0": slt0 + h_ * 256, "seg": seg, "first": n2 == 0, "last": n2 == snht - 1})
        bs = [0, 1] if dr == 0 else [1, 0]
        cseq = [(b, c) for b in bs for c in bs]

        def pr_(hh):
            return slice(0, 128) if hg else slice(64 * hh, 64 * hh + 64)

        def stage_a(hti, part=3):
            if part & 1:
                stage_a1(hti)
            if part & 2:
                stage_a2()

        def stage_a1(hti):
            t0 = hti["x0"]
            for k in range(KC):
                if k % 2 == 0:
                    self.act(HB[:, k, :], self.X[:, k, t0:t0 + 256], AF.Identity, bias=self.shift(1, k, g), scale=self.OPS[:, 1, k, g:g + 1])
                else:
                    self.ts(HB[:, k, :], self.X[:, k, t0:t0 + 256], self.OPS[:, 1, k, g:g + 1], self.shift(1, k, g), ALU.mult, ALU.add)
            for k in range(KC):
                self.mm(psq, W["q"][:, k, :], HB[:, k, :], k == 0, k == KC - 1)
            for k in range(KC):
                self.mm(psf, W["f"][:, k, :], HB[:, k, :], k == 0, k == KC - 1)
            if not hg:
                for k in range(KC):
                    self.mm(PS[3][:, 0:256], W["bz"][:, k, :], HB[:, k, :], k == 0, k == KC - 1)

        def stage_a2():
            for b in range(2):
                wv = W["i"] if hg else W["v"]
                for k in range(KC):
                    self.mm(PS[1][:, b * 256:b * 256 + vw], HB[:, k, b * 128:(b + 1) * 128], wv[:, k, :], k == 0, k == KC - 1)
            if dr == 1:
                for hh in heads:
                    for k in range(KC):
                        self.mm(PS[2][:, hh * 256:(hh + 1) * 256], W["g%d" % hh][:, k, :], HB[:, k, :], k == 0, k == KC - 1)

        def stage_b(gsi=0):
            if hg:
                self.act(F0, psf, AF.Tanh, scale=0.5)
                self.ts(F0, F0, self.OMLH[:, dr * 4 + u:dr * 4 + u + 1], self.LBH[:, dr * 4 + u:dr * 4 + u + 1], ALU.mult, ALU.add)
                self.ts(F1, F0, -1.0, 1.0, ALU.mult, ALU.add, eng="pool")
                kf = F1
            else:
                self.cp(BZ, PS[3][:, 0:256], eng="act")
                psl = PS[3][:, 256:512]
                self.mm(psl, self.GUP[64:128, j * 128:(j + 1) * 128], BZ[64:128, :], True, True)
                self.act(F0, psl, AF.Exp, scale=-1.0, bias=self.NGB[:, dr * 2 + j:dr * 2 + j + 1])
                self.act(F0, F0, AF.Ln, bias=self.ONEC[:, 0:1])
                self.act(F0, F0, AF.Exp, scale=-1.0 / GLA_TAU)
                kf = psf
            qs = None if hg else B_DK ** -0.5
            R1 = self.RESET[:, 0:256]
            if dr == 0:
                self.p.op("dve", lambda e: e.tensor_tensor_scan(out=F2, data0=R1, data1=F0, initial=1.0, op0=ALU.max, op1=ALU.mult),
                          reads=[R1, F0], writes=[F2])
                self.p.op("dve", lambda e: e.reciprocal(out=F3, in_=F2), reads=[F2], writes=[F3])
                self.tt(KI, kf, F3, ALU.mult)
                if hg:
                    self.tt(QD, psq, F2, ALU.mult)
                else:
                    self.stt(QD, psq, qs, F2, ALU.mult, ALU.mult)
                self.cp(AC, F2[:, 63:256:64])
            else:
                self.cp(F2[:, 1:256], F0[:, 0:255])
                self.memset(F2[:, 0:256:64], 1.0)
                self.p.op("dve", lambda e: e.tensor_tensor_scan(out=F3, data0=R1, data1=F2, initial=1.0, op0=ALU.max, op1=ALU.mult),
                          reads=[R1, F2], writes=[F3])
                self.tt(AC, F3[:, 63:256:64], F0[:, 63:256:64], ALU.mult)
                self.tt(KI, kf, F3, ALU.mult)
                self.p.op("dve", lambda e: e.reciprocal(out=F3, in_=F3), reads=[F3], writes=[F3])
                if hg:
                    self.tt(QD, psq, F3, ALU.mult)
                else:
                    self.stt(QD, psq, qs, F3, ALU.mult, ALU.mult)
            for b in range(2):
                if hg:
                    self.act(VT[:, b, 0:128], PS[1][:, b * 256:b * 256 + 128], AF.Silu)
                else:
                    self.cp(VT[:, b, :], PS[1][:, b * 256:(b + 1) * 256], eng="act")
            if dr == 1:
                for hh in heads:
                    self.act(GS[(hh + gsi) % 2], PS[2][:, hh * 256:(hh + 1) * 256], AF.Silu)

        def stage_c(PSO):
            pskv = []
            for idx, (b_, c_) in enumerate(cseq):
                bank = (PS[6] if c_ == 0 else PS[3]) if hg else (PS[6] if c_ == 0 else PS[1])
                w_ = 128 if hg else 256
                slot = 0 if idx < 2 else 1
                koff = 256 if (hg and c_ == 1) else 0
                pskv.append(bank[:, koff + slot * w_:koff + (slot + 1) * w_])
            first = [True, True]
            for hi, hh in enumerate(heads):
                pat = PS[5] if hi == 0 else PS[2 if dr == 0 else 3]
                pat_off = 0 if (hi == 0 or dr == 0) else 256
                for b in bs:
                    self.mm(pat[:, pat_off + b * 128:pat_off + (b + 1) * 128], KI[pr_(hh), b * 128:(b + 1) * 128],
                            QD[pr_(hh), b * 128:(b + 1) * 128], True, True)
                if hi == 0:
                    for b in bs:
                        self.tr(PS[3][:, b * 128:(b + 1) * 128], KI[:, b * 128:(b + 1) * 128].bitcast(F32))
                for b in bs:
                    self.tt(AT[b], pat[:, pat_off + b * 128:pat_off + (b + 1) * 128], MASK[:], ALU.mult)
                if hi == 0:
                    for b in bs:
                        self.cp(KIT[:, b, :], PS[3][:, b * 128:(b + 1) * 128], eng="act")
                    for idx, (b, c) in enumerate(cseq):
                        self.mm(pskv[idx], KIT[c * 64:(c + 1) * 64, b, :], VT[c * 64:(c + 1) * 64, b, 0:vw], True, True)
                for b in bs:
                    self.mm(PSO[hh][:, b * 128:(b + 1) * 128], VT[:, b, hh * 128:(hh + 1) * 128], AT[b], first[hh], False)
                    first[hh] = False
            return pskv

        def stage_d(pskv, item):
            if item["first"]:
                seg_init(item["seg"])
            states = []
            for idx, (b, c) in enumerate(cseq):
                ci = b * 2 + c
                a_c = AC[:, ci:ci + 1]
                if dr == 0:
                    S_cur = SB[st["si"] % NSB]
                    S_next = SB[(st["si"] + 1) % NSB]
                    states.append(S_cur)
                    if hg:
                        self.tt(TMP, pskv[idx], S_cur.bitcast(F32), ALU.add)
                    else:
                        self.tt(TMP[0:64, :], pskv[idx][0:64, 0:128], S_cur[0:64, :].bitcast(F32), ALU.add)
                        self.tt(TMP[64:128, :], pskv[idx][64:128, 128:256], S_cur[64:128, :].bitcast(F32), ALU.add)
                    self.ts(S_next, TMP, a_c, None, ALU.mult)
                    st["si"] += 1
                else:
                    C_cur = SC[st["sc"] % 2]
                    C_next = SC[(st["sc"] + 1) % 2]
                    S_sc = SB[st["si"] % NSB]
                    states.append(S_sc)
                    self.ts(S_sc, C_cur, a_c, None, ALU.mult)
                    if hg:
                        self.tt(C_next, pskv[idx], S_sc.bitcast(F32), ALU.add)
                    else:
                        self.tt(C_next[0:64, :], pskv[idx][0:64, 0:128], S_sc[0:64, :].bitcast(F32), ALU.add)
                        self.tt(C_next[64:128, :], pskv[idx][64:128, 128:256], S_sc[64:128, :].bitcast(F32), ALU.add)
                    st["si"] += 1
                    st["sc"] += 1
            if item["last"]:
                seg_emit(item["seg"])
            return states

        def stage_d2(states, PSO):
            for idx, (b, c) in enumerate(cseq):
                ccols = slice(b * 128 + c * 64, b * 128 + c * 64 + 64)
                for hh in heads:
                    self.mm(PSO[hh][:, ccols], states[idx][pr_(hh), :], QD[pr_(hh), ccols], False, idx == 3)

        E1 = self.f(2560, [128, 256])
        E2 = self.f(2816, [128, 256])

        def stage_e(hti, PSO, gsi, pst):
            lt0 = hti["lt0"]
            for hh in heads:
                head = u if hg else 4 + 2 * j + hh
                on = self.r(self.o_on + head * self.on_stride + lt0, [128, 256])
                pso = PSO[hh][:, 0:256]
                if dr == 0:
                    self.cp(on, pso, eng="act")
                else:
                    self.tt(E1, pso, on.bitcast(F32), ALU.add)
                    self.act(SQ, E1, AF.Square, scale=128.0 ** -0.5)
                    self.mm(pst, self.ONES[:], SQ, True, True)
                    self.act(E2, pst, AF.Ln, bias=self.EPSR[:, 0:1])
                    self.act(E2, E2, AF.Exp, scale=-0.5)
                    nw = self.PARS[:, 0:1] if hg else self.PARS[:, 1:2]
                    self.stt(E1, E1, nw, E2, ALU.mult, ALU.mult)
                    self.tt(on, E1, GS[(hh + gsi) % 2], ALU.mult)

        if hg:
            stage_a(order[0])
            prev = None
            for n_, hti in enumerate(order):
                PSOi = [PS[4] if n_ % 2 == 0 else PS[7]]
                stage_b(n_ % 2)
                if prev is not None:
                    stage_e(*prev)
                pskv = stage_c(PSOi)
                if n_ + 1 < len(order):
                    stage_a(order[n_ + 1], 1)
                states = stage_d(pskv, hti)
                stage_d2(states, PSOi)
                if n_ + 1 < len(order):
                    stage_a(order[n_ + 1], 2)
                prev = (hti, PSOi, n_ % 2, PS[2][:, 0:256])
            stage_e(*prev)
        else:
            PSOg = [PS[4], PS[7]]
            stage_a(order[0])
            for n_, hti in enumerate(order):
                stage_b(0)
                pskv = stage_c(PSOg)
                if n_ + 1 < len(order):
                    stage_a(order[n_ + 1], 1)
                states = stage_d(pskv, hti)
                stage_d2(states, PSOg)
                stage_e(hti, PSOg, 0, PS[5][:, 0:256])
                if n_ + 1 < len(order):
                    stage_a(order[n_ + 1], 2)

    def ab_outproj(self, l, x0, lt0, n, g):
        wo = self.dram["w_out_ab"].rearrange("(h p) o -> p h o", p=128)
        for half in range(2):
            WO = self.r(self.o_wab, [128, 8, 512])
            self.dma_r(WO, wo[:, :, half * 512:(half + 1) * 512], self.sem_wab[0])
            for o in range(4):
                for h in range(8):
                    on = self.r(self.o_on + h * self.on_stride + lt0, [128, 512])[:, 0:n]
                    self.mm(self.PS[o][:, 0:n], WO[:, h, o * 128:(o + 1) * 128], on, h == 0, h == 7)
            for o in range(4):
                oc = half * 4 + o
                xs_ = self.X[:, oc, x0:x0 + n]
                self.stt(xs_, self.PS[o][:, 0:n], self.GSC[:, 1, oc, g:g + 1], xs_, ALU.mult, ALU.add)
        self.ln_range(l * 3 + 1, x0, n, self.o_hb)

    def mixer_ab(self, l):
        self.o_on = 0
        self.on_stride = 2048
        self.o_wab = 16384
        self.o_hb = 23552
        o = 25600
        self.o_qd = o
        self.o_ki = o + 256
        self.o_vt = o + 512
        self.o_at = o + 1024
        self.o_sb = o + 1280
        assert self.o_sb + 768 <= self.NR
        self.ab_params()
        abn = int(os.environ.get("MK_ABN", "1000"))
        abskip = int(os.environ.get("MK_ABSKIP", "0"))
        cnt = 0
        self.wab_base = 0
        groups = (([(0, 1, 0, False, 0), (256, 1, 256, False, 1)], 0, 0, 512),
                  ([(512, 8, 0, True, None)], 1, 512, 2048))
        for (segs, g, gx0, ntok) in groups:
            for u in range(6):
                for dr in range(2):
                    cnt += 1
                    if cnt <= abskip or cnt > abskip + abn:
                        continue
                    self.ab_unit_pass(u, dr, segs, g)
            for off in range(0, ntok, 512):
                n = min(512, ntok - off)
                self.ab_outproj(l, gx0 + off, off, n, g)

    def c_consts(self):
        d = self.dram
        A = self.f(0, [128, 128])
        B = self.f(128, [128, 128])
        for (M, cm, st, base) in ((A, 1, -1, -16), (B, -1, 1, -16)):
            self.memset(M, 1.0)
            self.p.op("pool", lambda e: e.affine_select(out=M, in_=M, pattern=[[st, 128]], compare_op=ALU.is_equal,
                                                        fill=0.0, base=base, channel_multiplier=cm),
                      reads=[M], writes=[M])
        self.memset(A.rearrange("p (g c) -> p g c", c=32)[:, :, 16:32], 0.0)
        self.memset(B.rearrange("p (g c) -> p g c", c=32)[:, :, 0:16], 0.0)
        self.tt(self.PERMR[:], A, B, ALU.add)
        for (M, cm, st) in ((self.LM1, -1, 1), (self.LM3, 1, -1)):
            self.memset(M[:], 1.0)
            self.p.op("pool", lambda e: e.affine_select(out=M[:], in_=M[:], pattern=[[st, 128]], compare_op=ALU.is_ge,
                                                        fill=0.0, base=0, channel_multiplier=cm),
                      reads=[M[:]], writes=[M[:]])
        st_ = self.f(256, [128, 16])
        self.memset(st_, 0.0)
        self.dma(st_[0:1, :], d["sink"], self.sem_misc)
        ps = self.PS[7][:, 0:16]
        onesf = self.f(384, [128, 128])
        self.memset(onesf, 1.0)
        self.mm(ps, onesf, st_, True, True)
        self.act(self.ES[:], ps, AF.Exp)
        I32 = mybir.dt.int32
        pi_ = self.f(512, [128, 1]).bitcast(I32)
        self.p.op("pool", lambda e: e.iota(pi_, pattern=[[0, 1]], base=0, channel_multiplier=1), writes=[pi_])
        i16 = self.f(513, [128, 1]).bitcast(I32)
        self.ts(i16, pi_, 15, None, ALU.bitwise_and)
        m32 = self.f(514, [128, 1]).bitcast(I32)
        self.ts(m32, pi_, 32, None, ALU.bitwise_and)
        i16f = self.f(515, [128, 1])
        self.cp(i16f, i16)
        m32f = self.f(516, [128, 1])
        self.cp(m32f, m32)
        inv = self.f(517, [128, 1])
        self.act(inv, i16f, AF.Exp, scale=-float(np.log(ROPE_BASE)) / 16.0)
        b16 = self.f(518, [128, 1]).bitcast(I32)
        self.ts(b16, pi_, 16, None, ALU.bitwise_and)
        b16f = self.f(519, [128, 1])
        self.cp(b16f, b16)
        self.ts(self.SGN[:], b16f, 1.0 / 8.0, -1.0, ALU.mult, ALU.add)
        self.ts(self.MC[:], m32f, 1.0 / 32.0, None, ALU.mult)
        self.ts(self.MR[:], self.MC[:], -1.0, 1.0, ALU.mult, ALU.add)
        pos = self.f(640, [128, 64])
        self.p.op("pool", lambda e: e.iota(pos.bitcast(I32), pattern=[[1, 64]], base=0, channel_multiplier=0), writes=[pos])
        posf = self.f(704, [128, 64])
        self.cp(posf, pos.bitcast(I32))
        TWO_PI = 2.0 * float(np.pi)
        TRC = self.f(1024, [128, 64])
        TRS = self.f(1088, [128, 64])
        for (posoff, dsin, dcos) in ((0.0, self.TSIN[:], self.TCOS[:]), (-2.0, TRS, TRC)):
            ang = self.f(768, [128, 64])
            self.ts(ang, posf, posoff, None, ALU.add)
            self.ts(ang, ang, inv, None, ALU.mult)
            for (dst, shiftv) in ((dsin, 0.0), (dcos, float(np.pi) / 2.0)):
                a = self.f(832, [128, 64])
                kq = self.f(896, [128, 64])
                ki = self.f(960, [128, 64]).bitcast(I32)
                self.ts(a, ang, shiftv, None, ALU.add)
                self.ts(kq, a, 1.0 / TWO_PI, None, ALU.mult)
                self.cp(ki, kq)
                self.cp(kq, ki)
                self.stt(a, kq, -TWO_PI, a, ALU.mult, ALU.add)
                self.ts(kq, a, float(np.pi), -TWO_PI, ALU.is_gt, ALU.mult)
                self.tt(a, a, kq, ALU.add)
                self.ts(kq, a, -float(np.pi), TWO_PI, ALU.is_lt, ALU.mult)
                self.tt(a, a, kq, ALU.add)
                self.act(dst, a, AF.Sin)
        for (ci, T) in ((0, TRC), (1, TRS)):
            for jq in range(4):
                if jq == 0:
                    self.ts(self.TRW[:, ci, :], T[:, 0:12], self.SELV[:, 0:1], None, ALU.mult)
                else:
                    self.stt(self.TRW[:, ci, :], T[:, 8 * jq:8 * jq + 12], self.SELV[:, jq:jq + 1], self.TRW[:, ci, :], ALU.mult, ALU.add)

    def rope_tables_w(self, r0, nrows):
        n = nrows * 64
        cos_t = self.f(0, [128, 512])[:, 0:n]
        sin_t = self.f(512, [128, 512])[:, 0:n]
        for (dst, ci, T) in ((cos_t, 0, self.TCOS), (sin_t, 1, self.TSIN)):
            dv = dst.rearrange("p (r c) -> p r c", c=64)
            rowv = self.TRW[:, ci, r0:r0 + nrows].unsqueeze(2).to_broadcast([128, nrows, 64])
            colv = T[:, 0:64].unsqueeze(1).to_broadcast([128, nrows, 64])
            self.ts(dv, rowv, self.MR[:, 0:1], None, ALU.mult)
            self.stt(dv, colv, self.MC[:, 0:1], dv, ALU.mult, ALU.add)
        self.ts(sin_t, sin_t, self.SGN[:, 0:1], None, ALU.mult)
        return cos_t, sin_t

    def rope_tables(self, t):
        cos_t = self.f(0, [128, 512])
        sin_t = self.f(512, [128, 512])
        for (dst, T) in ((cos_t, self.TCOS), (sin_t, self.TSIN)):
            dv = dst.rearrange("p (r c) -> p r c", c=64)
            rowv = T[:, 8 * t:8 * t + 8].unsqueeze(2).to_broadcast([128, 8, 64])
            colv = T[:, 0:64].unsqueeze(1).to_broadcast([128, 8, 64])
            self.ts(dv, rowv, self.MR[:, 0:1], None, ALU.mult)
            self.stt(dv, colv, self.MC[:, 0:1], dv, ALU.mult, ALU.add)
        self.ts(sin_t, sin_t, self.SGN[:, 0:1], None, ALU.mult)
        return cos_t, sin_t

    def rope_apply(self, dst, ps, cos_t, sin_t, n):
        ZQ = self.r(self.o_zq, [128, 512])[:, 0:n]
        self.cp(ZQ, ps, eng="act")
        pz = self.PS[6][:, 0:n]
        self.mm(pz, self.PERMR[:], ZQ, True, True)
        t1 = self.f(1024, [128, 512])[:, 0:n]
        self.tt(t1, ZQ.bitcast(F32), cos_t[:, 0:n], ALU.mult)
        t2 = self.f(1536, [128, 512])[:, 0:n]
        self.tt(t2, pz, sin_t[:, 0:n], ALU.mult)
        self.tt(dst, t1, t2, ALU.add)

    def c_modulate(self, x0, n, g):
        HB = self.r(self.o_hb, [128, KC, 512])
        for k in range(KC):
            self.ts(HB[:, k, 0:n], self.X[:, k, x0:x0 + n], self.OPS[:, 1, k, g:g + 1], self.shift(1, k, g), ALU.mult, ALU.add)
        return HB

    def va_slot(self, kap):
        return [(0, 0, 64), (64, 2, 0), (130, 0, 64), (194, 2, 0)][kap]

    def c_kv_project(self, HB, n, WKV, kf_dst, va_dst, vblk0, rope, emit=None):
        for kp in range(2):
            ps = self.PS[kp][:, 0:n]
            for k in range(KC):
                self.mm(ps, WKV[:, k, kp * 128:(kp + 1) * 128], HB[:, k, 0:n], k == 0, k == KC - 1)
            if rope is not None:
                self.rope_apply(kf_dst(kp), ps, rope[0], rope[1], n)
            else:
                self.cp(kf_dst(kp), ps, eng="act")
        for b in range(n // 128):
            ps = self.PS[2 + b % 2][:, 0:256]
            for k in range(KC):
                self.mm(ps, HB[:, k, b * 128:(b + 1) * 128], WKV[:, k, 256:512], k == 0, k == KC - 1)
            va = va_dst(vblk0 + b)
            self.cp(va[:, 0:132].rearrange("p (a c) -> p a c", c=66)[:, :, 0:64], ps[:, 0:128].rearrange("p (a c) -> p a c", c=64), eng="act")
            self.cp(va[:, 130:262].rearrange("p (a c) -> p a c", c=66)[:, :, 0:64], ps[:, 128:256].rearrange("p (a c) -> p a c", c=64), eng="act")
            self.ts(va[:, 64:66], self.RESET[:, 1:3], 0.0, 1.0, ALU.mult, ALU.add)
            self.ts(va[:, 194:196], self.RESET[:, 1:3], 0.0, 1.0, ALU.mult, ALU.add)
            if emit is not None:
                sq, tok0 = emit
                stv = self.f(2048 + (b % 2) * 256, [128, 256])
                self.cp(stv, ps)
                self.dma(self.dram["nv"][sq, tok0 + b * 128:tok0 + (b + 1) * 128, :], stv, self.p.named_sem("nv%d" % (b % 2)))
                ps2 = self.PS[4 + b % 2][:, 0:256]
                for k in range(KC):
                    self.mm(ps2, HB[:, k, b * 128:(b + 1) * 128], WKV[:, k, 0:256], k == 0, k == KC - 1)
                stk = self.f(2560 + (b % 2) * 256, [128, 256])
                self.cp(stk, ps2, eng="act")
                self.dma(self.dram["nk"][sq, tok0 + b * 128:tok0 + (b + 1) * 128, :], stk, self.p.named_sem("nk%d" % (b % 2)))

    def c_q_project(self, HB, n, rope):
        wq = self.dram["w_qkv"].rearrange("(k p) n -> p k n", p=128)
        QF = self.r(self.o_qf, [128, 8, 512])
        WS = self.r(self.o_ws, [128, KC, 4, 128])
        for grp in range(2):
            for s_ in range(2):
                for i in range(4):
                    c0 = grp * 512 + s_ * 256 + i * 64
                    self.dma_r(WS[:, :, i, s_ * 64:(s_ + 1) * 64], wq[:, :, c0:c0 + 64], self.p.named_sem("wq%d_%d" % (s_, i)))
            for i in range(4):
                pair = grp * 4 + i
                ps = self.PS[pair % 2][:, 0:n]
                for k in range(KC):
                    self.mm(ps, WS[:, k, i, :], HB[:, k, 0:n], k == 0, k == KC - 1)
                if rope is not None:
                    self.rope_apply(QF[:, pair, 0:n], ps, rope[0], rope[1], n)
                else:
                    self.cp(QF[:, pair, 0:n], ps, eng="act")
        return QF

    def c_attend(self, QF, nqb, keysets):
        OT = self.r(self.o_ot, [128, 4, 1024])
        NPT = 4
        PT = [self.r(self.o_ws + i * 512, [128, 512]) for i in range(NPT)]
        SCB = [self.PS[0], self.PS[1], self.PS[7]]
        tasks = []
        for h in range(C_HEADS):
            for ki, ks in enumerate(keysets):
                tasks.append((h, ki, ks))
        nk_for = [sum(1 for ks in keysets if ks[2] <= qb <= ks[3]) for qb in range(nqb)]

        def hinfo(h):
            kap = h // 4
            half = kap % 2
            pair = (h % 4) + (0 if h < 8 else 4)
            return kap, half, pair, slice(64 * half, 64 * half + 64)

        def score(i):
            h, ki, (kf_fn, va_fn, qlo, qhi, masks, halo) = tasks[i]
            kap, half, pair, hp = hinfo(h)
            ncol = (qhi - qlo + 1) * 128
            ps = SCB[i % 3][:, 0:ncol]
            pt = PT[i % NPT][:, 0:ncol]
            self.mm(ps, kf_fn(kap, half), QF[hp, pair, qlo * 128:qlo * 128 + ncol], True, True)
            self.act(pt, ps, AF.Exp, scale=HD ** -0.5)
            for qb in range(qlo, qhi + 1):
                if qb in masks:
                    sl = slice((qb - qlo) * 128, (qb - qlo + 1) * 128)
                    if halo is None:
                        self.tt(pt[:, sl], pt[:, sl].bitcast(F32), masks[qb][:], ALU.mult)
                    else:
                        self.stt(pt[:, sl], pt[:, sl].bitcast(F32), halo, masks[qb][:], ALU.mult, ALU.mult)

        seen = {}

        def pv(i):
            h, ki, (kf_fn, va_fn, qlo, qhi, masks, halo) = tasks[i]
            kap, half, pair, hp = hinfo(h)
            c0, o_off, d_off = self.va_slot(kap)
            ncol = (qhi - qlo + 1) * 128
            pt = PT[i % NPT][:, 0:ncol]
            for qb in range(qlo, qhi + 1):
                sl = slice((qb - qlo) * 128, (qb - qlo + 1) * 128)
                seen[(h, qb)] = seen.get((h, qb), 0) + 1
                self.mm(self.PS[2 + qb][:, 0:66], pt[:, sl], va_fn(kap)[:, c0:c0 + 66], seen[(h, qb)] == 1, seen[(h, qb)] == nk_for[qb])
            if ki == len(keysets) - 1:
                for qb in range(nqb):
                    po = self.PS[2 + qb]
                    rd = self.f(2048 + 16 * qb, [128, 1])
                    self.ts(rd, po[:, d_off:d_off + 1], self.ES[:, h:h + 1], None, ALU.add)
                    self.p.op("dve", lambda e: e.reciprocal(out=rd, in_=rd), reads=[rd], writes=[rd])
                    self.ts(OT[:, qb, h * 64:(h + 1) * 64], po[:, o_off:o_off + 64], rd, None, ALU.mult)

        LOOK = 2
        for i in range(min(LOOK, len(tasks))):
            score(i)
        for i in range(len(tasks)):
            if i + LOOK < len(tasks):
                score(i + LOOK)
            pv(i)
        return OT

    def c_outproj(self, l, OT, x0, nqb, g):
        n = nqb * 128
        OA = self.r(self.o_hb, [128, KC, 512])
        for qb in range(nqb):
            for hb in range(2):
                ps = self.PS[hb]
                for j in range(4):
                    c = hb * 4 + j
                    self.tr(ps[:, j * 128:(j + 1) * 128], OT[:, qb, c * 128:(c + 1) * 128].bitcast(F32))
                self.cp(OA[:, hb * 4:(hb + 1) * 4, qb * 128:(qb + 1) * 128], ps[:].rearrange("p (a b) -> p a b", a=4),
                        eng="act" if hb else "dve")
        wo = self.dram["w_out_c"].rearrange("(c p) o -> p c o", p=128)
        for half in range(2):
            WO = self.r(self.o_qf, [128, KC, 512])
            self.dma_r(WO, wo[:, :, half * 512:(half + 1) * 512], self.p.named_sem("woc"))
            for o in range(4):
                for c in range(KC):
                    self.mm(self.PS[4 + o][:, 0:n], WO[:, c, o * 128:(o + 1) * 128], OA[:, c, 0:n], c == 0, c == KC - 1)
            for o in range(4):
                oc = half * 4 + o
                xs_ = self.X[:, oc, x0:x0 + n]
                self.stt(xs_, self.PS[4 + o][:, 0:n], self.GSC[:, 1, oc, g:g + 1], xs_, ALU.mult, ALU.add)
        self.ln_range(l * 3 + 1, x0, n, self.o_ws)

    def mixer_c(self, l):
        d = self.dram
        self.o_kf = 0
        self.o_va = 4096
        self.o_kc = self.o_va + 16 * 262
        self.o_vca = self.o_kc + 1024
        self.o_hb = self.o_vca + 4 * 262
        self.o_ws = self.o_hb + 4096
        self.o_qf = self.o_ws + 4096
        self.o_ot = self.o_qf + 4096
        self.o_zq = self.o_ot + 4096
        assert self.o_zq + 512 <= self.NR, self.o_zq
        self.c_consts()
        KF = self.r(self.o_kf, [128, 2, 2048])
        VA = self.r(self.o_va, [128, 16, 262])
        KCF = self.r(self.o_kc, [128, 2, 512])
        VCA = self.r(self.o_vca, [128, 4, 262])
        wq = d["w_qkv"].rearrange("(k p) n -> p k n", p=128)
        WKV = self.r(self.o_ws, [128, KC, 512])

        def load_wkv():
            self.dma_r(WKV, wq[:, :, 1024:1536], self.p.named_sem("wkv"))
        for sq in range(2):
            x0 = sq * SEQ
            HB = self.c_modulate(x0, SEQ, 0)
            load_wkv()
            self.c_kv_project(HB, SEQ, WKV, lambda kp: KF[:, kp, 0:SEQ], lambda b: VA[:, b, :], 0, None, emit=(sq, 0))
            QF = self.c_q_project(HB, SEQ, None)
            keysets = []
            for kb in range(2):
                keysets.append((lambda kap, half, kb=kb: KF[64 * half:64 * half + 64, kap // 2, kb * 128:(kb + 1) * 128],
                                lambda kap, kb=kb: VA[:, kb, :], 0, 1, {}, None))
            OT = self.c_attend(QF, 2, keysets)
            self.c_outproj(l, OT, x0, 2, 0)
        stg = self.r(self.o_ot, [128, 4, 256])
        self.dma_r(stg, d["ck"].rearrange("(b p) c -> p b c", p=128), self.p.named_sem("ck"))
        for b in range(4):
            for kp in range(2):
                ps = self.PS[(b * 2 + kp) % 2][:, 0:128]
                self.tr(ps, stg[:, b, kp * 128:(kp + 1) * 128].bitcast(F32))
                self.cp(KCF[:, kp, b * 128:(b + 1) * 128], ps, eng="act" if kp else "dve")
        cvv = d["cv"].rearrange("(b p) c -> p b c", p=128)
        for kap in range(4):
            c0 = [0, 66, 130, 196][kap]
            self.dma_r(VCA[:, :, c0:c0 + 64], cvv[:, :, kap * 64:(kap + 1) * 64], self.p.named_sem("cv%d" % kap))
        for b in range(4):
            self.ts(VCA[:, b, 64:66], self.RESET[:, 1:3], 0.0, 1.0, ALU.mult, ALU.add)
            self.ts(VCA[:, b, 194:196], self.RESET[:, 1:3], 0.0, 1.0, ALU.mult, ALU.add)
        load_wkv()
        W0 = self.W0
        for (off, n_, r0) in ((0, 512, 0), (512, 256, 8)):
            HB = self.c_modulate(W0 + off, n_, 1)
            rope = self.rope_tables_w(r0, n_ // 64)
            self.c_kv_project(HB, n_, WKV, lambda kp, off=off, n_=n_: KF[:, kp, off:off + n_], lambda b: VA[:, b, :], off // 128, rope)
        HB = self.c_modulate(W0 + 128, 512, 1)
        rope = self.rope_tables_w(2, 8)
        QF = self.c_q_project(HB, 512, rope)
        keysets = []
        for cb in range(4):
            keysets.append((lambda kap, half, cb=cb: KCF[64 * half:64 * half + 64, kap // 2, cb * 128:(cb + 1) * 128],
                            lambda kap, cb=cb: VCA[:, cb, :], 0, 3, {}, None))
        for kb in range(6):
            qlo = max(1, kb - 1) - 1
            qhi = min(4, kb + 1) - 1
            masks = {}
            if 0 <= kb - 2 <= 3:
                masks[kb - 2] = self.LM1
            if 0 <= kb <= 3:
                masks[kb] = self.LM3
            halo = None
            if kb == 0:
                halo = self.SELV[:, 12:13]
            if kb == 5:
                halo = self.SELV[:, 13:14]
            keysets.append((lambda kap, half, kb=kb: KF[64 * half:64 * half + 64, kap // 2, kb * 128:(kb + 1) * 128],
                            lambda kap, kb=kb: VA[:, kb, :], qlo, qhi, masks, halo))
        OT = self.c_attend(QF, 4, keysets)
        self.c_outproj(l, OT, W0 + 128, 4, 1)

    def build(self):
        self.o_w13 = 0
        self.o_w2 = 8192
        self.o_xm = 12288
        self.o_hid = 16384
        self.o_ln = self.o_hid + 18 * 512
        self.o_sg = 0
        self.o_lnf = 1024
        self.consts()
        self.EPSLN = self.sb("EPSLN", [128, 1])
        self.memset(self.EPSLN[:], LN_EPS / (ALPHA * ALPHA))
        self.ONEC = self.sb("ONEC", [128, 1])
        self.memset(self.ONEC[:], 1.0)
        self.load_small()
        self.dma(self.SELV[:], self.dram["selv"], self.p.named_sem("selv"))
        self.load_x()
        self.W0 = TP
        full = [[(0, 512, 0), (512, 512, 1)], [(1024, 512, 1), (1536, 512, 1)], [(2048, 512, 1)]]
        win = [[(0, 512, 0), (self.W0, 512, 1)], [(self.W0 + 512, 256, 1)]]
        own = [[(0, 512, 0), (self.W0 + 128, 512, 1)]]
        for l in range(DEPTH):
            self.mod_vectors(l)
            for subs in (full if l == 0 else win):
                self.ffn_multi(l, 0, subs)
            if self.stage == 1 + 3 * l:
                break
            if l == 0:
                self.mixer_ab(l)
                self.select_window()
            else:
                self.mixer_c(l)
            if self.stage == 2 + 3 * l:
                break
            for subs in (win if l == 0 else own):
                self.ffn_multi(l, 1, subs)
            if self.stage == 3 + 3 * l:
                break
        self.store_x()
        self.p.finish("sp")
        return self.nc


_CACHE = {}


def get_program(stage=99):
    if stage not in _CACHE:
        _CACHE[stage] = Builder(stage).build()
    return _CACHE[stage]


def _selv(j):
    v = np.zeros((16,), np.float32)
    v[j] = 1.0
    if j > 0:
        v[4 + j - 1] = 1.0
        v[12] = 1.0
    if j < 3:
        v[8 + j + 1] = 1.0
        v[13] = 1.0
    return np.ascontiguousarray(np.broadcast_to(v, (128, 16))).astype(np.float32)


def shard_inputs(inp):
    f = lambda a: np.ascontiguousarray(np.asarray(a, dtype=np.float32))
    maps = []
    for c in range(NCORES):
        sb = c // 4
        m = {
            "xp": f(inp["x_prompt"][2 * c:2 * c + 2].reshape(TP, D)),
            "xs": f(inp["x_sample"][sb]),
            "st_h": f(inp["state_hgrn"][sb, 0]),
            "st_g": f(inp["state_gla"][sb, 0]),
            "ck": f(inp["cache_k"][sb, 0].reshape(PAST, C_KV * HD)),
            "cv": f(inp["cache_v"][sb, 0].reshape(PAST, C_KV * HD)),
            "cvec": f(np.stack([inp["c_ctx"], inp["c"][sb]], axis=0)),
            "selv": _selv(c % 4),
            "w_mod": f(inp["w_mod"]),
            "b_mod": f(inp["b_mod"]),
            "ln_g": f(inp["ln_g"].reshape(DEPTH * 3, D)),
            "ln_b": f(inp["ln_b"].reshape(DEPTH * 3, D)),
            "ffn_w1": f(inp["ffn_w1"]),
            "ffn_w3": f(inp["ffn_w3"]),
            "ffn_w2": f(inp["ffn_w2"]),
            "w_in_ab": f(inp["w_in_ab"][0]),
            "hgrn_lb": f(inp["hgrn_lb"]),
            "gate_up": f(inp["gla_gate_up"][0]),
            "gate_b": f(inp["gla_gate_b"][0]),
            "norm_a": f(inp["norm_a"]),
            "norm_b": f(inp["norm_b"]),
            "w_out_ab": f(inp["w_out_ab"][0]),
            "w_qkv": f(inp["w_qkv_c"][0]),
            "sink": f(inp["sink_c"]),
            "w_out_c": f(inp["w_out_c"][0]),
        }
        maps.append(m)
    return maps


def gather_outputs(res):
    r = res.results
    B = 16
    yp = np.concatenate([r[c]["yp"].reshape(2, SEQ, D) for c in range(NCORES)], axis=0)
    ys = np.stack([np.concatenate([r[4 * b + q]["ys"] for q in range(4)], axis=0) for b in range(2)], axis=0)
    nsh = np.concatenate([r[c]["ns_h"].reshape(2, 1, 2, A_HEADS, 128, 128) for c in range(NCORES)], axis=0)
    nsg = np.concatenate([r[c]["ns_g"].reshape(2, 1, 2, B_HEADS, B_DK, 128) for c in range(NCORES)], axis=0)
    nk = np.concatenate([r[c]["nk"].reshape(2, 1, SEQ, C_KV, HD) for c in range(NCORES)], axis=0)
    nv = np.concatenate([r[c]["nv"].reshape(2, 1, SEQ, C_KV, HD) for c in range(NCORES)], axis=0)
    return (yp.astype(np.float32), ys.astype(np.float32), nsh.astype(np.float32), nsg.astype(np.float32),
            nk.astype(np.float32), nv.astype(np.float32))


def kernel(**inputs):
    stage = int(os.environ.get("MK_STAGE", "99"))
    nc = get_program(stage)
    maps = shard_inputs(inputs)
    res = run_bass_kernel_spmd(nc, maps, core_ids=list(range(NCORES)))
    return gather_outputs(res)
```
